# Optimizing a Trainium2 kernel written in Bass

```python
import math
import jax
import jax.numpy as jnp
from jax import lax
import numpy as np

D_MODEL = 1024
BATCH = 4
SEQ = 4096
DEPTH = 2

CTX_LEN = 256
GRID_W = 64
EPS = 1e-6

SSD_HEADS = 16
SSD_HEAD_DIM = 64
SSD_WIDTH = SSD_HEADS * SSD_HEAD_DIM
SSD_GROUPS = 2
SSD_STATE = 128
SSD_CONV = 5
SSD_CHUNK = 128
SSD_XBC = SSD_WIDTH + 2 * SSD_GROUPS * SSD_STATE
DT_MIN = 0.001
DT_MAX = 0.1

ATT_HEADS = 8
ATT_KV_HEADS = 2
ATT_HEAD_DIM = 64
ATT_WIDTH = ATT_HEADS * ATT_HEAD_DIM
ATT_KV_WIDTH = ATT_KV_HEADS * ATT_HEAD_DIM
WINDOW = 128
ATT_BLOCK = 128
ROPE_BASE = 10000.0

CONV_WIDTH = 512
CONV_KERNEL = 31

MIX_WIDTH = SSD_WIDTH + ATT_WIDTH + CONV_WIDTH
IN_SIZES = (SSD_XBC, SSD_WIDTH, 2 * SSD_HEADS, ATT_WIDTH, ATT_KV_WIDTH, ATT_KV_WIDTH, ATT_WIDTH, 2 * CONV_WIDTH, CONV_WIDTH)
IN_WIDTH = sum(IN_SIZES)

kernel_name = 'hybrid_ssd_swa_conformer_prefix_block'


def rmsnorm(x, w):
    xf = x.astype(jnp.float32)
    y = xf * lax.rsqrt(jnp.mean(xf * xf, axis=-1, keepdims=True) + EPS)
    return (y * w.astype(jnp.float32)).astype(x.dtype)


def group_rmsnorm(x, w, groups):
    shp = x.shape
    xf = x.astype(jnp.float32).reshape(shp[:-1] + (groups, shp[-1] // groups))
    xf = xf * lax.rsqrt(jnp.mean(xf * xf, axis=-1, keepdims=True) + EPS)
    return (xf.reshape(shp) * w.astype(jnp.float32)).astype(x.dtype)


def layernorm(x, w, b):
    xf = x.astype(jnp.float32)
    xc = xf - jnp.mean(xf, axis=-1, keepdims=True)
    var = jnp.mean(xc * xc, axis=-1, keepdims=True)
    return (xc * lax.rsqrt(var + EPS) * w.astype(jnp.float32) + b.astype(jnp.float32)).astype(x.dtype)


def split_cols(u):
    parts, off = [], 0
    for size in IN_SIZES:
        parts.append(u[..., off:off + size])
        off += size
    return parts


def dwconv_centred(x, w, b):
    k, ch = w.shape
    pad = (k - 1) // 2
    y = lax.conv_general_dilated(x, w[:, None, :].astype(x.dtype), window_strides=(1,), padding=[(pad, pad)],
                                 dimension_numbers=('NWC', 'WIO', 'NWC'), feature_group_count=ch)
    return y + b.astype(x.dtype)


def axial_rope_tables(n_tokens):
    rows = n_tokens // GRID_W
    row = jnp.repeat(jnp.arange(rows, dtype=jnp.float32), GRID_W)
    col = jnp.tile(jnp.arange(GRID_W, dtype=jnp.float32), rows)
    axis_dim = ATT_HEAD_DIM // 2
    inv_freq = ROPE_BASE ** (-jnp.arange(0, axis_dim, 2, dtype=jnp.float32) / axis_dim)
    ang = jnp.stack([row[:, None] * inv_freq, col[:, None] * inv_freq], axis=1)
    return jnp.cos(ang), jnp.sin(ang)


def apply_axial_rope(t, cos, sin):
    shp = t.shape
    tr = t.astype(jnp.float32).reshape(shp[:-1] + (2, 2, ATT_HEAD_DIM // 4))
    t1, t2 = tr[..., 0, :], tr[..., 1, :]
    cs, sn = cos[None, :, None], sin[None, :, None]
    out = jnp.stack([t1 * cs - t2 * sn, t2 * cs + t1 * sn], axis=-2)
    return out.reshape(shp).astype(t.dtype)


def ssd_chunked_scan(x, dt, a, b_in, c_in, init_state):
    bsz, n, nh, hp = x.shape
    ng, ns = b_in.shape[-2:]
    rep = nh // ng
    q = SSD_CHUNK
    nc = n // q
    f32 = jnp.float32
    xd = (x.astype(f32) * dt[..., None]).reshape(bsz, nc, q, ng, rep, hp)
    da = (dt * a).reshape(bsz, nc, q, ng, rep)
    bc = b_in.astype(f32).reshape(bsz, nc, q, ng, ns)
    cc = c_in.astype(f32).reshape(bsz, nc, q, ng, ns)
    cs = jnp.moveaxis(jnp.cumsum(da, axis=2), 2, -1)
    tril = jnp.tril(jnp.ones((q, q), dtype=bool))
    seg = cs[..., :, None] - cs[..., None, :]
    decay = jnp.exp(jnp.where(tril, seg, -jnp.inf))
    cb = jnp.einsum('bclgn,bcsgn->bcgls', cc, bc)
    y_diag = jnp.einsum('bcgrls,bcsgrp->bclgrp', cb[:, :, :, None] * decay, xd)
    decay_to_end = jnp.exp(cs[..., -1:] - cs)
    states = jnp.einsum('bclgn,bcgrl,bclgrp->bcgrpn', bc, decay_to_end, xd)
    chunk_decay = jnp.exp(cs[..., -1])

    def step(s, inp):
        st, dec = inp
        return s * dec[..., None, None] + st, s

    s0 = init_state.astype(f32).reshape(bsz, ng, rep, hp, ns)
    final, entering = lax.scan(step, s0, (jnp.moveaxis(states, 1, 0), jnp.moveaxis(chunk_decay, 1, 0)))
    entering = jnp.moveaxis(entering, 0, 1)
    y_off = jnp.einsum('bclgn,bcgrpn,bcgrl->bclgrp', cc, entering, jnp.exp(cs))
    y = (y_diag + y_off).reshape(bsz, n, nh, hp)
    return y, final.reshape(bsz, nh, hp, ns)


def ssd_bidirectional(xs, dt, bm, cm, a, init_f, init_b):
    y_f, fin_f = ssd_chunked_scan(xs, dt[:, :, 0], a[0], bm, cm, init_f)
    rev = lambda t: jnp.flip(t, axis=1)
    y_b, fin_b = ssd_chunked_scan(rev(xs), rev(dt[:, :, 1]), a[1], rev(bm), rev(cm), init_b)
    return y_f + rev(y_b), fin_f, fin_b


def window_attention(q, k, v, kc, vc, sink):
    bsz, seq = q.shape[:2]
    nb = seq // ATT_BLOCK
    rep = ATT_HEADS // ATT_KV_HEADS
    scale = ATT_HEAD_DIM ** -0.5
    n_ctx = kc.shape[1]
    qb = q.reshape(bsz, nb, ATT_BLOCK, ATT_KV_HEADS, rep, ATT_HEAD_DIM)

    def band(t):
        tp = jnp.pad(t, ((0, 0), (ATT_BLOCK, ATT_BLOCK), (0, 0), (0, 0)))
        tp = tp.reshape(bsz, nb + 2, ATT_BLOCK, ATT_KV_HEADS, ATT_HEAD_DIM)
        return jnp.concatenate([tp[:, :-2], tp[:, 1:-1], tp[:, 2:]], axis=2)

    kw, vw = band(k), band(v)
    s_win = jnp.einsum('bnqgrd,bnkgd->bngrqk', qb, kw).astype(jnp.float32) * scale
    qpos = jnp.arange(nb)[:, None] * ATT_BLOCK + jnp.arange(ATT_BLOCK)[None]
    kpos = jnp.arange(nb)[:, None] * ATT_BLOCK - ATT_BLOCK + jnp.arange(3 * ATT_BLOCK)[None]
    valid = ((jnp.abs(qpos[:, :, None] - kpos[:, None, :]) <= WINDOW)
             & (kpos[:, None, :] >= 0) & (kpos[:, None, :] < seq))
    s_win = jnp.where(valid[None, :, None, None], s_win, jnp.finfo(jnp.float32).min)
    s_ctx = jnp.einsum('bnqgrd,btgd->bngrqt', qb, kc).astype(jnp.float32) * scale
    sink_col = jnp.broadcast_to(sink.astype(jnp.float32).reshape(1, 1, ATT_KV_HEADS, rep, 1, 1), s_ctx.shape[:-1] + (1,))
    p = jax.nn.softmax(jnp.concatenate([sink_col, s_ctx, s_win], axis=-1), axis=-1).astype(v.dtype)
    out = (jnp.einsum('bngrqt,btgd->bnqgrd', p[..., 1:1 + n_ctx], vc)
           + jnp.einsum('bngrqk,bnkgd->bnqgrd', p[..., 1 + n_ctx:], vw))
    return out.reshape(bsz, seq, ATT_WIDTH)


def context_attention(qc, kc, vc, sink):
    bsz, n_ctx = qc.shape[:2]
    rep = ATT_HEADS // ATT_KV_HEADS
    qg = qc.reshape(bsz, n_ctx, ATT_KV_HEADS, rep, ATT_HEAD_DIM)
    s = jnp.einsum('bqgrd,btgd->bgrqt', qg, kc).astype(jnp.float32) * ATT_HEAD_DIM ** -0.5
    sink_col = jnp.broadcast_to(sink.astype(jnp.float32).reshape(1, ATT_KV_HEADS, rep, 1, 1), s.shape[:-1] + (1,))
    p = jax.nn.softmax(jnp.concatenate([sink_col, s], axis=-1), axis=-1).astype(vc.dtype)
    out = jnp.einsum('bgrqt,btgd->bqgrd', p[..., 1:], vc)
    return out.reshape(bsz, n_ctx, ATT_WIDTH)


def conformer_conv(u, dw_w, dw_b, ln_w, ln_b, pw_w, pw_b):
    val, gt = u[..., :CONV_WIDTH], u[..., CONV_WIDTH:]
    hdn = val * jax.nn.sigmoid(gt)
    hdn = dwconv_centred(hdn, dw_w, dw_b)
    hdn = jax.nn.silu(layernorm(hdn, ln_w, ln_b))
    return hdn @ pw_w + pw_b


def hybrid_layer(x, xc, c, c_ctx, w_mod, b_mod, norm_w, w_in, ssd_conv_w, ssd_conv_b, ssd_dt_bias, ssd_a_log,
                 ssd_d, ssd_norm_w, attn_sink, conv_dw_w, conv_dw_b, conv_ln_w, conv_ln_b, conv_pw_w, conv_pw_b,
                 w_out, update_ctx):
    bsz, seq = x.shape[:2]
    n_ctx = xc.shape[1]
    f32 = jnp.float32
    shift, scale, gate = jnp.split((jax.nn.silu(c) @ w_mod + b_mod)[:, None, :], 3, axis=-1)
    shift_c, scale_c, gate_c = jnp.split(jax.nn.silu(c_ctx) @ w_mod + b_mod, 3, axis=-1)
    h = rmsnorm(x, norm_w) * (1 + scale) + shift
    hc = rmsnorm(xc, norm_w) * (1 + scale_c) + shift_c
    p_lat = split_cols(h @ w_in)
    p_ctx = split_cols(hc @ w_in)

    a = -jnp.exp(ssd_a_log.astype(f32))

    def ssd_inputs(xbc, dt_raw):
        xbc = jax.nn.silu(dwconv_centred(xbc, ssd_conv_w, ssd_conv_b))
        n = xbc.shape[1]
        gn = SSD_GROUPS * SSD_STATE
        xs = xbc[..., :SSD_WIDTH].reshape(bsz, n, SSD_HEADS, SSD_HEAD_DIM)
        bm = xbc[..., SSD_WIDTH:SSD_WIDTH + gn].reshape(bsz, n, SSD_GROUPS, SSD_STATE)
        cm = xbc[..., SSD_WIDTH + gn:].reshape(bsz, n, SSD_GROUPS, SSD_STATE)
        dt = jax.nn.softplus(dt_raw.astype(f32).reshape(bsz, n, 2, SSD_HEADS) + ssd_dt_bias.astype(f32))
        return xs, dt, bm, cm

    def ssd_out(y, xs, z):
        n = xs.shape[1]
        y = y + ssd_d.astype(f32)[:, None] * xs.astype(f32)
        y = y.reshape(bsz, n, SSD_WIDTH).astype(z.dtype)
        return group_rmsnorm(y * jax.nn.silu(z), ssd_norm_w, SSD_GROUPS)

    zero = jnp.zeros((bsz, SSD_HEADS, SSD_HEAD_DIM, SSD_STATE), f32)
    xs_c, dt_c, bm_c, cm_c = ssd_inputs(p_ctx[0], p_ctx[2])
    y_c, fin_f, fin_b = ssd_bidirectional(xs_c, dt_c, bm_c, cm_c, a, zero, zero)
    xs_l, dt_l, bm_l, cm_l = ssd_inputs(p_lat[0], p_lat[2])
    y_l, _, _ = ssd_bidirectional(xs_l, dt_l, bm_l, cm_l, a, fin_f, fin_b)
    o_ssd = ssd_out(y_l, xs_l, p_lat[1])

    cos, sin = axial_rope_tables(seq)
    q = apply_axial_rope(p_lat[3].reshape(bsz, seq, ATT_HEADS, ATT_HEAD_DIM), cos, sin)
    k = apply_axial_rope(p_lat[4].reshape(bsz, seq, ATT_KV_HEADS, ATT_HEAD_DIM), cos, sin)
    v = p_lat[5].reshape(bsz, seq, ATT_KV_HEADS, ATT_HEAD_DIM)
    kc = p_ctx[4].reshape(bsz, n_ctx, ATT_KV_HEADS, ATT_HEAD_DIM)
    vc = p_ctx[5].reshape(bsz, n_ctx, ATT_KV_HEADS, ATT_HEAD_DIM)
    o_att = window_attention(q, k, v, kc, vc, attn_sink) * jax.nn.silu(p_lat[6])

    o_conv = conformer_conv(p_lat[7], conv_dw_w, conv_dw_b, conv_ln_w, conv_ln_b, conv_pw_w, conv_pw_b) * jax.nn.silu(p_lat[8])

    x = x + gate * (jnp.concatenate([o_ssd, o_att, o_conv], axis=-1) @ w_out)

    if update_ctx:
        qc = p_ctx[3].reshape(bsz, n_ctx, ATT_HEADS, ATT_HEAD_DIM)
        o_ssd_c = ssd_out(y_c, xs_c, p_ctx[1])
        o_att_c = context_attention(qc, kc, vc, attn_sink) * jax.nn.silu(p_ctx[6])
        o_conv_c = conformer_conv(p_ctx[7], conv_dw_w, conv_dw_b, conv_ln_w, conv_ln_b, conv_pw_w, conv_pw_b) * jax.nn.silu(p_ctx[8])
        xc = xc + gate_c * (jnp.concatenate([o_ssd_c, o_att_c, o_conv_c], axis=-1) @ w_out)
    return x, xc


def setup_inputs(seed: int = 0) -> dict:
    key = jax.random.key(seed)
    ks = jax.random.split(key, 24)
    f32 = jnp.float32

    def nrm(k, shape, s):
        return jax.random.normal(k, shape, f32) * s

    u = jax.random.uniform(ks[10], (DEPTH, 2, SSD_HEADS), f32)
    dt0 = jnp.exp(u * (math.log(DT_MAX) - math.log(DT_MIN)) + math.log(DT_MIN))
    return {
        'x': nrm(ks[0], (BATCH, SEQ, D_MODEL), 1.0),
        'c': nrm(ks[1], (BATCH, D_MODEL), 1.0),
        'ctx': nrm(ks[2], (BATCH, CTX_LEN, D_MODEL), 1.0),
        'c_ctx': nrm(ks[3], (D_MODEL,), 1.0),
        'w_mod': nrm(ks[4], (DEPTH, D_MODEL, 3 * D_MODEL), 0.5 * D_MODEL ** -0.5),
        'b_mod': nrm(ks[5], (DEPTH, 3 * D_MODEL), 0.02),
        'norm_w': 1.0 + nrm(ks[6], (DEPTH, D_MODEL), 0.02),
        'w_in': nrm(ks[7], (DEPTH, D_MODEL, IN_WIDTH), D_MODEL ** -0.5),
        'ssd_conv_w': nrm(ks[8], (DEPTH, SSD_CONV, SSD_XBC), SSD_CONV ** -0.5),
        'ssd_conv_b': nrm(ks[9], (DEPTH, SSD_XBC), 0.02),
        'ssd_dt_bias': dt0 + jnp.log(-jnp.expm1(-dt0)),
        'ssd_a_log': jnp.log(jax.random.uniform(ks[11], (DEPTH, 2, SSD_HEADS), f32, 1.0, 16.0)),
        'ssd_d': 1.0 + nrm(ks[12], (DEPTH, SSD_HEADS), 0.1),
        'ssd_norm_w': 1.0 + nrm(ks[13], (DEPTH, SSD_WIDTH), 0.02),
        'attn_sink': nrm(ks[14], (DEPTH, ATT_HEADS), 0.5),
        'conv_dw_w': nrm(ks[15], (DEPTH, CONV_KERNEL, CONV_WIDTH), CONV_KERNEL ** -0.5),
        'conv_dw_b': nrm(ks[16], (DEPTH, CONV_WIDTH), 0.02),
        'conv_ln_w': 1.0 + nrm(ks[17], (DEPTH, CONV_WIDTH), 0.02),
        'conv_ln_b': nrm(ks[18], (DEPTH, CONV_WIDTH), 0.02),
        'conv_pw_w': nrm(ks[19], (DEPTH, CONV_WIDTH, CONV_WIDTH), CONV_WIDTH ** -0.5),
        'conv_pw_b': nrm(ks[20], (DEPTH, CONV_WIDTH), 0.02),
        'w_out': nrm(ks[21], (DEPTH, MIX_WIDTH, D_MODEL), MIX_WIDTH ** -0.5),
        'final_norm_w': 1.0 + nrm(ks[22], (D_MODEL,), 0.02),
    }


def reference(x, c, ctx, c_ctx, w_mod, b_mod, norm_w, w_in, ssd_conv_w, ssd_conv_b, ssd_dt_bias, ssd_a_log,
              ssd_d, ssd_norm_w, attn_sink, conv_dw_w, conv_dw_b, conv_ln_w, conv_ln_b, conv_pw_w, conv_pw_b,
              w_out, final_norm_w):
    xc = ctx
    for l in range(DEPTH):
        x, xc = hybrid_layer(x, xc, c, c_ctx, w_mod[l], b_mod[l], norm_w[l], w_in[l], ssd_conv_w[l], ssd_conv_b[l],
                             ssd_dt_bias[l], ssd_a_log[l], ssd_d[l], ssd_norm_w[l], attn_sink[l], conv_dw_w[l],
                             conv_dw_b[l], conv_ln_w[l], conv_ln_b[l], conv_pw_w[l], conv_pw_b[l], w_out[l],
                             l < DEPTH - 1)
    return rmsnorm(x, final_norm_w)
```

```python
import numpy as np
from contextlib import ExitStack, contextmanager
import concourse.bass as bass
import concourse.mybir as mybir
from concourse.bass_utils import run_bass_kernel_spmd

F32 = mybir.dt.float32
BF16 = mybir.dt.bfloat16
AF = mybir.ActivationFunctionType
ALU = mybir.AluOpType
NDMA = 48
EPS = 1e-6
D = 1024
NEG = -30000.0


class Tok:
    __slots__ = ("sem", "val", "key")

    def __init__(self, sem, val, key):
        self.sem, self.val, self.key = sem, val, key


class Buf:
    __slots__ = ("w", "r", "multi")

    def __init__(self, multi=False):
        self.w = {}
        self.r = {}
        self.multi = multi


class KB:
    def __init__(self, nc, es):
        self.nc = nc
        self.eng = {"pe": nc.tensor, "act": nc.scalar, "dve": nc.vector, "pool": nc.gpsimd, "sp": nc.sync}
        self.sem = {e: es.enter_context(nc.semaphore("s_" + e)) for e in self.eng}
        self.cnt = {e: 0 for e in self.eng}
        self.known = {e: {} for e in self.eng}
        self.dsem = [es.enter_context(nc.semaphore("d%d" % i)) for i in range(NDMA)]
        self.dval = [0] * NDMA
        self.dnext = 0
        self.pend = {e: [] for e in self.eng}
        self.bufs = {}
        self.ccsem = es.enter_context(nc.semaphore("ccsem"))
        self.ccval = 0
        self.ninst = 0
        self.nmm = 0
        self.marks = []

    def buf(self, ap, key=None):
        t = ap.tensor
        name = t.name
        dram = str(ap.space).lower().find("dram") >= 0 or str(ap.space).lower().find("hbm") >= 0
        k = (name, key if dram else None)
        b = self.bufs.get(k)
        if b is None:
            b = Buf(multi=dram)
            self.bufs[k] = b
        return b

    def _wait(self, e, tok):
        if e == "pe" and tok.key == "pe":
            return
        k = self.known[e]
        if k.get(tok.key, 0) >= tok.val:
            return
        self.eng[e].wait_ge(tok.sem, tok.val)
        k[tok.key] = tok.val

    def _deps(self, e, reads, writes):
        for b in reads:
            for t in b.w.values():
                self._wait(e, t)
        for b in writes:
            if not b.multi:
                for t in b.w.values():
                    self._wait(e, t)
            for t in b.r.values():
                self._wait(e, t)

    def _record(self, tok, reads, writes):
        for b in reads:
            b.r[tok.key] = tok
        for b in writes:
            if b.multi:
                b.w[tok.key] = tok
            else:
                b.w = {tok.key: tok}
                b.r = {}

    def op(self, e, fn, rd, wr, signal=True, key=None):
        reads = [self.buf(a, key) for a in rd if a is not None and not isinstance(a, (int, float))]
        writes = [self.buf(a, key) for a in wr if a is not None]
        self._deps(e, reads, writes)
        ins = fn(self.eng[e])
        self.ninst += 1
        if not signal:
            self.pend[e].append((reads, writes))
            return
        self.cnt[e] += 1
        ins.then_inc(self.sem[e], 1)
        tok = Tok(self.sem[e], self.cnt[e], e)
        for (r, w) in self.pend[e]:
            self._record(tok, r, w)
        self.pend[e] = []
        self._record(tok, reads, writes)

    def DMA(self, q, out, in_, key=None, **kw):
        reads = [self.buf(in_, key)]
        writes = [self.buf(out, key)]
        self._deps(q, reads, writes)
        i = self.dnext
        self.dnext = (i + 1) % NDMA
        k = "d%d" % i
        if self.dval[i] > 0:
            self._wait(q, Tok(self.dsem[i], self.dval[i], k))
        ins = self.eng[q].dma_start(out=out, in_=in_, **kw)
        self.ninst += 1
        self.dval[i] += 16
        ins.then_inc(self.dsem[i], 16)
        tok = Tok(self.dsem[i], self.dval[i], k)
        self._record(tok, reads, writes)

    def CC(self, in_ap, out_ap, groups):
        reads = [self.buf(in_ap)]
        writes = [self.buf(out_ap)]
        self._deps("pool", reads, writes)
        ins = self.eng["pool"].collective_compute("AllGather", ALU.bypass, replica_groups=groups, ins=[in_ap], outs=[out_ap])
        self.ninst += 1
        self.ccval += 1
        ins.then_inc(self.ccsem)
        tok = Tok(self.ccsem, self.ccval, "cc")
        self._record(tok, reads, writes)

    def barrier(self):
        assert all(len(p) == 0 for p in self.pend.values())
        toks = [Tok(self.sem[e], self.cnt[e], e) for e in self.eng if self.cnt[e] > 0]
        if self.ccval:
            toks.append(Tok(self.ccsem, self.ccval, "cc"))
        toks += [Tok(self.dsem[i], self.dval[i], "d%d" % i) for i in range(NDMA) if self.dval[i]]
        for e in self.eng:
            for t in toks:
                if t.key != e:
                    self._wait(e, t)

    def finish(self):
        for i in range(NDMA):
            if self.dval[i]:
                self._wait("sp", Tok(self.dsem[i], self.dval[i], "d%d" % i))

    def ACT(self, out, in_, func, bias=None, scale=None, accum=None):
        kw = {}
        if bias is not None:
            kw["bias"] = bias
        if scale is not None:
            kw["scale"] = scale
        if accum is not None:
            kw["accum_out"] = accum
        self.op("act", lambda e: e.activation(out=out, in_=in_, func=func, **kw), [in_, bias, scale], [out, accum])

    def TT(self, e, out, in0, in1, op):
        self.op(e, lambda g: g.tensor_tensor(out=out, in0=in0, in1=in1, op=op), [in0, in1], [out])

    def TS(self, e, out, in0, s1, s2, op0, op1=None):
        if op1 is None:
            self.op(e, lambda g: g.tensor_scalar(out=out, in0=in0, scalar1=s1, scalar2=None, op0=op0), [in0, s1], [out])
        else:
            self.op(e, lambda g: g.tensor_scalar(out=out, in0=in0, scalar1=s1, scalar2=s2, op0=op0, op1=op1), [in0, s1, s2], [out])

    def STT(self, out, in0, scalar, in1, op0, op1, accum=None):
        if accum is None:
            self.op("dve", lambda g: g.scalar_tensor_tensor(out=out, in0=in0, scalar=scalar, in1=in1, op0=op0, op1=op1), [in0, scalar, in1], [out])
        else:
            self.op("dve", lambda g: g.scalar_tensor_tensor(out=out, in0=in0, scalar=scalar, in1=in1, op0=op0, op1=op1, accum_out=accum),
                    [in0, scalar, in1], [out, accum])

    def CP(self, e, out, in_):
        if e == "act":
            self.ACT(out, in_, AF.Copy)
        else:
            self.op(e, lambda g: g.tensor_copy(out=out, in_=in_), [in_], [out])

    def RECIP(self, out, in_):
        self.op("dve", lambda g: g.reciprocal(out=out, in_=in_), [in_], [out])

    def MSET(self, e, out, val):
        self.op(e, lambda g: g.memset(out, val), [], [out])

    def mark(self, label):
        self.marks.append((label, self.nmm))

    def MM(self, out, lhsT, rhs, start, stop, signal=None):
        self.nmm += 1
        self.op("pe", lambda g: g.matmul(out, lhsT, rhs, start=start, stop=stop), [lhsT, rhs], [out],
                signal=(stop if signal is None else signal))

    def TR(self, out, in_, ident, signal=True):
        self.nmm += 1
        self.op("pe", lambda g: g.transpose(out=out, in_=in_, identity=ident), [in_, ident], [out], signal=signal)


@contextmanager
def scope(kb):
    with ExitStack() as es:
        yield es
        kb.barrier()


def pipe(n, stages):
    K = len(stages)
    for t in range(n + K - 1):
        for k in range(K):
            i = t - k
            if 0 <= i < n:
                stages[k](i)


def bc(ap, shape):
    return ap.unsqueeze(2).to_broadcast(shape)


def build(L, T, NL=2, npairs=4):
    nc = bass.Bass("TRN2", target_bir_lowering=False)
    Ltot = T + L
    NCH = Ltot // 128
    seqs = [dict(name="c", off=0, n=T, xoff=2, goff=15), dict(name="l", off=T, n=L, xoff=T + 6, goff=T + 45)]
    Wx = Ltot + 8
    Wg = Ltot + 60

    def din(name, shape, dt=F32):
        return nc.dram_tensor(name, list(shape), dt, kind="ExternalInput").ap()

    def dsc(name, shape, dt):
        return nc.dram_tensor(name, list(shape), dt, kind="Internal").ap()

    x_in = din("x", [L, D])
    ctx_in = din("ctx", [T, D])
    c_in = din("c", [128, 8])
    cc_in = din("c_ctx", [128, 8])
    w_mod = din("w_mod", [NL, D, 3 * D])
    b_mod = din("b_mod", [NL, 3 * D])
    norm_w = din("norm_w", [NL, D])
    w_in = din("w_in", [NL, D, 5408])
    ssd_conv_w = din("ssd_conv_w", [NL, 5, 1536])
    ssd_conv_b = din("ssd_conv_b", [NL, 1536])
    ssd_dt_bias = din("ssd_dt_bias", [NL, 32])
    ssd_a_log = din("ssd_a_log", [NL, 32])
    ssd_d = din("ssd_d", [NL, 16])
    ssd_norm_w = din("ssd_norm_w", [NL, 1024])
    attn_sink = din("attn_sink", [NL, 8])
    conv_dw_w = din("conv_dw_w", [NL, 31, 512])
    conv_dw_b = din("conv_dw_b", [NL, 512])
    conv_ln_w = din("conv_ln_w", [NL, 512])
    conv_ln_b = din("conv_ln_b", [NL, 512])
    conv_pw_w = din("conv_pw_w", [NL, 512, 512])
    conv_pw_b = din("conv_pw_b", [NL, 512])
    w_out = din("w_out", [NL, 2048, D])
    final_norm_w = din("final_norm_w", [1, D])
    cst = din("cst", [128, 7, 128])
    sel_in = din("sel", [128, 2])
    groups = [[2 * i, 2 * i + 1] for i in range(npairs)]
    rope = din("rope", [Ltot, 256])
    out = nc.dram_tensor("out", [L, D], F32, kind="ExternalOutput").ap()

    XR = dsc("XR", [Ltot, D], F32)
    XBCT = dsc("XBCT", [1536, Wx], BF16)
    GLUT = dsc("GLUT", [512, Wg], BF16)
    GCT = dsc("GCT", [512, Ltot], BF16)
    ZS = dsc("ZS", [Ltot, 1024], BF16)
    GAT = dsc("GAT", [512, Ltot], BF16)
    DTS = dsc("DTS", [Ltot, 32], F32)
    DTT = dsc("DTT", [32, Ltot], F32)
    QT = dsc("QT", [64, 8, Ltot], BF16)
    KT = dsc("KT", [64, 2, Ltot], BF16)
    VV = dsc("VV", [Ltot, 130], BF16)
    BCT = dsc("BCT", [128, 4, Ltot], BF16)
    XS = dsc("XS", [Ltot, 1024], BF16)
    BTM = dsc("BTM", [Ltot, 256], BF16)
    ABR = dsc("ABR", [2, 2, 6, NCH, 2048], BF16)
    SBD = dsc("SBD", [NCH, 128, 1024], BF16)
    OT = dsc("OT", [2048, Ltot], BF16)
    MOD = dsc("MOD", [NL, 2, 3 * D], F32)
    SFD = dsc("SFD", [NCH, 128, 1024], BF16)
    SND1 = dsc("SND1", [128, 470], BF16)
    RCV1 = nc.dram_tensor("RCV1", [256, 470], BF16, kind="Internal", addr_space="Local").ap()
    SND2 = dsc("SND2", [128, 1024], F32)
    RCV2 = nc.dram_tensor("RCV2", [256, 1024], F32, kind="Internal", addr_space="Local").ap()

    with ExitStack() as es0:
        kb = KB(nc, es0)

        uid = [0]

        def SB(es, name, shape, dt):
            uid[0] += 1
            return es.enter_context(nc.sbuf_tensor("%s_%d" % (name, uid[0]), list(shape), dt))

        def PS(es, name, shape, dt):
            uid[0] += 1
            return es.enter_context(nc.psum_tensor("%s_%d" % (name, uid[0]), list(shape), dt))

        cf = SB(es0, "cf", [128, 7, 128], F32)
        selB = SB(es0, "selB", [128, 2], F32)
        kb.DMA("sp", selB[:], sel_in[:, :])
        kb.DMA("sp", cf[:], cst[:, :, :])
        identf = cf[:, 0, :]
        triU = cf[:, 1, :]
        triL = cf[:, 2, :]
        onesf = cf[:, 3, :]
        cb_ = SB(es0, "cb", [128, 7, 128], BF16)
        kb.CP("dve", cb_[:], cf[:])
        identb = cb_[:, 0, :]
        onesb = cb_[:, 3, :]
        mneg = SB(es0, "mneg", [128, 3, 4, 128], BF16)
        for d_ in range(3):
            for r_ in range(4):
                kb.CP("dve", mneg[:, d_, r_, :], cf[:, 4 + d_, :])
        ones512 = SB(es0, "ones512", [128, 128], BF16)
        kb.MSET("dve", ones512[:], 1.0 / 512.0)
        zpad = SB(es0, "zpad", [128, 64], BF16)
        kb.MSET("dve", zpad[:], 0.0)
        for s in seqs:
            for side in range(2):
                cx = s["xoff"] - 2 if side == 0 else s["xoff"] + s["n"]
                kb.DMA("sp", XBCT[:, cx:cx + 2].rearrange("(b p) w -> p b w", p=128), zpad[:, 0:24].rearrange("p (b w) -> p b w", w=2), key=s["name"])
                cg = s["goff"] - 15 if side == 0 else s["goff"] + s["n"]
                kb.DMA("sp", GLUT[:, cg:cg + 15].rearrange("(b p) w -> p b w", p=128), zpad[:, 0:60].rearrange("p (b w) -> p b w", w=15), key=s["name"])
        onesrow = SB(es0, "onesrow", [128, 2048], BF16)
        kb.MSET("pool", onesrow[:], 1.0)
        for s in seqs:
            c0_, n0_ = s["off"] // 128, s["n"] // 128
            for d_ in range(2):
                for r_ in range(3):
                    kb.DMA("sp", ABR[d_, 0, 3 + r_, c0_:c0_ + n0_, :], onesrow[0:n0_, :], key=s["name"])
                    kb.DMA("sp", ABR[d_, 1, r_, c0_:c0_ + n0_, :], onesrow[0:n0_, :], key=s["name"])
        epsb = SB(es0, "epsb", [128, 1], F32)
        kb.MSET("pool", epsb[:], EPS)

        for l in range(NL):
            last = (l == NL - 1)
            with scope(kb) as esl:
                kb.mark("setup%d" % l)
                modB = SB(esl, "modB", [128, 2, 3, D], F32)
                aB = SB(esl, "aB", [128, 32], F32)
                dtbB = SB(esl, "dtbB", [128, 32], F32)
                snwB = SB(esl, "snwB", [128, 1024], F32)
                sinkB = SB(esl, "sinkB", [128, 8], F32)
                colsT = SB(esl, "colsT", [128, 12, 41], F32)
                dI = SB(esl, "dI", [128, 16, 128], BF16)
                nbr1 = SB(esl, "nbr1", [128, 470], BF16)
                nx = nbr1[:, 0:24].rearrange("p (b w) -> p b w", w=2)
                ng = nbr1[:, 24:84].rearrange("p (b w) -> p b w", w=15)
                with scope(kb) as ess:
                    pss = PS(ess, "pss", [128, 512], F32)
                    csil = SB(ess, "csil", [128, 2, 8], F32)
                    kb.DMA("sp", csil[:, 0, :], cc_in[:, :])
                    kb.DMA("sp", csil[:, 1, :], c_in[:, :])
                    kb.ACT(csil[:], csil[:], AF.Silu)
                    bm = SB(ess, "bm", [1, 3 * D], F32)
                    kb.DMA("sp", bm[:], b_mod[l:l + 1, :])
                    modrow = SB(ess, "modrow", [1, 2, 3 * D], F32)
                    wm = [SB(ess, "wm%d" % i, [128, 8, 512], F32) for i in range(2)]
                    for cbk in range(6):
                        w_ = wm[cbk % 2]
                        kb.DMA("sp", w_[:], w_mod[l, :, cbk * 512:(cbk + 1) * 512].rearrange("(p k) n -> p k n", k=8))
                        for m in range(2):
                            for kt in range(8):
                                kb.MM(pss[0:1, :], csil[:, m, kt:kt + 1], w_[:, kt, :], start=(kt == 0), stop=(kt == 7))
                            kb.TT("dve", modrow[:, m, cbk * 512:(cbk + 1) * 512], pss[0:1, :], bm[:, cbk * 512:(cbk + 1) * 512], ALU.add)
                    kb.DMA("sp", MOD[l:l + 1, :, :], modrow[:])
                    nwB = SB(ess, "nwB", [128, D], F32)
                    kb.DMA("sp", nwB[:], norm_w[l, :].partition_broadcast(128))
                    for m in range(2):
                        kb.DMA("sp", modB[:, m, :, :], MOD[l, m, :].rearrange("(t d) -> t d", t=3).partition_broadcast(128))
                        tmpw = SB(ess, "tmpw%d" % m, [128, D], F32)
                        kb.STT(tmpw[:], modB[:, m, 1, :], 1.0, nwB[:], ALU.add, ALU.mult)
                        kb.CP("dve", modB[:, m, 1, :], modB[:, m, 0, :])
                        kb.CP("dve", modB[:, m, 0, :], tmpw[:])
                    kb.DMA("sp", aB[:], ssd_a_log[l, :].partition_broadcast(128))
                    kb.ACT(aB[:], aB[:], AF.Exp)
                    kb.TS("dve", aB[:], aB[:], -1.0, None, ALU.mult)
                    kb.DMA("sp", dtbB[:], ssd_dt_bias[l, :].partition_broadcast(128))
                    kb.DMA("sp", snwB[:], ssd_norm_w[l, :].partition_broadcast(128))
                    kb.DMA("sp", sinkB[:], attn_sink[l, :].partition_broadcast(128))
                    kb.ACT(sinkB[:], sinkB[:], AF.Exp)
                    dB = SB(ess, "dB", [128, 16], F32)
                    kb.DMA("sp", dB[:], ssd_d[l, :].partition_broadcast(128))
                    for h in range(16):
                        kb.TS("dve", dI[:, h, :], identf, dB[:, h:h + 1], None, ALU.mult)
                    rows = SB(ess, "rows", [41, 1536], F32)
                    kb.MSET("dve", rows[:], 0.0)
                    kb.DMA("sp", rows[0:5, :], ssd_conv_w[l, :, :])
                    kb.DMA("sp", rows[5:6, :], ssd_conv_b[l:l + 1, :])
                    kb.DMA("sp", rows[6:37, 0:512], conv_dw_w[l, :, :])
                    kb.DMA("sp", rows[37:38, 0:512], conv_dw_b[l:l + 1, :])
                    kb.DMA("sp", rows[38:39, 0:512], conv_ln_w[l:l + 1, :])
                    kb.DMA("sp", rows[39:40, 0:512], conv_ln_b[l:l + 1, :])
                    kb.DMA("sp", rows[40:41, 0:512], conv_pw_b[l:l + 1, :])
                    for b in range(12):
                        kb.TR(pss[:, b * 41:(b + 1) * 41], rows[:, b * 128:(b + 1) * 128], identf[0:41, 0:41], signal=(b == 11))
                    kb.CP("dve", colsT[:].rearrange("p b r -> p (b r)"), pss[:, 0:492])

                kb.mark("A%d" % l)
                sts = []
                for s in seqs:
                    TS_ = 512 if s["name"] == "l" else 256
                    for t0 in range(0, s["n"], TS_):
                        sts.append((s, t0, TS_))
                with scope(kb) as esa:
                    wfm = SB(esa, "wfm", [128, 8, 3584], BF16)
                    wtm = SB(esa, "wtm", [128, 8, 1824], BF16)
                    wv = w_in[l].rearrange("(k p) n -> p k n", p=128)
                    with scope(kb) as esw:
                        wst = [SB(esw, "wst%d" % i, [128, 5408], F32) for i in range(2)]
                        ce = ["dve", "pool", "act"]
                        nce = 0
                        for kt in range(8):
                            w_ = wst[kt % 2]
                            kb.DMA("sp", w_[:, 0:2704], wv[:, kt, 0:2704])
                            kb.DMA("sp", w_[:, 2704:5408], wv[:, kt, 2704:5408])
                            segs = [(wfm, 0, 0, 768), (wfm, 768, 768, 768), (wfm, 2560, 4896, 512),
                                    (wtm, 0, 1536, 512), (wtm, 512, 2048, 512), (wtm, 1024, 2592, 512), (wtm, 1536, 3104, 256),
                                    (wtm, 1792, 2560, 32), (wfm, 3072, 3360, 512)]
                            for i in range(4):
                                segs.append((wfm, 1536 + 256 * i, 3872 + 128 * i, 128))
                                segs.append((wfm, 1536 + 256 * i + 128, 4384 + 128 * i, 128))
                            for (dstt, d0, s0, n_) in segs:
                                kb.CP(ce[nce % 3], dstt[:, kt, d0:d0 + n_], w_[:, s0:s0 + n_])
                                nce += 1
                    xin = [SB(esa, "xin%d" % i, [128, D], F32) for i in range(3)]
                    hb = [SB(esa, "hb%d" % i, [128, D], BF16) for i in range(2)]
                    junk = SB(esa, "junk", [128, D], BF16)
                    tmpf = SB(esa, "tmpf", [128, D], F32)
                    hT = [SB(esa, "hT%d" % i, [128, 8, 512], BF16) for i in range(2)]
                    st4 = SB(esa, "st4", [128, 8], F32)
                    fst = [SB(esa, "fst%d" % i, [128, 512], BF16) for i in range(3)]
                    sig = SB(esa, "sig", [128, 512], F32)
                    zs = [SB(esa, "zs%d" % i, [128, 1024], BF16) for i in range(2)]
                    qk = [SB(esa, "qk%d" % i, [128, 640], BF16) for i in range(2)]
                    vst = [SB(esa, "vst%d" % i, [128, 2, 65], BF16) for i in range(2)]
                    for i in range(2):
                        kb.MSET("pool", vst[i][:], 1.0)
                    qkT = [SB(esa, "qkT%d" % i, [64, 10, 128], BF16) for i in range(2)]
                    dtw = [SB(esa, "dtw%d" % i, [128, 6, 32], F32) for i in range(2)]
                    dtT = [SB(esa, "dtT%d" % i, [32, 128], F32) for i in range(2)]
                    rp = [SB(esa, "rp%d" % i, [128, 256], F32) for i in range(2)]
                    rt = SB(esa, "rt", [128, 2, 640], F32)
                    pT = [PS(esa, "pT%d" % i, [128, 1024], BF16) for i in range(2)]
                    ppool = [PS(esa, "pp%d" % i, [128, 512], F32) for i in range(6)]
                    subs = [(si, j) for si, (s, t0, TS_) in enumerate(sts) for j in range(TS_ // 128)]
                    cntA = dict(fm=0, tm=0, sc=0)

                    def a_load(k):
                        si, j = subs[k]
                        s, t0, TS_ = sts[si]
                        if l == 0:
                            src = (ctx_in if s["name"] == "c" else x_in)[t0 + j * 128:t0 + (j + 1) * 128, :]
                        else:
                            src = XR[s["off"] + t0 + j * 128:s["off"] + t0 + (j + 1) * 128, :]
                        kb.DMA("sp", xin[k % 3][:], src, key=(s["name"], t0 + j * 128))

                    def a_norm(k):
                        si, j = subs[k]
                        s, t0, TS_ = sts[si]
                        m = 0 if s["name"] == "c" else 1
                        xi, hbj = xin[k % 3], hb[k % 2]
                        col = st4[:, k % 8:k % 8 + 1]
                        kb.ACT(junk[:], xi[:], AF.Square, accum=col)
                        kb.TS("dve", col, col, 1.0 / D, EPS, ALU.mult, ALU.add)
                        kb.ACT(col, col, AF.Sqrt)
                        kb.RECIP(col, col)
                        kb.STT(tmpf[:], xi[:], col, modB[:, m, 0, :], ALU.mult, ALU.mult)
                        kb.TT("pool", hbj[:], tmpf[:], modB[:, m, 1, :], ALU.add)

                    def a_transp(k):
                        si, j = subs[k]
                        hTt, p_, hbj = hT[si % 2], pT[k % 2], hb[k % 2]
                        for kt in range(8):
                            kb.TR(p_[:, kt * 128:(kt + 1) * 128], hbj[:, kt * 128:(kt + 1) * 128], identb, signal=(kt == 7))
                        kb.CP("act" if k % 2 else "dve", hTt[:, :, j * 128:(j + 1) * 128], p_[:].rearrange("p (k t) -> p k t", k=8))

                    def a_fmblock(si, cbk):
                        s, t0, TS_ = sts[si]
                        hTt = hT[si % 2]
                        p_ = ppool[cntA["tm"] % 6]
                        cntA["tm"] += 1
                        for kt in range(8):
                            kb.MM(p_[:, 0:TS_], wfm[:, kt, cbk * 128:(cbk + 1) * 128], hTt[:, kt, 0:TS_], start=(kt == 0), stop=(kt == 7))
                        return p_

                    def a_fmunit(si, u):
                        s, t0, TS_ = sts[si]
                        g0 = s["off"] + t0
                        if u < 12:
                            p_ = a_fmblock(si, u)
                            f_ = fst[u % 3]
                            kb.CP("dve", f_[:, 0:TS_], p_[:, 0:TS_])
                            kb.DMA("sp", XBCT[u * 128:(u + 1) * 128, s["xoff"] + t0:s["xoff"] + t0 + TS_], f_[:, 0:TS_], key=s["name"])
                        elif u < 16:
                            i = u - 12
                            pv = a_fmblock(si, 12 + 2 * i)
                            pg = a_fmblock(si, 13 + 2 * i)
                            kb.ACT(sig[:, 0:TS_], pg[:, 0:TS_], AF.Sigmoid)
                            f_ = fst[i % 3]
                            kb.TT("dve", f_[:, 0:TS_], pv[:, 0:TS_], sig[:, 0:TS_], ALU.mult)
                            kb.DMA("sp", GLUT[i * 128:(i + 1) * 128, s["goff"] + t0:s["goff"] + t0 + TS_], f_[:, 0:TS_], key=s["name"])
                        else:
                            i = u - 16
                            p_ = a_fmblock(si, 20 + i)
                            f_ = fst[i % 3]
                            kb.ACT(f_[:, 0:TS_], p_[:, 0:TS_], AF.Silu)
                            dstT = GCT if i < 4 else GAT
                            kb.DMA("sp", dstT[(i % 4) * 128:(i % 4 + 1) * 128, g0:g0 + TS_], f_[:, 0:TS_], key=s["name"])

                    def a_tmsub(si, j):
                        s, t0, TS_ = sts[si]
                        hTt = hT[si % 2]
                        gt = s["off"] + t0 + j * 128
                        sl = cntA["sc"] % 2
                        cntA["sc"] += 1

                        def tm_block(c0, c1):
                            p_ = ppool[cntA["tm"] % 6]
                            cntA["tm"] += 1
                            for kt in range(8):
                                kb.MM(p_[:, 0:c1 - c0], hTt[:, kt, j * 128:(j + 1) * 128], wtm[:, kt, c0:c1], start=(kt == 0), stop=(kt == 7))
                            return p_
                        kb.DMA("sp", rp[sl][:], rope[gt:gt + 128, :])
                        for i in range(2):
                            p_ = tm_block(i * 512, (i + 1) * 512)
                            kb.ACT(zs[sl][:, i * 512:(i + 1) * 512], p_[:], AF.Silu)
                        kb.DMA("sp", ZS[gt:gt + 128, :], zs[sl][:], key=s["name"])
                        pq = tm_block(1024, 1536)
                        pk = tm_block(1536, 1824)
                        for (p_, nh, o0, tb) in ((pq, 8, 0, 128), (pk, 2, 512, 0)):
                            srcv = p_[:, 0:nh * 64]
                            a_ = rt[:, 0, 0:nh * 64]
                            b_ = rt[:, 1, 0:nh * 64]
                            cos2 = rp[sl][:, tb:tb + 64].unsqueeze(1).to_broadcast([128, nh, 64])
                            kb.TT("dve", a_.rearrange("p (h d) -> p h d", h=nh), srcv.rearrange("p (h d) -> p h d", h=nh), cos2, ALU.mult)
                            s4 = srcv.rearrange("p (h a x f) -> p h a x f", h=nh, a=2, x=2)
                            b4 = b_.rearrange("p (h a x f) -> p h a x f", h=nh, a=2, x=2)
                            sm = rp[sl][:, tb + 64:tb + 96].rearrange("p (a f) -> p a f", a=2).unsqueeze(1).to_broadcast([128, nh, 2, 16])
                            spl = rp[sl][:, tb + 96:tb + 128].rearrange("p (a f) -> p a f", a=2).unsqueeze(1).to_broadcast([128, nh, 2, 16])
                            kb.TT("dve", b4[:, :, :, 0, :], s4[:, :, :, 1, :], sm, ALU.mult)
                            kb.TT("dve", b4[:, :, :, 1, :], s4[:, :, :, 0, :], spl, ALU.mult)
                            kb.TT("pool", qk[sl][:, o0:o0 + nh * 64], a_, b_, ALU.add)
                        kb.CP("act", vst[sl][:, :, 0:64], pk[:, 128:256].rearrange("p (g d) -> p g d", g=2))
                        kb.DMA("sp", VV[gt:gt + 128, :], vst[sl][:].rearrange("p g d -> p (g d)"), key=s["name"])
                        dw = dtw[sl]
                        kb.TT("dve", dw[:, 0, :], pk[:, 256:288], dtbB[:], ALU.add)
                        kb.ACT(dw[:, 1, :], dw[:, 0, :], AF.Abs)
                        kb.ACT(dw[:, 2, :], dw[:, 1, :], AF.Exp, scale=-1.0)
                        kb.ACT(dw[:, 3, :], dw[:, 2, :], AF.Ln, bias=1.0)
                        kb.STT(dw[:, 5, :], dw[:, 0, :], 0.0, dw[:, 3, :], ALU.max, ALU.add)
                        kb.TS("dve", dw[:, 4, :], dw[:, 5, :], 1e-30, None, ALU.max)
                        kb.DMA("sp", DTS[gt:gt + 128, :], dw[:, 4, :], key=s["name"])
                        return (s, gt, sl, j)

                    def a_tmsub2(arg):
                        s, gt, sl, j = arg
                        dw = dtw[sl]
                        p_ = pT[j % 2]
                        for h in range(8):
                            kb.TR(p_[0:64, h * 128:(h + 1) * 128], qk[sl][:, h * 64:(h + 1) * 64], identb, signal=(h == 7))
                        kb.CP("act", qkT[sl][:, 0:8, :], p_[0:64, :].rearrange("p (h t) -> p h t", h=8))
                        kb.DMA("sp", QT[:, :, gt:gt + 128], qkT[sl][:, 0:8, :], key=s["name"])
                        p2 = pT[(j + 1) % 2]
                        for h in range(2):
                            kb.TR(p2[0:64, h * 128:(h + 1) * 128], qk[sl][:, 512 + h * 64:512 + (h + 1) * 64], identb, signal=(h == 1))
                        kb.CP("dve", qkT[sl][:, 8:10, :], p2[0:64, 0:256].rearrange("p (h t) -> p h t", h=2))
                        kb.DMA("sp", KT[:, :, gt:gt + 128], qkT[sl][:, 8:10, :], key=s["name"])
                        p3 = ppool[cntA["tm"] % 6]
                        cntA["tm"] += 1
                        kb.TR(p3[0:32, 0:128], dw[:, 4, :], identf)
                        kb.CP("dve", dtT[sl][:], p3[0:32, 0:128])
                        kb.DMA("sp", DTT[:, gt:gt + 128], dtT[sl][:], key=s["name"])

                    nsubs = len(subs)
                    ks_of = [[k for k, (si2, j) in enumerate(subs) if si2 == si] for si in range(len(sts))]
                    pend2 = [None]
                    a_load(0)
                    for k in ks_of[0]:
                        if k + 1 < nsubs:
                            a_load(k + 1)
                        a_norm(k)
                        a_transp(k)
                    for si in range(len(sts)):
                        nsub = len(ks_of[si])
                        nxt = ks_of[si + 1] if si + 1 < len(sts) else []
                        per = 24 // nsub
                        for q in range(nsub):
                            kn = nxt[q] if q < len(nxt) else None
                            if kn is not None:
                                if kn + 1 < nsubs:
                                    a_load(kn + 1)
                                a_norm(kn)
                            for u in range(q * per, (q + 1) * per):
                                a_fmunit(si, u)
                            arg2 = a_tmsub(si, q)
                            if pend2[0] is not None:
                                a_tmsub2(pend2[0])
                            pend2[0] = arg2
                            if kn is not None:
                                a_transp(kn)
                        for q in range(nsub, len(nxt)):
                            kn = nxt[q]
                            if kn + 1 < nsubs:
                                a_load(kn + 1)
                            a_norm(kn)
                            a_transp(kn)
                    a_tmsub2(pend2[0])

                with scope(kb) as ex:
                    sL = seqs[1]
                    end_x, end_g, end_t = sL["xoff"] + sL["n"], sL["goff"] + sL["n"], sL["off"] + sL["n"]
                    snd = SB(ex, "snd", [128, 470], BF16)
                    kb.MSET("dve", snd[:], 0.0)
                    kb.DMA("sp", snd[:, 0:24].rearrange("p (b w) -> p b w", w=2), XBCT[:, end_x - 2:end_x].rearrange("(b p) w -> p b w", p=128), key="l")
                    kb.DMA("sp", snd[:, 24:84].rearrange("p (b w) -> p b w", w=15), GLUT[:, end_g - 15:end_g].rearrange("(b p) w -> p b w", p=128), key="l")
                    kb.DMA("sp", snd[0:64, 84:340].rearrange("p (g t) -> p g t", g=2), KT[:, :, end_t - 128:end_t], key="l")
                    kb.DMA("sp", snd[:, 340:470], VV[end_t - 128:end_t, :], key="l")
                    kb.DMA("sp", SND1[:, :], snd[:])
                    kb.CC(SND1[:, :], RCV1[:, :], groups)
                    rcv = SB(ex, "rcv", [128, 2, 470], BF16)
                    kb.DMA("sp", rcv[:], RCV1.rearrange("(r p) c -> p r c", p=128))
                    tmpx = SB(ex, "tmpx", [128, 470], F32)
                    kb.TS("dve", tmpx[:], rcv[:, 0, :], selB[:, 0:1], None, ALU.mult)
                    kb.STT(nbr1[:], rcv[:, 1, :], selB[:, 1:2], tmpx[:], ALU.mult, ALU.add)

                with scope(kb) as esb:
                    kb.mark("B0%d" % l)
                    kb.mark("B1%d" % l)
                    with scope(kb) as e1:
                        dg = SB(e1, "dg", [128, 60, 128], BF16)
                        for b in range(12):
                            for j in range(5):
                                if (b + j) % 2:
                                    kb.TS("dve", dg[:, b * 5 + j, :], identf, colsT[:, b, j:j + 1], None, ALU.mult)
                                else:
                                    kb.ACT(dg[:, b * 5 + j, :], identf, AF.Copy, scale=colsT[:, b, j:j + 1])
                        win = [SB(e1, "win%d" % i, [128, 12, 516], BF16) for i in range(2)]
                        xc = [SB(e1, "xc%d" % i, [128, 12, 512], BF16) for i in range(2)]
                        xst = [SB(e1, "xst%d" % i, [128, 1280], BF16) for i in range(2)]
                        pcv = [PS(e1, "pcv%d" % i, [128, 512], F32) for i in range(2)]
                        ptr = [PS(e1, "ptr%d" % i, [128, 1024], BF16) for i in range(2)]
                        ptb = [PS(e1, "ptb%d" % i, [128, 256], BF16) for i in range(2)]
                        cntB = dict(j=0)

                        def b1_s0(i):
                            s, t0, TS_ = sts[i]
                            if i == len(sts) - 1:
                                kb.DMA("sp", win[i % 2][:, :, 0:TS_ + 2], XBCT[:, s["xoff"] + t0 - 2:s["xoff"] + t0 + TS_].rearrange("(b p) w -> p b w", p=128), key=s["name"])
                                for r in range(2):
                                    kb.CP("pool", win[i % 2][:, :, TS_ + 2 + r:TS_ + 3 + r], nx[:, :, 1 - r:2 - r])
                            else:
                                kb.DMA("sp", win[i % 2][:, :, 0:TS_ + 4], XBCT[:, s["xoff"] + t0 - 2:s["xoff"] + t0 + TS_ + 2].rearrange("(b p) w -> p b w", p=128), key=s["name"])

                        def b1_s1(i):
                            s, t0, TS_ = sts[i]
                            w_, x_ = win[i % 2], xc[i % 2]
                            g0 = s["off"] + t0
                            for b in range(12):
                                p_ = pcv[b % 2]
                                for j in range(5):
                                    kb.MM(p_[:, 0:TS_], dg[:, b * 5 + j, :], w_[:, b, j:j + TS_], start=(j == 0), stop=(j == 4))
                                kb.ACT(x_[:, b, 0:TS_], p_[:, 0:TS_], AF.Silu, bias=colsT[:, b, 5:6])
                            kb.DMA("sp", BCT[:, :, g0:g0 + TS_], x_[:, 8:12, 0:TS_], key=s["name"])

                        def b1_s2(i):
                            s, t0, TS_ = sts[i]
                            x_ = xc[i % 2]
                            g0 = s["off"] + t0
                            for j in range(TS_ // 128):
                                gt = g0 + j * 128
                                nj = cntB["j"]
                                cntB["j"] += 1
                                p_, pb_, xs_ = ptr[nj % 2], ptb[nj % 2], xst[nj % 2]
                                for b in range(8):
                                    kb.TR(p_[:, b * 128:(b + 1) * 128], x_[:, b, j * 128:(j + 1) * 128], identb, signal=(b == 7))
                                for b in range(2):
                                    kb.TR(pb_[:, b * 128:(b + 1) * 128], x_[:, 8 + b, j * 128:(j + 1) * 128], identb, signal=(b == 1))
                                kb.CP("dve", xs_[:, 0:1024], p_[:])
                                kb.CP("pool" if False else "dve", xs_[:, 1024:1280], pb_[:])
                                kb.DMA("sp", XS[gt:gt + 128, :], xs_[:, 0:1024], key=s["name"])
                                kb.DMA("sp", BTM[gt:gt + 128, :], xs_[:, 1024:1280], key=s["name"])
                        rmask = SB(e1, "rmask", [32, 16, 128], F32)
                        kb.MSET("dve", rmask[:], 1.0)
                        kb.MSET("dve", rmask[:, :, 0:1], 0.0)
                        dr = SB(e1, "dr", [32, 16, 128], F32)
                        da = SB(e1, "da", [32, 16, 128], F32)
                        cs = SB(e1, "cs", [32, 16, 128], F32)
                        ld = SB(e1, "ld", [32, 16, 128], F32)
                        al = SB(e1, "al", [32, 16, 128], F32)
                        r1 = SB(e1, "r1", [32, 16, 128], F32)
                        sp3 = SB(e1, "sp3", [32, 3, 2048], BF16)
                        b0_items = []
                        for s in seqs:
                            for cb0 in range(0, s["n"] // 128, 32):
                                for d_ in range(2):
                                    b0_items.append((s, cb0, d_))

                        def b0_piece(k):
                            s, cb0, d_ = b0_items[k]
                            nch = s["n"] // 128
                            c0 = s["off"] // 128
                            n_ = min(32, nch - cb0)
                            tk0 = s["off"] + cb0 * 128
                            kb.DMA("sp", dr[0:n_], DTT[d_ * 16:(d_ + 1) * 16, tk0:tk0 + n_ * 128].rearrange("h (c t) -> c h t", t=128), key=s["name"])
                            kb.TT("dve", da[0:n_], dr[0:n_], bc(aB[0:n_, d_ * 16:(d_ + 1) * 16], [n_, 16, 128]), ALU.mult)
                            kb.op("dve", lambda g: g.tensor_tensor_scan(out=cs[0:n_].rearrange("c h t -> c (h t)"), data0=rmask[0:n_].rearrange("c h t -> c (h t)"),
                                                                          data1=da[0:n_].rearrange("c h t -> c (h t)"), initial=0.0, op0=ALU.mult, op1=ALU.add),
                                  [rmask[0:n_], da[0:n_]], [cs[0:n_]])
                            if d_ == 1:
                                kb.TT("dve", r1[0:n_], da[0:n_], cs[0:n_], ALU.subtract)
                                kb.CP("dve", al[0:n_, :, 0:1], cs[0:n_, :, 127:128])
                                kb.TT("dve", cs[0:n_], r1[0:n_], al[0:n_, :, 0:1].to_broadcast([n_, 16, 128]), ALU.add)
                            kb.ACT(ld[0:n_], dr[0:n_], AF.Ln)
                            kb.TT("dve", al[0:n_], ld[0:n_], cs[0:n_], ALU.subtract)
                            for side, srcx in ((0, al), (1, cs)):
                                sx = srcx[0:n_].rearrange("c h t -> c (h t)")
                                r1f = r1[0:n_].rearrange("c h t -> c (h t)")
                                kb.CP("act", sp3[0:n_, 0, :], sx)
                                kb.TT("dve", r1f, sx, sp3[0:n_, 0, :], ALU.subtract)
                                kb.CP("act", sp3[0:n_, 1, :], r1f)
                                kb.TT("dve", r1f, r1f, sp3[0:n_, 1, :], ALU.subtract)
                                kb.CP("act", sp3[0:n_, 2, :], r1f)
                                for r_ in range(3):
                                    kb.DMA("sp", ABR[d_, side, 3 * side + r_, c0 + cb0:c0 + cb0 + n_, :], sp3[0:n_, r_, :], key=s["name"])

                        nb0 = 0
                        for t_ in range(len(sts) + 2):
                            for k_, st_ in enumerate([b1_s0, b1_s1, b1_s2]):
                                if 0 <= t_ - k_ < len(sts):
                                    st_(t_ - k_)
                            if nb0 < len(b0_items):
                                b0_piece(nb0)
                                nb0 += 1
                        while nb0 < len(b0_items):
                            b0_piece(nb0)
                            nb0 += 1

                    with scope(kb) as e2:
                        Sf = SB(e2, "Sf", [128, 1024], F32)
                        Sb_ = SB(e2, "Sb", [128, 1024], F32)
                        Sfb = SB(e2, "Sfb", [128, 1024], BF16)
                        Sbb = [SB(e2, "Sbb%d" % i, [128, 1024], BF16) for i in range(2)]
                        kb.MSET("dve", Sf[:], 0.0)
                        kb.MSET("dve", Sb_[:], 0.0)
                        kb.MSET("pool", Sfb[:], 0.0)
                        xsb = [SB(e2, "xsb%d" % i, [128, 1024], BF16) for i in range(3)]
                        btm = [SB(e2, "btm%d" % i, [128, 256], BF16) for i in range(3)]
                        dts = [SB(e2, "dts%d" % i, [128, 32], F32) for i in range(3)]
                        xdw = [SB(e2, "xdw%d" % i, [128, 1024], BF16) for i in range(2)]
                        sm_ = [SB(e2, "sm%d" % i, [128, 8, 32], F32) for i in range(2)]
                        psm = PS(e2, "psm", [128, 512], F32)
                        pO = PS(e2, "pO", [128, 1024], F32)

                        def small_terms(sl, d_, dt_ap):
                            S_ = sm_[sl]
                            o = d_ * 16
                            kb.TT("dve", S_[:, 0, o:o + 16], dt_ap[:, o:o + 16], aB[:, o:o + 16], ALU.mult)
                            kb.MM(psm[:, o:o + 16], triU if d_ == 0 else triL, S_[:, 0, o:o + 16], start=True, stop=True)
                            kb.MM(psm[:, 64 + o:64 + o + 16], onesf, S_[:, 0, o:o + 16], start=True, stop=True)
                            kb.CP("dve", S_[:, 1, o:o + 16], psm[:, o:o + 16])
                            kb.TT("dve", S_[:, 2, o:o + 16], psm[:, 64 + o:64 + o + 16], S_[:, 1, o:o + 16], ALU.subtract)
                            kb.ACT(S_[:, 3, o:o + 16], S_[:, 2, o:o + 16], AF.Exp)
                            kb.TT("dve", S_[:, 3, o:o + 16], S_[:, 3, o:o + 16], dt_ap[:, o:o + 16], ALU.mult)
                            kb.ACT(S_[:, 4, o:o + 16], S_[:, 1, o:o + 16], AF.Exp)
                            kb.ACT(S_[:, 5, o:o + 16], psm[:, 64 + o:64 + o + 16], AF.Exp)
                            return S_[:, 4, o:o + 16], S_[:, 3, o:o + 16], S_[:, 5, o:o + 16]

                        def mk_xdw(xd_t, xs_ap, w_ap):
                            kb.TT("pool", xd_t[:].rearrange("p (h d) -> p h d", h=16), xs_ap.rearrange("p (h d) -> p h d", h=16), bc(w_ap, [128, 16, 64]), ALU.mult)

                        def state_update(Smaster, bt_ap, etot_ap, xd_t):
                            for g in range(2):
                                kb.MM(pO[:, g * 512:(g + 1) * 512], bt_ap[:, g * 128:(g + 1) * 128], xd_t[:, g * 512:(g + 1) * 512], start=True, stop=True)
                            kb.TT("dve", Smaster[:].rearrange("p (h d) -> p h d", h=16), Smaster[:].rearrange("p (h d) -> p h d", h=16), bc(etot_ap, [128, 16, 64]), ALU.mult)
                            kb.TT("dve", Smaster[:], Smaster[:], pO[:], ALU.add)

                        kb.mark("B2%d" % l)
                        chf = []
                        for s in seqs:
                            for c in range(s["n"] // 128):
                                chf.append((s, c))

                        def bf_s0(i):
                            s, c = chf[i]
                            gt = s["off"] + c * 128
                            kb.DMA("sp", xsb[i % 3][:], XS[gt:gt + 128, :], key=s["name"])
                            kb.DMA("sp", btm[i % 3][:], BTM[gt:gt + 128, :], key=s["name"])
                            kb.DMA("sp", dts[i % 3][:], DTS[gt:gt + 128, :], key=s["name"])

                        def bf_s1(i):
                            s, c = chf[i]
                            gc = (s["off"] + c * 128) // 128
                            _, w_ap, et_ap = small_terms(i % 2, 0, dts[i % 3])
                            mk_xdw(xdw[i % 2], xsb[i % 3][:], w_ap)
                            kb.CP("act", Sbb[i % 2][:], Sf[:])
                            kb.DMA("sp", SFD[gc, :, :], Sbb[i % 2][:], key=gc)
                            state_update(Sf, btm[i % 3], et_ap, xdw[i % 2])
                        pipe(len(chf), [bf_s0, bf_s1])
                        nbS = SB(e2, "nbS", [128, 1024], F32)
                        with scope(kb) as ex2:
                            rcs2 = SB(ex2, "rcs2", [128, 2, 1024], F32)
                            kb.DMA("sp", SND2[:, :], Sf[:])
                            kb.CC(SND2[:, :], RCV2[:, :], groups)
                            kb.DMA("sp", rcs2[:], RCV2.rearrange("(r p) c -> p r c", p=128))
                            kb.TS("dve", nbS[:], rcs2[:, 0, :], selB[:, 0:1], None, ALU.mult)
                            kb.STT(nbS[:], rcs2[:, 1, :], selB[:, 1:2], nbS[:], ALU.mult, ALU.add)
                        chd = []
                        for s in seqs:
                            for c in reversed(range(s["n"] // 128)):
                                chd.append((s, c))
                        first_lat = seqs[0]["n"] // 128

                        def b2_s0(i):
                            s, c = chd[i]
                            gt = s["off"] + c * 128
                            kb.DMA("sp", xsb[i % 3][:], XS[gt:gt + 128, :], key=s["name"])
                            kb.DMA("sp", btm[i % 3][:], BTM[gt:gt + 128, :], key=s["name"])
                            kb.DMA("sp", dts[i % 3][:], DTS[gt:gt + 128, :], key=s["name"])

                        def b2_s1(i):
                            s, c = chd[i]
                            gc = (s["off"] + c * 128) // 128
                            if i == first_lat:
                                kb.CP("dve", Sb_[:], nbS[:])
                            _, w_ap, et_ap = small_terms(i % 2, 1, dts[i % 3])
                            mk_xdw(xdw[i % 2], xsb[i % 3][:], w_ap)
                            kb.CP("act", Sbb[i % 2][:], Sb_[:])
                            kb.DMA("sp", SBD[gc, :, :], Sbb[i % 2][:], key=gc)
                            state_update(Sb_, btm[i % 3], et_ap, xdw[i % 2])
                        pipe(len(chd), [b2_s0, b2_s1])

                        kb.mark("B3%d" % l)
                        with scope(kb) as e3:
                            bct = [SB(e3, "bct%d" % i, [128, 4, 128], BF16) for i in range(3)]
                            zsb = [SB(e3, "zsb%d" % i, [128, 1024], BF16) for i in range(2)]
                            sbe = [SB(e3, "sbe%d" % i, [128, 1024], BF16) for i in range(2)]
                            sfe = [SB(e3, "sfe%d" % i, [128, 1024], BF16) for i in range(2)]
                            ab6 = [SB(e3, "ab6%d" % i, [6, 2, 2, 2048], BF16) for i in range(2)]
                            cbT = SB(e3, "cbT", [128, 2, 128], BF16)
                            E_ = [SB(e3, "E%d" % i, [128, 4, 128], BF16) for i in range(2)]
                            MT = SB(e3, "MT", [128, 2, 16, 128], BF16)
                            ysb = [SB(e3, "ysb%d" % i, [128, 1024], F32) for i in range(2)]
                            t1 = SB(e3, "t1", [128, 1024], F32)
                            t2 = SB(e3, "t2", [128, 1024], F32)
                            us = [SB(e3, "u%d" % i, [128, 1024], F32) for i in range(2)]
                            obs = [SB(e3, "ob%d" % i, [128, 1024], BF16) for i in range(2)]
                            sq = SB(e3, "sq", [128, 512], F32)
                            oT = [SB(e3, "oT%d" % i, [128, 8, 128], BF16) for i in range(2)]
                            st2 = SB(e3, "st2", [128, 2, 2], F32)
                            pDs = [PS(e3, "pD%d" % i, [128, 512], F32) for i in range(2)]
                            pYs = [PS(e3, "pY%d" % i, [128, 512], F32) for i in range(2)]
                            pX = PS(e3, "pX", [128, 1024], BF16)
                            cntD = dict(d=0)
                            cha = []
                            for s in seqs:
                                for c in range(s["n"] // 128):
                                    cha.append((s, c))

                            def ny(s):
                                return not (last and s["name"] == "c")

                            def b3_s0(i):
                                s, c = cha[i]
                                gt = s["off"] + c * 128
                                gc = gt // 128
                                kb.DMA("sp", xsb[i % 3][:], XS[gt:gt + 128, :], key=s["name"])
                                kb.DMA("sp", dts[i % 3][:], DTS[gt:gt + 128, :], key=s["name"])
                                if ny(s):
                                    kb.DMA("sp", bct[i % 3][:], BCT[:, :, gt:gt + 128], key=s["name"])
                                    kb.DMA("sp", ab6[i % 2][:], ABR[:, :, :, gc, :].rearrange("d s r x -> r d s x"), key=s["name"])

                            def b3_s1(i):
                                s, c = cha[i]
                                gt = s["off"] + c * 128
                                gc = gt // 128
                                sl3, sl = i % 3, i % 2
                                if not ny(s):
                                    return
                                ef, wf, etf = small_terms(sl, 0, dts[sl3])
                                kb.DMA("sp", zsb[sl][:], ZS[gt:gt + 128, :], key=s["name"])
                                kb.DMA("sp", sbe[sl][:], SBD[gc, :, :], key=gc)
                                kb.DMA("sp", sfe[sl][:], SFD[gc, :, :], key=gc)
                                small_terms(sl, 1, dts[sl3])
                                for g in range(2):
                                    kb.MM(psm[:, 256 + g * 128:256 + (g + 1) * 128], bct[sl3][:, g, :], bct[sl3][:, 2 + g, :], start=True, stop=True)
                                kb.CP("dve", cbT[:].rearrange("p g t -> p (g t)"), psm[:, 256:512])
                                for g in range(2):
                                    for d_ in range(2):
                                        for hb_ in range(2):
                                            pD = pDs[cntD["d"] % 2]
                                            e_ = E_[cntD["d"] % 2]
                                            cntD["d"] += 1
                                            kb.MM(pD[:], identb, mneg[:, d_, :, :].rearrange("p r t -> p (r t)"), start=True, stop=False, signal=False)
                                            for hh in range(4):
                                                h = g * 8 + hb_ * 4 + hh
                                                kb.MM(pD[:, hh * 128:(hh + 1) * 128], ab6[sl][:, d_, 0, h * 128:(h + 1) * 128],
                                                      ab6[sl][:, d_, 1, h * 128:(h + 1) * 128], start=False, stop=(hh == 3), signal=(hh == 3))
                                            kb.ACT(e_[:].rearrange("p h t -> p (h t)"), pD[:], AF.Exp)
                                            h0 = g * 8 + hb_ * 4
                                            kb.TT("dve", MT[:, d_, h0:h0 + 4, :], e_[:], cbT[:, g, :].unsqueeze(1).to_broadcast([128, 4, 128]), ALU.mult)
                                    pY = pYs[g]
                                    for hh in range(8):
                                        h = g * 8 + hh
                                        rhs = xsb[sl3][:, h * 64:(h + 1) * 64]
                                        kb.MM(pY[:, hh * 64:(hh + 1) * 64], MT[:, 0, h, :], rhs, start=True, stop=False, signal=False)
                                        kb.MM(pY[:, hh * 64:(hh + 1) * 64], MT[:, 1, h, :], rhs, start=False, stop=False, signal=False)
                                        kb.MM(pY[:, hh * 64:(hh + 1) * 64], dI[:, h, :], rhs, start=False, stop=True, signal=(hh == 7))
                                    kb.CP("act", ysb[sl][:, g * 512:(g + 1) * 512], pY[:])

                            def b3_s2(i):
                                s, c = cha[i]
                                if not ny(s):
                                    return
                                sl3, sl = i % 3, i % 2
                                S_ = sm_[sl]
                                ef, eb = S_[:, 4, 0:16], S_[:, 4, 16:32]
                                u = us[sl]
                                for g in range(2):
                                    kb.MM(pO[:, g * 512:(g + 1) * 512], bct[sl3][:, 2 + g, :], sfe[sl][:, g * 512:(g + 1) * 512], start=True, stop=True)
                                kb.TT("dve", t1[:].rearrange("p (h d) -> p h d", h=16), pO[:].rearrange("p (h d) -> p h d", h=16), bc(ef, [128, 16, 64]), ALU.mult)
                                for g in range(2):
                                    kb.MM(pO[:, g * 512:(g + 1) * 512], bct[sl3][:, 2 + g, :], sbe[sl][:, g * 512:(g + 1) * 512], start=True, stop=True)
                                kb.TT("dve", t2[:].rearrange("p (h d) -> p h d", h=16), pO[:].rearrange("p (h d) -> p h d", h=16), bc(eb, [128, 16, 64]), ALU.mult)
                                kb.TT("pool", u[:], t1[:], t2[:], ALU.add)
                                kb.TT("pool", u[:], u[:], ysb[sl][:], ALU.add)
                                kb.TT("pool", u[:], u[:], zsb[sl][:], ALU.mult)

                            def b3_s3(i):
                                s, c = cha[i]
                                if not ny(s):
                                    return
                                sl = i % 2
                                u = us[sl]
                                ss_ = st2[:, sl, :]
                                for g in range(2):
                                    kb.STT(sq[:], u[:, g * 512:(g + 1) * 512], 1.0, u[:, g * 512:(g + 1) * 512], ALU.mult, ALU.mult, accum=st2[:, sl, g:g + 1])
                                kb.TS("dve", ss_, ss_, 1.0 / 512.0, EPS, ALU.mult, ALU.add)
                                kb.ACT(ss_, ss_, AF.Ln)
                                kb.ACT(ss_, ss_, AF.Exp, scale=-0.5)
                                for g in range(2):
                                    kb.STT(obs[sl][:, g * 512:(g + 1) * 512], u[:, g * 512:(g + 1) * 512], st2[:, sl, g:g + 1], snwB[:, g * 512:(g + 1) * 512], ALU.mult, ALU.mult)

                            def b3_s4(i):
                                s, c = cha[i]
                                if not ny(s):
                                    return
                                gt = s["off"] + c * 128
                                sl = i % 2
                                for b in range(8):
                                    kb.TR(pX[:, b * 128:(b + 1) * 128], obs[sl][:, b * 128:(b + 1) * 128], identb, signal=(b == 7))
                                kb.CP("act", oT[sl][:].rearrange("p b t -> p (b t)"), pX[:])
                                kb.DMA("sp", OT[0:1024, gt:gt + 128].rearrange("(b p) t -> p b t", p=128), oT[sl][:], key=s["name"])
                            pipe(len(cha), [b3_s0, b3_s1, b3_s2, b3_s3, b3_s4])

                kb.mark("C%d" % l)
                with scope(kb) as ec:
                    ktb = SB(ec, "ktb", [64, 2, Ltot + 128], BF16)
                    vvb = SB(ec, "vvb", [128, NCH + 1, 130], BF16)
                    kb.CP("dve", ktb[:, :, Ltot:Ltot + 128], nbr1[0:64, 84:340].rearrange("p (g t) -> p g t", g=2))
                    kb.CP("dve", vvb[:, NCH, :], nbr1[:, 340:470])
                    for s in seqs:
                        o_, n_ = s["off"], s["n"]
                        kb.DMA("sp", ktb[:, :, o_:o_ + n_], KT[:, :, o_:o_ + n_], key=s["name"])
                        kb.DMA("sp", vvb[:, o_ // 128:(o_ + n_) // 128, :], VV[o_:o_ + n_, :].rearrange("(c p) x -> p c x", p=128), key=s["name"])
                    qtb = [SB(ec, "qtb%d" % i, [64, 8, 128], BF16) for i in range(3)]
                    gab = [SB(ec, "gab%d" % i, [64, 8, 128], BF16) for i in range(3)]
                    Ee = [SB(ec, "Ee%d" % i, [128, 512], BF16) for i in range(20)]
                    den = [SB(ec, "den%d" % i, [64, 512], F32) for i in range(2)]
                    oa = [SB(ec, "oa%d" % i, [64, 512], F32) for i in range(2)]
                    oaT = [SB(ec, "oaT%d" % i, [64, 512], BF16) for i in range(2)]
                    pS = [PS(ec, "pS%d" % i, [128, 512], F32) for i in range(3)]
                    pV = [PS(ec, "pV%d" % i, [64, 512], F32) for i in range(2)]
                    pU = [PS(ec, "pU%d" % i, [64, 512], F32) for i in range(2)]
                    qbs = []
                    for s in seqs:
                        if last and s["name"] == "c":
                            continue
                        for qb in range(s["n"] // 128):
                            qbs.append((s, qb))
                    cntC = dict(s=0, o=0)

                    def keyblocks(s, qb):
                        kbl = [(0, None), (1, None)]
                        if s["name"] == "l":
                            nb = s["n"] // 128
                            c0 = s["off"] // 128
                            if qb > 0:
                                kbl.append((c0 + qb - 1, 1))
                            kbl.append((c0 + qb, None))
                            if qb < nb - 1:
                                kbl.append((c0 + qb + 1, 0))
                            else:
                                kbl.append((NCH, 2))
                        return kbl

                    def c_s0(i):
                        s, qb = qbs[i]
                        gt = s["off"] + qb * 128
                        kb.DMA("sp", qtb[i % 3][:], QT[:, :, gt:gt + 128], key=s["name"])
                        kb.DMA("sp", gab[i % 3][:], GAT[:, gt:gt + 128].rearrange("(h d) t -> d h t", d=64), key=s["name"])

                    def c_s1(i):
                        s, qb = qbs[i]
                        q_ = qtb[i % 3]
                        for g in range(2):
                            for ki, (kc, md) in enumerate(keyblocks(s, qb)):
                                p_ = pS[cntC["s"] % 3]
                                cntC["s"] += 1
                                if md is not None:
                                    kb.MM(p_[:], identb, mneg[:, md, :, :].rearrange("p r t -> p (r t)"), start=True, stop=False, signal=False)
                                kb.MM(p_[:], ktb[:, g, kc * 128:(kc + 1) * 128], q_[:, g * 4:(g + 1) * 4, :].rearrange("p h t -> p (h t)"),
                                      start=(md is None), stop=True)
                                kb.ACT(Ee[(i % 2) * 10 + g * 5 + ki][:], p_[:], AF.Exp)

                    def c_s2(i):
                        s, qb = qbs[i]
                        gt = s["off"] + qb * 128
                        g_ = gab[i % 3]
                        kbl = keyblocks(s, qb)
                        for g in range(2):
                            no = cntC["o"]
                            cntC["o"] += 1
                            pv, pu = pV[no % 2], pU[no % 2]
                            for ki, (kc, md) in enumerate(kbl):
                                kb.MM(pv[:], vvb[:, kc, g * 65:g * 65 + 64], Ee[(i % 2) * 10 + g * 5 + ki][:], start=(ki == 0), stop=(ki == len(kbl) - 1))
                            for ki, (kc, md) in enumerate(kbl):
                                kb.MM(pu[:], onesb[:, 0:64], Ee[(i % 2) * 10 + g * 5 + ki][:], start=(ki == 0), stop=(ki == len(kbl) - 1))
                            dn = den[no % 2]
                            kb.TT("dve", dn[:].rearrange("p (r t) -> p r t", r=4), pu[:].rearrange("p (r t) -> p r t", r=4), bc(sinkB[0:64, g * 4:(g + 1) * 4], [64, 4, 128]), ALU.add)
                            kb.ACT(dn[:], dn[:], AF.Ln)
                            kb.ACT(dn[:], dn[:], AF.Exp, scale=-1.0)
                            o_ = oa[no % 2]
                            kb.TT("dve", o_[:], pv[:], dn[:], ALU.mult)
                            ot_ = oaT[no % 2]
                            kb.TT("pool", ot_[:], o_[:], g_[:, g * 4:(g + 1) * 4, :].rearrange("p h t -> p (h t)"), ALU.mult)
                            kb.DMA("sp", OT[1024 + g * 256:1024 + (g + 1) * 256, gt:gt + 128].rearrange("(r d) t -> d r t", d=64), ot_[:].rearrange("p (r t) -> p r t", r=4), key=s["name"])
                    pipe(len(qbs), [c_s0, c_s1, c_s2])

                kb.mark("D%d" % l)
                with scope(kb) as ed:
                    dgc = SB(ed, "dgc", [128, 124, 128], BF16)
                    for b in range(4):
                        for j in range(31):
                            if (b + j) % 2:
                                kb.TS("dve", dgc[:, b * 31 + j, :], identf, colsT[:, b, 6 + j:7 + j], None, ALU.mult)
                            else:
                                kb.ACT(dgc[:, b * 31 + j, :], identf, AF.Copy, scale=colsT[:, b, 6 + j:7 + j])
                    pww = SB(ed, "pww", [128, 4, 512], BF16)
                    pwst = SB(ed, "pwst", [128, 4, 512], F32)
                    kb.DMA("sp", pwst[:], conv_pw_w[l].rearrange("(k p) n -> p k n", p=128))
                    kb.CP("act", pww[:], pwst[:])
                    gw = [SB(ed, "gw%d" % i, [128, 4, 542], BF16) for i in range(2)]
                    gcb = [SB(ed, "gcb%d" % i, [128, 4, 512], BF16) for i in range(2)]
                    hbb = [SB(ed, "hbb%d" % i, [128, 4, 512], BF16) for i in range(2)]
                    hsq = [SB(ed, "hsq%d" % i, [128, 4, 512], BF16) for i in range(2)]
                    mean = SB(ed, "mean", [128, 512], F32)
                    rstd = SB(ed, "rstd", [128, 512], F32)
                    xcn = [SB(ed, "xcn%d" % i, [128, 512], F32) for i in range(2)]
                    h2 = [SB(ed, "h2%d" % i, [128, 4, 512], BF16) for i in range(2)]
                    oc = [SB(ed, "oc%d" % i, [128, 512], BF16) for i in range(2)]
                    pc = [PS(ed, "pc%d" % i, [128, 512], F32) for i in range(2)]
                    pm = PS(ed, "pm", [128, 512], F32)
                    pe2 = PS(ed, "pe2", [128, 512], F32)
                    pw_ = [PS(ed, "pw%d" % i, [128, 512], F32) for i in range(2)]
                    dst = [x for x in sts if not (last and x[0]["name"] == "c")]

                    def d_s0(i):
                        s, t0, TS_ = dst[i]
                        if i == len(dst) - 1:
                            kb.DMA("sp", gw[i % 2][:, :, 0:TS_ + 15], GLUT[:, s["goff"] + t0 - 15:s["goff"] + t0 + TS_].rearrange("(b p) w -> p b w", p=128), key=s["name"])
                            for r in range(15):
                                kb.CP("pool", gw[i % 2][:, :, TS_ + 15 + r:TS_ + 16 + r], ng[:, :, 14 - r:15 - r])
                        else:
                            kb.DMA("sp", gw[i % 2][:, :, 0:TS_ + 30], GLUT[:, s["goff"] + t0 - 15:s["goff"] + t0 + TS_ + 15].rearrange("(b p) w -> p b w", p=128), key=s["name"])

                    def d_s1(i):
                        s, t0, TS_ = dst[i]
                        w_ = gw[i % 2]
                        for b in range(4):
                            p_ = pc[b % 2]
                            for j in range(31):
                                kb.MM(p_[:, 0:TS_], dgc[:, b * 31 + j, :], w_[:, b, j:j + TS_], start=(j == 0), stop=(j == 30))
                            kb.ACT(hbb[i % 2][:, b, 0:TS_], p_[:, 0:TS_], AF.Identity, bias=colsT[:, b, 37:38])
                            kb.TT("pool", hsq[i % 2][:, b, 0:TS_], hbb[i % 2][:, b, 0:TS_], hbb[i % 2][:, b, 0:TS_], ALU.mult)

                    def d_s2(i):
                        s, t0, TS_ = dst[i]
                        g0 = s["off"] + t0
                        kb.DMA("sp", gcb[i % 2][:, :, 0:TS_], GCT[:, g0:g0 + TS_].rearrange("(b p) w -> p b w", p=128), key=s["name"])
                        hb_, hs_ = hbb[i % 2], hsq[i % 2]
                        for b in range(4):
                            kb.MM(pm[:, 0:TS_], ones512[:], hb_[:, b, 0:TS_], start=(b == 0), stop=(b == 3))
                        for b in range(4):
                            kb.MM(pe2[:, 0:TS_], ones512[:], hs_[:, b, 0:TS_], start=(b == 0), stop=(b == 3))
                        kb.CP("act", mean[:, 0:TS_], pm[:, 0:TS_])
                        kb.TT("dve", rstd[:, 0:TS_], mean[:, 0:TS_], mean[:, 0:TS_], ALU.mult)
                        kb.TT("dve", rstd[:, 0:TS_], pe2[:, 0:TS_], rstd[:, 0:TS_], ALU.subtract)
                        kb.ACT(rstd[:, 0:TS_], rstd[:, 0:TS_], AF.Sqrt, bias=epsb[:])
                        kb.RECIP(rstd[:, 0:TS_], rstd[:, 0:TS_])
                        for b in range(4):
                            x_ = xcn[b % 2]
                            kb.TT("dve", x_[:, 0:TS_], hb_[:, b, 0:TS_], mean[:, 0:TS_], ALU.subtract)
                            kb.TT("pool", x_[:, 0:TS_], x_[:, 0:TS_], rstd[:, 0:TS_], ALU.mult)
                            kb.ACT(h2[i % 2][:, b, 0:TS_], x_[:, 0:TS_], AF.Silu, bias=colsT[:, b, 39:40], scale=colsT[:, b, 38:39])

                    def d_s3(i):
                        s, t0, TS_ = dst[i]
                        g0 = s["off"] + t0
                        for co in range(4):
                            p_ = pw_[co % 2]
                            for ci in range(4):
                                kb.MM(p_[:, 0:TS_], pww[:, ci, co * 128:(co + 1) * 128], h2[i % 2][:, ci, 0:TS_], start=(ci == 0), stop=(ci == 3))
                            o_ = oc[co % 2]
                            kb.STT(o_[:, 0:TS_], p_[:, 0:TS_], colsT[:, co, 40:41], gcb[i % 2][:, co, 0:TS_], ALU.add, ALU.mult)
                            kb.DMA("sp", OT[1536 + co * 128:1536 + (co + 1) * 128, g0:g0 + TS_], o_[:, 0:TS_], key=s["name"])
                    pipe(len(dst), [d_s0, d_s1, d_s2, d_s3])

                kb.mark("E%d" % l)
                with scope(kb) as ee:
                    wo = SB(ee, "wo", [128, 16, D], BF16)
                    wov = w_out[l].rearrange("(k p) n -> p k n", p=128)
                    with scope(kb) as eow:
                        wos = [SB(eow, "wos%d" % i, [128, 4, D], F32) for i in range(2)]
                        for k4 in range(4):
                            kb.DMA("sp", wos[k4 % 2][:], wov[:, k4 * 4:(k4 + 1) * 4, :])
                            for j4 in range(4):
                                kb.CP(["dve", "pool", "act", "pool"][j4], wo[:, k4 * 4 + j4, :], wos[k4 % 2][:, j4, :])
                    fnw = SB(ee, "fnw", [128, D], F32)
                    kb.DMA("sp", fnw[:], final_norm_w[0, :].partition_broadcast(128))
                    otb = [SB(ee, "otb%d" % i, [128, 16, 128], BF16) for i in range(3)]
                    xo = [SB(ee, "xo%d" % i, [128, D], F32) for i in range(3)]
                    xn = [SB(ee, "xn%d" % i, [128, D], F32) for i in range(2)]
                    yo = [SB(ee, "yo%d" % i, [128, D], F32) for i in range(2)]
                    jk = SB(ee, "jk", [128, D], BF16)
                    s1 = SB(ee, "s1", [128, 2], F32)
                    po = [PS(ee, "po%d" % i, [128, 1024], F32) for i in range(2)]
                    che = []
                    for s in seqs:
                        if last and s["name"] == "c":
                            continue
                        for c in range(s["n"] // 128):
                            che.append((s, c))

                    def e_s0(i):
                        s, c = che[i]
                        gt = s["off"] + c * 128
                        kb.DMA("sp", otb[i % 3][:], OT[:, gt:gt + 128].rearrange("(k p) t -> p k t", p=128), key=s["name"])
                        if l == 0:
                            src = (ctx_in if s["name"] == "c" else x_in)[c * 128:(c + 1) * 128, :]
                        else:
                            src = XR[gt:gt + 128, :]
                        kb.DMA("sp", xo[i % 3][:], src, key=(s["name"], c * 128))

                    def e_s1(i):
                        s, c = che[i]
                        gt = s["off"] + c * 128
                        m = 0 if s["name"] == "c" else 1
                        sl = i % 2
                        p_ = po[sl]
                        for nb_ in range(2):
                            for kt in range(16):
                                kb.MM(p_[:, nb_ * 512:(nb_ + 1) * 512], otb[i % 3][:, kt, :], wo[:, kt, nb_ * 512:(nb_ + 1) * 512], start=(kt == 0), stop=(kt == 15))
                        kb.TT("dve", xn[sl][:], p_[:], modB[:, m, 2, :], ALU.mult)
                        kb.TT("pool", xn[sl][:], xn[sl][:], xo[i % 3][:], ALU.add)
                        if not last:
                            kb.DMA("sp", XR[gt:gt + 128, :], xn[sl][:], key=(s["name"], c * 128))
                        else:
                            sc_ = s1[:, sl:sl + 1]
                            kb.ACT(jk[:], xn[sl][:], AF.Square, accum=sc_)
                            kb.TS("dve", sc_, sc_, 1.0 / D, EPS, ALU.mult, ALU.add)
                            kb.ACT(sc_, sc_, AF.Sqrt)
                            kb.RECIP(sc_, sc_)
                            kb.STT(yo[sl][:], xn[sl][:], sc_, fnw[:], ALU.mult, ALU.mult)
                            kb.DMA("sp", out[c * 128:(c + 1) * 128, :], yo[sl][:], key="out")
                    pipe(len(che), [e_s0, e_s1])
        kb.mark("end")
        kb.finish()
    return nc, kb


def host_consts(L, T, pos, grid_w=64):
    k = np.arange(128)[:, None]
    t = np.arange(128)[None, :]
    cst = np.zeros((128, 7, 128), np.float32)
    cst[:, 0] = np.eye(128)
    cst[:, 1] = (k <= t)
    cst[:, 2] = (k >= t)
    cst[:, 3] = 1.0
    cst[:, 4] = np.where(k <= t, 0.0, NEG)
    cst[:, 5] = np.where(k >= t, 0.0, NEG)
    cst[:, 6] = np.where(k + t >= 127, 0.0, NEG)
    rope = np.zeros((T + L, 256), np.float32)
    rope[:, 0:64] = 1.0
    rope[:, 128:192] = 0.125
    row = (pos // grid_w).astype(np.float32)
    col = (pos % grid_w).astype(np.float32)
    inv = (10000.0 ** (-np.arange(0, 32, 2, dtype=np.float32) / 32.0)).astype(np.float32)
    ang = np.stack([row[:, None] * inv, col[:, None] * inv], axis=1).astype(np.float32)
    cs_, sn_ = np.cos(ang), np.sin(ang)
    cos2 = np.repeat(cs_[:, :, None, :], 2, axis=2).reshape(L, 64)
    rope[T:, 0:64] = cos2
    rope[T:, 64:96] = (-sn_).reshape(L, 32)
    rope[T:, 96:128] = sn_.reshape(L, 32)
    rope[T:, 128:192] = cos2 * 0.125
    rope[T:, 192:224] = (-sn_).reshape(L, 32) * 0.125
    rope[T:, 224:256] = sn_.reshape(L, 32) * 0.125
    return cst, rope


_CACHE = {}


def run(inputs, L, T, NL, batches):
    npairs = len(batches)
    key = (L, T, NL, npairs)
    if key not in _CACHE:
        _CACHE[key] = build(L, T, NL, npairs)[0]
    nc = _CACHE[key]
    f = lambda a: np.ascontiguousarray(np.asarray(a, dtype=np.float32))
    base = {
        "c_ctx": f(inputs["c_ctx"]).reshape(128, 8),
        "w_mod": f(inputs["w_mod"]), "b_mod": f(inputs["b_mod"]), "norm_w": f(inputs["norm_w"]),
        "ssd_conv_b": f(inputs["ssd_conv_b"]),
        "ssd_d": f(inputs["ssd_d"]), "ssd_norm_w": f(inputs["ssd_norm_w"]), "attn_sink": f(inputs["attn_sink"]),
        "conv_dw_b": f(inputs["conv_dw_b"]), "conv_ln_w": f(inputs["conv_ln_w"]),
        "conv_ln_b": f(inputs["conv_ln_b"]), "conv_pw_w": f(inputs["conv_pw_w"]), "conv_pw_b": f(inputs["conv_pw_b"]),
        "w_out": f(inputs["w_out"]), "final_norm_w": f(inputs["final_norm_w"]).reshape(1, D),
    }
    w_in = f(inputs["w_in"])
    w_in_m = w_in.copy()
    w_in_m[:, :, 2560:2576] = w_in[:, :, 2576:2592]
    w_in_m[:, :, 2576:2592] = w_in[:, :, 2560:2576]
    dtb = f(inputs["ssd_dt_bias"])
    alog = f(inputs["ssd_a_log"])
    scw = f(inputs["ssd_conv_w"])
    dww = f(inputs["conv_dw_w"])
    half = []
    for h in range(2):
        pos = np.arange(L) if h == 0 else (2 * L - 1 - np.arange(L))
        cst, rope = host_consts(L, T, pos)
        sel = np.zeros((128, 2), np.float32)
        sel[:, 1 - h] = 1.0
        d = dict(base)
        d.update({
            "cst": cst, "rope": rope, "sel": sel,
            "w_in": w_in if h == 0 else w_in_m,
            "ssd_dt_bias": f(dtb if h == 0 else dtb[:, ::-1]).reshape(NL, 32),
            "ssd_a_log": f(alog if h == 0 else alog[:, ::-1]).reshape(NL, 32),
            "ssd_conv_w": f(scw if h == 0 else scw[:, ::-1]),
            "conv_dw_w": f(dww if h == 0 else dww[:, ::-1]),
        })
        half.append(d)
    x = f(inputs["x"])
    c = f(inputs["c"])
    ctx = f(inputs["ctx"])
    in_maps = []
    for b in batches:
        for h in range(2):
            m = dict(half[h])
            if h == 0:
                m["x"] = f(x[b, 0:L])
                m["ctx"] = f(ctx[b])
            else:
                m["x"] = f(x[b, L:2 * L][::-1])
                m["ctx"] = f(ctx[b][::-1])
            m["c"] = c[b].reshape(128, 8)
            in_maps.append(m)
    res = run_bass_kernel_spmd(nc, in_maps, core_ids=list(range(2 * npairs)))
    outs = []
    for i in range(npairs):
        o0 = res.results[2 * i]["out"]
        o1 = res.results[2 * i + 1]["out"][::-1]
        outs.append(np.concatenate([o0, o1], axis=0))
    return outs


def kernel(**inputs):
    B = np.asarray(inputs["x"]).shape[0]
    outs = run(inputs, 2048, 256, 2, list(range(B)))
    return np.stack(outs, axis=0).astype(np.float32)
```

```python
import numpy as np
from contextlib import ExitStack, contextmanager
import concourse.bass as bass
import concourse.mybir as mybir
from concourse.bass_utils import run_bass_kernel_spmd

F32 = mybir.dt.float32
BF16 = mybir.dt.bfloat16
AF = mybir.ActivationFunctionType
ALU = mybir.AluOpType
NDMA = 48
EPS = 1e-6
D = 1024
NEG = -30000.0


class Tok:
    __slots__ = ("sem", "val", "key")

    def __init__(self, sem, val, key):
        self.sem, self.val, self.key = sem, val, key


class Buf:
    __slots__ = ("w", "r", "multi")

    def __init__(self, multi=False):
        self.w = {}
        self.r = {}
        self.multi = multi


class KB:
    def __init__(self, nc, es):
        self.nc = nc
        self.eng = {"pe": nc.tensor, "act": nc.scalar, "dve": nc.vector, "pool": nc.gpsimd, "sp": nc.sync}
        self.sem = {e: es.enter_context(nc.semaphore("s_" + e)) for e in self.eng}
        self.cnt = {e: 0 for e in self.eng}
        self.known = {e: {} for e in self.eng}
        self.dsem = [es.enter_context(nc.semaphore("d%d" % i)) for i in range(NDMA)]
        self.dval = [0] * NDMA
        self.dnext = 0
        self.pend = {e: [] for e in self.eng}
        self.bufs = {}
        self.ccsem = es.enter_context(nc.semaphore("ccsem"))
        self.ccval = 0
        self.ninst = 0
        self.nmm = 0
        self.marks = []

    def buf(self, ap, key=None):
        t = ap.tensor
        name = t.name
        dram = str(ap.space).lower().find("dram") >= 0 or str(ap.space).lower().find("hbm") >= 0
        k = (name, key if dram else None)
        b = self.bufs.get(k)
        if b is None:
            b = Buf(multi=dram)
            self.bufs[k] = b
        return b

    def _wait(self, e, tok):
        if e == "pe" and tok.key == "pe":
            return
        k = self.known[e]
        if k.get(tok.key, 0) >= tok.val:
            return
        self.eng[e].wait_ge(tok.sem, tok.val)
        k[tok.key] = tok.val

    def _deps(self, e, reads, writes):
        for b in reads:
            for t in b.w.values():
                self._wait(e, t)
        for b in writes:
            if not b.multi:
                for t in b.w.values():
                    self._wait(e, t)
            for t in b.r.values():
                self._wait(e, t)

    def _record(self, tok, reads, writes):
        for b in reads:
            b.r[tok.key] = tok
        for b in writes:
            if b.multi:
                b.w[tok.key] = tok
            else:
                b.w = {tok.key: tok}
                b.r = {}

    def op(self, e, fn, rd, wr, signal=True, key=None):
        reads = [self.buf(a, key) for a in rd if a is not None and not isinstance(a, (int, float))]
        writes = [self.buf(a, key) for a in wr if a is not None]
        self._deps(e, reads, writes)
        ins = fn(self.eng[e])
        self.ninst += 1
        if not signal:
            self.pend[e].append((reads, writes))
            return
        self.cnt[e] += 1
        ins.then_inc(self.sem[e], 1)
        tok = Tok(self.sem[e], self.cnt[e], e)
        for (r, w) in self.pend[e]:
            self._record(tok, r, w)
        self.pend[e] = []
        self._record(tok, reads, writes)

    def DMA(self, q, out, in_, key=None, **kw):
        reads = [self.buf(in_, key)]
        writes = [self.buf(out, key)]
        self._deps(q, reads, writes)
        i = self.dnext
        self.dnext = (i + 1) % NDMA
        k = "d%d" % i
        if self.dval[i] > 0:
            self._wait(q, Tok(self.dsem[i], self.dval[i], k))
        ins = self.eng[q].dma_start(out=out, in_=in_, **kw)
        self.ninst += 1
        self.dval[i] += 16
        ins.then_inc(self.dsem[i], 16)
        tok = Tok(self.dsem[i], self.dval[i], k)
        self._record(tok, reads, writes)

    def CC(self, in_ap, out_ap, groups):
        reads = [self.buf(in_ap)]
        writes = [self.buf(out_ap)]
        self._deps("pool", reads, writes)
        ins = self.eng["pool"].collective_compute("AllGather", ALU.bypass, replica_groups=groups, ins=[in_ap], outs=[out_ap])
        self.ninst += 1
        self.ccval += 1
        ins.then_inc(self.ccsem)
        tok = Tok(self.ccsem, self.ccval, "cc")
        self._record(tok, reads, writes)

    def barrier(self):
        assert all(len(p) == 0 for p in self.pend.values())
        toks = [Tok(self.sem[e], self.cnt[e], e) for e in self.eng if self.cnt[e] > 0]
        if self.ccval:
            toks.append(Tok(self.ccsem, self.ccval, "cc"))
        toks += [Tok(self.dsem[i], self.dval[i], "d%d" % i) for i in range(NDMA) if self.dval[i]]
        for e in self.eng:
            for t in toks:
                if t.key != e:
                    self._wait(e, t)

    def finish(self):
        for i in range(NDMA):
            if self.dval[i]:
                self._wait("sp", Tok(self.dsem[i], self.dval[i], "d%d" % i))

    def ACT(self, out, in_, func, bias=None, scale=None, accum=None):
        kw = {}
        if bias is not None:
            kw["bias"] = bias
        if scale is not None:
            kw["scale"] = scale
        if accum is not None:
            kw["accum_out"] = accum
        self.op("act", lambda e: e.activation(out=out, in_=in_, func=func, **kw), [in_, bias, scale], [out, accum])

    def TT(self, e, out, in0, in1, op):
        self.op(e, lambda g: g.tensor_tensor(out=out, in0=in0, in1=in1, op=op), [in0, in1], [out])

    def TS(self, e, out, in0, s1, s2, op0, op1=None):
        if op1 is None:
            self.op(e, lambda g: g.tensor_scalar(out=out, in0=in0, scalar1=s1, scalar2=None, op0=op0), [in0, s1], [out])
        else:
            self.op(e, lambda g: g.tensor_scalar(out=out, in0=in0, scalar1=s1, scalar2=s2, op0=op0, op1=op1), [in0, s1, s2], [out])

    def STT(self, out, in0, scalar, in1, op0, op1, accum=None):
        if accum is None:
            self.op("dve", lambda g: g.scalar_tensor_tensor(out=out, in0=in0, scalar=scalar, in1=in1, op0=op0, op1=op1), [in0, scalar, in1], [out])
        else:
            self.op("dve", lambda g: g.scalar_tensor_tensor(out=out, in0=in0, scalar=scalar, in1=in1, op0=op0, op1=op1, accum_out=accum),
                    [in0, scalar, in1], [out, accum])

    def CP(self, e, out, in_):
        if e == "act":
            self.ACT(out, in_, AF.Copy)
        else:
            self.op(e, lambda g: g.tensor_copy(out=out, in_=in_), [in_], [out])

    def RECIP(self, out, in_):
        self.op("dve", lambda g: g.reciprocal(out=out, in_=in_), [in_], [out])

    def MSET(self, e, out, val):
        self.op(e, lambda g: g.memset(out, val), [], [out])

    def mark(self, label):
        self.marks.append((label, self.nmm))

    def MM(self, out, lhsT, rhs, start, stop, signal=None):
        self.nmm += 1
        self.op("pe", lambda g: g.matmul(out, lhsT, rhs, start=start, stop=stop), [lhsT, rhs], [out],
                signal=(stop if signal is None else signal))

    def TR(self, out, in_, ident, signal=True):
        self.nmm += 1
        self.op("pe", lambda g: g.transpose(out=out, in_=in_, identity=ident), [in_, ident], [out], signal=signal)


@contextmanager
def scope(kb):
    with ExitStack() as es:
        yield es
        kb.barrier()


def pipe(n, stages):
    K = len(stages)
    for t in range(n + K - 1):
        for k in range(K):
            i = t - k
            if 0 <= i < n:
                stages[k](i)


def bc(ap, shape):
    return ap.unsqueeze(2).to_broadcast(shape)


def build(L, T, NL=2, npairs=4):
    nc = bass.Bass("TRN2", target_bir_lowering=False)
    Ltot = T + L
    NCH = Ltot // 128
    seqs = [dict(name="c", off=0, n=T, xoff=2, goff=15), dict(name="l", off=T, n=L, xoff=T + 6, goff=T + 45)]
    Wx = Ltot + 8
    Wg = Ltot + 60

    def din(name, shape, dt=F32):
        return nc.dram_tensor(name, list(shape), dt, kind="ExternalInput").ap()

    def dsc(name, shape, dt):
        return nc.dram_tensor(name, list(shape), dt, kind="Internal").ap()

    x_in = din("x", [L, D])
    ctx_in = din("ctx", [T, D])
    c_in = din("c", [128, 8])
    cc_in = din("c_ctx", [128, 8])
    w_mod = din("w_mod", [NL, D, 3 * D])
    b_mod = din("b_mod", [NL, 3 * D])
    norm_w = din("norm_w", [NL, D])
    w_in = din("w_in", [NL, D, 5408])
    ssd_conv_w = din("ssd_conv_w", [NL, 5, 1536])
    ssd_conv_b = din("ssd_conv_b", [NL, 1536])
    ssd_dt_bias = din("ssd_dt_bias", [NL, 32])
    ssd_a_log = din("ssd_a_log", [NL, 32])
    ssd_d = din("ssd_d", [NL, 16])
    ssd_norm_w = din("ssd_norm_w", [NL, 1024])
    attn_sink = din("attn_sink", [NL, 8])
    conv_dw_w = din("conv_dw_w", [NL, 31, 512])
    conv_dw_b = din("conv_dw_b", [NL, 512])
    conv_ln_w = din("conv_ln_w", [NL, 512])
    conv_ln_b = din("conv_ln_b", [NL, 512])
    conv_pw_w = din("conv_pw_w", [NL, 512, 512])
    conv_pw_b = din("conv_pw_b", [NL, 512])
    w_out = din("w_out", [NL, 2048, D])
    final_norm_w = din("final_norm_w", [1, D])
    cst = din("cst", [128, 7, 128])
    sel_in = din("sel", [128, 2])
    groups = [[2 * i, 2 * i + 1] for i in range(npairs)]
    rope = din("rope", [Ltot, 256])
    out = nc.dram_tensor("out", [L, D], F32, kind="ExternalOutput").ap()

    XR = dsc("XR", [Ltot, D], F32)
    XBCT = dsc("XBCT", [1536, Wx], BF16)
    GLUT = dsc("GLUT", [512, Wg], BF16)
    GCT = dsc("GCT", [512, Ltot], BF16)
    ZS = dsc("ZS", [Ltot, 1024], BF16)
    GAT = dsc("GAT", [512, Ltot], BF16)
    DTS = dsc("DTS", [Ltot, 32], F32)
    DTT = dsc("DTT", [32, Ltot], F32)
    QT = dsc("QT", [64, 8, Ltot], BF16)
    KT = dsc("KT", [64, 2, Ltot], BF16)
    VV = dsc("VV", [Ltot, 130], BF16)
    BCT = dsc("BCT", [128, 4, Ltot], BF16)
    XS = dsc("XS", [Ltot, 1024], BF16)
    BTM = dsc("BTM", [Ltot, 256], BF16)
    ABR = dsc("ABR", [2, 2, 6, NCH, 2048], BF16)
    SBD = dsc("SBD", [NCH, 128, 1024], BF16)
    OT = dsc("OT", [2048, Ltot], BF16)
    MOD = dsc("MOD", [NL, 2, 3 * D], F32)
    SFD = dsc("SFD", [NCH, 128, 1024], BF16)
    SND1 = dsc("SND1", [128, 470], BF16)
    RCV1 = nc.dram_tensor("RCV1", [256, 470], BF16, kind="Internal", addr_space="Local").ap()
    SND2 = dsc("SND2", [128, 1024], F32)
    RCV2 = nc.dram_tensor("RCV2", [256, 1024], F32, kind="Internal", addr_space="Local").ap()

    with ExitStack() as es0:
        kb = KB(nc, es0)

        uid = [0]

        def SB(es, name, shape, dt):
            uid[0] += 1
            return es.enter_context(nc.sbuf_tensor("%s_%d" % (name, uid[0]), list(shape), dt))

        def PS(es, name, shape, dt):
            uid[0] += 1
            return es.enter_context(nc.psum_tensor("%s_%d" % (name, uid[0]), list(shape), dt))

        cf = SB(es0, "cf", [128, 7, 128], F32)
        selB = SB(es0, "selB", [128, 2], F32)
        kb.DMA("sp", selB[:], sel_in[:, :])
        kb.DMA("sp", cf[:], cst[:, :, :])
        identf = cf[:, 0, :]
        triU = cf[:, 1, :]
        triL = cf[:, 2, :]
        onesf = cf[:, 3, :]
        cb_ = SB(es0, "cb", [128, 7, 128], BF16)
        kb.CP("dve", cb_[:], cf[:])
        identb = cb_[:, 0, :]
        onesb = cb_[:, 3, :]
        mneg = SB(es0, "mneg", [128, 3, 4, 128], BF16)
        for d_ in range(3):
            for r_ in range(4):
                kb.CP("dve", mneg[:, d_, r_, :], cf[:, 4 + d_, :])
        ones512 = SB(es0, "ones512", [128, 128], BF16)
        kb.MSET("dve", ones512[:], 1.0 / 512.0)
        zpad = SB(es0, "zpad", [128, 64], BF16)
        kb.MSET("dve", zpad[:], 0.0)
        for s in seqs:
            for side in range(2):
                cx = s["xoff"] - 2 if side == 0 else s["xoff"] + s["n"]
                kb.DMA("sp", XBCT[:, cx:cx + 2].rearrange("(b p) w -> p b w", p=128), zpad[:, 0:24].rearrange("p (b w) -> p b w", w=2), key=s["name"])
                cg = s["goff"] - 15 if side == 0 else s["goff"] + s["n"]
                kb.DMA("sp", GLUT[:, cg:cg + 15].rearrange("(b p) w -> p b w", p=128), zpad[:, 0:60].rearrange("p (b w) -> p b w", w=15), key=s["name"])
        onesrow = SB(es0, "onesrow", [128, 2048], BF16)
        kb.MSET("pool", onesrow[:], 1.0)
        for s in seqs:
            c0_, n0_ = s["off"] // 128, s["n"] // 128
            for d_ in range(2):
                for r_ in range(3):
                    kb.DMA("sp", ABR[d_, 0, 3 + r_, c0_:c0_ + n0_, :], onesrow[0:n0_, :], key=s["name"])
                    kb.DMA("sp", ABR[d_, 1, r_, c0_:c0_ + n0_, :], onesrow[0:n0_, :], key=s["name"])
        epsb = SB(es0, "epsb", [128, 1], F32)
        kb.MSET("pool", epsb[:], EPS)

        for l in range(NL):
            last = (l == NL - 1)
            with scope(kb) as esl:
                kb.mark("setup%d" % l)
                modB = SB(esl, "modB", [128, 2, 3, D], F32)
                aB = SB(esl, "aB", [128, 32], F32)
                dtbB = SB(esl, "dtbB", [128, 32], F32)
                snwB = SB(esl, "snwB", [128, 1024], F32)
                sinkB = SB(esl, "sinkB", [128, 8], F32)
                colsT = SB(esl, "colsT", [128, 12, 41], F32)
                dI = SB(esl, "dI", [128, 16, 128], BF16)
                nbr1 = SB(esl, "nbr1", [128, 470], BF16)
                nx = nbr1[:, 0:24].rearrange("p (b w) -> p b w", w=2)
                ng = nbr1[:, 24:84].rearrange("p (b w) -> p b w", w=15)
                with scope(kb) as ess:
                    pss = PS(ess, "pss", [128, 512], F32)
                    csil = SB(ess, "csil", [128, 2, 8], F32)
                    kb.DMA("sp", csil[:, 0, :], cc_in[:, :])
                    kb.DMA("sp", csil[:, 1, :], c_in[:, :])
                    kb.ACT(csil[:], csil[:], AF.Silu)
                    bm = SB(ess, "bm", [1, 3 * D], F32)
                    kb.DMA("sp", bm[:], b_mod[l:l + 1, :])
                    modrow = SB(ess, "modrow", [1, 2, 3 * D], F32)
                    wm = [SB(ess, "wm%d" % i, [128, 8, 512], F32) for i in range(2)]
                    for cbk in range(6):
                        w_ = wm[cbk % 2]
                        kb.DMA("sp", w_[:], w_mod[l, :, cbk * 512:(cbk + 1) * 512].rearrange("(p k) n -> p k n", k=8))
                        for m in range(2):
                            for kt in range(8):
                                kb.MM(pss[0:1, :], csil[:, m, kt:kt + 1], w_[:, kt, :], start=(kt == 0), stop=(kt == 7))
                            kb.TT("dve", modrow[:, m, cbk * 512:(cbk + 1) * 512], pss[0:1, :], bm[:, cbk * 512:(cbk + 1) * 512], ALU.add)
                    kb.DMA("sp", MOD[l:l + 1, :, :], modrow[:])
                    nwB = SB(ess, "nwB", [128, D], F32)
                    kb.DMA("sp", nwB[:], norm_w[l, :].partition_broadcast(128))
                    for m in range(2):
                        kb.DMA("sp", modB[:, m, :, :], MOD[l, m, :].rearrange("(t d) -> t d", t=3).partition_broadcast(128))
                        tmpw = SB(ess, "tmpw%d" % m, [128, D], F32)
                        kb.STT(tmpw[:], modB[:, m, 1, :], 1.0, nwB[:], ALU.add, ALU.mult)
                        kb.CP("dve", modB[:, m, 1, :], modB[:, m, 0, :])
                        kb.CP("dve", modB[:, m, 0, :], tmpw[:])
                    kb.DMA("sp", aB[:], ssd_a_log[l, :].partition_broadcast(128))
                    kb.ACT(aB[:], aB[:], AF.Exp)
                    kb.TS("dve", aB[:], aB[:], -1.0, None, ALU.mult)
                    kb.DMA("sp", dtbB[:], ssd_dt_bias[l, :].partition_broadcast(128))
                    kb.DMA("sp", snwB[:], ssd_norm_w[l, :].partition_broadcast(128))
                    kb.DMA("sp", sinkB[:], attn_sink[l, :].partition_broadcast(128))
                    kb.ACT(sinkB[:], sinkB[:], AF.Exp)
                    dB = SB(ess, "dB", [128, 16], F32)
                    kb.DMA("sp", dB[:], ssd_d[l, :].partition_broadcast(128))
                    for h in range(16):
                        kb.TS("dve", dI[:, h, :], identf, dB[:, h:h + 1], None, ALU.mult)
                    rows = SB(ess, "rows", [41, 1536], F32)
                    kb.MSET("dve", rows[:], 0.0)
                    kb.DMA("sp", rows[0:5, :], ssd_conv_w[l, :, :])
                    kb.DMA("sp", rows[5:6, :], ssd_conv_b[l:l + 1, :])
                    kb.DMA("sp", rows[6:37, 0:512], conv_dw_w[l, :, :])
                    kb.DMA("sp", rows[37:38, 0:512], conv_dw_b[l:l + 1, :])
                    kb.DMA("sp", rows[38:39, 0:512], conv_ln_w[l:l + 1, :])
                    kb.DMA("sp", rows[39:40, 0:512], conv_ln_b[l:l + 1, :])
                    kb.DMA("sp", rows[40:41, 0:512], conv_pw_b[l:l + 1, :])
                    for b in range(12):
                        kb.TR(pss[:, b * 41:(b + 1) * 41], rows[:, b * 128:(b + 1) * 128], identf[0:41, 0:41], signal=(b == 11))
                    kb.CP("dve", colsT[:].rearrange("p b r -> p (b r)"), pss[:, 0:492])

                kb.mark("A%d" % l)
                sts = []
                for s in seqs:
                    TS_ = 512 if s["name"] == "l" else 256
                    for t0 in range(0, s["n"], TS_):
                        sts.append((s, t0, TS_))
                with scope(kb) as esa:
                    wfm = SB(esa, "wfm", [128, 8, 3584], BF16)
                    wtm = SB(esa, "wtm", [128, 8, 1824], BF16)
                    wv = w_in[l].rearrange("(k p) n -> p k n", p=128)
                    with scope(kb) as esw:
                        wst = [SB(esw, "wst%d" % i, [128, 5408], F32) for i in range(2)]
                        ce = ["dve", "pool", "act"]
                        nce = 0
                        for kt in range(8):
                            w_ = wst[kt % 2]
                            kb.DMA("sp", w_[:, 0:2704], wv[:, kt, 0:2704])
                            kb.DMA("sp", w_[:, 2704:5408], wv[:, kt, 2704:5408])
                            segs = [(wfm, 0, 0, 768), (wfm, 768, 768, 768), (wfm, 2560, 4896, 512),
                                    (wtm, 0, 1536, 512), (wtm, 512, 2048, 512), (wtm, 1024, 2592, 512), (wtm, 1536, 3104, 256),
                                    (wtm, 1792, 2560, 32), (wfm, 3072, 3360, 512)]
                            for i in range(4):
                                segs.append((wfm, 1536 + 256 * i, 3872 + 128 * i, 128))
                                segs.append((wfm, 1536 + 256 * i + 128, 4384 + 128 * i, 128))
                            for (dstt, d0, s0, n_) in segs:
                                kb.CP(ce[nce % 3], dstt[:, kt, d0:d0 + n_], w_[:, s0:s0 + n_])
                                nce += 1
                    xin = [SB(esa, "xin%d" % i, [128, D], F32) for i in range(3)]
                    hb = [SB(esa, "hb%d" % i, [128, D], BF16) for i in range(2)]
                    junk = SB(esa, "junk", [128, D], BF16)
                    tmpf = SB(esa, "tmpf", [128, D], F32)
                    hT = [SB(esa, "hT%d" % i, [128, 8, 512], BF16) for i in range(2)]
                    st4 = SB(esa, "st4", [128, 8], F32)
                    fst = [SB(esa, "fst%d" % i, [128, 512], BF16) for i in range(3)]
                    sig = SB(esa, "sig", [128, 512], F32)
                    zs = [SB(esa, "zs%d" % i, [128, 1024], BF16) for i in range(2)]
                    qk = [SB(esa, "qk%d" % i, [128, 640], BF16) for i in range(2)]
                    vst = [SB(esa, "vst%d" % i, [128, 2, 65], BF16) for i in range(2)]
                    for i in range(2):
                        kb.MSET("pool", vst[i][:], 1.0)
                    qkT = [SB(esa, "qkT%d" % i, [64, 10, 128], BF16) for i in range(2)]
                    dtw = [SB(esa, "dtw%d" % i, [128, 6, 32], F32) for i in range(2)]
                    dtT = [SB(esa, "dtT%d" % i, [32, 128], F32) for i in range(2)]
                    rp = [SB(esa, "rp%d" % i, [128, 256], F32) for i in range(2)]
                    rt = SB(esa, "rt", [128, 2, 640], F32)
                    pT = [PS(esa, "pT%d" % i, [128, 1024], BF16) for i in range(2)]
                    ppool = [PS(esa, "pp%d" % i, [128, 512], F32) for i in range(6)]
                    subs = [(si, j) for si, (s, t0, TS_) in enumerate(sts) for j in range(TS_ // 128)]
                    cntA = dict(fm=0, tm=0, sc=0)

                    def a_load(k):
                        si, j = subs[k]
                        s, t0, TS_ = sts[si]
                        if l == 0:
                            src = (ctx_in if s["name"] == "c" else x_in)[t0 + j * 128:t0 + (j + 1) * 128, :]
                        else:
                            src = XR[s["off"] + t0 + j * 128:s["off"] + t0 + (j + 1) * 128, :]
                        kb.DMA("sp", xin[k % 3][:], src, key=(s["name"], t0 + j * 128))

                    def a_norm(k):
                        si, j = subs[k]
                        s, t0, TS_ = sts[si]
                        m = 0 if s["name"] == "c" else 1
                        xi, hbj = xin[k % 3], hb[k % 2]
                        col = st4[:, k % 8:k % 8 + 1]
                        kb.ACT(junk[:], xi[:], AF.Square, accum=col)
                        kb.TS("dve", col, col, 1.0 / D, EPS, ALU.mult, ALU.add)
                        kb.ACT(col, col, AF.Sqrt)
                        kb.RECIP(col, col)
                        kb.STT(tmpf[:], xi[:], col, modB[:, m, 0, :], ALU.mult, ALU.mult)
                        kb.TT("pool", hbj[:], tmpf[:], modB[:, m, 1, :], ALU.add)

                    smallq = []

                    def drain(n):
                        while n > 0 and smallq:
                            smallq.pop(0)()
                            n -= 1

                    def a_transp(k, defer=True):
                        si, j = subs[k]
                        hTt, p_, hbj = hT[si % 2], pT[k % 2], hb[k % 2]

                        def piece(q4):
                            for kt in range(2 * q4, 2 * q4 + 2):
                                kb.TR(p_[:, kt * 128:(kt + 1) * 128], hbj[:, kt * 128:(kt + 1) * 128], identb, signal=(kt == 7))
                            if q4 == 3:
                                kb.CP("act" if k % 2 else "dve", hTt[:, :, j * 128:(j + 1) * 128], p_[:].rearrange("p (k t) -> p k t", k=8))
                        for q4 in range(4):
                            if defer:
                                smallq.append(lambda q4=q4: piece(q4))
                            else:
                                piece(q4)

                    def a_fmblock(si, cbk):
                        s, t0, TS_ = sts[si]
                        hTt = hT[si % 2]
                        p_ = ppool[cntA["tm"] % 6]
                        cntA["tm"] += 1
                        for kt in range(8):
                            kb.MM(p_[:, 0:TS_], wfm[:, kt, cbk * 128:(cbk + 1) * 128], hTt[:, kt, 0:TS_], start=(kt == 0), stop=(kt == 7))
                        drain(1)
                        return p_

                    def a_fmunit(si, u):
                        s, t0, TS_ = sts[si]
                        g0 = s["off"] + t0
                        if u < 12:
                            p_ = a_fmblock(si, u)
                            f_ = fst[u % 3]
                            kb.CP("dve", f_[:, 0:TS_], p_[:, 0:TS_])
                            kb.DMA("sp", XBCT[u * 128:(u + 1) * 128, s["xoff"] + t0:s["xoff"] + t0 + TS_], f_[:, 0:TS_], key=s["name"])
                        elif u < 16:
                            i = u - 12
                            pv = a_fmblock(si, 12 + 2 * i)
                            pg = a_fmblock(si, 13 + 2 * i)
                            kb.ACT(sig[:, 0:TS_], pg[:, 0:TS_], AF.Sigmoid)
                            f_ = fst[i % 3]
                            kb.TT("dve", f_[:, 0:TS_], pv[:, 0:TS_], sig[:, 0:TS_], ALU.mult)
                            kb.DMA("sp", GLUT[i * 128:(i + 1) * 128, s["goff"] + t0:s["goff"] + t0 + TS_], f_[:, 0:TS_], key=s["name"])
                        else:
                            i = u - 16
                            p_ = a_fmblock(si, 20 + i)
                            f_ = fst[i % 3]
                            kb.ACT(f_[:, 0:TS_], p_[:, 0:TS_], AF.Silu)
                            dstT = GCT if i < 4 else GAT
                            kb.DMA("sp", dstT[(i % 4) * 128:(i % 4 + 1) * 128, g0:g0 + TS_], f_[:, 0:TS_], key=s["name"])

                    def a_tmsub(si, j):
                        s, t0, TS_ = sts[si]
                        hTt = hT[si % 2]
                        gt = s["off"] + t0 + j * 128
                        sl = cntA["sc"] % 2
                        cntA["sc"] += 1

                        def tm_block(c0, c1):
                            p_ = ppool[cntA["tm"] % 6]
                            cntA["tm"] += 1
                            for kt in range(8):
                                kb.MM(p_[:, 0:c1 - c0], hTt[:, kt, j * 128:(j + 1) * 128], wtm[:, kt, c0:c1], start=(kt == 0), stop=(kt == 7))
                            drain(1)
                            return p_
                        kb.DMA("sp", rp[sl][:], rope[gt:gt + 128, :])
                        for i in range(2):
                            p_ = tm_block(i * 512, (i + 1) * 512)
                            kb.ACT(zs[sl][:, i * 512:(i + 1) * 512], p_[:], AF.Silu)
                        kb.DMA("sp", ZS[gt:gt + 128, :], zs[sl][:], key=s["name"])
                        pq = tm_block(1024, 1536)
                        pk = tm_block(1536, 1824)
                        for (p_, nh, o0, tb) in ((pq, 8, 0, 128), (pk, 2, 512, 0)):
                            srcv = p_[:, 0:nh * 64]
                            a_ = rt[:, 0, 0:nh * 64]
                            b_ = rt[:, 1, 0:nh * 64]
                            cos2 = rp[sl][:, tb:tb + 64].unsqueeze(1).to_broadcast([128, nh, 64])
                            kb.TT("dve", a_.rearrange("p (h d) -> p h d", h=nh), srcv.rearrange("p (h d) -> p h d", h=nh), cos2, ALU.mult)
                            s4 = srcv.rearrange("p (h a x f) -> p h a x f", h=nh, a=2, x=2)
                            b4 = b_.rearrange("p (h a x f) -> p h a x f", h=nh, a=2, x=2)
                            sm = rp[sl][:, tb + 64:tb + 96].rearrange("p (a f) -> p a f", a=2).unsqueeze(1).to_broadcast([128, nh, 2, 16])
                            spl = rp[sl][:, tb + 96:tb + 128].rearrange("p (a f) -> p a f", a=2).unsqueeze(1).to_broadcast([128, nh, 2, 16])
                            kb.TT("dve", b4[:, :, :, 0, :], s4[:, :, :, 1, :], sm, ALU.mult)
                            kb.TT("dve", b4[:, :, :, 1, :], s4[:, :, :, 0, :], spl, ALU.mult)
                            kb.TT("pool", qk[sl][:, o0:o0 + nh * 64], a_, b_, ALU.add)
                        kb.CP("act", vst[sl][:, :, 0:64], pk[:, 128:256].rearrange("p (g d) -> p g d", g=2))
                        kb.DMA("sp", VV[gt:gt + 128, :], vst[sl][:].rearrange("p g d -> p (g d)"), key=s["name"])
                        dw = dtw[sl]
                        kb.TT("dve", dw[:, 0, :], pk[:, 256:288], dtbB[:], ALU.add)
                        kb.ACT(dw[:, 1, :], dw[:, 0, :], AF.Abs)
                        kb.ACT(dw[:, 2, :], dw[:, 1, :], AF.Exp, scale=-1.0)
                        kb.ACT(dw[:, 3, :], dw[:, 2, :], AF.Ln, bias=1.0)
                        kb.STT(dw[:, 5, :], dw[:, 0, :], 0.0, dw[:, 3, :], ALU.max, ALU.add)
                        kb.TS("dve", dw[:, 4, :], dw[:, 5, :], 1e-30, None, ALU.max)
                        kb.DMA("sp", DTS[gt:gt + 128, :], dw[:, 4, :], key=s["name"])
                        return (s, gt, sl, j)

                    def a_tmsub2(arg):
                        s, gt, sl, j = arg
                        dw = dtw[sl]
                        p_ = pT[j % 2]
                        p2 = pT[(j + 1) % 2]

                        def pq(q4):
                            for h in range(2 * q4, 2 * q4 + 2):
                                kb.TR(p_[0:64, h * 128:(h + 1) * 128], qk[sl][:, h * 64:(h + 1) * 64], identb, signal=(h == 7))
                            if q4 == 3:
                                kb.CP("act", qkT[sl][:, 0:8, :], p_[0:64, :].rearrange("p (h t) -> p h t", h=8))
                                kb.DMA("sp", QT[:, :, gt:gt + 128], qkT[sl][:, 0:8, :], key=s["name"])

                        def pk2():
                            for h in range(2):
                                kb.TR(p2[0:64, h * 128:(h + 1) * 128], qk[sl][:, 512 + h * 64:512 + (h + 1) * 64], identb, signal=(h == 1))
                            kb.CP("dve", qkT[sl][:, 8:10, :], p2[0:64, 0:256].rearrange("p (h t) -> p h t", h=2))
                            kb.DMA("sp", KT[:, :, gt:gt + 128], qkT[sl][:, 8:10, :], key=s["name"])
                            p3 = ppool[cntA["tm"] % 6]
                            cntA["tm"] += 1
                            kb.TR(p3[0:32, 0:128], dw[:, 4, :], identf)
                            kb.CP("dve", dtT[sl][:], p3[0:32, 0:128])
                            kb.DMA("sp", DTT[:, gt:gt + 128], dtT[sl][:], key=s["name"])
                        for q4 in range(4):
                            smallq.append(lambda q4=q4: pq(q4))
                        smallq.append(pk2)

                    nsubs = len(subs)
                    ks_of = [[k for k, (si2, j) in enumerate(subs) if si2 == si] for si in range(len(sts))]
                    pend2 = [None]
                    a_load(0)
                    for k in ks_of[0]:
                        if k + 1 < nsubs:
                            a_load(k + 1)
                        a_norm(k)
                        a_transp(k, defer=False)
                    for si in range(len(sts)):
                        nsub = len(ks_of[si])
                        nxt = ks_of[si + 1] if si + 1 < len(sts) else []
                        per = 24 // nsub
                        for q in range(nsub):
                            kn = nxt[q] if q < len(nxt) else None
                            if kn is not None:
                                if kn + 1 < nsubs:
                                    a_load(kn + 1)
                                a_norm(kn)
                            if pend2[0] is not None:
                                a_tmsub2(pend2[0])
                                pend2[0] = None
                            if kn is not None:
                                a_transp(kn)
                            for u in range(q * per, (q + 1) * per):
                                a_fmunit(si, u)
                            pend2[0] = a_tmsub(si, q)
                            drain(100)
                        for q in range(nsub, len(nxt)):
                            kn = nxt[q]
                            if kn + 1 < nsubs:
                                a_load(kn + 1)
                            a_norm(kn)
                            a_transp(kn, defer=False)
                    a_tmsub2(pend2[0])
                    drain(100)

                with scope(kb) as ex:
                    sL = seqs[1]
                    end_x, end_g, end_t = sL["xoff"] + sL["n"], sL["goff"] + sL["n"], sL["off"] + sL["n"]
                    snd = SB(ex, "snd", [128, 470], BF16)
                    kb.MSET("dve", snd[:], 0.0)
                    kb.DMA("sp", snd[:, 0:24].rearrange("p (b w) -> p b w", w=2), XBCT[:, end_x - 2:end_x].rearrange("(b p) w -> p b w", p=128), key="l")
                    kb.DMA("sp", snd[:, 24:84].rearrange("p (b w) -> p b w", w=15), GLUT[:, end_g - 15:end_g].rearrange("(b p) w -> p b w", p=128), key="l")
                    kb.DMA("sp", snd[0:64, 84:340].rearrange("p (g t) -> p g t", g=2), KT[:, :, end_t - 128:end_t], key="l")
                    kb.DMA("sp", snd[:, 340:470], VV[end_t - 128:end_t, :], key="l")
                    kb.DMA("sp", SND1[:, :], snd[:])
                    kb.CC(SND1[:, :], RCV1[:, :], groups)
                    rcv = SB(ex, "rcv", [128, 2, 470], BF16)
                    kb.DMA("sp", rcv[:], RCV1.rearrange("(r p) c -> p r c", p=128))
                    tmpx = SB(ex, "tmpx", [128, 470], F32)
                    kb.TS("dve", tmpx[:], rcv[:, 0, :], selB[:, 0:1], None, ALU.mult)
                    kb.STT(nbr1[:], rcv[:, 1, :], selB[:, 1:2], tmpx[:], ALU.mult, ALU.add)

                with scope(kb) as esb:
                    kb.mark("B0%d" % l)
                    kb.mark("B1%d" % l)
                    with scope(kb) as e1:
                        dg = SB(e1, "dg", [128, 60, 128], BF16)
                        for b in range(12):
                            for j in range(5):
                                if (b + j) % 2:
                                    kb.TS("dve", dg[:, b * 5 + j, :], identf, colsT[:, b, j:j + 1], None, ALU.mult)
                                else:
                                    kb.ACT(dg[:, b * 5 + j, :], identf, AF.Copy, scale=colsT[:, b, j:j + 1])
                        win = [SB(e1, "win%d" % i, [128, 12, 516], BF16) for i in range(2)]
                        xc = [SB(e1, "xc%d" % i, [128, 12, 512], BF16) for i in range(2)]
                        xst = [SB(e1, "xst%d" % i, [128, 1280], BF16) for i in range(2)]
                        pcv = [PS(e1, "pcv%d" % i, [128, 512], F32) for i in range(2)]
                        ptr = [PS(e1, "ptr%d" % i, [128, 1024], BF16) for i in range(2)]
                        ptb = [PS(e1, "ptb%d" % i, [128, 256], BF16) for i in range(2)]
                        cntB = dict(j=0)

                        def b1_s0(i):
                            s, t0, TS_ = sts[i]
                            if i == len(sts) - 1:
                                kb.DMA("sp", win[i % 2][:, :, 0:TS_ + 2], XBCT[:, s["xoff"] + t0 - 2:s["xoff"] + t0 + TS_].rearrange("(b p) w -> p b w", p=128), key=s["name"])
                                for r in range(2):
                                    kb.CP("pool", win[i % 2][:, :, TS_ + 2 + r:TS_ + 3 + r], nx[:, :, 1 - r:2 - r])
                            else:
                                kb.DMA("sp", win[i % 2][:, :, 0:TS_ + 4], XBCT[:, s["xoff"] + t0 - 2:s["xoff"] + t0 + TS_ + 2].rearrange("(b p) w -> p b w", p=128), key=s["name"])

                        def b1_s1(i):
                            s, t0, TS_ = sts[i]
                            w_, x_ = win[i % 2], xc[i % 2]
                            g0 = s["off"] + t0
                            for b in range(12):
                                p_ = pcv[b % 2]
                                for j in range(5):
                                    kb.MM(p_[:, 0:TS_], dg[:, b * 5 + j, :], w_[:, b, j:j + TS_], start=(j == 0), stop=(j == 4))
                                kb.ACT(x_[:, b, 0:TS_], p_[:, 0:TS_], AF.Silu, bias=colsT[:, b, 5:6])
                            kb.DMA("sp", BCT[:, :, g0:g0 + TS_], x_[:, 8:12, 0:TS_], key=s["name"])

                        def b1_s2(i):
                            s, t0, TS_ = sts[i]
                            x_ = xc[i % 2]
                            g0 = s["off"] + t0
                            for j in range(TS_ // 128):
                                gt = g0 + j * 128
                                nj = cntB["j"]
                                cntB["j"] += 1
                                p_, pb_, xs_ = ptr[nj % 2], ptb[nj % 2], xst[nj % 2]
                                for b in range(8):
                                    kb.TR(p_[:, b * 128:(b + 1) * 128], x_[:, b, j * 128:(j + 1) * 128], identb, signal=(b == 7))
                                for b in range(2):
                                    kb.TR(pb_[:, b * 128:(b + 1) * 128], x_[:, 8 + b, j * 128:(j + 1) * 128], identb, signal=(b == 1))
                                kb.CP("dve", xs_[:, 0:1024], p_[:])
                                kb.CP("pool" if False else "dve", xs_[:, 1024:1280], pb_[:])
                                kb.DMA("sp", XS[gt:gt + 128, :], xs_[:, 0:1024], key=s["name"])
                                kb.DMA("sp", BTM[gt:gt + 128, :], xs_[:, 1024:1280], key=s["name"])
                        rmask = SB(e1, "rmask", [32, 16, 128], F32)
                        kb.MSET("dve", rmask[:], 1.0)
                        kb.MSET("dve", rmask[:, :, 0:1], 0.0)
                        dr = SB(e1, "dr", [32, 16, 128], F32)
                        da = SB(e1, "da", [32, 16, 128], F32)
                        cs = SB(e1, "cs", [32, 16, 128], F32)
                        ld = SB(e1, "ld", [32, 16, 128], F32)
                        al = SB(e1, "al", [32, 16, 128], F32)
                        r1 = SB(e1, "r1", [32, 16, 128], F32)
                        sp3 = SB(e1, "sp3", [32, 3, 2048], BF16)
                        b0_items = []
                        for s in seqs:
                            for cb0 in range(0, s["n"] // 128, 32):
                                for d_ in range(2):
                                    b0_items.append((s, cb0, d_))

                        def b0_piece(k):
                            s, cb0, d_ = b0_items[k]
                            nch = s["n"] // 128
                            c0 = s["off"] // 128
                            n_ = min(32, nch - cb0)
                            tk0 = s["off"] + cb0 * 128
                            kb.DMA("sp", dr[0:n_], DTT[d_ * 16:(d_ + 1) * 16, tk0:tk0 + n_ * 128].rearrange("h (c t) -> c h t", t=128), key=s["name"])
                            kb.TT("dve", da[0:n_], dr[0:n_], bc(aB[0:n_, d_ * 16:(d_ + 1) * 16], [n_, 16, 128]), ALU.mult)
                            kb.op("dve", lambda g: g.tensor_tensor_scan(out=cs[0:n_].rearrange("c h t -> c (h t)"), data0=rmask[0:n_].rearrange("c h t -> c (h t)"),
                                                                          data1=da[0:n_].rearrange("c h t -> c (h t)"), initial=0.0, op0=ALU.mult, op1=ALU.add),
                                  [rmask[0:n_], da[0:n_]], [cs[0:n_]])
                            if d_ == 1:
                                kb.TT("dve", r1[0:n_], da[0:n_], cs[0:n_], ALU.subtract)
                                kb.CP("dve", al[0:n_, :, 0:1], cs[0:n_, :, 127:128])
                                kb.TT("dve", cs[0:n_], r1[0:n_], al[0:n_, :, 0:1].to_broadcast([n_, 16, 128]), ALU.add)
                            kb.ACT(ld[0:n_], dr[0:n_], AF.Ln)
                            kb.TT("dve", al[0:n_], ld[0:n_], cs[0:n_], ALU.subtract)
                            for side, srcx in ((0, al), (1, cs)):
                                sx = srcx[0:n_].rearrange("c h t -> c (h t)")
                                r1f = r1[0:n_].rearrange("c h t -> c (h t)")
                                kb.CP("act", sp3[0:n_, 0, :], sx)
                                kb.TT("dve", r1f, sx, sp3[0:n_, 0, :], ALU.subtract)
                                kb.CP("act", sp3[0:n_, 1, :], r1f)
                                kb.TT("dve", r1f, r1f, sp3[0:n_, 1, :], ALU.subtract)
                                kb.CP("act", sp3[0:n_, 2, :], r1f)
                                for r_ in range(3):
                                    kb.DMA("sp", ABR[d_, side, 3 * side + r_, c0 + cb0:c0 + cb0 + n_, :], sp3[0:n_, r_, :], key=s["name"])

                        nb0 = 0
                        for t_ in range(len(sts) + 2):
                            for k_, st_ in enumerate([b1_s0, b1_s1, b1_s2]):
                                if 0 <= t_ - k_ < len(sts):
                                    st_(t_ - k_)
                            if nb0 < len(b0_items):
                                b0_piece(nb0)
                                nb0 += 1
                        while nb0 < len(b0_items):
                            b0_piece(nb0)
                            nb0 += 1

                    with scope(kb) as e2:
                        Sf = SB(e2, "Sf", [128, 1024], F32)
                        Sb_ = SB(e2, "Sb", [128, 1024], F32)
                        Sfb = SB(e2, "Sfb", [128, 1024], BF16)
                        Sbb = [SB(e2, "Sbb%d" % i, [128, 1024], BF16) for i in range(2)]
                        kb.MSET("dve", Sf[:], 0.0)
                        kb.MSET("dve", Sb_[:], 0.0)
                        kb.MSET("pool", Sfb[:], 0.0)
                        xsb = [SB(e2, "xsb%d" % i, [128, 1024], BF16) for i in range(3)]
                        btm = [SB(e2, "btm%d" % i, [128, 256], BF16) for i in range(3)]
                        dts = [SB(e2, "dts%d" % i, [128, 32], F32) for i in range(3)]
                        xdw = [SB(e2, "xdw%d" % i, [128, 1024], BF16) for i in range(2)]
                        sm_ = [SB(e2, "sm%d" % i, [128, 8, 32], F32) for i in range(2)]
                        psm = PS(e2, "psm", [128, 512], F32)
                        pO = PS(e2, "pO", [128, 1024], F32)

                        def small_terms(sl, d_, dt_ap):
                            S_ = sm_[sl]
                            o = d_ * 16
                            kb.TT("dve", S_[:, 0, o:o + 16], dt_ap[:, o:o + 16], aB[:, o:o + 16], ALU.mult)
                            kb.MM(psm[:, o:o + 16], triU if d_ == 0 else triL, S_[:, 0, o:o + 16], start=True, stop=True)
                            kb.MM(psm[:, 64 + o:64 + o + 16], onesf, S_[:, 0, o:o + 16], start=True, stop=True)
                            kb.CP("dve", S_[:, 1, o:o + 16], psm[:, o:o + 16])
                            kb.TT("dve", S_[:, 2, o:o + 16], psm[:, 64 + o:64 + o + 16], S_[:, 1, o:o + 16], ALU.subtract)
                            kb.ACT(S_[:, 3, o:o + 16], S_[:, 2, o:o + 16], AF.Exp)
                            kb.TT("dve", S_[:, 3, o:o + 16], S_[:, 3, o:o + 16], dt_ap[:, o:o + 16], ALU.mult)
                            kb.ACT(S_[:, 4, o:o + 16], S_[:, 1, o:o + 16], AF.Exp)
                            kb.ACT(S_[:, 5, o:o + 16], psm[:, 64 + o:64 + o + 16], AF.Exp)
                            return S_[:, 4, o:o + 16], S_[:, 3, o:o + 16], S_[:, 5, o:o + 16]

                        def mk_xdw(xd_t, xs_ap, w_ap):
                            kb.TT("pool", xd_t[:].rearrange("p (h d) -> p h d", h=16), xs_ap.rearrange("p (h d) -> p h d", h=16), bc(w_ap, [128, 16, 64]), ALU.mult)

                        def state_update(Smaster, bt_ap, etot_ap, xd_t):
                            for g in range(2):
                                kb.MM(pO[:, g * 512:(g + 1) * 512], bt_ap[:, g * 128:(g + 1) * 128], xd_t[:, g * 512:(g + 1) * 512], start=True, stop=True)
                            kb.TT("dve", Smaster[:].rearrange("p (h d) -> p h d", h=16), Smaster[:].rearrange("p (h d) -> p h d", h=16), bc(etot_ap, [128, 16, 64]), ALU.mult)
                            kb.TT("dve", Smaster[:], Smaster[:], pO[:], ALU.add)

                        kb.mark("B2%d" % l)
                        chf = []
                        for s in seqs:
                            for c in range(s["n"] // 128):
                                chf.append((s, c))

                        def bf_s0(i):
                            s, c = chf[i]
                            gt = s["off"] + c * 128
                            kb.DMA("sp", xsb[i % 3][:], XS[gt:gt + 128, :], key=s["name"])
                            kb.DMA("sp", btm[i % 3][:], BTM[gt:gt + 128, :], key=s["name"])
                            kb.DMA("sp", dts[i % 3][:], DTS[gt:gt + 128, :], key=s["name"])

                        def bf_s1(i):
                            s, c = chf[i]
                            gc = (s["off"] + c * 128) // 128
                            _, w_ap, et_ap = small_terms(i % 2, 0, dts[i % 3])
                            mk_xdw(xdw[i % 2], xsb[i % 3][:], w_ap)
                            kb.CP("act", Sbb[i % 2][:], Sf[:])
                            kb.DMA("sp", SFD[gc, :, :], Sbb[i % 2][:], key=gc)
                            state_update(Sf, btm[i % 3], et_ap, xdw[i % 2])
                        pipe(len(chf), [bf_s0, bf_s1])
                        nbS = SB(e2, "nbS", [128, 1024], F32)
                        with scope(kb) as ex2:
                            rcs2 = SB(ex2, "rcs2", [128, 2, 1024], F32)
                            kb.DMA("sp", SND2[:, :], Sf[:])
                            kb.CC(SND2[:, :], RCV2[:, :], groups)
                            kb.DMA("sp", rcs2[:], RCV2.rearrange("(r p) c -> p r c", p=128))
                            kb.TS("dve", nbS[:], rcs2[:, 0, :], selB[:, 0:1], None, ALU.mult)
                            kb.STT(nbS[:], rcs2[:, 1, :], selB[:, 1:2], nbS[:], ALU.mult, ALU.add)
                        chd = []
                        for s in seqs:
                            for c in reversed(range(s["n"] // 128)):
                                chd.append((s, c))
                        first_lat = seqs[0]["n"] // 128

                        def b2_s0(i):
                            s, c = chd[i]
                            gt = s["off"] + c * 128
                            kb.DMA("sp", xsb[i % 3][:], XS[gt:gt + 128, :], key=s["name"])
                            kb.DMA("sp", btm[i % 3][:], BTM[gt:gt + 128, :], key=s["name"])
                            kb.DMA("sp", dts[i % 3][:], DTS[gt:gt + 128, :], key=s["name"])

                        def b2_s1(i):
                            s, c = chd[i]
                            gc = (s["off"] + c * 128) // 128
                            if i == first_lat:
                                kb.CP("dve", Sb_[:], nbS[:])
                            _, w_ap, et_ap = small_terms(i % 2, 1, dts[i % 3])
                            mk_xdw(xdw[i % 2], xsb[i % 3][:], w_ap)
                            kb.CP("act", Sbb[i % 2][:], Sb_[:])
                            kb.DMA("sp", SBD[gc, :, :], Sbb[i % 2][:], key=gc)
                            state_update(Sb_, btm[i % 3], et_ap, xdw[i % 2])
                        pipe(len(chd), [b2_s0, b2_s1])

                        kb.mark("B3%d" % l)
                        with scope(kb) as e3:
                            bct = [SB(e3, "bct%d" % i, [128, 4, 128], BF16) for i in range(3)]
                            zsb = [SB(e3, "zsb%d" % i, [128, 1024], BF16) for i in range(2)]
                            sbe = [SB(e3, "sbe%d" % i, [128, 1024], BF16) for i in range(2)]
                            sfe = [SB(e3, "sfe%d" % i, [128, 1024], BF16) for i in range(2)]
                            ab6 = [SB(e3, "ab6%d" % i, [6, 2, 2, 2048], BF16) for i in range(2)]
                            cbT = SB(e3, "cbT", [128, 2, 128], BF16)
                            E_ = [SB(e3, "E%d" % i, [128, 4, 128], BF16) for i in range(2)]
                            MT = SB(e3, "MT", [128, 2, 16, 128], BF16)
                            ysb = [SB(e3, "ysb%d" % i, [128, 1024], F32) for i in range(2)]
                            t1 = SB(e3, "t1", [128, 1024], F32)
                            t2 = SB(e3, "t2", [128, 1024], F32)
                            us = [SB(e3, "u%d" % i, [128, 1024], F32) for i in range(2)]
                            obs = [SB(e3, "ob%d" % i, [128, 1024], BF16) for i in range(2)]
                            sq = SB(e3, "sq", [128, 512], F32)
                            oT = [SB(e3, "oT%d" % i, [128, 8, 128], BF16) for i in range(2)]
                            st2 = SB(e3, "st2", [128, 2, 2], F32)
                            pDs = [PS(e3, "pD%d" % i, [128, 512], F32) for i in range(2)]
                            pYs = [PS(e3, "pY%d" % i, [128, 512], F32) for i in range(2)]
                            pX = PS(e3, "pX", [128, 1024], BF16)
                            cntD = dict(d=0)
                            cha = []
                            for s in seqs:
                                for c in range(s["n"] // 128):
                                    cha.append((s, c))

                            def ny(s):
                                return not (last and s["name"] == "c")

                            def b3_s0(i):
                                s, c = cha[i]
                                gt = s["off"] + c * 128
                                gc = gt // 128
                                kb.DMA("sp", xsb[i % 3][:], XS[gt:gt + 128, :], key=s["name"])
                                kb.DMA("sp", dts[i % 3][:], DTS[gt:gt + 128, :], key=s["name"])
                                if ny(s):
                                    kb.DMA("sp", bct[i % 3][:], BCT[:, :, gt:gt + 128], key=s["name"])
                                    kb.DMA("sp", ab6[i % 2][:], ABR[:, :, :, gc, :].rearrange("d s r x -> r d s x"), key=s["name"])

                            def b3_s1(i):
                                s, c = cha[i]
                                gt = s["off"] + c * 128
                                gc = gt // 128
                                sl3, sl = i % 3, i % 2
                                if not ny(s):
                                    return
                                ef, wf, etf = small_terms(sl, 0, dts[sl3])
                                kb.DMA("sp", zsb[sl][:], ZS[gt:gt + 128, :], key=s["name"])
                                kb.DMA("sp", sbe[sl][:], SBD[gc, :, :], key=gc)
                                kb.DMA("sp", sfe[sl][:], SFD[gc, :, :], key=gc)
                                small_terms(sl, 1, dts[sl3])
                                for g in range(2):
                                    kb.MM(psm[:, 256 + g * 128:256 + (g + 1) * 128], bct[sl3][:, g, :], bct[sl3][:, 2 + g, :], start=True, stop=True)
                                kb.CP("dve", cbT[:].rearrange("p g t -> p (g t)"), psm[:, 256:512])
                                for g in range(2):
                                    for d_ in range(2):
                                        for hb_ in range(2):
                                            pD = pDs[cntD["d"] % 2]
                                            e_ = E_[cntD["d"] % 2]
                                            cntD["d"] += 1
                                            kb.MM(pD[:], identb, mneg[:, d_, :, :].rearrange("p r t -> p (r t)"), start=True, stop=False, signal=False)
                                            for hh in range(4):
                                                h = g * 8 + hb_ * 4 + hh
                                                kb.MM(pD[:, hh * 128:(hh + 1) * 128], ab6[sl][:, d_, 0, h * 128:(h + 1) * 128],
                                                      ab6[sl][:, d_, 1, h * 128:(h + 1) * 128], start=False, stop=(hh == 3), signal=(hh == 3))
                                            kb.ACT(e_[:].rearrange("p h t -> p (h t)"), pD[:], AF.Exp)
                                            h0 = g * 8 + hb_ * 4
                                            kb.TT("dve", MT[:, d_, h0:h0 + 4, :], e_[:], cbT[:, g, :].unsqueeze(1).to_broadcast([128, 4, 128]), ALU.mult)
                                    pY = pYs[g]
                                    for hh in range(8):
                                        h = g * 8 + hh
                                        rhs = xsb[sl3][:, h * 64:(h + 1) * 64]
                                        kb.MM(pY[:, hh * 64:(hh + 1) * 64], MT[:, 0, h, :], rhs, start=True, stop=False, signal=False)
                                        kb.MM(pY[:, hh * 64:(hh + 1) * 64], MT[:, 1, h, :], rhs, start=False, stop=False, signal=False)
                                        kb.MM(pY[:, hh * 64:(hh + 1) * 64], dI[:, h, :], rhs, start=False, stop=True, signal=(hh == 7))
                                    kb.CP("act", ysb[sl][:, g * 512:(g + 1) * 512], pY[:])

                            def b3_s2(i):
                                s, c = cha[i]
                                if not ny(s):
                                    return
                                sl3, sl = i % 3, i % 2
                                S_ = sm_[sl]
                                ef, eb = S_[:, 4, 0:16], S_[:, 4, 16:32]
                                u = us[sl]
                                for g in range(2):
                                    kb.MM(pO[:, g * 512:(g + 1) * 512], bct[sl3][:, 2 + g, :], sfe[sl][:, g * 512:(g + 1) * 512], start=True, stop=True)
                                kb.TT("dve", t1[:].rearrange("p (h d) -> p h d", h=16), pO[:].rearrange("p (h d) -> p h d", h=16), bc(ef, [128, 16, 64]), ALU.mult)
                                for g in range(2):
                                    kb.MM(pO[:, g * 512:(g + 1) * 512], bct[sl3][:, 2 + g, :], sbe[sl][:, g * 512:(g + 1) * 512], start=True, stop=True)
                                kb.TT("dve", t2[:].rearrange("p (h d) -> p h d", h=16), pO[:].rearrange("p (h d) -> p h d", h=16), bc(eb, [128, 16, 64]), ALU.mult)
                                kb.TT("pool", u[:], t1[:], t2[:], ALU.add)
                                kb.TT("pool", u[:], u[:], ysb[sl][:], ALU.add)
                                kb.TT("pool", u[:], u[:], zsb[sl][:], ALU.mult)

                            def b3_s3(i):
                                s, c = cha[i]
                                if not ny(s):
                                    return
                                sl = i % 2
                                u = us[sl]
                                ss_ = st2[:, sl, :]
                                for g in range(2):
                                    kb.STT(sq[:], u[:, g * 512:(g + 1) * 512], 1.0, u[:, g * 512:(g + 1) * 512], ALU.mult, ALU.mult, accum=st2[:, sl, g:g + 1])
                                kb.TS("dve", ss_, ss_, 1.0 / 512.0, EPS, ALU.mult, ALU.add)
                                kb.ACT(ss_, ss_, AF.Ln)
                                kb.ACT(ss_, ss_, AF.Exp, scale=-0.5)
                                for g in range(2):
                                    kb.STT(obs[sl][:, g * 512:(g + 1) * 512], u[:, g * 512:(g + 1) * 512], st2[:, sl, g:g + 1], snwB[:, g * 512:(g + 1) * 512], ALU.mult, ALU.mult)

                            def b3_s4(i):
                                s, c = cha[i]
                                if not ny(s):
                                    return
                                gt = s["off"] + c * 128
                                sl = i % 2
                                for b in range(8):
                                    kb.TR(pX[:, b * 128:(b + 1) * 128], obs[sl][:, b * 128:(b + 1) * 128], identb, signal=(b == 7))
                                kb.CP("act", oT[sl][:].rearrange("p b t -> p (b t)"), pX[:])
                                kb.DMA("sp", OT[0:1024, gt:gt + 128].rearrange("(b p) t -> p b t", p=128), oT[sl][:], key=s["name"])
                            pipe(len(cha), [b3_s0, b3_s1, b3_s2, b3_s3, b3_s4])

                kb.mark("C%d" % l)
                with scope(kb) as ec:
                    ktb = SB(ec, "ktb", [64, 2, Ltot + 128], BF16)
                    vvb = SB(ec, "vvb", [128, NCH + 1, 130], BF16)
                    kb.CP("dve", ktb[:, :, Ltot:Ltot + 128], nbr1[0:64, 84:340].rearrange("p (g t) -> p g t", g=2))
                    kb.CP("dve", vvb[:, NCH, :], nbr1[:, 340:470])
                    for s in seqs:
                        o_, n_ = s["off"], s["n"]
                        kb.DMA("sp", ktb[:, :, o_:o_ + n_], KT[:, :, o_:o_ + n_], key=s["name"])
                        kb.DMA("sp", vvb[:, o_ // 128:(o_ + n_) // 128, :], VV[o_:o_ + n_, :].rearrange("(c p) x -> p c x", p=128), key=s["name"])
                    qtb = [SB(ec, "qtb%d" % i, [64, 8, 128], BF16) for i in range(3)]
                    gab = [SB(ec, "gab%d" % i, [64, 8, 128], BF16) for i in range(3)]
                    Ee = [SB(ec, "Ee%d" % i, [128, 512], BF16) for i in range(20)]
                    den = [SB(ec, "den%d" % i, [64, 512], F32) for i in range(2)]
                    oa = [SB(ec, "oa%d" % i, [64, 512], F32) for i in range(2)]
                    oaT = [SB(ec, "oaT%d" % i, [64, 512], BF16) for i in range(2)]
                    pS = [PS(ec, "pS%d" % i, [128, 512], F32) for i in range(3)]
                    pV = [PS(ec, "pV%d" % i, [64, 512], F32) for i in range(2)]
                    pU = [PS(ec, "pU%d" % i, [64, 512], F32) for i in range(2)]
                    qbs = []
                    for s in seqs:
                        if last and s["name"] == "c":
                            continue
                        for qb in range(s["n"] // 128):
                            qbs.append((s, qb))
                    cntC = dict(s=0, o=0)

                    def keyblocks(s, qb):
                        kbl = [(0, None), (1, None)]
                        if s["name"] == "l":
                            nb = s["n"] // 128
                            c0 = s["off"] // 128
                            if qb > 0:
                                kbl.append((c0 + qb - 1, 1))
                            kbl.append((c0 + qb, None))
                            if qb < nb - 1:
                                kbl.append((c0 + qb + 1, 0))
                            else:
                                kbl.append((NCH, 2))
                        return kbl

                    def c_s0(i):
                        s, qb = qbs[i]
                        gt = s["off"] + qb * 128
                        kb.DMA("sp", qtb[i % 3][:], QT[:, :, gt:gt + 128], key=s["name"])
                        kb.DMA("sp", gab[i % 3][:], GAT[:, gt:gt + 128].rearrange("(h d) t -> d h t", d=64), key=s["name"])

                    def c_s1(i):
                        s, qb = qbs[i]
                        q_ = qtb[i % 3]
                        for g in range(2):
                            for ki, (kc, md) in enumerate(keyblocks(s, qb)):
                                p_ = pS[cntC["s"] % 3]
                                cntC["s"] += 1
                                if md is not None:
                                    kb.MM(p_[:], identb, mneg[:, md, :, :].rearrange("p r t -> p (r t)"), start=True, stop=False, signal=False)
                                kb.MM(p_[:], ktb[:, g, kc * 128:(kc + 1) * 128], q_[:, g * 4:(g + 1) * 4, :].rearrange("p h t -> p (h t)"),
                                      start=(md is None), stop=True)
                                kb.ACT(Ee[(i % 2) * 10 + g * 5 + ki][:], p_[:], AF.Exp)

                    def c_s2(i):
                        s, qb = qbs[i]
                        gt = s["off"] + qb * 128
                        g_ = gab[i % 3]
                        kbl = keyblocks(s, qb)
                        for g in range(2):
                            no = cntC["o"]
                            cntC["o"] += 1
                            pv, pu = pV[no % 2], pU[no % 2]
                            for ki, (kc, md) in enumerate(kbl):
                                kb.MM(pv[:], vvb[:, kc, g * 65:g * 65 + 64], Ee[(i % 2) * 10 + g * 5 + ki][:], start=(ki == 0), stop=(ki == len(kbl) - 1))
                            for ki, (kc, md) in enumerate(kbl):
                                kb.MM(pu[:], onesb[:, 0:64], Ee[(i % 2) * 10 + g * 5 + ki][:], start=(ki == 0), stop=(ki == len(kbl) - 1))
                            dn = den[no % 2]
                            kb.TT("dve", dn[:].rearrange("p (r t) -> p r t", r=4), pu[:].rearrange("p (r t) -> p r t", r=4), bc(sinkB[0:64, g * 4:(g + 1) * 4], [64, 4, 128]), ALU.add)
                            kb.ACT(dn[:], dn[:], AF.Ln)
                            kb.ACT(dn[:], dn[:], AF.Exp, scale=-1.0)
                            o_ = oa[no % 2]
                            kb.TT("dve", o_[:], pv[:], dn[:], ALU.mult)
                            ot_ = oaT[no % 2]
                            kb.TT("pool", ot_[:], o_[:], g_[:, g * 4:(g + 1) * 4, :].rearrange("p h t -> p (h t)"), ALU.mult)
                            kb.DMA("sp", OT[1024 + g * 256:1024 + (g + 1) * 256, gt:gt + 128].rearrange("(r d) t -> d r t", d=64), ot_[:].rearrange("p (r t) -> p r t", r=4), key=s["name"])
                    pipe(len(qbs), [c_s0, c_s1, c_s2])

                kb.mark("D%d" % l)
                with scope(kb) as ed:
                    dgc = SB(ed, "dgc", [128, 124, 128], BF16)
                    for b in range(4):
                        for j in range(31):
                            if (b + j) % 2:
                                kb.TS("dve", dgc[:, b * 31 + j, :], identf, colsT[:, b, 6 + j:7 + j], None, ALU.mult)
                            else:
                                kb.ACT(dgc[:, b * 31 + j, :], identf, AF.Copy, scale=colsT[:, b, 6 + j:7 + j])
                    pww = SB(ed, "pww", [128, 4, 512], BF16)
                    pwst = SB(ed, "pwst", [128, 4, 512], F32)
                    kb.DMA("sp", pwst[:], conv_pw_w[l].rearrange("(k p) n -> p k n", p=128))
                    kb.CP("act", pww[:], pwst[:])
                    gw = [SB(ed, "gw%d" % i, [128, 4, 542], BF16) for i in range(2)]
                    gcb = [SB(ed, "gcb%d" % i, [128, 4, 512], BF16) for i in range(2)]
                    hbb = [SB(ed, "hbb%d" % i, [128, 4, 512], BF16) for i in range(2)]
                    hsq = [SB(ed, "hsq%d" % i, [128, 4, 512], BF16) for i in range(2)]
                    mean = SB(ed, "mean", [128, 512], F32)
                    rstd = SB(ed, "rstd", [128, 512], F32)
                    xcn = [SB(ed, "xcn%d" % i, [128, 512], F32) for i in range(2)]
                    h2 = [SB(ed, "h2%d" % i, [128, 4, 512], BF16) for i in range(2)]
                    oc = [SB(ed, "oc%d" % i, [128, 512], BF16) for i in range(2)]
                    pc = [PS(ed, "pc%d" % i, [128, 512], F32) for i in range(2)]
                    pm = PS(ed, "pm", [128, 512], F32)
                    pe2 = PS(ed, "pe2", [128, 512], F32)
                    pw_ = [PS(ed, "pw%d" % i, [128, 512], F32) for i in range(2)]
                    dst = [x for x in sts if not (last and x[0]["name"] == "c")]

                    def d_s0(i):
                        s, t0, TS_ = dst[i]
                        if i == len(dst) - 1:
                            kb.DMA("sp", gw[i % 2][:, :, 0:TS_ + 15], GLUT[:, s["goff"] + t0 - 15:s["goff"] + t0 + TS_].rearrange("(b p) w -> p b w", p=128), key=s["name"])
                            for r in range(15):
                                kb.CP("pool", gw[i % 2][:, :, TS_ + 15 + r:TS_ + 16 + r], ng[:, :, 14 - r:15 - r])
                        else:
                            kb.DMA("sp", gw[i % 2][:, :, 0:TS_ + 30], GLUT[:, s["goff"] + t0 - 15:s["goff"] + t0 + TS_ + 15].rearrange("(b p) w -> p b w", p=128), key=s["name"])

                    def d_s1(i):
                        s, t0, TS_ = dst[i]
                        w_ = gw[i % 2]
                        for b in range(4):
                            p_ = pc[b % 2]
                            for j in range(31):
                                kb.MM(p_[:, 0:TS_], dgc[:, b * 31 + j, :], w_[:, b, j:j + TS_], start=(j == 0), stop=(j == 30))
                            kb.ACT(hbb[i % 2][:, b, 0:TS_], p_[:, 0:TS_], AF.Identity, bias=colsT[:, b, 37:38])
                            kb.TT("dve", hsq[i % 2][:, b, 0:TS_], hbb[i % 2][:, b, 0:TS_], hbb[i % 2][:, b, 0:TS_], ALU.mult)

                    def d_s2(i):
                        s, t0, TS_ = dst[i]
                        g0 = s["off"] + t0
                        kb.DMA("sp", gcb[i % 2][:, :, 0:TS_], GCT[:, g0:g0 + TS_].rearrange("(b p) w -> p b w", p=128), key=s["name"])
                        hb_, hs_ = hbb[i % 2], hsq[i % 2]
                        for b in range(4):
                            kb.MM(pm[:, 0:TS_], ones512[:], hb_[:, b, 0:TS_], start=(b == 0), stop=(b == 3))
                        for b in range(4):
                            kb.MM(pe2[:, 0:TS_], ones512[:], hs_[:, b, 0:TS_], start=(b == 0), stop=(b == 3))
                        kb.CP("act", mean[:, 0:TS_], pm[:, 0:TS_])
                        kb.TT("dve", rstd[:, 0:TS_], mean[:, 0:TS_], mean[:, 0:TS_], ALU.mult)
                        kb.TT("dve", rstd[:, 0:TS_], pe2[:, 0:TS_], rstd[:, 0:TS_], ALU.subtract)
                        kb.ACT(rstd[:, 0:TS_], rstd[:, 0:TS_], AF.Ln, bias=epsb[:])
                        kb.ACT(rstd[:, 0:TS_], rstd[:, 0:TS_], AF.Exp, scale=-0.5)
                        for b in range(4):
                            x_ = xcn[b % 2]
                            kb.TT("dve", x_[:, 0:TS_], hb_[:, b, 0:TS_], mean[:, 0:TS_], ALU.subtract)
                            kb.TT("dve", x_[:, 0:TS_], x_[:, 0:TS_], rstd[:, 0:TS_], ALU.mult)
                            kb.ACT(h2[i % 2][:, b, 0:TS_], x_[:, 0:TS_], AF.Silu, bias=colsT[:, b, 39:40], scale=colsT[:, b, 38:39])

                    def d_s3(i):
                        s, t0, TS_ = dst[i]
                        g0 = s["off"] + t0
                        for co in range(4):
                            p_ = pw_[co % 2]
                            for ci in range(4):
                                kb.MM(p_[:, 0:TS_], pww[:, ci, co * 128:(co + 1) * 128], h2[i % 2][:, ci, 0:TS_], start=(ci == 0), stop=(ci == 3))
                            o_ = oc[co % 2]
                            kb.STT(o_[:, 0:TS_], p_[:, 0:TS_], colsT[:, co, 40:41], gcb[i % 2][:, co, 0:TS_], ALU.add, ALU.mult)
                            kb.DMA("sp", OT[1536 + co * 128:1536 + (co + 1) * 128, g0:g0 + TS_], o_[:, 0:TS_], key=s["name"])
                    pipe(len(dst), [d_s0, d_s1, d_s2, d_s3])

                kb.mark("E%d" % l)
                with scope(kb) as ee:
                    wo = SB(ee, "wo", [128, 16, D], BF16)
                    wov = w_out[l].rearrange("(k p) n -> p k n", p=128)
                    with scope(kb) as eow:
                        wos = [SB(eow, "wos%d" % i, [128, 4, D], F32) for i in range(2)]
                        for k4 in range(4):
                            kb.DMA("sp", wos[k4 % 2][:], wov[:, k4 * 4:(k4 + 1) * 4, :])
                            for j4 in range(4):
                                kb.CP(["dve", "pool", "act", "pool"][j4], wo[:, k4 * 4 + j4, :], wos[k4 % 2][:, j4, :])
                    fnw = SB(ee, "fnw", [128, D], F32)
                    kb.DMA("sp", fnw[:], final_norm_w[0, :].partition_broadcast(128))
                    otb = [SB(ee, "otb%d" % i, [128, 16, 512], BF16) for i in range(2)]
                    xo = [SB(ee, "xo%d" % i, [128, D], F32) for i in range(3)]
                    xn = [SB(ee, "xn%d" % i, [128, D], F32) for i in range(2)]
                    yo = [SB(ee, "yo%d" % i, [128, D], F32) for i in range(2)]
                    jk = SB(ee, "jk", [128, D], BF16)
                    s1 = SB(ee, "s1", [128, 2], F32)
                    po = [PS(ee, "po%d" % i, [128, 1024], F32) for i in range(2)]
                    che = []
                    for s in seqs:
                        if last and s["name"] == "c":
                            continue
                        for c in range(s["n"] // 128):
                            che.append((s, c))

                    def e_s0(i):
                        s, c = che[i]
                        gt = s["off"] + c * 128
                        bw = 512 if s["name"] == "l" else 256
                        if (c * 128) % bw == 0:
                            kb.DMA("sp", otb[((c * 128) // (512 if s["name"] == "l" else 256) + (1 if s["name"] == "l" else 0)) % 2][:, :, 0:bw], OT[:, gt:gt + bw].rearrange("(k p) t -> p k t", p=128), key=s["name"])
                        if l == 0:
                            src = (ctx_in if s["name"] == "c" else x_in)[c * 128:(c + 1) * 128, :]
                        else:
                            src = XR[gt:gt + 128, :]
                        kb.DMA("sp", xo[i % 3][:], src, key=(s["name"], c * 128))

                    def e_s1(i):
                        s, c = che[i]
                        gt = s["off"] + c * 128
                        m = 0 if s["name"] == "c" else 1
                        sl = i % 2
                        bw_ = 512 if s["name"] == "l" else 256
                        p_ = po[sl]
                        for nb_ in range(2):
                            for kt in range(16):
                                kb.MM(p_[:, nb_ * 512:(nb_ + 1) * 512], otb[((c * 128) // (512 if s["name"] == "l" else 256) + (1 if s["name"] == "l" else 0)) % 2][:, kt, (c * 128) % bw_:(c * 128) % bw_ + 128],
                                      wo[:, kt, nb_ * 512:(nb_ + 1) * 512], start=(kt == 0), stop=(kt == 15))
                        kb.TT("dve", xn[sl][:], p_[:], modB[:, m, 2, :], ALU.mult)
                        kb.TT("pool", xn[sl][:], xn[sl][:], xo[i % 3][:], ALU.add)
                        if not last:
                            kb.DMA("sp", XR[gt:gt + 128, :], xn[sl][:], key=(s["name"], c * 128))
                        else:
                            sc_ = s1[:, sl:sl + 1]
                            kb.ACT(jk[:], xn[sl][:], AF.Square, accum=sc_)
                            kb.TS("dve", sc_, sc_, 1.0 / D, EPS, ALU.mult, ALU.add)
                            kb.ACT(sc_, sc_, AF.Sqrt)
                            kb.RECIP(sc_, sc_)
                            kb.STT(yo[sl][:], xn[sl][:], sc_, fnw[:], ALU.mult, ALU.mult)
                            kb.DMA("sp", out[c * 128:(c + 1) * 128, :], yo[sl][:], key="out")
                    pipe(len(che), [e_s0, e_s1])
        kb.mark("end")
        kb.finish()
    return nc, kb


def host_consts(L, T, pos, grid_w=64):
    k = np.arange(128)[:, None]
    t = np.arange(128)[None, :]
    cst = np.zeros((128, 7, 128), np.float32)
    cst[:, 0] = np.eye(128)
    cst[:, 1] = (k <= t)
    cst[:, 2] = (k >= t)
    cst[:, 3] = 1.0
    cst[:, 4] = np.where(k <= t, 0.0, NEG)
    cst[:, 5] = np.where(k >= t, 0.0, NEG)
    cst[:, 6] = np.where(k + t >= 127, 0.0, NEG)
    rope = np.zeros((T + L, 256), np.float32)
    rope[:, 0:64] = 1.0
    rope[:, 128:192] = 0.125
    row = (pos // grid_w).astype(np.float32)
    col = (pos % grid_w).astype(np.float32)
    inv = (10000.0 ** (-np.arange(0, 32, 2, dtype=np.float32) / 32.0)).astype(np.float32)
    ang = np.stack([row[:, None] * inv, col[:, None] * inv], axis=1).astype(np.float32)
    cs_, sn_ = np.cos(ang), np.sin(ang)
    cos2 = np.repeat(cs_[:, :, None, :], 2, axis=2).reshape(L, 64)
    rope[T:, 0:64] = cos2
    rope[T:, 64:96] = (-sn_).reshape(L, 32)
    rope[T:, 96:128] = sn_.reshape(L, 32)
    rope[T:, 128:192] = cos2 * 0.125
    rope[T:, 192:224] = (-sn_).reshape(L, 32) * 0.125
    rope[T:, 224:256] = sn_.reshape(L, 32) * 0.125
    return cst, rope


_CACHE = {}


def run(inputs, L, T, NL, batches):
    npairs = len(batches)
    key = (L, T, NL, npairs)
    if key not in _CACHE:
        _CACHE[key] = build(L, T, NL, npairs)[0]
    nc = _CACHE[key]
    f = lambda a: np.ascontiguousarray(np.asarray(a, dtype=np.float32))
    base = {
        "c_ctx": f(inputs["c_ctx"]).reshape(128, 8),
        "w_mod": f(inputs["w_mod"]), "b_mod": f(inputs["b_mod"]), "norm_w": f(inputs["norm_w"]),
        "ssd_conv_b": f(inputs["ssd_conv_b"]),
        "ssd_d": f(inputs["ssd_d"]), "ssd_norm_w": f(inputs["ssd_norm_w"]), "attn_sink": f(inputs["attn_sink"]),
        "conv_dw_b": f(inputs["conv_dw_b"]), "conv_ln_w": f(inputs["conv_ln_w"]),
        "conv_ln_b": f(inputs["conv_ln_b"]), "conv_pw_w": f(inputs["conv_pw_w"]), "conv_pw_b": f(inputs["conv_pw_b"]),
        "w_out": f(inputs["w_out"]), "final_norm_w": f(inputs["final_norm_w"]).reshape(1, D),
    }
    w_in = f(inputs["w_in"])
    w_in_m = w_in.copy()
    w_in_m[:, :, 2560:2576] = w_in[:, :, 2576:2592]
    w_in_m[:, :, 2576:2592] = w_in[:, :, 2560:2576]
    dtb = f(inputs["ssd_dt_bias"])
    alog = f(inputs["ssd_a_log"])
    scw = f(inputs["ssd_conv_w"])
    dww = f(inputs["conv_dw_w"])
    half = []
    for h in range(2):
        pos = np.arange(L) if h == 0 else (2 * L - 1 - np.arange(L))
        cst, rope = host_consts(L, T, pos)
        sel = np.zeros((128, 2), np.float32)
        sel[:, 1 - h] = 1.0
        d = dict(base)
        d.update({
            "cst": cst, "rope": rope, "sel": sel,
            "w_in": w_in if h == 0 else w_in_m,
            "ssd_dt_bias": f(dtb if h == 0 else dtb[:, ::-1]).reshape(NL, 32),
            "ssd_a_log": f(alog if h == 0 else alog[:, ::-1]).reshape(NL, 32),
            "ssd_conv_w": f(scw if h == 0 else scw[:, ::-1]),
            "conv_dw_w": f(dww if h == 0 else dww[:, ::-1]),
        })
        half.append(d)
    x = f(inputs["x"])
    c = f(inputs["c"])
    ctx = f(inputs["ctx"])
    in_maps = []
    for b in batches:
        for h in range(2):
            m = dict(half[h])
            if h == 0:
                m["x"] = f(x[b, 0:L])
                m["ctx"] = f(ctx[b])
            else:
                m["x"] = f(x[b, L:2 * L][::-1])
                m["ctx"] = f(ctx[b][::-1])
            m["c"] = c[b].reshape(128, 8)
            in_maps.append(m)
    res = run_bass_kernel_spmd(nc, in_maps, core_ids=list(range(2 * npairs)))
    outs = []
    for i in range(npairs):
        o0 = res.results[2 * i]["out"]
        o1 = res.results[2 * i + 1]["out"][::-1]
        outs.append(np.concatenate([o0, o1], axis=0))
    return outs


def kernel(**inputs):
    B = np.asarray(inputs["x"]).shape[0]
    outs = run(inputs, 2048, 256, 2, list(range(B)))
    return np.stack(outs, axis=0).astype(np.float32)
```

```python
import numpy as np
from contextlib import ExitStack, contextmanager
import concourse.bass as bass
import concourse.mybir as mybir
from concourse.bass_utils import run_bass_kernel_spmd

F32 = mybir.dt.float32
BF16 = mybir.dt.bfloat16
AF = mybir.ActivationFunctionType
ALU = mybir.AluOpType
NDMA = 48
EPS = 1e-6
D = 1024
NEG = -30000.0


class Tok:
    __slots__ = ("sem", "val", "key")

    def __init__(self, sem, val, key):
        self.sem, self.val, self.key = sem, val, key


class Buf:
    __slots__ = ("w", "r", "multi")

    def __init__(self, multi=False):
        self.w = {}
        self.r = {}
        self.multi = multi


class KB:
    def __init__(self, nc, es):
        self.nc = nc
        self.eng = {"pe": nc.tensor, "act": nc.scalar, "dve": nc.vector, "pool": nc.gpsimd, "sp": nc.sync}
        self.sem = {e: es.enter_context(nc.semaphore("s_" + e)) for e in self.eng}
        self.cnt = {e: 0 for e in self.eng}
        self.known = {e: {} for e in self.eng}
        self.dsem = [es.enter_context(nc.semaphore("d%d" % i)) for i in range(NDMA)]
        self.dval = [0] * NDMA
        self.dnext = 0
        self.pend = {e: [] for e in self.eng}
        self.bufs = {}
        self.ccsem = es.enter_context(nc.semaphore("ccsem"))
        self.ccval = 0
        self.ninst = 0
        self.nmm = 0
        self.marks = []

    def buf(self, ap, key=None):
        t = ap.tensor
        name = t.name
        dram = str(ap.space).lower().find("dram") >= 0 or str(ap.space).lower().find("hbm") >= 0
        k = (name, key if dram else None)
        b = self.bufs.get(k)
        if b is None:
            b = Buf(multi=dram)
            self.bufs[k] = b
        return b

    def _wait(self, e, tok):
        if e == "pe" and tok.key == "pe":
            return
        k = self.known[e]
        if k.get(tok.key, 0) >= tok.val:
            return
        self.eng[e].wait_ge(tok.sem, tok.val)
        k[tok.key] = tok.val

    def _deps(self, e, reads, writes):
        for b in reads:
            for t in b.w.values():
                self._wait(e, t)
        for b in writes:
            if not b.multi:
                for t in b.w.values():
                    self._wait(e, t)
            for t in b.r.values():
                self._wait(e, t)

    def _record(self, tok, reads, writes):
        for b in reads:
            b.r[tok.key] = tok
        for b in writes:
            if b.multi:
                b.w[tok.key] = tok
            else:
                b.w = {tok.key: tok}
                b.r = {}

    def op(self, e, fn, rd, wr, signal=True, key=None):
        reads = [self.buf(a, key) for a in rd if a is not None and not isinstance(a, (int, float))]
        writes = [self.buf(a, key) for a in wr if a is not None]
        self._deps(e, reads, writes)
        ins = fn(self.eng[e])
        self.ninst += 1
        if not signal:
            self.pend[e].append((reads, writes))
            return
        self.cnt[e] += 1
        ins.then_inc(self.sem[e], 1)
        tok = Tok(self.sem[e], self.cnt[e], e)
        for (r, w) in self.pend[e]:
            self._record(tok, r, w)
        self.pend[e] = []
        self._record(tok, reads, writes)

    def DMA(self, q, out, in_, key=None, **kw):
        reads = [self.buf(in_, key)]
        writes = [self.buf(out, key)]
        self._deps(q, reads, writes)
        i = self.dnext
        self.dnext = (i + 1) % NDMA
        k = "d%d" % i
        if self.dval[i] > 0:
            self._wait(q, Tok(self.dsem[i], self.dval[i], k))
        ins = self.eng[q].dma_start(out=out, in_=in_, **kw)
        self.ninst += 1
        self.dval[i] += 16
        ins.then_inc(self.dsem[i], 16)
        tok = Tok(self.dsem[i], self.dval[i], k)
        self._record(tok, reads, writes)

    def CC(self, in_ap, out_ap, groups):
        reads = [self.buf(in_ap)]
        writes = [self.buf(out_ap)]
        self._deps("pool", reads, writes)
        ins = self.eng["pool"].collective_compute("AllGather", ALU.bypass, replica_groups=groups, ins=[in_ap], outs=[out_ap])
        self.ninst += 1
        self.ccval += 1
        ins.then_inc(self.ccsem)
        tok = Tok(self.ccsem, self.ccval, "cc")
        self._record(tok, reads, writes)

    def barrier(self):
        assert all(len(p) == 0 for p in self.pend.values())
        toks = [Tok(self.sem[e], self.cnt[e], e) for e in self.eng if self.cnt[e] > 0]
        if self.ccval:
            toks.append(Tok(self.ccsem, self.ccval, "cc"))
        toks += [Tok(self.dsem[i], self.dval[i], "d%d" % i) for i in range(NDMA) if self.dval[i]]
        for e in self.eng:
            for t in toks:
                if t.key != e:
                    self._wait(e, t)

    def finish(self):
        for i in range(NDMA):
            if self.dval[i]:
                self._wait("sp", Tok(self.dsem[i], self.dval[i], "d%d" % i))

    def ACT(self, out, in_, func, bias=None, scale=None, accum=None):
        kw = {}
        if bias is not None:
            kw["bias"] = bias
        if scale is not None:
            kw["scale"] = scale
        if accum is not None:
            kw["accum_out"] = accum
        self.op("act", lambda e: e.activation(out=out, in_=in_, func=func, **kw), [in_, bias, scale], [out, accum])

    def TT(self, e, out, in0, in1, op):
        self.op(e, lambda g: g.tensor_tensor(out=out, in0=in0, in1=in1, op=op), [in0, in1], [out])

    def TS(self, e, out, in0, s1, s2, op0, op1=None):
        if op1 is None:
            self.op(e, lambda g: g.tensor_scalar(out=out, in0=in0, scalar1=s1, scalar2=None, op0=op0), [in0, s1], [out])
        else:
            self.op(e, lambda g: g.tensor_scalar(out=out, in0=in0, scalar1=s1, scalar2=s2, op0=op0, op1=op1), [in0, s1, s2], [out])

    def STT(self, out, in0, scalar, in1, op0, op1, accum=None):
        if accum is None:
            self.op("dve", lambda g: g.scalar_tensor_tensor(out=out, in0=in0, scalar=scalar, in1=in1, op0=op0, op1=op1), [in0, scalar, in1], [out])
        else:
            self.op("dve", lambda g: g.scalar_tensor_tensor(out=out, in0=in0, scalar=scalar, in1=in1, op0=op0, op1=op1, accum_out=accum),
                    [in0, scalar, in1], [out, accum])

    def CP(self, e, out, in_):
        if e == "act":
            self.ACT(out, in_, AF.Copy)
        else:
            self.op(e, lambda g: g.tensor_copy(out=out, in_=in_), [in_], [out])

    def RECIP(self, out, in_):
        self.op("dve", lambda g: g.reciprocal(out=out, in_=in_), [in_], [out])

    def MSET(self, e, out, val):
        self.op(e, lambda g: g.memset(out, val), [], [out])

    def mark(self, label):
        self.marks.append((label, self.nmm))

    def MM(self, out, lhsT, rhs, start, stop, signal=None):
        self.nmm += 1
        self.op("pe", lambda g: g.matmul(out, lhsT, rhs, start=start, stop=stop), [lhsT, rhs], [out],
                signal=(stop if signal is None else signal))

    def TR(self, out, in_, ident, signal=True):
        self.nmm += 1
        self.op("pe", lambda g: g.transpose(out=out, in_=in_, identity=ident), [in_, ident], [out], signal=signal)


@contextmanager
def scope(kb):
    with ExitStack() as es:
        yield es
        kb.barrier()


def pipe(n, stages):
    K = len(stages)
    for t in range(n + K - 1):
        for k in range(K):
            i = t - k
            if 0 <= i < n:
                stages[k](i)


def bc(ap, shape):
    return ap.unsqueeze(2).to_broadcast(shape)


def build(L, T, NL=2, npairs=4):
    nc = bass.Bass("TRN2", target_bir_lowering=False)
    Ltot = T + L
    NCH = Ltot // 128
    seqs = [dict(name="c", off=0, n=T, xoff=2, goff=15), dict(name="l", off=T, n=L, xoff=T + 6, goff=T + 45)]
    Wx = Ltot + 8
    Wg = Ltot + 60

    def din(name, shape, dt=F32):
        return nc.dram_tensor(name, list(shape), dt, kind="ExternalInput").ap()

    def dsc(name, shape, dt):
        return nc.dram_tensor(name, list(shape), dt, kind="Internal").ap()

    x_in = din("x", [L, D])
    ctx_in = din("ctx", [T, D])
    c_in = din("c", [128, 8])
    cc_in = din("c_ctx", [128, 8])
    w_mod = din("w_mod", [NL, D, 3 * D])
    b_mod = din("b_mod", [NL, 3 * D])
    norm_w = din("norm_w", [NL, D])
    w_in = din("w_in", [NL, D, 5408])
    ssd_conv_w = din("ssd_conv_w", [NL, 5, 1536])
    ssd_conv_b = din("ssd_conv_b", [NL, 1536])
    ssd_dt_bias = din("ssd_dt_bias", [NL, 32])
    ssd_a_log = din("ssd_a_log", [NL, 32])
    ssd_d = din("ssd_d", [NL, 16])
    ssd_norm_w = din("ssd_norm_w", [NL, 1024])
    attn_sink = din("attn_sink", [NL, 8])
    conv_dw_w = din("conv_dw_w", [NL, 31, 512])
    conv_dw_b = din("conv_dw_b", [NL, 512])
    conv_ln_w = din("conv_ln_w", [NL, 512])
    conv_ln_b = din("conv_ln_b", [NL, 512])
    conv_pw_w = din("conv_pw_w", [NL, 512, 512])
    conv_pw_b = din("conv_pw_b", [NL, 512])
    w_out = din("w_out", [NL, 2048, D])
    final_norm_w = din("final_norm_w", [1, D])
    cst = din("cst", [128, 7, 128])
    sel_in = din("sel", [128, 2])
    groups = [[2 * i, 2 * i + 1] for i in range(npairs)]
    rope = din("rope", [Ltot, 256])
    out = nc.dram_tensor("out", [L, D], F32, kind="ExternalOutput").ap()

    XR = dsc("XR", [Ltot, D], F32)
    XBCT = dsc("XBCT", [1536, Wx], BF16)
    GLUT = dsc("GLUT", [512, Wg], BF16)
    GCT = dsc("GCT", [512, Ltot], BF16)
    ZS = dsc("ZS", [Ltot, 1024], BF16)
    GAT = dsc("GAT", [512, Ltot], BF16)
    DTS = dsc("DTS", [Ltot, 32], F32)
    DTT = dsc("DTT", [32, Ltot], F32)
    QT = dsc("QT", [64, 8, Ltot], BF16)
    KT = dsc("KT", [64, 2, Ltot], BF16)
    VV = dsc("VV", [Ltot, 130], BF16)
    BCT = dsc("BCT", [128, 4, Ltot], BF16)
    XS = dsc("XS", [Ltot, 1024], BF16)
    BTM = dsc("BTM", [Ltot, 256], BF16)
    ABR = dsc("ABR", [2, 2, 6, NCH, 2048], BF16)
    SBD = dsc("SBD", [NCH, 128, 1024], BF16)
    OT = dsc("OT", [2048, Ltot], BF16)
    MOD = dsc("MOD", [NL, 2, 3 * D], F32)
    SFD = dsc("SFD", [NCH, 128, 1024], BF16)
    SND1 = dsc("SND1", [128, 470], BF16)
    RCV1 = nc.dram_tensor("RCV1", [256, 470], BF16, kind="Internal", addr_space="Local").ap()
    SND2 = dsc("SND2", [128, 1024], F32)
    RCV2 = nc.dram_tensor("RCV2", [256, 1024], F32, kind="Internal", addr_space="Local").ap()

    with ExitStack() as es0:
        kb = KB(nc, es0)

        uid = [0]

        def SB(es, name, shape, dt):
            uid[0] += 1
            return es.enter_context(nc.sbuf_tensor("%s_%d" % (name, uid[0]), list(shape), dt))

        def PS(es, name, shape, dt):
            uid[0] += 1
            return es.enter_context(nc.psum_tensor("%s_%d" % (name, uid[0]), list(shape), dt))

        cf = SB(es0, "cf", [128, 7, 128], F32)
        selB = SB(es0, "selB", [128, 2], F32)
        kb.DMA("sp", selB[:], sel_in[:, :])
        kb.DMA("sp", cf[:], cst[:, :, :])
        identf = cf[:, 0, :]
        triU = cf[:, 1, :]
        triL = cf[:, 2, :]
        onesf = cf[:, 3, :]
        cb_ = SB(es0, "cb", [128, 7, 128], BF16)
        kb.CP("dve", cb_[:], cf[:])
        identb = cb_[:, 0, :]
        onesb = cb_[:, 3, :]
        mneg = SB(es0, "mneg", [128, 3, 4, 128], BF16)
        for d_ in range(3):
            for r_ in range(4):
                kb.CP("dve", mneg[:, d_, r_, :], cf[:, 4 + d_, :])
        ones512 = SB(es0, "ones512", [128, 128], BF16)
        kb.MSET("dve", ones512[:], 1.0 / 512.0)
        zpad = SB(es0, "zpad", [128, 64], BF16)
        kb.MSET("dve", zpad[:], 0.0)
        for s in seqs:
            for side in range(2):
                cx = s["xoff"] - 2 if side == 0 else s["xoff"] + s["n"]
                kb.DMA("sp", XBCT[:, cx:cx + 2].rearrange("(b p) w -> p b w", p=128), zpad[:, 0:24].rearrange("p (b w) -> p b w", w=2), key=s["name"])
                cg = s["goff"] - 15 if side == 0 else s["goff"] + s["n"]
                kb.DMA("sp", GLUT[:, cg:cg + 15].rearrange("(b p) w -> p b w", p=128), zpad[:, 0:60].rearrange("p (b w) -> p b w", w=15), key=s["name"])
        onesrow = SB(es0, "onesrow", [128, 2048], BF16)
        kb.MSET("pool", onesrow[:], 1.0)
        for s in seqs:
            c0_, n0_ = s["off"] // 128, s["n"] // 128
            for d_ in range(2):
                for r_ in range(3):
                    kb.DMA("sp", ABR[d_, 0, 3 + r_, c0_:c0_ + n0_, :], onesrow[0:n0_, :], key=s["name"])
                    kb.DMA("sp", ABR[d_, 1, r_, c0_:c0_ + n0_, :], onesrow[0:n0_, :], key=s["name"])
        epsb = SB(es0, "epsb", [128, 1], F32)
        kb.MSET("pool", epsb[:], EPS)

        for l in range(NL):
            last = (l == NL - 1)
            with scope(kb) as esl:
                kb.mark("setup%d" % l)
                modB = SB(esl, "modB", [128, 2, 3, D], F32)
                aB = SB(esl, "aB", [128, 32], F32)
                dtbB = SB(esl, "dtbB", [128, 32], F32)
                snwB = SB(esl, "snwB", [128, 1024], F32)
                sinkB = SB(esl, "sinkB", [128, 8], F32)
                colsT = SB(esl, "colsT", [128, 12, 41], F32)
                dI = SB(esl, "dI", [128, 16, 128], BF16)
                nbr1 = SB(esl, "nbr1", [128, 470], BF16)
                nx = nbr1[:, 0:24].rearrange("p (b w) -> p b w", w=2)
                ng = nbr1[:, 24:84].rearrange("p (b w) -> p b w", w=15)
                with scope(kb) as ess:
                    pss = PS(ess, "pss", [128, 512], F32)
                    csil = SB(ess, "csil", [128, 2, 8], F32)
                    kb.DMA("sp", csil[:, 0, :], cc_in[:, :])
                    kb.DMA("sp", csil[:, 1, :], c_in[:, :])
                    kb.ACT(csil[:], csil[:], AF.Silu)
                    bm = SB(ess, "bm", [1, 3 * D], F32)
                    kb.DMA("sp", bm[:], b_mod[l:l + 1, :])
                    modrow = SB(ess, "modrow", [1, 2, 3 * D], F32)
                    wm = [SB(ess, "wm%d" % i, [128, 8, 512], F32) for i in range(2)]
                    for cbk in range(6):
                        w_ = wm[cbk % 2]
                        kb.DMA("sp", w_[:], w_mod[l, :, cbk * 512:(cbk + 1) * 512].rearrange("(p k) n -> p k n", k=8))
                        for m in range(2):
                            for kt in range(8):
                                kb.MM(pss[0:1, :], csil[:, m, kt:kt + 1], w_[:, kt, :], start=(kt == 0), stop=(kt == 7))
                            kb.TT("dve", modrow[:, m, cbk * 512:(cbk + 1) * 512], pss[0:1, :], bm[:, cbk * 512:(cbk + 1) * 512], ALU.add)
                    kb.DMA("sp", MOD[l:l + 1, :, :], modrow[:])
                    nwB = SB(ess, "nwB", [128, D], F32)
                    kb.DMA("sp", nwB[:], norm_w[l, :].partition_broadcast(128))
                    for m in range(2):
                        kb.DMA("sp", modB[:, m, :, :], MOD[l, m, :].rearrange("(t d) -> t d", t=3).partition_broadcast(128))
                        tmpw = SB(ess, "tmpw%d" % m, [128, D], F32)
                        kb.STT(tmpw[:], modB[:, m, 1, :], 1.0, nwB[:], ALU.add, ALU.mult)
                        kb.CP("dve", modB[:, m, 1, :], modB[:, m, 0, :])
                        kb.CP("dve", modB[:, m, 0, :], tmpw[:])
                    kb.DMA("sp", aB[:], ssd_a_log[l, :].partition_broadcast(128))
                    kb.ACT(aB[:], aB[:], AF.Exp)
                    kb.TS("dve", aB[:], aB[:], -1.0, None, ALU.mult)
                    kb.DMA("sp", dtbB[:], ssd_dt_bias[l, :].partition_broadcast(128))
                    kb.DMA("sp", snwB[:], ssd_norm_w[l, :].partition_broadcast(128))
                    kb.DMA("sp", sinkB[:], attn_sink[l, :].partition_broadcast(128))
                    kb.ACT(sinkB[:], sinkB[:], AF.Exp)
                    dB = SB(ess, "dB", [128, 16], F32)
                    kb.DMA("sp", dB[:], ssd_d[l, :].partition_broadcast(128))
                    for h in range(16):
                        kb.TS("dve", dI[:, h, :], identf, dB[:, h:h + 1], None, ALU.mult)
                    rows = SB(ess, "rows", [41, 1536], F32)
                    kb.MSET("dve", rows[:], 0.0)
                    kb.DMA("sp", rows[0:5, :], ssd_conv_w[l, :, :])
                    kb.DMA("sp", rows[5:6, :], ssd_conv_b[l:l + 1, :])
                    kb.DMA("sp", rows[6:37, 0:512], conv_dw_w[l, :, :])
                    kb.DMA("sp", rows[37:38, 0:512], conv_dw_b[l:l + 1, :])
                    kb.DMA("sp", rows[38:39, 0:512], conv_ln_w[l:l + 1, :])
                    kb.DMA("sp", rows[39:40, 0:512], conv_ln_b[l:l + 1, :])
                    kb.DMA("sp", rows[40:41, 0:512], conv_pw_b[l:l + 1, :])
                    for b in range(12):
                        kb.TR(pss[:, b * 41:(b + 1) * 41], rows[:, b * 128:(b + 1) * 128], identf[0:41, 0:41], signal=(b == 11))
                    kb.CP("dve", colsT[:].rearrange("p b r -> p (b r)"), pss[:, 0:492])

                kb.mark("A%d" % l)
                sts = []
                for s in seqs:
                    TS_ = 512 if s["name"] == "l" else 256
                    for t0 in range(0, s["n"], TS_):
                        sts.append((s, t0, TS_))
                with scope(kb) as esa:
                    wfm = SB(esa, "wfm", [128, 8, 3584], BF16)
                    wtm = SB(esa, "wtm", [128, 8, 1824], BF16)
                    wv = w_in[l].rearrange("(k p) n -> p k n", p=128)
                    with scope(kb) as esw:
                        wst = [SB(esw, "wst%d" % i, [128, 5408], F32) for i in range(2)]
                        ce = ["dve", "pool", "act"]
                        nce = 0
                        for kt in range(8):
                            w_ = wst[kt % 2]
                            kb.DMA("sp", w_[:, 0:2704], wv[:, kt, 0:2704])
                            kb.DMA("sp", w_[:, 2704:5408], wv[:, kt, 2704:5408])
                            segs = [(wfm, 0, 0, 768), (wfm, 768, 768, 768), (wfm, 2560, 4896, 512),
                                    (wtm, 0, 1536, 512), (wtm, 512, 2048, 512), (wtm, 1024, 2592, 512), (wtm, 1536, 3104, 256),
                                    (wtm, 1792, 2560, 32), (wfm, 3072, 3360, 512)]
                            for i in range(4):
                                segs.append((wfm, 1536 + 256 * i, 3872 + 128 * i, 128))
                                segs.append((wfm, 1536 + 256 * i + 128, 4384 + 128 * i, 128))
                            for (dstt, d0, s0, n_) in segs:
                                kb.CP(ce[nce % 3], dstt[:, kt, d0:d0 + n_], w_[:, s0:s0 + n_])
                                nce += 1
                    xin = [SB(esa, "xin%d" % i, [128, D], F32) for i in range(3)]
                    hb = [SB(esa, "hb%d" % i, [128, D], BF16) for i in range(2)]
                    junk = SB(esa, "junk", [128, D], BF16)
                    tmpf = SB(esa, "tmpf", [128, D], F32)
                    hT = [SB(esa, "hT%d" % i, [128, 8, 512], BF16) for i in range(2)]
                    st4 = SB(esa, "st4", [128, 8], F32)
                    fst = [SB(esa, "fst%d" % i, [128, 512], BF16) for i in range(3)]
                    sig = SB(esa, "sig", [128, 512], F32)
                    zs = [SB(esa, "zs%d" % i, [128, 1024], BF16) for i in range(2)]
                    qk = [SB(esa, "qk%d" % i, [128, 640], BF16) for i in range(2)]
                    vst = [SB(esa, "vst%d" % i, [128, 2, 65], BF16) for i in range(2)]
                    for i in range(2):
                        kb.MSET("pool", vst[i][:], 1.0)
                    qkT = [SB(esa, "qkT%d" % i, [64, 10, 128], BF16) for i in range(2)]
                    dtw = [SB(esa, "dtw%d" % i, [128, 6, 32], F32) for i in range(2)]
                    dtT = [SB(esa, "dtT%d" % i, [32, 128], F32) for i in range(2)]
                    rp = [SB(esa, "rp%d" % i, [128, 256], F32) for i in range(2)]
                    rt = SB(esa, "rt", [128, 2, 640], F32)
                    pT = [PS(esa, "pT%d" % i, [128, 1024], BF16) for i in range(2)]
                    ppool = [PS(esa, "pp%d" % i, [128, 512], F32) for i in range(6)]
                    subs = [(si, j) for si, (s, t0, TS_) in enumerate(sts) for j in range(TS_ // 128)]
                    cntA = dict(fm=0, tm=0, sc=0)

                    def a_load(k):
                        si, j = subs[k]
                        s, t0, TS_ = sts[si]
                        if l == 0:
                            src = (ctx_in if s["name"] == "c" else x_in)[t0 + j * 128:t0 + (j + 1) * 128, :]
                        else:
                            src = XR[s["off"] + t0 + j * 128:s["off"] + t0 + (j + 1) * 128, :]
                        kb.DMA("sp", xin[k % 3][:], src, key=(s["name"], t0 + j * 128))

                    def a_norm(k):
                        si, j = subs[k]
                        s, t0, TS_ = sts[si]
                        m = 0 if s["name"] == "c" else 1
                        xi, hbj = xin[k % 3], hb[k % 2]
                        col = st4[:, k % 8:k % 8 + 1]
                        kb.ACT(junk[:], xi[:], AF.Square, accum=col)
                        kb.TS("dve", col, col, 1.0 / D, EPS, ALU.mult, ALU.add)
                        kb.ACT(col, col, AF.Sqrt)
                        kb.RECIP(col, col)
                        kb.STT(tmpf[:], xi[:], col, modB[:, m, 0, :], ALU.mult, ALU.mult)
                        kb.TT("pool", hbj[:], tmpf[:], modB[:, m, 1, :], ALU.add)

                    smallq = []

                    def drain(n):
                        while n > 0 and smallq:
                            smallq.pop(0)()
                            n -= 1

                    def a_transp(k, defer=True):
                        si, j = subs[k]
                        hTt, p_, hbj = hT[si % 2], pT[k % 2], hb[k % 2]

                        def piece(q4):
                            for kt in range(2 * q4, 2 * q4 + 2):
                                kb.TR(p_[:, kt * 128:(kt + 1) * 128], hbj[:, kt * 128:(kt + 1) * 128], identb, signal=(kt == 7))
                            if q4 == 3:
                                kb.CP("act" if k % 2 else "dve", hTt[:, :, j * 128:(j + 1) * 128], p_[:].rearrange("p (k t) -> p k t", k=8))
                        for q4 in range(4):
                            if defer:
                                smallq.append(lambda q4=q4: piece(q4))
                            else:
                                piece(q4)

                    def a_fmblock(si, cbk):
                        s, t0, TS_ = sts[si]
                        hTt = hT[si % 2]
                        p_ = ppool[cntA["tm"] % 6]
                        cntA["tm"] += 1
                        for kt in range(8):
                            kb.MM(p_[:, 0:TS_], wfm[:, kt, cbk * 128:(cbk + 1) * 128], hTt[:, kt, 0:TS_], start=(kt == 0), stop=(kt == 7))
                        drain(1)
                        return p_

                    def a_fmunit(si, u):
                        s, t0, TS_ = sts[si]
                        g0 = s["off"] + t0
                        if u < 12:
                            p_ = a_fmblock(si, u)
                            f_ = fst[u % 3]
                            kb.CP("dve", f_[:, 0:TS_], p_[:, 0:TS_])
                            kb.DMA("sp", XBCT[u * 128:(u + 1) * 128, s["xoff"] + t0:s["xoff"] + t0 + TS_], f_[:, 0:TS_], key=s["name"])
                        elif u < 16:
                            i = u - 12
                            pv = a_fmblock(si, 12 + 2 * i)
                            pg = a_fmblock(si, 13 + 2 * i)
                            kb.ACT(sig[:, 0:TS_], pg[:, 0:TS_], AF.Sigmoid)
                            f_ = fst[i % 3]
                            kb.TT("dve", f_[:, 0:TS_], pv[:, 0:TS_], sig[:, 0:TS_], ALU.mult)
                            kb.DMA("sp", GLUT[i * 128:(i + 1) * 128, s["goff"] + t0:s["goff"] + t0 + TS_], f_[:, 0:TS_], key=s["name"])
                        else:
                            i = u - 16
                            p_ = a_fmblock(si, 20 + i)
                            f_ = fst[i % 3]
                            kb.ACT(f_[:, 0:TS_], p_[:, 0:TS_], AF.Silu)
                            dstT = GCT if i < 4 else GAT
                            kb.DMA("sp", dstT[(i % 4) * 128:(i % 4 + 1) * 128, g0:g0 + TS_], f_[:, 0:TS_], key=s["name"])

                    def a_tmsub(si, j):
                        s, t0, TS_ = sts[si]
                        hTt = hT[si % 2]
                        gt = s["off"] + t0 + j * 128
                        sl = cntA["sc"] % 2
                        cntA["sc"] += 1

                        def tm_block(c0, c1):
                            p_ = ppool[cntA["tm"] % 6]
                            cntA["tm"] += 1
                            for kt in range(8):
                                kb.MM(p_[:, 0:c1 - c0], hTt[:, kt, j * 128:(j + 1) * 128], wtm[:, kt, c0:c1], start=(kt == 0), stop=(kt == 7))
                            drain(1)
                            return p_
                        kb.DMA("sp", rp[sl][:], rope[gt:gt + 128, :])
                        for i in range(2):
                            p_ = tm_block(i * 512, (i + 1) * 512)
                            kb.ACT(zs[sl][:, i * 512:(i + 1) * 512], p_[:], AF.Silu)
                        kb.DMA("sp", ZS[gt:gt + 128, :], zs[sl][:], key=s["name"])
                        pq = tm_block(1024, 1536)
                        pk = tm_block(1536, 1824)
                        for (p_, nh, o0, tb) in ((pq, 8, 0, 128), (pk, 2, 512, 0)):
                            srcv = p_[:, 0:nh * 64]
                            a_ = rt[:, 0, 0:nh * 64]
                            b_ = rt[:, 1, 0:nh * 64]
                            cos2 = rp[sl][:, tb:tb + 64].unsqueeze(1).to_broadcast([128, nh, 64])
                            kb.TT("dve", a_.rearrange("p (h d) -> p h d", h=nh), srcv.rearrange("p (h d) -> p h d", h=nh), cos2, ALU.mult)
                            s4 = srcv.rearrange("p (h a x f) -> p h a x f", h=nh, a=2, x=2)
                            b4 = b_.rearrange("p (h a x f) -> p h a x f", h=nh, a=2, x=2)
                            sm = rp[sl][:, tb + 64:tb + 96].rearrange("p (a f) -> p a f", a=2).unsqueeze(1).to_broadcast([128, nh, 2, 16])
                            spl = rp[sl][:, tb + 96:tb + 128].rearrange("p (a f) -> p a f", a=2).unsqueeze(1).to_broadcast([128, nh, 2, 16])
                            kb.TT("dve", b4[:, :, :, 0, :], s4[:, :, :, 1, :], sm, ALU.mult)
                            kb.TT("dve", b4[:, :, :, 1, :], s4[:, :, :, 0, :], spl, ALU.mult)
                            kb.TT("pool", qk[sl][:, o0:o0 + nh * 64], a_, b_, ALU.add)
                        kb.CP("act", vst[sl][:, :, 0:64], pk[:, 128:256].rearrange("p (g d) -> p g d", g=2))
                        kb.DMA("sp", VV[gt:gt + 128, :], vst[sl][:].rearrange("p g d -> p (g d)"), key=s["name"])
                        dw = dtw[sl]
                        kb.TT("dve", dw[:, 0, :], pk[:, 256:288], dtbB[:], ALU.add)
                        kb.ACT(dw[:, 1, :], dw[:, 0, :], AF.Abs)
                        kb.ACT(dw[:, 2, :], dw[:, 1, :], AF.Exp, scale=-1.0)
                        kb.ACT(dw[:, 3, :], dw[:, 2, :], AF.Ln, bias=1.0)
                        kb.STT(dw[:, 5, :], dw[:, 0, :], 0.0, dw[:, 3, :], ALU.max, ALU.add)
                        kb.TS("dve", dw[:, 4, :], dw[:, 5, :], 1e-30, None, ALU.max)
                        kb.DMA("sp", DTS[gt:gt + 128, :], dw[:, 4, :], key=s["name"])
                        return (s, gt, sl, j)

                    def a_tmsub2(arg):
                        s, gt, sl, j = arg
                        dw = dtw[sl]
                        p_ = pT[j % 2]
                        p2 = pT[(j + 1) % 2]

                        def pq(q4):
                            for h in range(2 * q4, 2 * q4 + 2):
                                kb.TR(p_[0:64, h * 128:(h + 1) * 128], qk[sl][:, h * 64:(h + 1) * 64], identb, signal=(h == 7))
                            if q4 == 3:
                                kb.CP("act", qkT[sl][:, 0:8, :], p_[0:64, :].rearrange("p (h t) -> p h t", h=8))
                                kb.DMA("sp", QT[:, :, gt:gt + 128], qkT[sl][:, 0:8, :], key=s["name"])

                        def pk2():
                            for h in range(2):
                                kb.TR(p2[0:64, h * 128:(h + 1) * 128], qk[sl][:, 512 + h * 64:512 + (h + 1) * 64], identb, signal=(h == 1))
                            kb.CP("dve", qkT[sl][:, 8:10, :], p2[0:64, 0:256].rearrange("p (h t) -> p h t", h=2))
                            kb.DMA("sp", KT[:, :, gt:gt + 128], qkT[sl][:, 8:10, :], key=s["name"])
                            p3 = ppool[cntA["tm"] % 6]
                            cntA["tm"] += 1
                            kb.TR(p3[0:32, 0:128], dw[:, 4, :], identf)
                            kb.CP("dve", dtT[sl][:], p3[0:32, 0:128])
                            kb.DMA("sp", DTT[:, gt:gt + 128], dtT[sl][:], key=s["name"])
                        for q4 in range(4):
                            smallq.append(lambda q4=q4: pq(q4))
                        smallq.append(pk2)

                    nsubs = len(subs)
                    ks_of = [[k for k, (si2, j) in enumerate(subs) if si2 == si] for si in range(len(sts))]
                    pend2 = [None]
                    a_load(0)
                    for k in ks_of[0]:
                        if k + 1 < nsubs:
                            a_load(k + 1)
                        a_norm(k)
                        a_transp(k, defer=False)
                    for si in range(len(sts)):
                        nsub = len(ks_of[si])
                        nxt = ks_of[si + 1] if si + 1 < len(sts) else []
                        per = 24 // nsub
                        for q in range(nsub):
                            kn = nxt[q] if q < len(nxt) else None
                            if kn is not None:
                                if kn + 1 < nsubs:
                                    a_load(kn + 1)
                                a_norm(kn)
                            if pend2[0] is not None:
                                a_tmsub2(pend2[0])
                                pend2[0] = None
                            if kn is not None:
                                a_transp(kn)
                            for u in range(q * per, (q + 1) * per):
                                a_fmunit(si, u)
                            pend2[0] = a_tmsub(si, q)
                            drain(100)
                        for q in range(nsub, len(nxt)):
                            kn = nxt[q]
                            if kn + 1 < nsubs:
                                a_load(kn + 1)
                            a_norm(kn)
                            a_transp(kn, defer=False)
                    a_tmsub2(pend2[0])
                    drain(100)

                with scope(kb) as ex:
                    sL = seqs[1]
                    end_x, end_g, end_t = sL["xoff"] + sL["n"], sL["goff"] + sL["n"], sL["off"] + sL["n"]
                    snd = SB(ex, "snd", [128, 470], BF16)
                    kb.MSET("dve", snd[:], 0.0)
                    kb.DMA("sp", snd[:, 0:24].rearrange("p (b w) -> p b w", w=2), XBCT[:, end_x - 2:end_x].rearrange("(b p) w -> p b w", p=128), key="l")
                    kb.DMA("sp", snd[:, 24:84].rearrange("p (b w) -> p b w", w=15), GLUT[:, end_g - 15:end_g].rearrange("(b p) w -> p b w", p=128), key="l")
                    kb.DMA("sp", snd[0:64, 84:340].rearrange("p (g t) -> p g t", g=2), KT[:, :, end_t - 128:end_t], key="l")
                    kb.DMA("sp", snd[:, 340:470], VV[end_t - 128:end_t, :], key="l")
                    kb.DMA("sp", SND1[:, :], snd[:])
                    kb.CC(SND1[:, :], RCV1[:, :], groups)
                    rcv = SB(ex, "rcv", [128, 2, 470], BF16)
                    kb.DMA("sp", rcv[:], RCV1.rearrange("(r p) c -> p r c", p=128))
                    tmpx = SB(ex, "tmpx", [128, 470], F32)
                    kb.TS("dve", tmpx[:], rcv[:, 0, :], selB[:, 0:1], None, ALU.mult)
                    kb.STT(nbr1[:], rcv[:, 1, :], selB[:, 1:2], tmpx[:], ALU.mult, ALU.add)

                with scope(kb) as esb:
                    kb.mark("B0%d" % l)
                    kb.mark("B1%d" % l)
                    with scope(kb) as e1:
                        dg = SB(e1, "dg", [128, 60, 128], BF16)
                        for b in range(12):
                            for j in range(5):
                                if (b + j) % 2:
                                    kb.TS("dve", dg[:, b * 5 + j, :], identf, colsT[:, b, j:j + 1], None, ALU.mult)
                                else:
                                    kb.ACT(dg[:, b * 5 + j, :], identf, AF.Copy, scale=colsT[:, b, j:j + 1])
                        win = [SB(e1, "win%d" % i, [128, 12, 516], BF16) for i in range(2)]
                        xc = [SB(e1, "xc%d" % i, [128, 12, 512], BF16) for i in range(2)]
                        xst = [SB(e1, "xst%d" % i, [128, 1280], BF16) for i in range(2)]
                        pcv = [PS(e1, "pcv%d" % i, [128, 512], F32) for i in range(2)]
                        ptr = [PS(e1, "ptr%d" % i, [128, 1024], BF16) for i in range(2)]
                        ptb = [PS(e1, "ptb%d" % i, [128, 256], BF16) for i in range(2)]
                        cntB = dict(j=0)

                        def b1_s0(i):
                            s, t0, TS_ = sts[i]
                            if i == len(sts) - 1:
                                kb.DMA("sp", win[i % 2][:, :, 0:TS_ + 2], XBCT[:, s["xoff"] + t0 - 2:s["xoff"] + t0 + TS_].rearrange("(b p) w -> p b w", p=128), key=s["name"])
                                for r in range(2):
                                    kb.CP("pool", win[i % 2][:, :, TS_ + 2 + r:TS_ + 3 + r], nx[:, :, 1 - r:2 - r])
                            else:
                                kb.DMA("sp", win[i % 2][:, :, 0:TS_ + 4], XBCT[:, s["xoff"] + t0 - 2:s["xoff"] + t0 + TS_ + 2].rearrange("(b p) w -> p b w", p=128), key=s["name"])

                        def b1_s1(i):
                            s, t0, TS_ = sts[i]
                            w_, x_ = win[i % 2], xc[i % 2]
                            g0 = s["off"] + t0
                            for b in range(12):
                                p_ = pcv[b % 2]
                                for j in range(5):
                                    kb.MM(p_[:, 0:TS_], dg[:, b * 5 + j, :], w_[:, b, j:j + TS_], start=(j == 0), stop=(j == 4))
                                kb.ACT(x_[:, b, 0:TS_], p_[:, 0:TS_], AF.Silu, bias=colsT[:, b, 5:6])
                            kb.DMA("sp", BCT[:, :, g0:g0 + TS_], x_[:, 8:12, 0:TS_], key=s["name"])

                        def b1_s2(i):
                            s, t0, TS_ = sts[i]
                            x_ = xc[i % 2]
                            g0 = s["off"] + t0
                            for j in range(TS_ // 128):
                                gt = g0 + j * 128
                                nj = cntB["j"]
                                cntB["j"] += 1
                                p_, pb_, xs_ = ptr[nj % 2], ptb[nj % 2], xst[nj % 2]
                                for b in range(8):
                                    kb.TR(p_[:, b * 128:(b + 1) * 128], x_[:, b, j * 128:(j + 1) * 128], identb, signal=(b == 7))
                                for b in range(2):
                                    kb.TR(pb_[:, b * 128:(b + 1) * 128], x_[:, 8 + b, j * 128:(j + 1) * 128], identb, signal=(b == 1))
                                kb.CP("dve", xs_[:, 0:1024], p_[:])
                                kb.CP("pool" if False else "dve", xs_[:, 1024:1280], pb_[:])
                                kb.DMA("sp", XS[gt:gt + 128, :], xs_[:, 0:1024], key=s["name"])
                                kb.DMA("sp", BTM[gt:gt + 128, :], xs_[:, 1024:1280], key=s["name"])
                        rmask = SB(e1, "rmask", [32, 16, 128], F32)
                        kb.MSET("dve", rmask[:], 1.0)
                        kb.MSET("dve", rmask[:, :, 0:1], 0.0)
                        dr = SB(e1, "dr", [32, 16, 128], F32)
                        da = SB(e1, "da", [32, 16, 128], F32)
                        cs = SB(e1, "cs", [32, 16, 128], F32)
                        ld = SB(e1, "ld", [32, 16, 128], F32)
                        al = SB(e1, "al", [32, 16, 128], F32)
                        r1 = SB(e1, "r1", [32, 16, 128], F32)
                        sp3 = SB(e1, "sp3", [32, 3, 2048], BF16)
                        assert NCH <= 32
                        b0_items = [0, 1]

                        def b0_piece(d_):
                            n_ = NCH
                            for s in seqs:
                                c0, nch = s["off"] // 128, s["n"] // 128
                                kb.DMA("sp", dr[c0:c0 + nch], DTT[d_ * 16:(d_ + 1) * 16, s["off"]:s["off"] + s["n"]].rearrange("h (c t) -> c h t", t=128), key=s["name"])
                            kb.TT("dve", da[0:n_], dr[0:n_], bc(aB[0:n_, d_ * 16:(d_ + 1) * 16], [n_, 16, 128]), ALU.mult)
                            kb.op("dve", lambda g: g.tensor_tensor_scan(out=cs[0:n_].rearrange("c h t -> c (h t)"), data0=rmask[0:n_].rearrange("c h t -> c (h t)"),
                                                                          data1=da[0:n_].rearrange("c h t -> c (h t)"), initial=0.0, op0=ALU.mult, op1=ALU.add),
                                  [rmask[0:n_], da[0:n_]], [cs[0:n_]])
                            if d_ == 1:
                                kb.TT("dve", r1[0:n_], da[0:n_], cs[0:n_], ALU.subtract)
                                kb.CP("dve", al[0:n_, :, 0:1], cs[0:n_, :, 127:128])
                                kb.TT("dve", cs[0:n_], r1[0:n_], al[0:n_, :, 0:1].to_broadcast([n_, 16, 128]), ALU.add)
                            kb.ACT(ld[0:n_], dr[0:n_], AF.Ln)
                            kb.TT("dve", al[0:n_], ld[0:n_], cs[0:n_], ALU.subtract)
                            for side, srcx in ((0, al), (1, cs)):
                                sx = srcx[0:n_].rearrange("c h t -> c (h t)")
                                r1f = r1[0:n_].rearrange("c h t -> c (h t)")
                                kb.CP("act", sp3[0:n_, 0, :], sx)
                                kb.TT("dve", r1f, sx, sp3[0:n_, 0, :], ALU.subtract)
                                kb.CP("act", sp3[0:n_, 1, :], r1f)
                                kb.TT("dve", r1f, r1f, sp3[0:n_, 1, :], ALU.subtract)
                                kb.CP("act", sp3[0:n_, 2, :], r1f)
                                for r_ in range(3):
                                    for s in seqs:
                                        c0, nch = s["off"] // 128, s["n"] // 128
                                        kb.DMA("sp", ABR[d_, side, 3 * side + r_, c0:c0 + nch, :], sp3[c0:c0 + nch, r_, :], key=s["name"])

                        nb0 = 0
                        for t_ in range(len(sts) + 2):
                            for k_, st_ in enumerate([b1_s0, b1_s1, b1_s2]):
                                if 0 <= t_ - k_ < len(sts):
                                    st_(t_ - k_)
                            if t_ in (1, 3):
                                b0_piece(nb0)
                                nb0 += 1
                        assert nb0 == 2

                    with scope(kb) as e2:
                        Sf = SB(e2, "Sf", [128, 1024], F32)
                        Sb_ = SB(e2, "Sb", [128, 1024], F32)
                        Sfb = SB(e2, "Sfb", [128, 1024], BF16)
                        Sbb = [SB(e2, "Sbb%d" % i, [128, 1024], BF16) for i in range(2)]
                        kb.MSET("dve", Sf[:], 0.0)
                        kb.MSET("dve", Sb_[:], 0.0)
                        kb.MSET("pool", Sfb[:], 0.0)
                        xsb = [SB(e2, "xsb%d" % i, [128, 1024], BF16) for i in range(3)]
                        btm = [SB(e2, "btm%d" % i, [128, 256], BF16) for i in range(3)]
                        dts = [SB(e2, "dts%d" % i, [128, 32], F32) for i in range(3)]
                        xdw = [SB(e2, "xdw%d" % i, [128, 1024], BF16) for i in range(2)]
                        sm_ = [SB(e2, "sm%d" % i, [128, 8, 32], F32) for i in range(2)]
                        psm = PS(e2, "psm", [128, 512], F32)
                        pO = PS(e2, "pO", [128, 1024], F32)

                        def small_terms(sl, d_, dt_ap):
                            S_ = sm_[sl]
                            o = d_ * 16
                            kb.TT("dve", S_[:, 0, o:o + 16], dt_ap[:, o:o + 16], aB[:, o:o + 16], ALU.mult)
                            kb.MM(psm[:, o:o + 16], triU if d_ == 0 else triL, S_[:, 0, o:o + 16], start=True, stop=True)
                            kb.MM(psm[:, 64 + o:64 + o + 16], onesf, S_[:, 0, o:o + 16], start=True, stop=True)
                            kb.CP("dve", S_[:, 1, o:o + 16], psm[:, o:o + 16])
                            kb.TT("dve", S_[:, 2, o:o + 16], psm[:, 64 + o:64 + o + 16], S_[:, 1, o:o + 16], ALU.subtract)
                            kb.ACT(S_[:, 3, o:o + 16], S_[:, 2, o:o + 16], AF.Exp)
                            kb.TT("dve", S_[:, 3, o:o + 16], S_[:, 3, o:o + 16], dt_ap[:, o:o + 16], ALU.mult)
                            kb.ACT(S_[:, 4, o:o + 16], S_[:, 1, o:o + 16], AF.Exp)
                            kb.ACT(S_[:, 5, o:o + 16], psm[:, 64 + o:64 + o + 16], AF.Exp)
                            return S_[:, 4, o:o + 16], S_[:, 3, o:o + 16], S_[:, 5, o:o + 16]

                        def mk_xdw(xd_t, xs_ap, w_ap):
                            kb.TT("pool", xd_t[:].rearrange("p (h d) -> p h d", h=16), xs_ap.rearrange("p (h d) -> p h d", h=16), bc(w_ap, [128, 16, 64]), ALU.mult)

                        def state_update(Smaster, bt_ap, etot_ap, xd_t):
                            for g in range(2):
                                kb.MM(pO[:, g * 512:(g + 1) * 512], bt_ap[:, g * 128:(g + 1) * 128], xd_t[:, g * 512:(g + 1) * 512], start=True, stop=True)
                            kb.TT("dve", Smaster[:].rearrange("p (h d) -> p h d", h=16), Smaster[:].rearrange("p (h d) -> p h d", h=16), bc(etot_ap, [128, 16, 64]), ALU.mult)
                            kb.TT("dve", Smaster[:], Smaster[:], pO[:], ALU.add)

                        kb.mark("B2%d" % l)
                        chf = []
                        for s in seqs:
                            for c in range(s["n"] // 128):
                                chf.append((s, c))

                        def bf_s0(i):
                            s, c = chf[i]
                            gt = s["off"] + c * 128
                            kb.DMA("sp", xsb[i % 3][:], XS[gt:gt + 128, :], key=s["name"])
                            kb.DMA("sp", btm[i % 3][:], BTM[gt:gt + 128, :], key=s["name"])
                            kb.DMA("sp", dts[i % 3][:], DTS[gt:gt + 128, :], key=s["name"])

                        def bf_s1(i):
                            s, c = chf[i]
                            gc = (s["off"] + c * 128) // 128
                            _, w_ap, et_ap = small_terms(i % 2, 0, dts[i % 3])
                            mk_xdw(xdw[i % 2], xsb[i % 3][:], w_ap)
                            kb.CP("act", Sbb[i % 2][:], Sf[:])
                            kb.DMA("sp", SFD[gc, :, :], Sbb[i % 2][:], key=gc)
                            state_update(Sf, btm[i % 3], et_ap, xdw[i % 2])
                        pipe(len(chf), [bf_s0, bf_s1])
                        nbS = SB(e2, "nbS", [128, 1024], F32)
                        with scope(kb) as ex2:
                            rcs2 = SB(ex2, "rcs2", [128, 2, 1024], F32)
                            kb.DMA("sp", SND2[:, :], Sf[:])
                            kb.CC(SND2[:, :], RCV2[:, :], groups)
                            kb.DMA("sp", rcs2[:], RCV2.rearrange("(r p) c -> p r c", p=128))
                            kb.TS("dve", nbS[:], rcs2[:, 0, :], selB[:, 0:1], None, ALU.mult)
                            kb.STT(nbS[:], rcs2[:, 1, :], selB[:, 1:2], nbS[:], ALU.mult, ALU.add)
                        chd = []
                        for s in seqs:
                            for c in reversed(range(s["n"] // 128)):
                                chd.append((s, c))
                        first_lat = seqs[0]["n"] // 128

                        def b2_s0(i):
                            s, c = chd[i]
                            gt = s["off"] + c * 128
                            kb.DMA("sp", xsb[i % 3][:], XS[gt:gt + 128, :], key=s["name"])
                            kb.DMA("sp", btm[i % 3][:], BTM[gt:gt + 128, :], key=s["name"])
                            kb.DMA("sp", dts[i % 3][:], DTS[gt:gt + 128, :], key=s["name"])

                        def b2_s1(i):
                            s, c = chd[i]
                            gc = (s["off"] + c * 128) // 128
                            if i == first_lat:
                                kb.CP("dve", Sb_[:], nbS[:])
                            _, w_ap, et_ap = small_terms(i % 2, 1, dts[i % 3])
                            mk_xdw(xdw[i % 2], xsb[i % 3][:], w_ap)
                            kb.CP("act", Sbb[i % 2][:], Sb_[:])
                            kb.DMA("sp", SBD[gc, :, :], Sbb[i % 2][:], key=gc)
                            state_update(Sb_, btm[i % 3], et_ap, xdw[i % 2])
                        pipe(len(chd), [b2_s0, b2_s1])

                        kb.mark("B3%d" % l)
                        with scope(kb) as e3:
                            bct = [SB(e3, "bct%d" % i, [128, 4, 128], BF16) for i in range(3)]
                            zsb = [SB(e3, "zsb%d" % i, [128, 1024], BF16) for i in range(2)]
                            sbe = [SB(e3, "sbe%d" % i, [128, 1024], BF16) for i in range(2)]
                            sfe = [SB(e3, "sfe%d" % i, [128, 1024], BF16) for i in range(2)]
                            ab6 = [SB(e3, "ab6%d" % i, [6, 2, 2, 2048], BF16) for i in range(2)]
                            cbT = SB(e3, "cbT", [128, 2, 128], BF16)
                            E_ = [SB(e3, "E%d" % i, [128, 4, 128], BF16) for i in range(2)]
                            MT = SB(e3, "MT", [128, 2, 16, 128], BF16)
                            ysb = [SB(e3, "ysb%d" % i, [128, 1024], F32) for i in range(2)]
                            t1 = SB(e3, "t1", [128, 1024], F32)
                            t2 = SB(e3, "t2", [128, 1024], F32)
                            us = [SB(e3, "u%d" % i, [128, 1024], F32) for i in range(2)]
                            obs = [SB(e3, "ob%d" % i, [128, 1024], BF16) for i in range(2)]
                            sq = SB(e3, "sq", [128, 512], F32)
                            oT = [SB(e3, "oT%d" % i, [128, 8, 128], BF16) for i in range(2)]
                            st2 = SB(e3, "st2", [128, 2, 2], F32)
                            pDs = [PS(e3, "pD%d" % i, [128, 512], F32) for i in range(2)]
                            pYs = [PS(e3, "pY%d" % i, [128, 512], F32) for i in range(2)]
                            pX = PS(e3, "pX", [128, 1024], BF16)
                            cntD = dict(d=0)
                            cha = []
                            for s in seqs:
                                for c in range(s["n"] // 128):
                                    cha.append((s, c))

                            def ny(s):
                                return not (last and s["name"] == "c")

                            def b3_s0(i):
                                s, c = cha[i]
                                gt = s["off"] + c * 128
                                gc = gt // 128
                                kb.DMA("sp", xsb[i % 3][:], XS[gt:gt + 128, :], key=s["name"])
                                kb.DMA("sp", dts[i % 3][:], DTS[gt:gt + 128, :], key=s["name"])
                                if ny(s):
                                    kb.DMA("sp", bct[i % 3][:], BCT[:, :, gt:gt + 128], key=s["name"])
                                    kb.DMA("sp", ab6[i % 2][:], ABR[:, :, :, gc, :].rearrange("d s r x -> r d s x"), key=s["name"])

                            def b3_s1(i):
                                s, c = cha[i]
                                gt = s["off"] + c * 128
                                gc = gt // 128
                                sl3, sl = i % 3, i % 2
                                if not ny(s):
                                    return
                                ef, wf, etf = small_terms(sl, 0, dts[sl3])
                                kb.DMA("sp", zsb[sl][:], ZS[gt:gt + 128, :], key=s["name"])
                                kb.DMA("sp", sbe[sl][:], SBD[gc, :, :], key=gc)
                                kb.DMA("sp", sfe[sl][:], SFD[gc, :, :], key=gc)
                                small_terms(sl, 1, dts[sl3])
                                for g in range(2):
                                    kb.MM(psm[:, 256 + g * 128:256 + (g + 1) * 128], bct[sl3][:, g, :], bct[sl3][:, 2 + g, :], start=True, stop=True)
                                kb.CP("dve", cbT[:].rearrange("p g t -> p (g t)"), psm[:, 256:512])
                                for g in range(2):
                                    for hb_ in range(2):
                                        es_ = []
                                        for d_ in range(2):
                                            pD = pDs[cntD["d"] % 2]
                                            e_ = E_[cntD["d"] % 2]
                                            cntD["d"] += 1
                                            kb.MM(pD[:], identb, mneg[:, d_, :, :].rearrange("p r t -> p (r t)"), start=True, stop=False, signal=False)
                                            for hh in range(4):
                                                h = g * 8 + hb_ * 4 + hh
                                                kb.MM(pD[:, hh * 128:(hh + 1) * 128], ab6[sl][:, d_, 0, h * 128:(h + 1) * 128],
                                                      ab6[sl][:, d_, 1, h * 128:(h + 1) * 128], start=False, stop=(hh == 3), signal=(hh == 3))
                                            kb.ACT(e_[:].rearrange("p h t -> p (h t)"), pD[:], AF.Exp)
                                            es_.append(e_)
                                        h0 = g * 8 + hb_ * 4
                                        kb.TT("dve", es_[0][:], es_[0][:], es_[1][:], ALU.add)
                                        kb.TT("dve", MT[:, 0, h0:h0 + 4, :], es_[0][:], cbT[:, g, :].unsqueeze(1).to_broadcast([128, 4, 128]), ALU.mult)
                                    pY = pYs[g]
                                    for hh in range(8):
                                        h = g * 8 + hh
                                        rhs = xsb[sl3][:, h * 64:(h + 1) * 64]
                                        kb.MM(pY[:, hh * 64:(hh + 1) * 64], MT[:, 0, h, :], rhs, start=True, stop=False, signal=False)
                                        kb.MM(pY[:, hh * 64:(hh + 1) * 64], dI[:, h, :], rhs, start=False, stop=True, signal=(hh == 7))
                                    kb.CP("act", ysb[sl][:, g * 512:(g + 1) * 512], pY[:])

                            def b3_s2(i):
                                s, c = cha[i]
                                if not ny(s):
                                    return
                                sl3, sl = i % 3, i % 2
                                S_ = sm_[sl]
                                ef, eb = S_[:, 4, 0:16], S_[:, 4, 16:32]
                                u = us[sl]
                                for g in range(2):
                                    kb.MM(pO[:, g * 512:(g + 1) * 512], bct[sl3][:, 2 + g, :], sfe[sl][:, g * 512:(g + 1) * 512], start=True, stop=True)
                                kb.TT("dve", t1[:].rearrange("p (h d) -> p h d", h=16), pO[:].rearrange("p (h d) -> p h d", h=16), bc(ef, [128, 16, 64]), ALU.mult)
                                for g in range(2):
                                    kb.MM(pO[:, g * 512:(g + 1) * 512], bct[sl3][:, 2 + g, :], sbe[sl][:, g * 512:(g + 1) * 512], start=True, stop=True)
                                kb.TT("dve", t2[:].rearrange("p (h d) -> p h d", h=16), pO[:].rearrange("p (h d) -> p h d", h=16), bc(eb, [128, 16, 64]), ALU.mult)
                                kb.TT("pool", u[:], t1[:], t2[:], ALU.add)
                                kb.TT("pool", u[:], u[:], ysb[sl][:], ALU.add)
                                kb.TT("pool", u[:], u[:], zsb[sl][:], ALU.mult)

                            def b3_s3(i):
                                s, c = cha[i]
                                if not ny(s):
                                    return
                                sl = i % 2
                                u = us[sl]
                                ss_ = st2[:, sl, :]
                                for g in range(2):
                                    kb.STT(sq[:], u[:, g * 512:(g + 1) * 512], 1.0, u[:, g * 512:(g + 1) * 512], ALU.mult, ALU.mult, accum=st2[:, sl, g:g + 1])
                                kb.TS("dve", ss_, ss_, 1.0 / 512.0, EPS, ALU.mult, ALU.add)
                                kb.ACT(ss_, ss_, AF.Ln)
                                kb.ACT(ss_, ss_, AF.Exp, scale=-0.5)
                                for g in range(2):
                                    kb.STT(obs[sl][:, g * 512:(g + 1) * 512], u[:, g * 512:(g + 1) * 512], st2[:, sl, g:g + 1], snwB[:, g * 512:(g + 1) * 512], ALU.mult, ALU.mult)

                            def b3_s4(i):
                                s, c = cha[i]
                                if not ny(s):
                                    return
                                gt = s["off"] + c * 128
                                sl = i % 2
                                for b in range(8):
                                    kb.TR(pX[:, b * 128:(b + 1) * 128], obs[sl][:, b * 128:(b + 1) * 128], identb, signal=(b == 7))
                                kb.CP("act", oT[sl][:].rearrange("p b t -> p (b t)"), pX[:])
                                kb.DMA("sp", OT[0:1024, gt:gt + 128].rearrange("(b p) t -> p b t", p=128), oT[sl][:], key=s["name"])
                            pipe(len(cha), [b3_s0, b3_s1, b3_s2, b3_s3, b3_s4])

                kb.mark("C%d" % l)
                with scope(kb) as ec:
                    ktb = SB(ec, "ktb", [64, 2, Ltot + 128], BF16)
                    vvb = SB(ec, "vvb", [128, NCH + 1, 130], BF16)
                    kb.CP("dve", ktb[:, :, Ltot:Ltot + 128], nbr1[0:64, 84:340].rearrange("p (g t) -> p g t", g=2))
                    kb.CP("dve", vvb[:, NCH, :], nbr1[:, 340:470])
                    for s in seqs:
                        o_, n_ = s["off"], s["n"]
                        kb.DMA("sp", ktb[:, :, o_:o_ + n_], KT[:, :, o_:o_ + n_], key=s["name"])
                        kb.DMA("sp", vvb[:, o_ // 128:(o_ + n_) // 128, :], VV[o_:o_ + n_, :].rearrange("(c p) x -> p c x", p=128), key=s["name"])
                    qtb = [SB(ec, "qtb%d" % i, [64, 8, 128], BF16) for i in range(3)]
                    gab = [SB(ec, "gab%d" % i, [64, 8, 128], BF16) for i in range(3)]
                    Ee = [SB(ec, "Ee%d" % i, [128, 512], BF16) for i in range(20)]
                    den = [SB(ec, "den%d" % i, [64, 512], F32) for i in range(2)]
                    oa = [SB(ec, "oa%d" % i, [64, 512], F32) for i in range(2)]
                    oaT = [SB(ec, "oaT%d" % i, [64, 512], BF16) for i in range(2)]
                    pS = [PS(ec, "pS%d" % i, [128, 512], F32) for i in range(3)]
                    pV = [PS(ec, "pV%d" % i, [64, 512], F32) for i in range(2)]
                    pU = [PS(ec, "pU%d" % i, [64, 512], F32) for i in range(2)]
                    qbs = []
                    for s in seqs:
                        if last and s["name"] == "c":
                            continue
                        for qb in range(s["n"] // 128):
                            qbs.append((s, qb))
                    cntC = dict(s=0, o=0)

                    def keyblocks(s, qb):
                        kbl = [(0, None), (1, None)]
                        if s["name"] == "l":
                            nb = s["n"] // 128
                            c0 = s["off"] // 128
                            if qb > 0:
                                kbl.append((c0 + qb - 1, 1))
                            kbl.append((c0 + qb, None))
                            if qb < nb - 1:
                                kbl.append((c0 + qb + 1, 0))
                            else:
                                kbl.append((NCH, 2))
                        return kbl

                    def c_s0(i):
                        s, qb = qbs[i]
                        gt = s["off"] + qb * 128
                        kb.DMA("sp", qtb[i % 3][:], QT[:, :, gt:gt + 128], key=s["name"])
                        kb.DMA("sp", gab[i % 3][:], GAT[:, gt:gt + 128].rearrange("(h d) t -> d h t", d=64), key=s["name"])

                    def c_s1(i):
                        s, qb = qbs[i]
                        q_ = qtb[i % 3]
                        for g in range(2):
                            for ki, (kc, md) in enumerate(keyblocks(s, qb)):
                                p_ = pS[cntC["s"] % 3]
                                cntC["s"] += 1
                                if md is not None:
                                    kb.MM(p_[:], identb, mneg[:, md, :, :].rearrange("p r t -> p (r t)"), start=True, stop=False, signal=False)
                                kb.MM(p_[:], ktb[:, g, kc * 128:(kc + 1) * 128], q_[:, g * 4:(g + 1) * 4, :].rearrange("p h t -> p (h t)"),
                                      start=(md is None), stop=True)
                                kb.ACT(Ee[(i % 2) * 10 + g * 5 + ki][:], p_[:], AF.Exp)

                    def c_s2(i):
                        s, qb = qbs[i]
                        gt = s["off"] + qb * 128
                        g_ = gab[i % 3]
                        kbl = keyblocks(s, qb)
                        for g in range(2):
                            no = cntC["o"]
                            cntC["o"] += 1
                            pv, pu = pV[no % 2], pU[no % 2]
                            for ki, (kc, md) in enumerate(kbl):
                                kb.MM(pv[:], vvb[:, kc, g * 65:g * 65 + 64], Ee[(i % 2) * 10 + g * 5 + ki][:], start=(ki == 0), stop=(ki == len(kbl) - 1))
                            for ki, (kc, md) in enumerate(kbl):
                                kb.MM(pu[:], onesb[:, 0:64], Ee[(i % 2) * 10 + g * 5 + ki][:], start=(ki == 0), stop=(ki == len(kbl) - 1))
                            dn = den[no % 2]
                            kb.TT("dve", dn[:].rearrange("p (r t) -> p r t", r=4), pu[:].rearrange("p (r t) -> p r t", r=4), bc(sinkB[0:64, g * 4:(g + 1) * 4], [64, 4, 128]), ALU.add)
                            kb.ACT(dn[:], dn[:], AF.Ln)
                            kb.ACT(dn[:], dn[:], AF.Exp, scale=-1.0)
                            o_ = oa[no % 2]
                            kb.TT("dve", o_[:], pv[:], dn[:], ALU.mult)
                            ot_ = oaT[no % 2]
                            kb.TT("pool", ot_[:], o_[:], g_[:, g * 4:(g + 1) * 4, :].rearrange("p h t -> p (h t)"), ALU.mult)
                            kb.DMA("sp", OT[1024 + g * 256:1024 + (g + 1) * 256, gt:gt + 128].rearrange("(r d) t -> d r t", d=64), ot_[:].rearrange("p (r t) -> p r t", r=4), key=s["name"])
                    pipe(len(qbs), [c_s0, c_s1, c_s2])

                kb.mark("D%d" % l)
                with scope(kb) as ed:
                    dgc = SB(ed, "dgc", [128, 124, 128], BF16)
                    for b in range(4):
                        for j in range(31):
                            if (b + j) % 2:
                                kb.TS("dve", dgc[:, b * 31 + j, :], identf, colsT[:, b, 6 + j:7 + j], None, ALU.mult)
                            else:
                                kb.ACT(dgc[:, b * 31 + j, :], identf, AF.Copy, scale=colsT[:, b, 6 + j:7 + j])
                    pww = SB(ed, "pww", [128, 4, 512], BF16)
                    pwst = SB(ed, "pwst", [128, 4, 512], F32)
                    kb.DMA("sp", pwst[:], conv_pw_w[l].rearrange("(k p) n -> p k n", p=128))
                    kb.CP("act", pww[:], pwst[:])
                    gw = [SB(ed, "gw%d" % i, [128, 4, 542], BF16) for i in range(2)]
                    gcb = [SB(ed, "gcb%d" % i, [128, 4, 512], BF16) for i in range(2)]
                    hbb = [SB(ed, "hbb%d" % i, [128, 4, 512], BF16) for i in range(2)]
                    hsq = [SB(ed, "hsq%d" % i, [128, 4, 512], BF16) for i in range(2)]
                    mean = SB(ed, "mean", [128, 512], F32)
                    rstd = SB(ed, "rstd", [128, 512], F32)
                    xcn = [SB(ed, "xcn%d" % i, [128, 512], F32) for i in range(2)]
                    h2 = [SB(ed, "h2%d" % i, [128, 4, 512], BF16) for i in range(2)]
                    oc = [SB(ed, "oc%d" % i, [128, 512], BF16) for i in range(2)]
                    pc = [PS(ed, "pc%d" % i, [128, 512], F32) for i in range(2)]
                    pm = PS(ed, "pm", [128, 512], F32)
                    pe2 = PS(ed, "pe2", [128, 512], F32)
                    pw_ = [PS(ed, "pw%d" % i, [128, 512], F32) for i in range(2)]
                    dst = [x for x in sts if not (last and x[0]["name"] == "c")]

                    def d_s0(i):
                        s, t0, TS_ = dst[i]
                        if i == len(dst) - 1:
                            kb.DMA("sp", gw[i % 2][:, :, 0:TS_ + 15], GLUT[:, s["goff"] + t0 - 15:s["goff"] + t0 + TS_].rearrange("(b p) w -> p b w", p=128), key=s["name"])
                            for r in range(15):
                                kb.CP("pool", gw[i % 2][:, :, TS_ + 15 + r:TS_ + 16 + r], ng[:, :, 14 - r:15 - r])
                        else:
                            kb.DMA("sp", gw[i % 2][:, :, 0:TS_ + 30], GLUT[:, s["goff"] + t0 - 15:s["goff"] + t0 + TS_ + 15].rearrange("(b p) w -> p b w", p=128), key=s["name"])

                    def d_s1(i):
                        s, t0, TS_ = dst[i]
                        w_ = gw[i % 2]
                        for b in range(4):
                            p_ = pc[b % 2]
                            for j in range(31):
                                kb.MM(p_[:, 0:TS_], dgc[:, b * 31 + j, :], w_[:, b, j:j + TS_], start=(j == 0), stop=(j == 30))
                            kb.ACT(hbb[i % 2][:, b, 0:TS_], p_[:, 0:TS_], AF.Identity, bias=colsT[:, b, 37:38])
                            kb.TT("dve", hsq[i % 2][:, b, 0:TS_], hbb[i % 2][:, b, 0:TS_], hbb[i % 2][:, b, 0:TS_], ALU.mult)

                    def d_s2(i):
                        s, t0, TS_ = dst[i]
                        g0 = s["off"] + t0
                        kb.DMA("sp", gcb[i % 2][:, :, 0:TS_], GCT[:, g0:g0 + TS_].rearrange("(b p) w -> p b w", p=128), key=s["name"])
                        hb_, hs_ = hbb[i % 2], hsq[i % 2]
                        for b in range(4):
                            kb.MM(pm[:, 0:TS_], ones512[:], hb_[:, b, 0:TS_], start=(b == 0), stop=(b == 3))
                        for b in range(4):
                            kb.MM(pe2[:, 0:TS_], ones512[:], hs_[:, b, 0:TS_], start=(b == 0), stop=(b == 3))
                        kb.CP("act", mean[:, 0:TS_], pm[:, 0:TS_])
                        kb.TT("dve", rstd[:, 0:TS_], mean[:, 0:TS_], mean[:, 0:TS_], ALU.mult)
                        kb.TT("dve", rstd[:, 0:TS_], pe2[:, 0:TS_], rstd[:, 0:TS_], ALU.subtract)
                        kb.ACT(rstd[:, 0:TS_], rstd[:, 0:TS_], AF.Ln, bias=epsb[:])
                        kb.ACT(rstd[:, 0:TS_], rstd[:, 0:TS_], AF.Exp, scale=-0.5)
                        for b in range(4):
                            x_ = xcn[b % 2]
                            kb.TT("dve", x_[:, 0:TS_], hb_[:, b, 0:TS_], mean[:, 0:TS_], ALU.subtract)
                            kb.TT("dve", x_[:, 0:TS_], x_[:, 0:TS_], rstd[:, 0:TS_], ALU.mult)
                            kb.ACT(h2[i % 2][:, b, 0:TS_], x_[:, 0:TS_], AF.Silu, bias=colsT[:, b, 39:40], scale=colsT[:, b, 38:39])

                    def d_s3(i):
                        s, t0, TS_ = dst[i]
                        g0 = s["off"] + t0
                        for co in range(4):
                            p_ = pw_[co % 2]
                            for ci in range(4):
                                kb.MM(p_[:, 0:TS_], pww[:, ci, co * 128:(co + 1) * 128], h2[i % 2][:, ci, 0:TS_], start=(ci == 0), stop=(ci == 3))
                            o_ = oc[co % 2]
                            kb.STT(o_[:, 0:TS_], p_[:, 0:TS_], colsT[:, co, 40:41], gcb[i % 2][:, co, 0:TS_], ALU.add, ALU.mult)
                            kb.DMA("sp", OT[1536 + co * 128:1536 + (co + 1) * 128, g0:g0 + TS_], o_[:, 0:TS_], key=s["name"])
                    pipe(len(dst), [d_s0, d_s1, d_s2, d_s3])

                kb.mark("E%d" % l)
                with scope(kb) as ee:
                    wo = SB(ee, "wo", [128, 16, D], BF16)
                    wov = w_out[l].rearrange("(k p) n -> p k n", p=128)
                    with scope(kb) as eow:
                        wos = [SB(eow, "wos%d" % i, [128, 4, D], F32) for i in range(2)]
                        for k4 in range(4):
                            kb.DMA("sp", wos[k4 % 2][:], wov[:, k4 * 4:(k4 + 1) * 4, :])
                            for j4 in range(4):
                                kb.CP(["dve", "pool", "act", "pool"][j4], wo[:, k4 * 4 + j4, :], wos[k4 % 2][:, j4, :])
                    fnw = SB(ee, "fnw", [128, D], F32)
                    kb.DMA("sp", fnw[:], final_norm_w[0, :].partition_broadcast(128))
                    otb = [SB(ee, "otb%d" % i, [128, 16, 128], BF16) for i in range(3)]
                    xo = [SB(ee, "xo%d" % i, [128, D], F32) for i in range(3)]
                    xn = [SB(ee, "xn%d" % i, [128, D], F32) for i in range(2)]
                    yo = [SB(ee, "yo%d" % i, [128, D], F32) for i in range(2)]
                    jk = SB(ee, "jk", [128, D], BF16)
                    s1 = SB(ee, "s1", [128, 2], F32)
                    po = [PS(ee, "po%d" % i, [128, 1024], F32) for i in range(2)]
                    che = []
                    for s in seqs:
                        if last and s["name"] == "c":
                            continue
                        for c in range(s["n"] // 128):
                            che.append((s, c))

                    def e_s0(i):
                        s, c = che[i]
                        gt = s["off"] + c * 128
                        kb.DMA("sp", otb[i % 3][:], OT[:, gt:gt + 128].rearrange("(k p) t -> p k t", p=128), key=s["name"])
                        if l == 0:
                            src = (ctx_in if s["name"] == "c" else x_in)[c * 128:(c + 1) * 128, :]
                        else:
                            src = XR[gt:gt + 128, :]
                        kb.DMA("sp", xo[i % 3][:], src, key=(s["name"], c * 128))

                    def e_s1(i):
                        s, c = che[i]
                        gt = s["off"] + c * 128
                        m = 0 if s["name"] == "c" else 1
                        sl = i % 2
                        bw_ = 512 if s["name"] == "l" else 256
                        p_ = po[sl]
                        for nb_ in range(2):
                            for kt in range(16):
                                kb.MM(p_[:, nb_ * 512:(nb_ + 1) * 512], otb[i % 3][:, kt, :], wo[:, kt, nb_ * 512:(nb_ + 1) * 512], start=(kt == 0), stop=(kt == 15))
                        kb.TT("dve", xn[sl][:], p_[:], modB[:, m, 2, :], ALU.mult)
                        kb.TT("pool", xn[sl][:], xn[sl][:], xo[i % 3][:], ALU.add)
                        if not last:
                            kb.DMA("sp", XR[gt:gt + 128, :], xn[sl][:], key=(s["name"], c * 128))
                        else:
                            sc_ = s1[:, sl:sl + 1]
                            kb.ACT(jk[:], xn[sl][:], AF.Square, accum=sc_)
                            kb.TS("dve", sc_, sc_, 1.0 / D, EPS, ALU.mult, ALU.add)
                            kb.ACT(sc_, sc_, AF.Sqrt)
                            kb.RECIP(sc_, sc_)
                            kb.STT(yo[sl][:], xn[sl][:], sc_, fnw[:], ALU.mult, ALU.mult)
                            kb.DMA("sp", out[c * 128:(c + 1) * 128, :], yo[sl][:], key="out")
                    pipe(len(che), [e_s0, e_s1])
        kb.mark("end")
        kb.finish()
    return nc, kb


def host_consts(L, T, pos, grid_w=64):
    k = np.arange(128)[:, None]
    t = np.arange(128)[None, :]
    cst = np.zeros((128, 7, 128), np.float32)
    cst[:, 0] = np.eye(128)
    cst[:, 1] = (k <= t)
    cst[:, 2] = (k >= t)
    cst[:, 3] = 1.0
    cst[:, 4] = np.where(k <= t, 0.0, NEG)
    cst[:, 5] = np.where(k >= t, 0.0, NEG)
    cst[:, 6] = np.where(k + t >= 127, 0.0, NEG)
    rope = np.zeros((T + L, 256), np.float32)
    rope[:, 0:64] = 1.0
    rope[:, 128:192] = 0.125
    row = (pos // grid_w).astype(np.float32)
    col = (pos % grid_w).astype(np.float32)
    inv = (10000.0 ** (-np.arange(0, 32, 2, dtype=np.float32) / 32.0)).astype(np.float32)
    ang = np.stack([row[:, None] * inv, col[:, None] * inv], axis=1).astype(np.float32)
    cs_, sn_ = np.cos(ang), np.sin(ang)
    cos2 = np.repeat(cs_[:, :, None, :], 2, axis=2).reshape(L, 64)
    rope[T:, 0:64] = cos2
    rope[T:, 64:96] = (-sn_).reshape(L, 32)
    rope[T:, 96:128] = sn_.reshape(L, 32)
    rope[T:, 128:192] = cos2 * 0.125
    rope[T:, 192:224] = (-sn_).reshape(L, 32) * 0.125
    rope[T:, 224:256] = sn_.reshape(L, 32) * 0.125
    return cst, rope


_CACHE = {}


def run(inputs, L, T, NL, batches):
    npairs = len(batches)
    key = (L, T, NL, npairs)
    if key not in _CACHE:
        _CACHE[key] = build(L, T, NL, npairs)[0]
    nc = _CACHE[key]
    f = lambda a: np.ascontiguousarray(np.asarray(a, dtype=np.float32))
    base = {
        "c_ctx": f(inputs["c_ctx"]).reshape(128, 8),
        "w_mod": f(inputs["w_mod"]), "b_mod": f(inputs["b_mod"]), "norm_w": f(inputs["norm_w"]),
        "ssd_conv_b": f(inputs["ssd_conv_b"]),
        "ssd_d": f(inputs["ssd_d"]), "ssd_norm_w": f(inputs["ssd_norm_w"]), "attn_sink": f(inputs["attn_sink"]),
        "conv_dw_b": f(inputs["conv_dw_b"]), "conv_ln_w": f(inputs["conv_ln_w"]),
        "conv_ln_b": f(inputs["conv_ln_b"]), "conv_pw_w": f(inputs["conv_pw_w"]), "conv_pw_b": f(inputs["conv_pw_b"]),
        "w_out": f(inputs["w_out"]), "final_norm_w": f(inputs["final_norm_w"]).reshape(1, D),
    }
    w_in = f(inputs["w_in"])
    w_in_m = w_in.copy()
    w_in_m[:, :, 2560:2576] = w_in[:, :, 2576:2592]
    w_in_m[:, :, 2576:2592] = w_in[:, :, 2560:2576]
    dtb = f(inputs["ssd_dt_bias"])
    alog = f(inputs["ssd_a_log"])
    scw = f(inputs["ssd_conv_w"])
    dww = f(inputs["conv_dw_w"])
    half = []
    for h in range(2):
        pos = np.arange(L) if h == 0 else (2 * L - 1 - np.arange(L))
        cst, rope = host_consts(L, T, pos)
        sel = np.zeros((128, 2), np.float32)
        sel[:, 1 - h] = 1.0
        d = dict(base)
        d.update({
            "cst": cst, "rope": rope, "sel": sel,
            "w_in": w_in if h == 0 else w_in_m,
            "ssd_dt_bias": f(dtb if h == 0 else dtb[:, ::-1]).reshape(NL, 32),
            "ssd_a_log": f(alog if h == 0 else alog[:, ::-1]).reshape(NL, 32),
            "ssd_conv_w": f(scw if h == 0 else scw[:, ::-1]),
            "conv_dw_w": f(dww if h == 0 else dww[:, ::-1]),
        })
        half.append(d)
    x = f(inputs["x"])
    c = f(inputs["c"])
    ctx = f(inputs["ctx"])
    in_maps = []
    for b in batches:
        for h in range(2):
            m = dict(half[h])
            if h == 0:
                m["x"] = f(x[b, 0:L])
                m["ctx"] = f(ctx[b])
            else:
                m["x"] = f(x[b, L:2 * L][::-1])
                m["ctx"] = f(ctx[b][::-1])
            m["c"] = c[b].reshape(128, 8)
            in_maps.append(m)
    res = run_bass_kernel_spmd(nc, in_maps, core_ids=list(range(2 * npairs)))
    outs = []
    for i in range(npairs):
        o0 = res.results[2 * i]["out"]
        o1 = res.results[2 * i + 1]["out"][::-1]
        outs.append(np.concatenate([o0, o1], axis=0))
    return outs


def kernel(**inputs):
    B = np.asarray(inputs["x"]).shape[0]
    outs = run(inputs, 2048, 256, 2, list(range(B)))
    return np.stack(outs, axis=0).astype(np.float32)
```

```python
import numpy as np
from contextlib import ExitStack, contextmanager
import concourse.bass as bass
import concourse.mybir as mybir
from concourse.bass_utils import run_bass_kernel_spmd

F32 = mybir.dt.float32
BF16 = mybir.dt.bfloat16
AF = mybir.ActivationFunctionType
ALU = mybir.AluOpType
NDMA = 48
EPS = 1e-6
D = 1024
NEG = -30000.0


class Tok:
    __slots__ = ("sem", "val", "key")

    def __init__(self, sem, val, key):
        self.sem, self.val, self.key = sem, val, key


class Buf:
    __slots__ = ("w", "r", "multi")

    def __init__(self, multi=False):
        self.w = {}
        self.r = {}
        self.multi = multi


class KB:
    def __init__(self, nc, es):
        self.nc = nc
        self.eng = {"pe": nc.tensor, "act": nc.scalar, "dve": nc.vector, "pool": nc.gpsimd, "sp": nc.sync}
        self.sem = {e: es.enter_context(nc.semaphore("s_" + e)) for e in self.eng}
        self.cnt = {e: 0 for e in self.eng}
        self.known = {e: {} for e in self.eng}
        self.dsem = [es.enter_context(nc.semaphore("d%d" % i)) for i in range(NDMA)]
        self.dval = [0] * NDMA
        self.dnext = 0
        self.pend = {e: [] for e in self.eng}
        self.bufs = {}
        self.ccsem = es.enter_context(nc.semaphore("ccsem"))
        self.ccval = 0
        self.ninst = 0
        self.nmm = 0
        self.marks = []

    def buf(self, ap, key=None):
        t = ap.tensor
        name = t.name
        dram = str(ap.space).lower().find("dram") >= 0 or str(ap.space).lower().find("hbm") >= 0
        k = (name, key if dram else None)
        b = self.bufs.get(k)
        if b is None:
            b = Buf(multi=dram)
            self.bufs[k] = b
        return b

    def _wait(self, e, tok):
        if e == "pe" and tok.key == "pe":
            return
        k = self.known[e]
        if k.get(tok.key, 0) >= tok.val:
            return
        self.eng[e].wait_ge(tok.sem, tok.val)
        k[tok.key] = tok.val

    def _deps(self, e, reads, writes):
        for b in reads:
            for t in b.w.values():
                self._wait(e, t)
        for b in writes:
            if not b.multi:
                for t in b.w.values():
                    self._wait(e, t)
            for t in b.r.values():
                self._wait(e, t)

    def _record(self, tok, reads, writes):
        for b in reads:
            b.r[tok.key] = tok
        for b in writes:
            if b.multi:
                b.w[tok.key] = tok
            else:
                b.w = {tok.key: tok}
                b.r = {}

    def op(self, e, fn, rd, wr, signal=True, key=None):
        reads = [self.buf(a, key) for a in rd if a is not None and not isinstance(a, (int, float))]
        writes = [self.buf(a, key) for a in wr if a is not None]
        self._deps(e, reads, writes)
        ins = fn(self.eng[e])
        self.ninst += 1
        if not signal:
            self.pend[e].append((reads, writes))
            return
        self.cnt[e] += 1
        ins.then_inc(self.sem[e], 1)
        tok = Tok(self.sem[e], self.cnt[e], e)
        for (r, w) in self.pend[e]:
            self._record(tok, r, w)
        self.pend[e] = []
        self._record(tok, reads, writes)

    def DMA(self, q, out, in_, key=None, **kw):
        reads = [self.buf(in_, key)]
        writes = [self.buf(out, key)]
        self._deps(q, reads, writes)
        i = self.dnext
        self.dnext = (i + 1) % NDMA
        k = "d%d" % i
        if self.dval[i] > 0:
            self._wait(q, Tok(self.dsem[i], self.dval[i], k))
        ins = self.eng[q].dma_start(out=out, in_=in_, **kw)
        self.ninst += 1
        self.dval[i] += 16
        ins.then_inc(self.dsem[i], 16)
        tok = Tok(self.dsem[i], self.dval[i], k)
        self._record(tok, reads, writes)

    def CC(self, in_ap, out_ap, groups):
        reads = [self.buf(in_ap)]
        writes = [self.buf(out_ap)]
        self._deps("pool", reads, writes)
        ins = self.eng["pool"].collective_compute("AllGather", ALU.bypass, replica_groups=groups, ins=[in_ap], outs=[out_ap])
        self.ninst += 1
        self.ccval += 1
        ins.then_inc(self.ccsem)
        tok = Tok(self.ccsem, self.ccval, "cc")
        self._record(tok, reads, writes)

    def barrier(self):
        assert all(len(p) == 0 for p in self.pend.values())
        toks = [Tok(self.sem[e], self.cnt[e], e) for e in self.eng if self.cnt[e] > 0]
        if self.ccval:
            toks.append(Tok(self.ccsem, self.ccval, "cc"))
        toks += [Tok(self.dsem[i], self.dval[i], "d%d" % i) for i in range(NDMA) if self.dval[i]]
        for e in self.eng:
            for t in toks:
                if t.key != e:
                    self._wait(e, t)

    def finish(self):
        for i in range(NDMA):
            if self.dval[i]:
                self._wait("sp", Tok(self.dsem[i], self.dval[i], "d%d" % i))

    def ACT(self, out, in_, func, bias=None, scale=None, accum=None):
        kw = {}
        if bias is not None:
            kw["bias"] = bias
        if scale is not None:
            kw["scale"] = scale
        if accum is not None:
            kw["accum_out"] = accum
        self.op("act", lambda e: e.activation(out=out, in_=in_, func=func, **kw), [in_, bias, scale], [out, accum])

    def TT(self, e, out, in0, in1, op):
        self.op(e, lambda g: g.tensor_tensor(out=out, in0=in0, in1=in1, op=op), [in0, in1], [out])

    def TS(self, e, out, in0, s1, s2, op0, op1=None):
        if op1 is None:
            self.op(e, lambda g: g.tensor_scalar(out=out, in0=in0, scalar1=s1, scalar2=None, op0=op0), [in0, s1], [out])
        else:
            self.op(e, lambda g: g.tensor_scalar(out=out, in0=in0, scalar1=s1, scalar2=s2, op0=op0, op1=op1), [in0, s1, s2], [out])

    def STT(self, out, in0, scalar, in1, op0, op1, accum=None):
        if accum is None:
            self.op("dve", lambda g: g.scalar_tensor_tensor(out=out, in0=in0, scalar=scalar, in1=in1, op0=op0, op1=op1), [in0, scalar, in1], [out])
        else:
            self.op("dve", lambda g: g.scalar_tensor_tensor(out=out, in0=in0, scalar=scalar, in1=in1, op0=op0, op1=op1, accum_out=accum),
                    [in0, scalar, in1], [out, accum])

    def CP(self, e, out, in_):
        if e == "act":
            self.ACT(out, in_, AF.Copy)
        else:
            self.op(e, lambda g: g.tensor_copy(out=out, in_=in_), [in_], [out])

    def RECIP(self, out, in_):
        self.op("dve", lambda g: g.reciprocal(out=out, in_=in_), [in_], [out])

    def MSET(self, e, out, val):
        self.op(e, lambda g: g.memset(out, val), [], [out])

    def mark(self, label):
        self.marks.append((label, self.nmm))

    def MM(self, out, lhsT, rhs, start, stop, signal=None):
        self.nmm += 1
        self.op("pe", lambda g: g.matmul(out, lhsT, rhs, start=start, stop=stop), [lhsT, rhs], [out],
                signal=(stop if signal is None else signal))

    def TR(self, out, in_, ident, signal=True):
        self.nmm += 1
        self.op("pe", lambda g: g.transpose(out=out, in_=in_, identity=ident), [in_, ident], [out], signal=signal)


@contextmanager
def scope(kb):
    with ExitStack() as es:
        yield es
        kb.barrier()


def pipe(n, stages):
    K = len(stages)
    for t in range(n + K - 1):
        for k in range(K):
            i = t - k
            if 0 <= i < n:
                stages[k](i)


def bc(ap, shape):
    return ap.unsqueeze(2).to_broadcast(shape)


def build(L, T, NL=2, npairs=4):
    nc = bass.Bass("TRN2", target_bir_lowering=False)
    Ltot = T + L
    NCH = Ltot // 128
    seqs = [dict(name="c", off=0, n=T, xoff=2, goff=15), dict(name="l", off=T, n=L, xoff=T + 6, goff=T + 45)]
    Wx = Ltot + 8
    Wg = Ltot + 60

    def din(name, shape, dt=F32):
        return nc.dram_tensor(name, list(shape), dt, kind="ExternalInput").ap()

    def dsc(name, shape, dt):
        return nc.dram_tensor(name, list(shape), dt, kind="Internal").ap()

    x_in = din("x", [L, D])
    ctx_in = din("ctx", [T, D])
    c_in = din("c", [128, 8])
    cc_in = din("c_ctx", [128, 8])
    w_mod = din("w_mod", [NL, D, 3 * D])
    b_mod = din("b_mod", [NL, 3 * D])
    norm_w = din("norm_w", [NL, D])
    w_in = din("w_in", [NL, D, 5408])
    ssd_conv_w = din("ssd_conv_w", [NL, 5, 1536])
    ssd_conv_b = din("ssd_conv_b", [NL, 1536])
    ssd_dt_bias = din("ssd_dt_bias", [NL, 32])
    ssd_a_log = din("ssd_a_log", [NL, 32])
    ssd_d = din("ssd_d", [NL, 16])
    ssd_norm_w = din("ssd_norm_w", [NL, 1024])
    attn_sink = din("attn_sink", [NL, 8])
    conv_dw_w = din("conv_dw_w", [NL, 31, 512])
    conv_dw_b = din("conv_dw_b", [NL, 512])
    conv_ln_w = din("conv_ln_w", [NL, 512])
    conv_ln_b = din("conv_ln_b", [NL, 512])
    conv_pw_w = din("conv_pw_w", [NL, 512, 512])
    conv_pw_b = din("conv_pw_b", [NL, 512])
    w_out = din("w_out", [NL, 2048, D])
    final_norm_w = din("final_norm_w", [1, D])
    cst = din("cst", [128, 7, 128])
    sel_in = din("sel", [128, 2])
    groups = [[2 * i, 2 * i + 1] for i in range(npairs)]
    rope = din("rope", [Ltot, 256])
    out = nc.dram_tensor("out", [L, D], F32, kind="ExternalOutput").ap()

    XR = dsc("XR", [Ltot, D], F32)
    XBCT = dsc("XBCT", [1536, Wx], BF16)
    GLUT = dsc("GLUT", [512, Wg], BF16)
    GCT = dsc("GCT", [512, Ltot], BF16)
    ZS = dsc("ZS", [Ltot, 1024], BF16)
    GAT = dsc("GAT", [512, Ltot], BF16)
    DTS = dsc("DTS", [Ltot, 32], F32)
    DTT = dsc("DTT", [32, Ltot], F32)
    QT = dsc("QT", [64, 8, Ltot], BF16)
    KT = dsc("KT", [64, 2, Ltot], BF16)
    VV = dsc("VV", [Ltot, 130], BF16)
    BCT = dsc("BCT", [128, 4, Ltot], BF16)
    XS = dsc("XS", [Ltot, 1024], BF16)
    BTM = dsc("BTM", [Ltot, 256], BF16)
    ABR = dsc("ABR", [2, 2, 6, NCH, 2048], BF16)
    SBD = dsc("SBD", [NCH, 128, 1024], BF16)
    OT = dsc("OT", [2048, Ltot], BF16)
    MOD = dsc("MOD", [NL, 2, 3 * D], F32)
    SFD = dsc("SFD", [NCH, 128, 1024], BF16)
    SND1 = dsc("SND1", [128, 470], BF16)
    RCV1 = nc.dram_tensor("RCV1", [256, 470], BF16, kind="Internal", addr_space="Local").ap()
    SND2 = dsc("SND2", [128, 1024], F32)
    RCV2 = nc.dram_tensor("RCV2", [256, 1024], F32, kind="Internal", addr_space="Local").ap()

    with ExitStack() as es0:
        kb = KB(nc, es0)

        uid = [0]

        def SB(es, name, shape, dt):
            uid[0] += 1
            return es.enter_context(nc.sbuf_tensor("%s_%d" % (name, uid[0]), list(shape), dt))

        def PS(es, name, shape, dt):
            uid[0] += 1
            return es.enter_context(nc.psum_tensor("%s_%d" % (name, uid[0]), list(shape), dt))

        cf = SB(es0, "cf", [128, 7, 128], F32)
        selB = SB(es0, "selB", [128, 2], F32)
        kb.DMA("sp", selB[:], sel_in[:, :])
        kb.DMA("sp", cf[:], cst[:, :, :])
        identf = cf[:, 0, :]
        triU = cf[:, 1, :]
        triL = cf[:, 2, :]
        onesf = cf[:, 3, :]
        cb_ = SB(es0, "cb", [128, 7, 128], BF16)
        kb.CP("dve", cb_[:], cf[:])
        identb = cb_[:, 0, :]
        onesb = cb_[:, 3, :]
        mneg = SB(es0, "mneg", [128, 3, 4, 128], BF16)
        for d_ in range(3):
            for r_ in range(4):
                kb.CP("dve", mneg[:, d_, r_, :], cf[:, 4 + d_, :])
        ones512 = SB(es0, "ones512", [128, 128], BF16)
        kb.MSET("dve", ones512[:], 1.0 / 512.0)
        zpad = SB(es0, "zpad", [128, 64], BF16)
        kb.MSET("dve", zpad[:], 0.0)
        for s in seqs:
            for side in range(2):
                cx = s["xoff"] - 2 if side == 0 else s["xoff"] + s["n"]
                kb.DMA("sp", XBCT[:, cx:cx + 2].rearrange("(b p) w -> p b w", p=128), zpad[:, 0:24].rearrange("p (b w) -> p b w", w=2), key=s["name"])
                cg = s["goff"] - 15 if side == 0 else s["goff"] + s["n"]
                kb.DMA("sp", GLUT[:, cg:cg + 15].rearrange("(b p) w -> p b w", p=128), zpad[:, 0:60].rearrange("p (b w) -> p b w", w=15), key=s["name"])
        onesrow = SB(es0, "onesrow", [128, 2048], BF16)
        kb.MSET("pool", onesrow[:], 1.0)
        for s in seqs:
            c0_, n0_ = s["off"] // 128, s["n"] // 128
            for d_ in range(2):
                for r_ in range(3):
                    kb.DMA("sp", ABR[d_, 0, 3 + r_, c0_:c0_ + n0_, :], onesrow[0:n0_, :], key=s["name"])
                    kb.DMA("sp", ABR[d_, 1, r_, c0_:c0_ + n0_, :], onesrow[0:n0_, :], key=s["name"])
        epsb = SB(es0, "epsb", [128, 1], F32)
        kb.MSET("pool", epsb[:], EPS)

        for l in range(NL):
            last = (l == NL - 1)
            with scope(kb) as esl:
                kb.mark("setup%d" % l)
                modB = SB(esl, "modB", [128, 2, 3, D], F32)
                aB = SB(esl, "aB", [128, 32], F32)
                dtbB = SB(esl, "dtbB", [128, 32], F32)
                snwB = SB(esl, "snwB", [128, 1024], F32)
                sinkB = SB(esl, "sinkB", [128, 8], F32)
                colsT = SB(esl, "colsT", [128, 12, 41], F32)
                dI = SB(esl, "dI", [128, 16, 128], BF16)
                nbr1 = SB(esl, "nbr1", [128, 470], BF16)
                nx = nbr1[:, 0:24].rearrange("p (b w) -> p b w", w=2)
                ng = nbr1[:, 24:84].rearrange("p (b w) -> p b w", w=15)
                with scope(kb) as ess:
                    pss = PS(ess, "pss", [128, 512], F32)
                    csil = SB(ess, "csil", [128, 2, 8], F32)
                    kb.DMA("sp", csil[:, 0, :], cc_in[:, :])
                    kb.DMA("sp", csil[:, 1, :], c_in[:, :])
                    kb.ACT(csil[:], csil[:], AF.Silu)
                    bm = SB(ess, "bm", [1, 3 * D], F32)
                    kb.DMA("sp", bm[:], b_mod[l:l + 1, :])
                    modrow = SB(ess, "modrow", [1, 2, 3 * D], F32)
                    wm = [SB(ess, "wm%d" % i, [128, 8, 512], F32) for i in range(2)]
                    for cbk in range(6):
                        w_ = wm[cbk % 2]
                        kb.DMA("sp", w_[:], w_mod[l, :, cbk * 512:(cbk + 1) * 512].rearrange("(p k) n -> p k n", k=8))
                        for m in range(2):
                            for kt in range(8):
                                kb.MM(pss[0:1, :], csil[:, m, kt:kt + 1], w_[:, kt, :], start=(kt == 0), stop=(kt == 7))
                            kb.TT("dve", modrow[:, m, cbk * 512:(cbk + 1) * 512], pss[0:1, :], bm[:, cbk * 512:(cbk + 1) * 512], ALU.add)
                    kb.DMA("sp", MOD[l:l + 1, :, :], modrow[:])
                    nwB = SB(ess, "nwB", [128, D], F32)
                    kb.DMA("sp", nwB[:], norm_w[l, :].partition_broadcast(128))
                    for m in range(2):
                        kb.DMA("sp", modB[:, m, :, :], MOD[l, m, :].rearrange("(t d) -> t d", t=3).partition_broadcast(128))
                        tmpw = SB(ess, "tmpw%d" % m, [128, D], F32)
                        kb.STT(tmpw[:], modB[:, m, 1, :], 1.0, nwB[:], ALU.add, ALU.mult)
                        kb.CP("dve", modB[:, m, 1, :], modB[:, m, 0, :])
                        kb.CP("dve", modB[:, m, 0, :], tmpw[:])
                    kb.DMA("sp", aB[:], ssd_a_log[l, :].partition_broadcast(128))
                    kb.ACT(aB[:], aB[:], AF.Exp)
                    kb.TS("dve", aB[:], aB[:], -1.0, None, ALU.mult)
                    kb.DMA("sp", dtbB[:], ssd_dt_bias[l, :].partition_broadcast(128))
                    kb.DMA("sp", snwB[:], ssd_norm_w[l, :].partition_broadcast(128))
                    kb.DMA("sp", sinkB[:], attn_sink[l, :].partition_broadcast(128))
                    kb.ACT(sinkB[:], sinkB[:], AF.Exp)
                    dB = SB(ess, "dB", [128, 16], F32)
                    kb.DMA("sp", dB[:], ssd_d[l, :].partition_broadcast(128))
                    for h in range(16):
                        kb.TS("dve", dI[:, h, :], identf, dB[:, h:h + 1], None, ALU.mult)
                    rows = SB(ess, "rows", [41, 1536], F32)
                    kb.MSET("dve", rows[:], 0.0)
                    kb.DMA("sp", rows[0:5, :], ssd_conv_w[l, :, :])
                    kb.DMA("sp", rows[5:6, :], ssd_conv_b[l:l + 1, :])
                    kb.DMA("sp", rows[6:37, 0:512], conv_dw_w[l, :, :])
                    kb.DMA("sp", rows[37:38, 0:512], conv_dw_b[l:l + 1, :])
                    kb.DMA("sp", rows[38:39, 0:512], conv_ln_w[l:l + 1, :])
                    kb.DMA("sp", rows[39:40, 0:512], conv_ln_b[l:l + 1, :])
                    kb.DMA("sp", rows[40:41, 0:512], conv_pw_b[l:l + 1, :])
                    for b in range(12):
                        kb.TR(pss[:, b * 41:(b + 1) * 41], rows[:, b * 128:(b + 1) * 128], identf[0:41, 0:41], signal=(b == 11))
                    kb.CP("dve", colsT[:].rearrange("p b r -> p (b r)"), pss[:, 0:492])

                kb.mark("A%d" % l)
                sts = []
                for s in seqs:
                    TS_ = 512 if s["name"] == "l" else 256
                    for t0 in range(0, s["n"], TS_):
                        sts.append((s, t0, TS_))
                with scope(kb) as esa:
                    wfm = SB(esa, "wfm", [128, 8, 3584], BF16)
                    wtm = SB(esa, "wtm", [128, 8, 1824], BF16)
                    wv = w_in[l].rearrange("(k p) n -> p k n", p=128)
                    with scope(kb) as esw:
                        wst = [SB(esw, "wst%d" % i, [128, 5408], F32) for i in range(2)]
                        ce = ["dve", "pool", "act"]
                        nce = 0
                        for kt in range(8):
                            w_ = wst[kt % 2]
                            kb.DMA("sp", w_[:, 0:2704], wv[:, kt, 0:2704])
                            kb.DMA("sp", w_[:, 2704:5408], wv[:, kt, 2704:5408])
                            segs = [(wfm, 0, 0, 768), (wfm, 768, 768, 768), (wfm, 2560, 4896, 512),
                                    (wtm, 0, 1536, 512), (wtm, 512, 2048, 512), (wtm, 1024, 2592, 512), (wtm, 1536, 3104, 256),
                                    (wtm, 1792, 2560, 32), (wfm, 3072, 3360, 512)]
                            for i in range(4):
                                segs.append((wfm, 1536 + 256 * i, 3872 + 128 * i, 128))
                                segs.append((wfm, 1536 + 256 * i + 128, 4384 + 128 * i, 128))
                            for (dstt, d0, s0, n_) in segs:
                                kb.CP(ce[nce % 3], dstt[:, kt, d0:d0 + n_], w_[:, s0:s0 + n_])
                                nce += 1
                    xin = [SB(esa, "xin%d" % i, [128, D], F32) for i in range(3)]
                    hb = [SB(esa, "hb%d" % i, [128, D], BF16) for i in range(2)]
                    junk = SB(esa, "junk", [128, D], BF16)
                    tmpf = SB(esa, "tmpf", [128, D], F32)
                    hT = [SB(esa, "hT%d" % i, [128, 8, 512], BF16) for i in range(2)]
                    st4 = SB(esa, "st4", [128, 8], F32)
                    fst = [SB(esa, "fst%d" % i, [128, 512], BF16) for i in range(3)]
                    sig = SB(esa, "sig", [128, 512], F32)
                    zs = [SB(esa, "zs%d" % i, [128, 1024], BF16) for i in range(2)]
                    qk = [SB(esa, "qk%d" % i, [128, 640], BF16) for i in range(2)]
                    vst = [SB(esa, "vst%d" % i, [128, 2, 65], BF16) for i in range(2)]
                    for i in range(2):
                        kb.MSET("pool", vst[i][:], 1.0)
                    qkT = [SB(esa, "qkT%d" % i, [64, 10, 128], BF16) for i in range(2)]
                    dtw = [SB(esa, "dtw%d" % i, [128, 6, 32], F32) for i in range(2)]
                    dtT = [SB(esa, "dtT%d" % i, [32, 128], F32) for i in range(2)]
                    rp = [SB(esa, "rp%d" % i, [128, 256], F32) for i in range(2)]
                    rt = SB(esa, "rt", [128, 2, 640], F32)
                    pT = [PS(esa, "pT%d" % i, [128, 1024], BF16) for i in range(2)]
                    ppool = [PS(esa, "pp%d" % i, [128, 512], F32) for i in range(6)]
                    subs = [(si, j) for si, (s, t0, TS_) in enumerate(sts) for j in range(TS_ // 128)]
                    cntA = dict(fm=0, tm=0, sc=0)

                    def a_load(k):
                        si, j = subs[k]
                        s, t0, TS_ = sts[si]
                        if l == 0:
                            src = (ctx_in if s["name"] == "c" else x_in)[t0 + j * 128:t0 + (j + 1) * 128, :]
                        else:
                            src = XR[s["off"] + t0 + j * 128:s["off"] + t0 + (j + 1) * 128, :]
                        kb.DMA("sp", xin[k % 3][:], src, key=(s["name"], t0 + j * 128))

                    def a_norm(k):
                        si, j = subs[k]
                        s, t0, TS_ = sts[si]
                        m = 0 if s["name"] == "c" else 1
                        xi, hbj = xin[k % 3], hb[k % 2]
                        col = st4[:, k % 8:k % 8 + 1]
                        kb.ACT(junk[:], xi[:], AF.Square, accum=col)
                        kb.TS("dve", col, col, 1.0 / D, EPS, ALU.mult, ALU.add)
                        kb.ACT(col, col, AF.Sqrt)
                        kb.RECIP(col, col)
                        kb.STT(tmpf[:], xi[:], col, modB[:, m, 0, :], ALU.mult, ALU.mult)
                        kb.TT("pool", hbj[:], tmpf[:], modB[:, m, 1, :], ALU.add)

                    smallq = []

                    def drain(n):
                        while n > 0 and smallq:
                            smallq.pop(0)()
                            n -= 1

                    def a_transp(k, defer=True):
                        si, j = subs[k]
                        hTt, p_, hbj = hT[si % 2], pT[k % 2], hb[k % 2]

                        def piece(q4):
                            for kt in range(2 * q4, 2 * q4 + 2):
                                kb.TR(p_[:, kt * 128:(kt + 1) * 128], hbj[:, kt * 128:(kt + 1) * 128], identb, signal=(kt == 7))
                            if q4 == 3:
                                kb.CP("act" if k % 2 else "dve", hTt[:, :, j * 128:(j + 1) * 128], p_[:].rearrange("p (k t) -> p k t", k=8))
                        for q4 in range(4):
                            if defer:
                                smallq.append(lambda q4=q4: piece(q4))
                            else:
                                piece(q4)

                    def a_fmblock(si, cbk):
                        s, t0, TS_ = sts[si]
                        hTt = hT[si % 2]
                        p_ = ppool[cntA["tm"] % 6]
                        cntA["tm"] += 1
                        for kt in range(8):
                            kb.MM(p_[:, 0:TS_], wfm[:, kt, cbk * 128:(cbk + 1) * 128], hTt[:, kt, 0:TS_], start=(kt == 0), stop=(kt == 7))
                        drain(1)
                        return p_

                    def a_fmunit(si, u):
                        s, t0, TS_ = sts[si]
                        g0 = s["off"] + t0
                        if u < 12:
                            p_ = a_fmblock(si, u)
                            f_ = fst[u % 3]
                            kb.CP("dve", f_[:, 0:TS_], p_[:, 0:TS_])
                            kb.DMA("sp", XBCT[u * 128:(u + 1) * 128, s["xoff"] + t0:s["xoff"] + t0 + TS_], f_[:, 0:TS_], key=s["name"])
                        elif u < 16:
                            i = u - 12
                            pv = a_fmblock(si, 12 + 2 * i)
                            pg = a_fmblock(si, 13 + 2 * i)
                            kb.ACT(sig[:, 0:TS_], pg[:, 0:TS_], AF.Sigmoid)
                            f_ = fst[i % 3]
                            kb.TT("dve", f_[:, 0:TS_], pv[:, 0:TS_], sig[:, 0:TS_], ALU.mult)
                            kb.DMA("sp", GLUT[i * 128:(i + 1) * 128, s["goff"] + t0:s["goff"] + t0 + TS_], f_[:, 0:TS_], key=s["name"])
                        else:
                            i = u - 16
                            p_ = a_fmblock(si, 20 + i)
                            f_ = fst[i % 3]
                            kb.ACT(f_[:, 0:TS_], p_[:, 0:TS_], AF.Silu)
                            dstT = GCT if i < 4 else GAT
                            kb.DMA("sp", dstT[(i % 4) * 128:(i % 4 + 1) * 128, g0:g0 + TS_], f_[:, 0:TS_], key=s["name"])

                    def a_tmsub(si, j):
                        s, t0, TS_ = sts[si]
                        hTt = hT[si % 2]
                        gt = s["off"] + t0 + j * 128
                        sl = cntA["sc"] % 2
                        cntA["sc"] += 1

                        def tm_block(c0, c1):
                            p_ = ppool[cntA["tm"] % 6]
                            cntA["tm"] += 1
                            for kt in range(8):
                                kb.MM(p_[:, 0:c1 - c0], hTt[:, kt, j * 128:(j + 1) * 128], wtm[:, kt, c0:c1], start=(kt == 0), stop=(kt == 7))
                            drain(1)
                            return p_
                        kb.DMA("sp", rp[sl][:], rope[gt:gt + 128, :])
                        for i in range(2):
                            p_ = tm_block(i * 512, (i + 1) * 512)
                            kb.ACT(zs[sl][:, i * 512:(i + 1) * 512], p_[:], AF.Silu)
                        kb.DMA("sp", ZS[gt:gt + 128, :], zs[sl][:], key=s["name"])
                        pq = tm_block(1024, 1536)
                        pk = tm_block(1536, 1824)
                        for (p_, nh, o0, tb) in ((pq, 8, 0, 128), (pk, 2, 512, 0)):
                            srcv = p_[:, 0:nh * 64]
                            a_ = rt[:, 0, 0:nh * 64]
                            b_ = rt[:, 1, 0:nh * 64]
                            cos2 = rp[sl][:, tb:tb + 64].unsqueeze(1).to_broadcast([128, nh, 64])
                            kb.TT("dve", a_.rearrange("p (h d) -> p h d", h=nh), srcv.rearrange("p (h d) -> p h d", h=nh), cos2, ALU.mult)
                            s4 = srcv.rearrange("p (h a x f) -> p h a x f", h=nh, a=2, x=2)
                            b4 = b_.rearrange("p (h a x f) -> p h a x f", h=nh, a=2, x=2)
                            sm = rp[sl][:, tb + 64:tb + 96].rearrange("p (a f) -> p a f", a=2).unsqueeze(1).to_broadcast([128, nh, 2, 16])
                            spl = rp[sl][:, tb + 96:tb + 128].rearrange("p (a f) -> p a f", a=2).unsqueeze(1).to_broadcast([128, nh, 2, 16])
                            kb.TT("dve", b4[:, :, :, 0, :], s4[:, :, :, 1, :], sm, ALU.mult)
                            kb.TT("dve", b4[:, :, :, 1, :], s4[:, :, :, 0, :], spl, ALU.mult)
                            kb.TT("pool", qk[sl][:, o0:o0 + nh * 64], a_, b_, ALU.add)
                        kb.CP("act", vst[sl][:, :, 0:64], pk[:, 128:256].rearrange("p (g d) -> p g d", g=2))
                        kb.DMA("sp", VV[gt:gt + 128, :], vst[sl][:].rearrange("p g d -> p (g d)"), key=s["name"])
                        dw = dtw[sl]
                        kb.TT("dve", dw[:, 0, :], pk[:, 256:288], dtbB[:], ALU.add)
                        kb.ACT(dw[:, 1, :], dw[:, 0, :], AF.Abs)
                        kb.ACT(dw[:, 2, :], dw[:, 1, :], AF.Exp, scale=-1.0)
                        kb.ACT(dw[:, 3, :], dw[:, 2, :], AF.Ln, bias=1.0)
                        kb.STT(dw[:, 5, :], dw[:, 0, :], 0.0, dw[:, 3, :], ALU.max, ALU.add)
                        kb.TS("dve", dw[:, 4, :], dw[:, 5, :], 1e-30, None, ALU.max)
                        kb.DMA("sp", DTS[gt:gt + 128, :], dw[:, 4, :], key=s["name"])
                        return (s, gt, sl, j)

                    def a_tmsub2(arg):
                        s, gt, sl, j = arg
                        dw = dtw[sl]
                        p_ = pT[j % 2]
                        p2 = pT[(j + 1) % 2]

                        def pq(q4):
                            for h in range(2 * q4, 2 * q4 + 2):
                                kb.TR(p_[0:64, h * 128:(h + 1) * 128], qk[sl][:, h * 64:(h + 1) * 64], identb, signal=(h == 7))
                            if q4 == 3:
                                kb.CP("act", qkT[sl][:, 0:8, :], p_[0:64, :].rearrange("p (h t) -> p h t", h=8))
                                kb.DMA("sp", QT[:, :, gt:gt + 128], qkT[sl][:, 0:8, :], key=s["name"])

                        def pk2():
                            for h in range(2):
                                kb.TR(p2[0:64, h * 128:(h + 1) * 128], qk[sl][:, 512 + h * 64:512 + (h + 1) * 64], identb, signal=(h == 1))
                            kb.CP("dve", qkT[sl][:, 8:10, :], p2[0:64, 0:256].rearrange("p (h t) -> p h t", h=2))
                            kb.DMA("sp", KT[:, :, gt:gt + 128], qkT[sl][:, 8:10, :], key=s["name"])
                            p3 = ppool[cntA["tm"] % 6]
                            cntA["tm"] += 1
                            kb.TR(p3[0:32, 0:128], dw[:, 4, :], identf)
                            kb.CP("dve", dtT[sl][:], p3[0:32, 0:128])
                            kb.DMA("sp", DTT[:, gt:gt + 128], dtT[sl][:], key=s["name"])
                        for q4 in range(4):
                            smallq.append(lambda q4=q4: pq(q4))
                        smallq.append(pk2)

                    nsubs = len(subs)
                    ks_of = [[k for k, (si2, j) in enumerate(subs) if si2 == si] for si in range(len(sts))]
                    pend2 = [None]
                    a_load(0)
                    for k in ks_of[0]:
                        if k + 1 < nsubs:
                            a_load(k + 1)
                        a_norm(k)
                        a_transp(k, defer=False)
                    for si in range(len(sts)):
                        nsub = len(ks_of[si])
                        nxt = ks_of[si + 1] if si + 1 < len(sts) else []
                        per = 24 // nsub
                        for q in range(nsub):
                            kn = nxt[q] if q < len(nxt) else None
                            if kn is not None:
                                if kn + 1 < nsubs:
                                    a_load(kn + 1)
                                a_norm(kn)
                            if pend2[0] is not None:
                                a_tmsub2(pend2[0])
                                pend2[0] = None
                            if kn is not None:
                                a_transp(kn)
                            for u in range(q * per, (q + 1) * per):
                                a_fmunit(si, u)
                            pend2[0] = a_tmsub(si, q)
                            drain(100)
                        for q in range(nsub, len(nxt)):
                            kn = nxt[q]
                            if kn + 1 < nsubs:
                                a_load(kn + 1)
                            a_norm(kn)
                            a_transp(kn, defer=False)
                    a_tmsub2(pend2[0])
                    drain(100)

                with scope(kb) as ex:
                    sL = seqs[1]
                    end_x, end_g, end_t = sL["xoff"] + sL["n"], sL["goff"] + sL["n"], sL["off"] + sL["n"]
                    snd = SB(ex, "snd", [128, 470], BF16)
                    kb.MSET("dve", snd[:], 0.0)
                    kb.DMA("sp", snd[:, 0:24].rearrange("p (b w) -> p b w", w=2), XBCT[:, end_x - 2:end_x].rearrange("(b p) w -> p b w", p=128), key="l")
                    kb.DMA("sp", snd[:, 24:84].rearrange("p (b w) -> p b w", w=15), GLUT[:, end_g - 15:end_g].rearrange("(b p) w -> p b w", p=128), key="l")
                    kb.DMA("sp", snd[0:64, 84:340].rearrange("p (g t) -> p g t", g=2), KT[:, :, end_t - 128:end_t], key="l")
                    kb.DMA("sp", snd[:, 340:470], VV[end_t - 128:end_t, :], key="l")
                    kb.DMA("sp", SND1[:, :], snd[:])
                    kb.CC(SND1[:, :], RCV1[:, :], groups)
                    rcv = SB(ex, "rcv", [128, 2, 470], BF16)
                    kb.DMA("sp", rcv[:], RCV1.rearrange("(r p) c -> p r c", p=128))
                    tmpx = SB(ex, "tmpx", [128, 470], F32)
                    kb.TS("dve", tmpx[:], rcv[:, 0, :], selB[:, 0:1], None, ALU.mult)
                    kb.STT(nbr1[:], rcv[:, 1, :], selB[:, 1:2], tmpx[:], ALU.mult, ALU.add)

                with scope(kb) as esb:
                    kb.mark("B0%d" % l)
                    kb.mark("B1%d" % l)
                    with scope(kb) as e1:
                        dg = SB(e1, "dg", [128, 60, 128], BF16)
                        for b in range(12):
                            for j in range(5):
                                if (b + j) % 2:
                                    kb.TS("dve", dg[:, b * 5 + j, :], identf, colsT[:, b, j:j + 1], None, ALU.mult)
                                else:
                                    kb.ACT(dg[:, b * 5 + j, :], identf, AF.Copy, scale=colsT[:, b, j:j + 1])
                        win = [SB(e1, "win%d" % i, [128, 12, 516], BF16) for i in range(2)]
                        xc = [SB(e1, "xc%d" % i, [128, 12, 512], BF16) for i in range(2)]
                        xst = [SB(e1, "xst%d" % i, [128, 1280], BF16) for i in range(2)]
                        pcv = [PS(e1, "pcv%d" % i, [128, 512], F32) for i in range(2)]
                        ptr = [PS(e1, "ptr%d" % i, [128, 1024], BF16) for i in range(2)]
                        ptb = [PS(e1, "ptb%d" % i, [128, 256], BF16) for i in range(2)]
                        cntB = dict(j=0)

                        def b1_s0(i):
                            s, t0, TS_ = sts[i]
                            if i == len(sts) - 1:
                                kb.DMA("sp", win[i % 2][:, :, 0:TS_ + 2], XBCT[:, s["xoff"] + t0 - 2:s["xoff"] + t0 + TS_].rearrange("(b p) w -> p b w", p=128), key=s["name"])
                                for r in range(2):
                                    kb.CP("pool", win[i % 2][:, :, TS_ + 2 + r:TS_ + 3 + r], nx[:, :, 1 - r:2 - r])
                            else:
                                kb.DMA("sp", win[i % 2][:, :, 0:TS_ + 4], XBCT[:, s["xoff"] + t0 - 2:s["xoff"] + t0 + TS_ + 2].rearrange("(b p) w -> p b w", p=128), key=s["name"])

                        def b1_s1(i):
                            s, t0, TS_ = sts[i]
                            w_, x_ = win[i % 2], xc[i % 2]
                            g0 = s["off"] + t0
                            for b in range(12):
                                p_ = pcv[b % 2]
                                for j in range(5):
                                    kb.MM(p_[:, 0:TS_], dg[:, b * 5 + j, :], w_[:, b, j:j + TS_], start=(j == 0), stop=(j == 4))
                                kb.ACT(x_[:, b, 0:TS_], p_[:, 0:TS_], AF.Silu, bias=colsT[:, b, 5:6])
                            kb.DMA("sp", BCT[:, :, g0:g0 + TS_], x_[:, 8:12, 0:TS_], key=s["name"])

                        def b1_s2(i):
                            s, t0, TS_ = sts[i]
                            x_ = xc[i % 2]
                            g0 = s["off"] + t0
                            for j in range(TS_ // 128):
                                gt = g0 + j * 128
                                nj = cntB["j"]
                                cntB["j"] += 1
                                p_, pb_, xs_ = ptr[nj % 2], ptb[nj % 2], xst[nj % 2]
                                for b in range(8):
                                    kb.TR(p_[:, b * 128:(b + 1) * 128], x_[:, b, j * 128:(j + 1) * 128], identb, signal=(b == 7))
                                for b in range(2):
                                    kb.TR(pb_[:, b * 128:(b + 1) * 128], x_[:, 8 + b, j * 128:(j + 1) * 128], identb, signal=(b == 1))
                                kb.CP("dve", xs_[:, 0:1024], p_[:])
                                kb.CP("pool" if False else "dve", xs_[:, 1024:1280], pb_[:])
                                kb.DMA("sp", XS[gt:gt + 128, :], xs_[:, 0:1024], key=s["name"])
                                kb.DMA("sp", BTM[gt:gt + 128, :], xs_[:, 1024:1280], key=s["name"])
                        rmask = SB(e1, "rmask", [32, 16, 128], F32)
                        kb.MSET("dve", rmask[:], 1.0)
                        kb.MSET("dve", rmask[:, :, 0:1], 0.0)
                        dr = SB(e1, "dr", [32, 16, 128], F32)
                        da = SB(e1, "da", [32, 16, 128], F32)
                        cs = SB(e1, "cs", [32, 16, 128], F32)
                        ld = SB(e1, "ld", [32, 16, 128], F32)
                        al = SB(e1, "al", [32, 16, 128], F32)
                        r1 = SB(e1, "r1", [32, 16, 128], F32)
                        sp3 = SB(e1, "sp3", [32, 3, 2048], BF16)
                        assert NCH <= 32
                        b0_items = [0, 1]

                        def b0_piece(d_):
                            n_ = NCH
                            for s in seqs:
                                c0, nch = s["off"] // 128, s["n"] // 128
                                kb.DMA("sp", dr[c0:c0 + nch], DTT[d_ * 16:(d_ + 1) * 16, s["off"]:s["off"] + s["n"]].rearrange("h (c t) -> c h t", t=128), key=s["name"])
                            kb.TT("dve", da[0:n_], dr[0:n_], bc(aB[0:n_, d_ * 16:(d_ + 1) * 16], [n_, 16, 128]), ALU.mult)
                            kb.op("dve", lambda g: g.tensor_tensor_scan(out=cs[0:n_].rearrange("c h t -> c (h t)"), data0=rmask[0:n_].rearrange("c h t -> c (h t)"),
                                                                          data1=da[0:n_].rearrange("c h t -> c (h t)"), initial=0.0, op0=ALU.mult, op1=ALU.add),
                                  [rmask[0:n_], da[0:n_]], [cs[0:n_]])
                            if d_ == 1:
                                kb.TT("dve", r1[0:n_], da[0:n_], cs[0:n_], ALU.subtract)
                                kb.CP("dve", al[0:n_, :, 0:1], cs[0:n_, :, 127:128])
                                kb.TT("dve", cs[0:n_], r1[0:n_], al[0:n_, :, 0:1].to_broadcast([n_, 16, 128]), ALU.add)
                            kb.ACT(ld[0:n_], dr[0:n_], AF.Ln)
                            kb.TT("dve", al[0:n_], ld[0:n_], cs[0:n_], ALU.subtract)
                            for side, srcx in ((0, al), (1, cs)):
                                sx = srcx[0:n_].rearrange("c h t -> c (h t)")
                                r1f = r1[0:n_].rearrange("c h t -> c (h t)")
                                kb.CP("act", sp3[0:n_, 0, :], sx)
                                kb.TT("dve", r1f, sx, sp3[0:n_, 0, :], ALU.subtract)
                                kb.CP("act", sp3[0:n_, 1, :], r1f)
                                kb.TT("dve", r1f, r1f, sp3[0:n_, 1, :], ALU.subtract)
                                kb.CP("act", sp3[0:n_, 2, :], r1f)
                                for r_ in range(3):
                                    for s in seqs:
                                        c0, nch = s["off"] // 128, s["n"] // 128
                                        kb.DMA("sp", ABR[d_, side, 3 * side + r_, c0:c0 + nch, :], sp3[c0:c0 + nch, r_, :], key=s["name"])

                        nb0 = 0
                        for t_ in range(len(sts) + 2):
                            for k_, st_ in enumerate([b1_s0, b1_s1, b1_s2]):
                                if 0 <= t_ - k_ < len(sts):
                                    st_(t_ - k_)
                            if t_ in (1, 3):
                                b0_piece(nb0)
                                nb0 += 1
                        assert nb0 == 2

                    with scope(kb) as e2:
                        Sf = SB(e2, "Sf", [128, 1024], F32)
                        Sb_ = SB(e2, "Sb", [128, 1024], F32)
                        Sfb = SB(e2, "Sfb", [128, 1024], BF16)
                        Sbb = [SB(e2, "Sbb%d" % i, [128, 1024], BF16) for i in range(2)]
                        kb.MSET("dve", Sf[:], 0.0)
                        kb.MSET("dve", Sb_[:], 0.0)
                        kb.MSET("pool", Sfb[:], 0.0)
                        xsb = [SB(e2, "xsb%d" % i, [128, 1024], BF16) for i in range(3)]
                        btm = [SB(e2, "btm%d" % i, [128, 256], BF16) for i in range(3)]
                        dts = [SB(e2, "dts%d" % i, [128, 32], F32) for i in range(3)]
                        xdw = [SB(e2, "xdw%d" % i, [128, 1024], BF16) for i in range(2)]
                        sm_ = [SB(e2, "sm%d" % i, [128, 8, 32], F32) for i in range(2)]
                        psm = PS(e2, "psm", [128, 512], F32)
                        pO = PS(e2, "pO", [128, 1024], F32)

                        def small_terms(sl, d_, dt_ap):
                            S_ = sm_[sl]
                            o = d_ * 16
                            kb.TT("dve", S_[:, 0, o:o + 16], dt_ap[:, o:o + 16], aB[:, o:o + 16], ALU.mult)
                            kb.MM(psm[:, o:o + 16], triU if d_ == 0 else triL, S_[:, 0, o:o + 16], start=True, stop=True)
                            kb.MM(psm[:, 64 + o:64 + o + 16], onesf, S_[:, 0, o:o + 16], start=True, stop=True)
                            kb.CP("dve", S_[:, 1, o:o + 16], psm[:, o:o + 16])
                            kb.TT("dve", S_[:, 2, o:o + 16], psm[:, 64 + o:64 + o + 16], S_[:, 1, o:o + 16], ALU.subtract)
                            kb.ACT(S_[:, 3, o:o + 16], S_[:, 2, o:o + 16], AF.Exp)
                            kb.TT("dve", S_[:, 3, o:o + 16], S_[:, 3, o:o + 16], dt_ap[:, o:o + 16], ALU.mult)
                            kb.ACT(S_[:, 4, o:o + 16], S_[:, 1, o:o + 16], AF.Exp)
                            kb.ACT(S_[:, 5, o:o + 16], psm[:, 64 + o:64 + o + 16], AF.Exp)
                            return S_[:, 4, o:o + 16], S_[:, 3, o:o + 16], S_[:, 5, o:o + 16]

                        def mk_xdw(xd_t, xs_ap, w_ap):
                            kb.TT("pool", xd_t[:].rearrange("p (h d) -> p h d", h=16), xs_ap.rearrange("p (h d) -> p h d", h=16), bc(w_ap, [128, 16, 64]), ALU.mult)

                        def state_update(Smaster, bt_ap, etot_ap, xd_t):
                            for g in range(2):
                                kb.MM(pO[:, g * 512:(g + 1) * 512], bt_ap[:, g * 128:(g + 1) * 128], xd_t[:, g * 512:(g + 1) * 512], start=True, stop=True)
                            kb.TT("dve", Smaster[:].rearrange("p (h d) -> p h d", h=16), Smaster[:].rearrange("p (h d) -> p h d", h=16), bc(etot_ap, [128, 16, 64]), ALU.mult)
                            kb.TT("dve", Smaster[:], Smaster[:], pO[:], ALU.add)

                        kb.mark("B2%d" % l)
                        chf = []
                        for s in seqs:
                            for c in range(s["n"] // 128):
                                chf.append((s, c))

                        def bf_s0(i):
                            s, c = chf[i]
                            gt = s["off"] + c * 128
                            kb.DMA("sp", xsb[i % 3][:], XS[gt:gt + 128, :], key=s["name"])
                            kb.DMA("sp", btm[i % 3][:], BTM[gt:gt + 128, :], key=s["name"])
                            kb.DMA("sp", dts[i % 3][:], DTS[gt:gt + 128, :], key=s["name"])

                        def bf_s1(i):
                            s, c = chf[i]
                            gc = (s["off"] + c * 128) // 128
                            _, w_ap, et_ap = small_terms(i % 2, 0, dts[i % 3])
                            mk_xdw(xdw[i % 2], xsb[i % 3][:], w_ap)
                            kb.CP("act", Sbb[i % 2][:], Sf[:])
                            kb.DMA("sp", SFD[gc, :, :], Sbb[i % 2][:], key=gc)
                            state_update(Sf, btm[i % 3], et_ap, xdw[i % 2])
                        pipe(len(chf), [bf_s0, bf_s1])
                        nbS = SB(e2, "nbS", [128, 1024], F32)
                        with scope(kb) as ex2:
                            rcs2 = SB(ex2, "rcs2", [128, 2, 1024], F32)
                            kb.DMA("sp", SND2[:, :], Sf[:])
                            kb.CC(SND2[:, :], RCV2[:, :], groups)
                            kb.DMA("sp", rcs2[:], RCV2.rearrange("(r p) c -> p r c", p=128))
                            kb.TS("dve", nbS[:], rcs2[:, 0, :], selB[:, 0:1], None, ALU.mult)
                            kb.STT(nbS[:], rcs2[:, 1, :], selB[:, 1:2], nbS[:], ALU.mult, ALU.add)
                        chd = []
                        for s in seqs:
                            for c in reversed(range(s["n"] // 128)):
                                chd.append((s, c))
                        first_lat = seqs[0]["n"] // 128

                        def b2_s0(i):
                            s, c = chd[i]
                            gt = s["off"] + c * 128
                            kb.DMA("sp", xsb[i % 3][:], XS[gt:gt + 128, :], key=s["name"])
                            kb.DMA("sp", btm[i % 3][:], BTM[gt:gt + 128, :], key=s["name"])
                            kb.DMA("sp", dts[i % 3][:], DTS[gt:gt + 128, :], key=s["name"])

                        def b2_s1(i):
                            s, c = chd[i]
                            gc = (s["off"] + c * 128) // 128
                            if i == first_lat:
                                kb.CP("dve", Sb_[:], nbS[:])
                            _, w_ap, et_ap = small_terms(i % 2, 1, dts[i % 3])
                            mk_xdw(xdw[i % 2], xsb[i % 3][:], w_ap)
                            kb.CP("act", Sbb[i % 2][:], Sb_[:])
                            kb.DMA("sp", SBD[gc, :, :], Sbb[i % 2][:], key=gc)
                            state_update(Sb_, btm[i % 3], et_ap, xdw[i % 2])
                        pipe(len(chd), [b2_s0, b2_s1])

                        kb.mark("B3%d" % l)
                        with scope(kb) as e3:
                            bct = [SB(e3, "bct%d" % i, [128, 4, 128], BF16) for i in range(3)]
                            zsb = [SB(e3, "zsb%d" % i, [128, 1024], BF16) for i in range(2)]
                            sbe = [SB(e3, "sbe%d" % i, [128, 1024], BF16) for i in range(2)]
                            sfe = [SB(e3, "sfe%d" % i, [128, 1024], BF16) for i in range(2)]
                            ab6 = [SB(e3, "ab6%d" % i, [6, 2, 2, 2048], BF16) for i in range(2)]
                            cbT = SB(e3, "cbT", [128, 2, 128], BF16)
                            E_ = [SB(e3, "E%d" % i, [128, 4, 128], BF16) for i in range(2)]
                            MT = SB(e3, "MT", [128, 2, 16, 128], BF16)
                            ysb = [SB(e3, "ysb%d" % i, [128, 1024], F32) for i in range(2)]
                            t1 = SB(e3, "t1", [128, 1024], F32)
                            t2 = SB(e3, "t2", [128, 1024], F32)
                            us = [SB(e3, "u%d" % i, [128, 1024], F32) for i in range(2)]
                            obs = [SB(e3, "ob%d" % i, [128, 1024], BF16) for i in range(2)]
                            sq = SB(e3, "sq", [128, 512], F32)
                            oT = [SB(e3, "oT%d" % i, [128, 8, 128], BF16) for i in range(2)]
                            st2 = SB(e3, "st2", [128, 2, 2], F32)
                            pDs = [PS(e3, "pD%d" % i, [128, 512], F32) for i in range(2)]
                            pYs = [PS(e3, "pY%d" % i, [128, 512], F32) for i in range(2)]
                            pX = PS(e3, "pX", [128, 1024], BF16)
                            cntD = dict(d=0)
                            cha = []
                            for s in seqs:
                                for c in range(s["n"] // 128):
                                    cha.append((s, c))

                            def ny(s):
                                return not (last and s["name"] == "c")

                            def b3_s0(i):
                                s, c = cha[i]
                                gt = s["off"] + c * 128
                                gc = gt // 128
                                kb.DMA("sp", xsb[i % 3][:], XS[gt:gt + 128, :], key=s["name"])
                                kb.DMA("sp", dts[i % 3][:], DTS[gt:gt + 128, :], key=s["name"])
                                if ny(s):
                                    kb.DMA("sp", bct[i % 3][:], BCT[:, :, gt:gt + 128], key=s["name"])
                                    kb.DMA("sp", ab6[i % 2][:], ABR[:, :, :, gc, :].rearrange("d s r x -> r d s x"), key=s["name"])

                            def b3_s1(i):
                                s, c = cha[i]
                                gt = s["off"] + c * 128
                                gc = gt // 128
                                sl3, sl = i % 3, i % 2
                                if not ny(s):
                                    return
                                ef, wf, etf = small_terms(sl, 0, dts[sl3])
                                kb.DMA("sp", zsb[sl][:], ZS[gt:gt + 128, :], key=s["name"])
                                kb.DMA("sp", sbe[sl][:], SBD[gc, :, :], key=gc)
                                kb.DMA("sp", sfe[sl][:], SFD[gc, :, :], key=gc)
                                small_terms(sl, 1, dts[sl3])
                                for g in range(2):
                                    kb.MM(psm[:, 256 + g * 128:256 + (g + 1) * 128], bct[sl3][:, g, :], bct[sl3][:, 2 + g, :], start=True, stop=True)
                                kb.CP("dve", cbT[:].rearrange("p g t -> p (g t)"), psm[:, 256:512])
                                for g in range(2):
                                    for hb_ in range(2):
                                        es_ = []
                                        for d_ in range(2):
                                            pD = pDs[cntD["d"] % 2]
                                            e_ = E_[cntD["d"] % 2]
                                            cntD["d"] += 1
                                            kb.MM(pD[:], identb, mneg[:, d_, :, :].rearrange("p r t -> p (r t)"), start=True, stop=False, signal=False)
                                            for hh in range(4):
                                                h = g * 8 + hb_ * 4 + hh
                                                kb.MM(pD[:, hh * 128:(hh + 1) * 128], ab6[sl][:, d_, 0, h * 128:(h + 1) * 128],
                                                      ab6[sl][:, d_, 1, h * 128:(h + 1) * 128], start=False, stop=(hh == 3), signal=(hh == 3))
                                            kb.ACT(e_[:].rearrange("p h t -> p (h t)"), pD[:], AF.Exp)
                                            es_.append(e_)
                                        h0 = g * 8 + hb_ * 4
                                        kb.TT("dve", es_[0][:], es_[0][:], es_[1][:], ALU.add)
                                        kb.TT("dve", MT[:, 0, h0:h0 + 4, :], es_[0][:], cbT[:, g, :].unsqueeze(1).to_broadcast([128, 4, 128]), ALU.mult)
                                    pY = pYs[g]
                                    for hh in range(8):
                                        h = g * 8 + hh
                                        rhs = xsb[sl3][:, h * 64:(h + 1) * 64]
                                        kb.MM(pY[:, hh * 64:(hh + 1) * 64], MT[:, 0, h, :], rhs, start=True, stop=False, signal=False)
                                        kb.MM(pY[:, hh * 64:(hh + 1) * 64], dI[:, h, :], rhs, start=False, stop=True, signal=(hh == 7))
                                    kb.CP("act", ysb[sl][:, g * 512:(g + 1) * 512], pY[:])

                            def b3_s2(i):
                                s, c = cha[i]
                                if not ny(s):
                                    return
                                sl3, sl = i % 3, i % 2
                                S_ = sm_[sl]
                                ef, eb = S_[:, 4, 0:16], S_[:, 4, 16:32]
                                u = us[sl]
                                for g in range(2):
                                    kb.MM(pO[:, g * 512:(g + 1) * 512], bct[sl3][:, 2 + g, :], sfe[sl][:, g * 512:(g + 1) * 512], start=True, stop=True)
                                kb.TT("dve", t1[:].rearrange("p (h d) -> p h d", h=16), pO[:].rearrange("p (h d) -> p h d", h=16), bc(ef, [128, 16, 64]), ALU.mult)
                                for g in range(2):
                                    kb.MM(pO[:, g * 512:(g + 1) * 512], bct[sl3][:, 2 + g, :], sbe[sl][:, g * 512:(g + 1) * 512], start=True, stop=True)
                                kb.TT("dve", t2[:].rearrange("p (h d) -> p h d", h=16), pO[:].rearrange("p (h d) -> p h d", h=16), bc(eb, [128, 16, 64]), ALU.mult)
                                kb.TT("pool", u[:], t1[:], t2[:], ALU.add)
                                kb.TT("pool", u[:], u[:], ysb[sl][:], ALU.add)
                                kb.TT("pool", u[:], u[:], zsb[sl][:], ALU.mult)

                            def b3_s3(i):
                                s, c = cha[i]
                                if not ny(s):
                                    return
                                sl = i % 2
                                u = us[sl]
                                ss_ = st2[:, sl, :]
                                for g in range(2):
                                    kb.STT(sq[:], u[:, g * 512:(g + 1) * 512], 1.0, u[:, g * 512:(g + 1) * 512], ALU.mult, ALU.mult, accum=st2[:, sl, g:g + 1])
                                kb.TS("dve", ss_, ss_, 1.0 / 512.0, EPS, ALU.mult, ALU.add)
                                kb.ACT(ss_, ss_, AF.Ln)
                                kb.ACT(ss_, ss_, AF.Exp, scale=-0.5)
                                for g in range(2):
                                    kb.STT(obs[sl][:, g * 512:(g + 1) * 512], u[:, g * 512:(g + 1) * 512], st2[:, sl, g:g + 1], snwB[:, g * 512:(g + 1) * 512], ALU.mult, ALU.mult)

                            def b3_s4(i):
                                s, c = cha[i]
                                if not ny(s):
                                    return
                                gt = s["off"] + c * 128
                                sl = i % 2
                                for b in range(8):
                                    kb.TR(pX[:, b * 128:(b + 1) * 128], obs[sl][:, b * 128:(b + 1) * 128], identb, signal=(b == 7))
                                kb.CP("act", oT[sl][:].rearrange("p b t -> p (b t)"), pX[:])
                                kb.DMA("sp", OT[0:1024, gt:gt + 128].rearrange("(b p) t -> p b t", p=128), oT[sl][:], key=s["name"])
                            pipe(len(cha), [b3_s0, b3_s1, b3_s2, b3_s3, b3_s4])

                kb.mark("C%d" % l)
                with scope(kb) as ec:
                    ktb = SB(ec, "ktb", [64, 2, Ltot + 128], BF16)
                    vvb = SB(ec, "vvb", [128, NCH + 1, 130], BF16)
                    kb.CP("dve", ktb[:, :, Ltot:Ltot + 128], nbr1[0:64, 84:340].rearrange("p (g t) -> p g t", g=2))
                    kb.CP("dve", vvb[:, NCH, :], nbr1[:, 340:470])
                    for s in seqs:
                        o_, n_ = s["off"], s["n"]
                        kb.DMA("sp", ktb[:, :, o_:o_ + n_], KT[:, :, o_:o_ + n_], key=s["name"])
                        kb.DMA("sp", vvb[:, o_ // 128:(o_ + n_) // 128, :], VV[o_:o_ + n_, :].rearrange("(c p) x -> p c x", p=128), key=s["name"])
                    qtb = [SB(ec, "qtb%d" % i, [64, 8, 128], BF16) for i in range(3)]
                    gab = [SB(ec, "gab%d" % i, [64, 8, 128], BF16) for i in range(3)]
                    Ee = [SB(ec, "Ee%d" % i, [128, 512], BF16) for i in range(20)]
                    den = [SB(ec, "den%d" % i, [64, 512], F32) for i in range(2)]
                    oa = [SB(ec, "oa%d" % i, [64, 512], F32) for i in range(2)]
                    oaT = [SB(ec, "oaT%d" % i, [64, 512], BF16) for i in range(2)]
                    pS = [PS(ec, "pS%d" % i, [128, 512], F32) for i in range(3)]
                    pV = [PS(ec, "pV%d" % i, [64, 512], F32) for i in range(2)]
                    pU = [PS(ec, "pU%d" % i, [64, 512], F32) for i in range(2)]
                    qbs = []
                    for s in seqs:
                        if last and s["name"] == "c":
                            continue
                        for qb in range(s["n"] // 128):
                            qbs.append((s, qb))
                    cntC = dict(s=0, o=0)

                    def keyblocks(s, qb):
                        kbl = [(0, None), (1, None)]
                        if s["name"] == "l":
                            nb = s["n"] // 128
                            c0 = s["off"] // 128
                            if qb > 0:
                                kbl.append((c0 + qb - 1, 1))
                            kbl.append((c0 + qb, None))
                            if qb < nb - 1:
                                kbl.append((c0 + qb + 1, 0))
                            else:
                                kbl.append((NCH, 2))
                        return kbl

                    def c_s0(i):
                        s, qb = qbs[i]
                        gt = s["off"] + qb * 128
                        kb.DMA("sp", qtb[i % 3][:], QT[:, :, gt:gt + 128], key=s["name"])
                        kb.DMA("sp", gab[i % 3][:], GAT[:, gt:gt + 128].rearrange("(h d) t -> d h t", d=64), key=s["name"])

                    def c_s1(i):
                        s, qb = qbs[i]
                        q_ = qtb[i % 3]
                        for g in range(2):
                            for ki, (kc, md) in enumerate(keyblocks(s, qb)):
                                p_ = pS[cntC["s"] % 3]
                                cntC["s"] += 1
                                if md is not None:
                                    kb.MM(p_[:], identb, mneg[:, md, :, :].rearrange("p r t -> p (r t)"), start=True, stop=False, signal=False)
                                kb.MM(p_[:], ktb[:, g, kc * 128:(kc + 1) * 128], q_[:, g * 4:(g + 1) * 4, :].rearrange("p h t -> p (h t)"),
                                      start=(md is None), stop=True)
                                kb.ACT(Ee[(i % 2) * 10 + g * 5 + ki][:], p_[:], AF.Exp)

                    def c_s2(i):
                        s, qb = qbs[i]
                        gt = s["off"] + qb * 128
                        g_ = gab[i % 3]
                        kbl = keyblocks(s, qb)
                        for g in range(2):
                            no = cntC["o"]
                            cntC["o"] += 1
                            pv, pu = pV[no % 2], pU[no % 2]
                            for ki, (kc, md) in enumerate(kbl):
                                kb.MM(pv[:], vvb[:, kc, g * 65:g * 65 + 64], Ee[(i % 2) * 10 + g * 5 + ki][:], start=(ki == 0), stop=(ki == len(kbl) - 1))
                            for ki, (kc, md) in enumerate(kbl):
                                kb.MM(pu[:], onesb[:, 0:64], Ee[(i % 2) * 10 + g * 5 + ki][:], start=(ki == 0), stop=(ki == len(kbl) - 1))
                            dn = den[no % 2]
                            kb.TT("dve", dn[:].rearrange("p (r t) -> p r t", r=4), pu[:].rearrange("p (r t) -> p r t", r=4), bc(sinkB[0:64, g * 4:(g + 1) * 4], [64, 4, 128]), ALU.add)
                            kb.ACT(dn[:], dn[:], AF.Ln)
                            kb.ACT(dn[:], dn[:], AF.Exp, scale=-1.0)
                            o_ = oa[no % 2]
                            kb.TT("dve", o_[:], pv[:], dn[:], ALU.mult)
                            ot_ = oaT[no % 2]
                            kb.TT("pool", ot_[:], o_[:], g_[:, g * 4:(g + 1) * 4, :].rearrange("p h t -> p (h t)"), ALU.mult)
                            kb.DMA("sp", OT[1024 + g * 256:1024 + (g + 1) * 256, gt:gt + 128].rearrange("(r d) t -> d r t", d=64), ot_[:].rearrange("p (r t) -> p r t", r=4), key=s["name"])
                    pipe(len(qbs), [c_s0, c_s1, c_s2])

                kb.mark("D%d" % l)
                with scope(kb) as ed:
                    dgc = SB(ed, "dgc", [128, 124, 128], BF16)
                    for b in range(4):
                        for j in range(31):
                            if (b + j) % 2:
                                kb.TS("dve", dgc[:, b * 31 + j, :], identf, colsT[:, b, 6 + j:7 + j], None, ALU.mult)
                            else:
                                kb.ACT(dgc[:, b * 31 + j, :], identf, AF.Copy, scale=colsT[:, b, 6 + j:7 + j])
                    pww = SB(ed, "pww", [128, 4, 512], BF16)
                    pwst = SB(ed, "pwst", [128, 4, 512], F32)
                    kb.DMA("sp", pwst[:], conv_pw_w[l].rearrange("(k p) n -> p k n", p=128))
                    kb.CP("act", pww[:], pwst[:])
                    gw = [SB(ed, "gw%d" % i, [128, 4, 542], BF16) for i in range(2)]
                    gcb = [SB(ed, "gcb%d" % i, [128, 4, 512], BF16) for i in range(2)]
                    hbb = [SB(ed, "hbb%d" % i, [128, 4, 512], BF16) for i in range(2)]
                    hsq = [SB(ed, "hsq%d" % i, [128, 4, 512], BF16) for i in range(2)]
                    mean = SB(ed, "mean", [128, 512], F32)
                    rstd = SB(ed, "rstd", [128, 512], F32)
                    xcn = [SB(ed, "xcn%d" % i, [128, 512], F32) for i in range(2)]
                    h2 = [SB(ed, "h2%d" % i, [128, 4, 512], BF16) for i in range(2)]
                    oc = [SB(ed, "oc%d" % i, [128, 512], BF16) for i in range(2)]
                    pc = [PS(ed, "pc%d" % i, [128, 512], F32) for i in range(2)]
                    pm = PS(ed, "pm", [128, 512], F32)
                    pe2 = PS(ed, "pe2", [128, 512], F32)
                    pw_ = [PS(ed, "pw%d" % i, [128, 512], F32) for i in range(2)]
                    dst = [x for x in sts if not (last and x[0]["name"] == "c")]

                    def d_s0(i):
                        s, t0, TS_ = dst[i]
                        if i == len(dst) - 1:
                            kb.DMA("sp", gw[i % 2][:, :, 0:TS_ + 15], GLUT[:, s["goff"] + t0 - 15:s["goff"] + t0 + TS_].rearrange("(b p) w -> p b w", p=128), key=s["name"])
                            for r in range(15):
                                kb.CP("pool", gw[i % 2][:, :, TS_ + 15 + r:TS_ + 16 + r], ng[:, :, 14 - r:15 - r])
                        else:
                            kb.DMA("sp", gw[i % 2][:, :, 0:TS_ + 30], GLUT[:, s["goff"] + t0 - 15:s["goff"] + t0 + TS_ + 15].rearrange("(b p) w -> p b w", p=128), key=s["name"])

                    def d_s1(i):
                        s, t0, TS_ = dst[i]
                        w_ = gw[i % 2]
                        for b in range(4):
                            p_ = pc[b % 2]
                            for j in range(31):
                                kb.MM(p_[:, 0:TS_], dgc[:, b * 31 + j, :], w_[:, b, j:j + TS_], start=(j == 0), stop=(j == 30))
                            kb.ACT(hbb[i % 2][:, b, 0:TS_], p_[:, 0:TS_], AF.Identity, bias=colsT[:, b, 37:38])
                            kb.TT("dve", hsq[i % 2][:, b, 0:TS_], hbb[i % 2][:, b, 0:TS_], hbb[i % 2][:, b, 0:TS_], ALU.mult)

                    def d_s2(i):
                        s, t0, TS_ = dst[i]
                        g0 = s["off"] + t0
                        kb.DMA("sp", gcb[i % 2][:, :, 0:TS_], GCT[:, g0:g0 + TS_].rearrange("(b p) w -> p b w", p=128), key=s["name"])
                        hb_, hs_ = hbb[i % 2], hsq[i % 2]
                        for b in range(4):
                            kb.MM(pm[:, 0:TS_], ones512[:], hb_[:, b, 0:TS_], start=(b == 0), stop=(b == 3))
                        for b in range(4):
                            kb.MM(pe2[:, 0:TS_], ones512[:], hs_[:, b, 0:TS_], start=(b == 0), stop=(b == 3))
                        kb.CP("act", mean[:, 0:TS_], pm[:, 0:TS_])
                        kb.TT("dve", rstd[:, 0:TS_], mean[:, 0:TS_], mean[:, 0:TS_], ALU.mult)
                        kb.TT("dve", rstd[:, 0:TS_], pe2[:, 0:TS_], rstd[:, 0:TS_], ALU.subtract)
                        kb.ACT(rstd[:, 0:TS_], rstd[:, 0:TS_], AF.Ln, bias=epsb[:])
                        kb.ACT(rstd[:, 0:TS_], rstd[:, 0:TS_], AF.Exp, scale=-0.5)
                        for b in range(4):
                            x_ = xcn[b % 2]
                            kb.TT("dve", x_[:, 0:TS_], hb_[:, b, 0:TS_], mean[:, 0:TS_], ALU.subtract)
                            kb.TT("dve", x_[:, 0:TS_], x_[:, 0:TS_], rstd[:, 0:TS_], ALU.mult)
                            kb.ACT(h2[i % 2][:, b, 0:TS_], x_[:, 0:TS_], AF.Silu, bias=colsT[:, b, 39:40], scale=colsT[:, b, 38:39])

                    def d_s3(i):
                        s, t0, TS_ = dst[i]
                        g0 = s["off"] + t0
                        for co in range(4):
                            p_ = pw_[co % 2]
                            for ci in range(4):
                                kb.MM(p_[:, 0:TS_], pww[:, ci, co * 128:(co + 1) * 128], h2[i % 2][:, ci, 0:TS_], start=(ci == 0), stop=(ci == 3))
                            o_ = oc[co % 2]
                            kb.STT(o_[:, 0:TS_], p_[:, 0:TS_], colsT[:, co, 40:41], gcb[i % 2][:, co, 0:TS_], ALU.add, ALU.mult)
                            kb.DMA("sp", OT[1536 + co * 128:1536 + (co + 1) * 128, g0:g0 + TS_], o_[:, 0:TS_], key=s["name"])
                    pipe(len(dst), [d_s0, d_s1, d_s2, d_s3])

                kb.mark("E%d" % l)
                with scope(kb) as ee:
                    wo = SB(ee, "wo", [128, 16, D], BF16)
                    wov = w_out[l].rearrange("(k p) n -> p k n", p=128)
                    with scope(kb) as eow:
                        wos = [SB(eow, "wos%d" % i, [128, 4, D], F32) for i in range(2)]
                        for k4 in range(4):
                            kb.DMA("sp", wos[k4 % 2][:], wov[:, k4 * 4:(k4 + 1) * 4, :])
                            for j4 in range(4):
                                kb.CP(["dve", "pool", "act", "pool"][j4], wo[:, k4 * 4 + j4, :], wos[k4 % 2][:, j4, :])
                    fnw = SB(ee, "fnw", [128, D], F32)
                    kb.DMA("sp", fnw[:], final_norm_w[0, :].partition_broadcast(128))
                    otb = [SB(ee, "otb%d" % i, [128, 16, 128], BF16) for i in range(3)]
                    xo = [SB(ee, "xo%d" % i, [128, D], F32) for i in range(3)]
                    xn = [SB(ee, "xn%d" % i, [128, D], F32) for i in range(2)]
                    yo = [SB(ee, "yo%d" % i, [128, D], F32) for i in range(2)]
                    jk = SB(ee, "jk", [128, D], BF16)
                    s1 = SB(ee, "s1", [128, 2], F32)
                    po = [PS(ee, "po%d" % i, [128, 1024], F32) for i in range(2)]
                    che = []
                    for s in seqs:
                        if last and s["name"] == "c":
                            continue
                        for c in range(s["n"] // 128):
                            che.append((s, c))

                    def e_s0(i):
                        s, c = che[i]
                        gt = s["off"] + c * 128
                        kb.DMA("sp", otb[i % 3][:], OT[:, gt:gt + 128].rearrange("(k p) t -> p k t", p=128), key=s["name"])
                        if l == 0:
                            src = (ctx_in if s["name"] == "c" else x_in)[c * 128:(c + 1) * 128, :]
                        else:
                            src = XR[gt:gt + 128, :]
                        kb.DMA("sp", xo[i % 3][:], src, key=(s["name"], c * 128))

                    def e_nop(i):
                        pass

                    def e_s1(i):
                        s, c = che[i]
                        m = 0 if s["name"] == "c" else 1
                        sl = i % 2
                        p_ = po[sl]
                        for nb_ in range(2):
                            for kt in range(16):
                                kb.MM(p_[:, nb_ * 512:(nb_ + 1) * 512], otb[i % 3][:, kt, :], wo[:, kt, nb_ * 512:(nb_ + 1) * 512], start=(kt == 0), stop=(kt == 15))
                        kb.TT("dve", xn[sl][:], p_[:], modB[:, m, 2, :], ALU.mult)
                        kb.TT("pool", xn[sl][:], xn[sl][:], xo[i % 3][:], ALU.add)

                    def e_s2(i):
                        s, c = che[i]
                        gt = s["off"] + c * 128
                        sl = i % 2
                        if not last:
                            kb.DMA("sp", XR[gt:gt + 128, :], xn[sl][:], key=(s["name"], c * 128))
                        else:
                            sc_ = s1[:, sl:sl + 1]
                            kb.STT(yo[sl][:], xn[sl][:], 1.0, xn[sl][:], ALU.mult, ALU.mult, accum=sc_)
                            kb.TS("dve", sc_, sc_, 1.0 / D, EPS, ALU.mult, ALU.add)
                            kb.ACT(sc_, sc_, AF.Ln)
                            kb.ACT(sc_, sc_, AF.Exp, scale=-0.5)
                            kb.STT(yo[sl][:], xn[sl][:], sc_, fnw[:], ALU.mult, ALU.mult)
                            kb.DMA("sp", out[c * 128:(c + 1) * 128, :], yo[sl][:], key="out")
                    pipe(len(che), [e_s0, e_nop, e_s1, e_s2])
        kb.mark("end")
        kb.finish()
    return nc, kb


def host_consts(L, T, pos, grid_w=64):
    k = np.arange(128)[:, None]
    t = np.arange(128)[None, :]
    cst = np.zeros((128, 7, 128), np.float32)
    cst[:, 0] = np.eye(128)
    cst[:, 1] = (k <= t)
    cst[:, 2] = (k >= t)
    cst[:, 3] = 1.0
    cst[:, 4] = np.where(k <= t, 0.0, NEG)
    cst[:, 5] = np.where(k >= t, 0.0, NEG)
    cst[:, 6] = np.where(k + t >= 127, 0.0, NEG)
    rope = np.zeros((T + L, 256), np.float32)
    rope[:, 0:64] = 1.0
    rope[:, 128:192] = 0.125
    row = (pos // grid_w).astype(np.float32)
    col = (pos % grid_w).astype(np.float32)
    inv = (10000.0 ** (-np.arange(0, 32, 2, dtype=np.float32) / 32.0)).astype(np.float32)
    ang = np.stack([row[:, None] * inv, col[:, None] * inv], axis=1).astype(np.float32)
    cs_, sn_ = np.cos(ang), np.sin(ang)
    cos2 = np.repeat(cs_[:, :, None, :], 2, axis=2).reshape(L, 64)
    rope[T:, 0:64] = cos2
    rope[T:, 64:96] = (-sn_).reshape(L, 32)
    rope[T:, 96:128] = sn_.reshape(L, 32)
    rope[T:, 128:192] = cos2 * 0.125
    rope[T:, 192:224] = (-sn_).reshape(L, 32) * 0.125
    rope[T:, 224:256] = sn_.reshape(L, 32) * 0.125
    return cst, rope


_CACHE = {}


def run(inputs, L, T, NL, batches):
    npairs = len(batches)
    key = (L, T, NL, npairs)
    if key not in _CACHE:
        _CACHE[key] = build(L, T, NL, npairs)[0]
    nc = _CACHE[key]
    f = lambda a: np.ascontiguousarray(np.asarray(a, dtype=np.float32))
    base = {
        "c_ctx": f(inputs["c_ctx"]).reshape(128, 8),
        "w_mod": f(inputs["w_mod"]), "b_mod": f(inputs["b_mod"]), "norm_w": f(inputs["norm_w"]),
        "ssd_conv_b": f(inputs["ssd_conv_b"]),
        "ssd_d": f(inputs["ssd_d"]), "ssd_norm_w": f(inputs["ssd_norm_w"]), "attn_sink": f(inputs["attn_sink"]),
        "conv_dw_b": f(inputs["conv_dw_b"]), "conv_ln_w": f(inputs["conv_ln_w"]),
        "conv_ln_b": f(inputs["conv_ln_b"]), "conv_pw_w": f(inputs["conv_pw_w"]), "conv_pw_b": f(inputs["conv_pw_b"]),
        "w_out": f(inputs["w_out"]), "final_norm_w": f(inputs["final_norm_w"]).reshape(1, D),
    }
    w_in = f(inputs["w_in"])
    w_in_m = w_in.copy()
    w_in_m[:, :, 2560:2576] = w_in[:, :, 2576:2592]
    w_in_m[:, :, 2576:2592] = w_in[:, :, 2560:2576]
    dtb = f(inputs["ssd_dt_bias"])
    alog = f(inputs["ssd_a_log"])
    scw = f(inputs["ssd_conv_w"])
    dww = f(inputs["conv_dw_w"])
    half = []
    for h in range(2):
        pos = np.arange(L) if h == 0 else (2 * L - 1 - np.arange(L))
        cst, rope = host_consts(L, T, pos)
        sel = np.zeros((128, 2), np.float32)
        sel[:, 1 - h] = 1.0
        d = dict(base)
        d.update({
            "cst": cst, "rope": rope, "sel": sel,
            "w_in": w_in if h == 0 else w_in_m,
            "ssd_dt_bias": f(dtb if h == 0 else dtb[:, ::-1]).reshape(NL, 32),
            "ssd_a_log": f(alog if h == 0 else alog[:, ::-1]).reshape(NL, 32),
            "ssd_conv_w": f(scw if h == 0 else scw[:, ::-1]),
            "conv_dw_w": f(dww if h == 0 else dww[:, ::-1]),
        })
        half.append(d)
    x = f(inputs["x"])
    c = f(inputs["c"])
    ctx = f(inputs["ctx"])
    in_maps = []
    for b in batches:
        for h in range(2):
            m = dict(half[h])
            if h == 0:
                m["x"] = f(x[b, 0:L])
                m["ctx"] = f(ctx[b])
            else:
                m["x"] = f(x[b, L:2 * L][::-1])
                m["ctx"] = f(ctx[b][::-1])
            m["c"] = c[b].reshape(128, 8)
            in_maps.append(m)
    res = run_bass_kernel_spmd(nc, in_maps, core_ids=list(range(2 * npairs)))
    outs = []
    for i in range(npairs):
        o0 = res.results[2 * i]["out"]
        o1 = res.results[2 * i + 1]["out"][::-1]
        outs.append(np.concatenate([o0, o1], axis=0))
    return outs


def kernel(**inputs):
    B = np.asarray(inputs["x"]).shape[0]
    outs = run(inputs, 2048, 256, 2, list(range(B)))
    return np.stack(outs, axis=0).astype(np.float32)
```

```python
import numpy as np
from contextlib import ExitStack, contextmanager
import concourse.bass as bass
import concourse.mybir as mybir
from concourse.bass_utils import run_bass_kernel_spmd

F32 = mybir.dt.float32
BF16 = mybir.dt.bfloat16
AF = mybir.ActivationFunctionType
ALU = mybir.AluOpType
NDMA = 48
EPS = 1e-6
D = 1024
NEG = -30000.0


class Tok:
    __slots__ = ("sem", "val", "key")

    def __init__(self, sem, val, key):
        self.sem, self.val, self.key = sem, val, key


class Buf:
    __slots__ = ("w", "r", "multi")

    def __init__(self, multi=False):
        self.w = {}
        self.r = {}
        self.multi = multi


class KB:
    def __init__(self, nc, es):
        self.nc = nc
        self.eng = {"pe": nc.tensor, "act": nc.scalar, "dve": nc.vector, "pool": nc.gpsimd, "sp": nc.sync}
        self.sem = {e: es.enter_context(nc.semaphore("s_" + e)) for e in self.eng}
        self.cnt = {e: 0 for e in self.eng}
        self.known = {e: {} for e in self.eng}
        self.dsem = [es.enter_context(nc.semaphore("d%d" % i)) for i in range(NDMA)]
        self.dval = [0] * NDMA
        self.dnext = 0
        self.pend = {e: [] for e in self.eng}
        self.bufs = {}
        self.ccsem = es.enter_context(nc.semaphore("ccsem"))
        self.ccval = 0
        self.ninst = 0
        self.nmm = 0
        self.marks = []

    def buf(self, ap, key=None):
        t = ap.tensor
        name = t.name
        dram = str(ap.space).lower().find("dram") >= 0 or str(ap.space).lower().find("hbm") >= 0
        k = (name, key if dram else None)
        b = self.bufs.get(k)
        if b is None:
            b = Buf(multi=dram)
            self.bufs[k] = b
        return b

    def _wait(self, e, tok):
        if e == "pe" and tok.key == "pe":
            return
        k = self.known[e]
        if k.get(tok.key, 0) >= tok.val:
            return
        self.eng[e].wait_ge(tok.sem, tok.val)
        k[tok.key] = tok.val

    def _deps(self, e, reads, writes):
        for b in reads:
            for t in b.w.values():
                self._wait(e, t)
        for b in writes:
            if not b.multi:
                for t in b.w.values():
                    self._wait(e, t)
            for t in b.r.values():
                self._wait(e, t)

    def _record(self, tok, reads, writes):
        for b in reads:
            b.r[tok.key] = tok
        for b in writes:
            if b.multi:
                b.w[tok.key] = tok
            else:
                b.w = {tok.key: tok}
                b.r = {}

    def op(self, e, fn, rd, wr, signal=True, key=None):
        reads = [self.buf(a, key) for a in rd if a is not None and not isinstance(a, (int, float))]
        writes = [self.buf(a, key) for a in wr if a is not None]
        self._deps(e, reads, writes)
        ins = fn(self.eng[e])
        self.ninst += 1
        if not signal:
            self.pend[e].append((reads, writes))
            return
        self.cnt[e] += 1
        ins.then_inc(self.sem[e], 1)
        tok = Tok(self.sem[e], self.cnt[e], e)
        for (r, w) in self.pend[e]:
            self._record(tok, r, w)
        self.pend[e] = []
        self._record(tok, reads, writes)

    def DMA(self, q, out, in_, key=None, **kw):
        reads = [self.buf(in_, key)]
        writes = [self.buf(out, key)]
        self._deps(q, reads, writes)
        i = self.dnext
        self.dnext = (i + 1) % NDMA
        k = "d%d" % i
        if self.dval[i] > 0:
            self._wait(q, Tok(self.dsem[i], self.dval[i], k))
        ins = self.eng[q].dma_start(out=out, in_=in_, **kw)
        self.ninst += 1
        self.dval[i] += 16
        ins.then_inc(self.dsem[i], 16)
        tok = Tok(self.dsem[i], self.dval[i], k)
        self._record(tok, reads, writes)

    def CC(self, in_ap, out_ap, groups):
        reads = [self.buf(in_ap)]
        writes = [self.buf(out_ap)]
        self._deps("pool", reads, writes)
        ins = self.eng["pool"].collective_compute("AllGather", ALU.bypass, replica_groups=groups, ins=[in_ap], outs=[out_ap])
        self.ninst += 1
        self.ccval += 1
        ins.then_inc(self.ccsem)
        tok = Tok(self.ccsem, self.ccval, "cc")
        self._record(tok, reads, writes)

    def barrier(self):
        assert all(len(p) == 0 for p in self.pend.values())
        toks = [Tok(self.sem[e], self.cnt[e], e) for e in self.eng if self.cnt[e] > 0]
        if self.ccval:
            toks.append(Tok(self.ccsem, self.ccval, "cc"))
        toks += [Tok(self.dsem[i], self.dval[i], "d%d" % i) for i in range(NDMA) if self.dval[i]]
        for e in self.eng:
            for t in toks:
                if t.key != e:
                    self._wait(e, t)

    def finish(self):
        for i in range(NDMA):
            if self.dval[i]:
                self._wait("sp", Tok(self.dsem[i], self.dval[i], "d%d" % i))

    def ACT(self, out, in_, func, bias=None, scale=None, accum=None):
        kw = {}
        if bias is not None:
            kw["bias"] = bias
        if scale is not None:
            kw["scale"] = scale
        if accum is not None:
            kw["accum_out"] = accum
        self.op("act", lambda e: e.activation(out=out, in_=in_, func=func, **kw), [in_, bias, scale], [out, accum])

    def TT(self, e, out, in0, in1, op):
        self.op(e, lambda g: g.tensor_tensor(out=out, in0=in0, in1=in1, op=op), [in0, in1], [out])

    def TS(self, e, out, in0, s1, s2, op0, op1=None):
        if op1 is None:
            self.op(e, lambda g: g.tensor_scalar(out=out, in0=in0, scalar1=s1, scalar2=None, op0=op0), [in0, s1], [out])
        else:
            self.op(e, lambda g: g.tensor_scalar(out=out, in0=in0, scalar1=s1, scalar2=s2, op0=op0, op1=op1), [in0, s1, s2], [out])

    def STT(self, out, in0, scalar, in1, op0, op1, accum=None):
        if accum is None:
            self.op("dve", lambda g: g.scalar_tensor_tensor(out=out, in0=in0, scalar=scalar, in1=in1, op0=op0, op1=op1), [in0, scalar, in1], [out])
        else:
            self.op("dve", lambda g: g.scalar_tensor_tensor(out=out, in0=in0, scalar=scalar, in1=in1, op0=op0, op1=op1, accum_out=accum),
                    [in0, scalar, in1], [out, accum])

    def CP(self, e, out, in_):
        if e == "act":
            self.ACT(out, in_, AF.Copy)
        else:
            self.op(e, lambda g: g.tensor_copy(out=out, in_=in_), [in_], [out])

    def RECIP(self, out, in_):
        self.op("dve", lambda g: g.reciprocal(out=out, in_=in_), [in_], [out])

    def MSET(self, e, out, val):
        self.op(e, lambda g: g.memset(out, val), [], [out])

    def mark(self, label):
        self.marks.append((label, self.nmm))

    def MM(self, out, lhsT, rhs, start, stop, signal=None):
        self.nmm += 1
        self.op("pe", lambda g: g.matmul(out, lhsT, rhs, start=start, stop=stop), [lhsT, rhs], [out],
                signal=(stop if signal is None else signal))

    def TR(self, out, in_, ident, signal=True):
        self.nmm += 1
        self.op("pe", lambda g: g.transpose(out=out, in_=in_, identity=ident), [in_, ident], [out], signal=signal)


@contextmanager
def scope(kb):
    with ExitStack() as es:
        yield es
        kb.barrier()


def pipe(n, stages):
    K = len(stages)
    for t in range(n + K - 1):
        for k in range(K):
            i = t - k
            if 0 <= i < n:
                stages[k](i)


def bc(ap, shape):
    return ap.unsqueeze(2).to_broadcast(shape)


def build(L, T, NL=2, npairs=4):
    nc = bass.Bass("TRN2", target_bir_lowering=False)
    Ltot = T + L
    NCH = Ltot // 128
    seqs = [dict(name="c", off=0, n=T, xoff=2, goff=15), dict(name="l", off=T, n=L, xoff=T + 6, goff=T + 45)]
    Wx = Ltot + 8
    Wg = Ltot + 60

    def din(name, shape, dt=F32):
        return nc.dram_tensor(name, list(shape), dt, kind="ExternalInput").ap()

    def dsc(name, shape, dt):
        return nc.dram_tensor(name, list(shape), dt, kind="Internal").ap()

    x_in = din("x", [L, D])
    ctx_in = din("ctx", [T, D])
    c_in = din("c", [128, 8])
    cc_in = din("c_ctx", [128, 8])
    w_mod = din("w_mod", [NL, D, 3 * D])
    b_mod = din("b_mod", [NL, 3 * D])
    norm_w = din("norm_w", [NL, D])
    w_in = din("w_in", [NL, D, 5408])
    ssd_conv_w = din("ssd_conv_w", [NL, 5, 1536])
    ssd_conv_b = din("ssd_conv_b", [NL, 1536])
    ssd_dt_bias = din("ssd_dt_bias", [NL, 32])
    ssd_a_log = din("ssd_a_log", [NL, 32])
    ssd_d = din("ssd_d", [NL, 16])
    ssd_norm_w = din("ssd_norm_w", [NL, 1024])
    attn_sink = din("attn_sink", [NL, 8])
    conv_dw_w = din("conv_dw_w", [NL, 31, 512])
    conv_dw_b = din("conv_dw_b", [NL, 512])
    conv_ln_w = din("conv_ln_w", [NL, 512])
    conv_ln_b = din("conv_ln_b", [NL, 512])
    conv_pw_w = din("conv_pw_w", [NL, 512, 512])
    conv_pw_b = din("conv_pw_b", [NL, 512])
    w_out = din("w_out", [NL, 2048, D])
    final_norm_w = din("final_norm_w", [1, D])
    cst = din("cst", [128, 7, 128])
    sel_in = din("sel", [128, 2])
    groups = [[2 * i, 2 * i + 1] for i in range(npairs)]
    rope = din("rope", [Ltot, 256])
    out = nc.dram_tensor("out", [L, D], F32, kind="ExternalOutput").ap()

    XR = dsc("XR", [Ltot, D], F32)
    XBCT = dsc("XBCT", [1536, Wx], BF16)
    GLUT = dsc("GLUT", [512, Wg], BF16)
    GCT = dsc("GCT", [512, Ltot], BF16)
    ZS = dsc("ZS", [Ltot, 1024], BF16)
    GAT = dsc("GAT", [512, Ltot], BF16)
    DTS = dsc("DTS", [Ltot, 32], F32)
    DTT = dsc("DTT", [32, Ltot], F32)
    QT = dsc("QT", [64, 8, Ltot], BF16)
    KT = dsc("KT", [64, 2, Ltot], BF16)
    VV = dsc("VV", [Ltot, 130], BF16)
    BCT = dsc("BCT", [128, 4, Ltot], BF16)
    XS = dsc("XS", [Ltot, 1024], BF16)
    BTM = dsc("BTM", [Ltot, 256], BF16)
    ABR = dsc("ABR", [2, 2, 6, NCH, 2048], BF16)
    SBD = dsc("SBD", [NCH, 128, 1024], BF16)
    OT = dsc("OT", [2048, Ltot], BF16)
    MOD = dsc("MOD", [NL, 2, 3 * D], F32)
    SFD = dsc("SFD", [NCH, 128, 1024], BF16)
    SND1 = dsc("SND1", [128, 470], BF16)
    RCV1 = nc.dram_tensor("RCV1", [256, 470], BF16, kind="Internal", addr_space="Local").ap()
    SND2 = dsc("SND2", [128, 1024], F32)
    RCV2 = nc.dram_tensor("RCV2", [256, 1024], F32, kind="Internal", addr_space="Local").ap()

    with ExitStack() as es0:
        kb = KB(nc, es0)

        uid = [0]

        def SB(es, name, shape, dt):
            uid[0] += 1
            return es.enter_context(nc.sbuf_tensor("%s_%d" % (name, uid[0]), list(shape), dt))

        def PS(es, name, shape, dt):
            uid[0] += 1
            return es.enter_context(nc.psum_tensor("%s_%d" % (name, uid[0]), list(shape), dt))

        cf = SB(es0, "cf", [128, 7, 128], F32)
        selB = SB(es0, "selB", [128, 2], F32)
        kb.DMA("sp", selB[:], sel_in[:, :])
        kb.DMA("sp", cf[:], cst[:, :, :])
        identf = cf[:, 0, :]
        triU = cf[:, 1, :]
        triL = cf[:, 2, :]
        onesf = cf[:, 3, :]
        cb_ = SB(es0, "cb", [128, 7, 128], BF16)
        kb.CP("dve", cb_[:], cf[:])
        identb = cb_[:, 0, :]
        onesb = cb_[:, 3, :]
        mneg = SB(es0, "mneg", [128, 3, 4, 128], BF16)
        for d_ in range(3):
            for r_ in range(4):
                kb.CP("dve", mneg[:, d_, r_, :], cf[:, 4 + d_, :])
        ones512 = SB(es0, "ones512", [128, 128], BF16)
        kb.MSET("dve", ones512[:], 1.0 / 512.0)
        zpad = SB(es0, "zpad", [128, 64], BF16)
        kb.MSET("dve", zpad[:], 0.0)
        for s in seqs:
            for side in range(2):
                cx = s["xoff"] - 2 if side == 0 else s["xoff"] + s["n"]
                kb.DMA("sp", XBCT[:, cx:cx + 2].rearrange("(b p) w -> p b w", p=128), zpad[:, 0:24].rearrange("p (b w) -> p b w", w=2), key=s["name"])
                cg = s["goff"] - 15 if side == 0 else s["goff"] + s["n"]
                kb.DMA("sp", GLUT[:, cg:cg + 15].rearrange("(b p) w -> p b w", p=128), zpad[:, 0:60].rearrange("p (b w) -> p b w", w=15), key=s["name"])
        onesrow = SB(es0, "onesrow", [128, 2048], BF16)
        kb.MSET("pool", onesrow[:], 1.0)
        for s in seqs:
            c0_, n0_ = s["off"] // 128, s["n"] // 128
            for d_ in range(2):
                for r_ in range(3):
                    kb.DMA("sp", ABR[d_, 0, 3 + r_, c0_:c0_ + n0_, :], onesrow[0:n0_, :], key=s["name"])
                    kb.DMA("sp", ABR[d_, 1, r_, c0_:c0_ + n0_, :], onesrow[0:n0_, :], key=s["name"])
        epsb = SB(es0, "epsb", [128, 1], F32)
        kb.MSET("pool", epsb[:], EPS)

        for l in range(NL):
            last = (l == NL - 1)
            with scope(kb) as esl:
                kb.mark("setup%d" % l)
                modB = SB(esl, "modB", [128, 2, 3, D], F32)
                aB = SB(esl, "aB", [128, 32], F32)
                dtbB = SB(esl, "dtbB", [128, 32], F32)
                snwB = SB(esl, "snwB", [128, 1024], F32)
                sinkB = SB(esl, "sinkB", [128, 8], F32)
                colsT = SB(esl, "colsT", [128, 12, 41], F32)
                dI = SB(esl, "dI", [128, 16, 128], BF16)
                nbr1 = SB(esl, "nbr1", [128, 470], BF16)
                nx = nbr1[:, 0:24].rearrange("p (b w) -> p b w", w=2)
                ng = nbr1[:, 24:84].rearrange("p (b w) -> p b w", w=15)
                with scope(kb) as ess:
                    pss = PS(ess, "pss", [128, 512], F32)
                    csil = SB(ess, "csil", [128, 2, 8], F32)
                    kb.DMA("sp", csil[:, 0, :], cc_in[:, :])
                    kb.DMA("sp", csil[:, 1, :], c_in[:, :])
                    kb.ACT(csil[:], csil[:], AF.Silu)
                    bm = SB(ess, "bm", [1, 3 * D], F32)
                    kb.DMA("sp", bm[:], b_mod[l:l + 1, :])
                    modrow = SB(ess, "modrow", [1, 2, 3 * D], F32)
                    wm = [SB(ess, "wm%d" % i, [128, 8, 512], F32) for i in range(2)]
                    for cbk in range(6):
                        w_ = wm[cbk % 2]
                        kb.DMA("sp", w_[:], w_mod[l, :, cbk * 512:(cbk + 1) * 512].rearrange("(p k) n -> p k n", k=8))
                        for m in range(2):
                            for kt in range(8):
                                kb.MM(pss[0:1, :], csil[:, m, kt:kt + 1], w_[:, kt, :], start=(kt == 0), stop=(kt == 7))
                            kb.TT("dve", modrow[:, m, cbk * 512:(cbk + 1) * 512], pss[0:1, :], bm[:, cbk * 512:(cbk + 1) * 512], ALU.add)
                    kb.DMA("sp", MOD[l:l + 1, :, :], modrow[:])
                    nwB = SB(ess, "nwB", [128, D], F32)
                    kb.DMA("sp", nwB[:], norm_w[l, :].partition_broadcast(128))
                    for m in range(2):
                        kb.DMA("sp", modB[:, m, :, :], MOD[l, m, :].rearrange("(t d) -> t d", t=3).partition_broadcast(128))
                        tmpw = SB(ess, "tmpw%d" % m, [128, D], F32)
                        kb.STT(tmpw[:], modB[:, m, 1, :], 1.0, nwB[:], ALU.add, ALU.mult)
                        kb.CP("dve", modB[:, m, 1, :], modB[:, m, 0, :])
                        kb.CP("dve", modB[:, m, 0, :], tmpw[:])
                    kb.DMA("sp", aB[:], ssd_a_log[l, :].partition_broadcast(128))
                    kb.ACT(aB[:], aB[:], AF.Exp)
                    kb.TS("dve", aB[:], aB[:], -1.0, None, ALU.mult)
                    kb.DMA("sp", dtbB[:], ssd_dt_bias[l, :].partition_broadcast(128))
                    kb.DMA("sp", snwB[:], ssd_norm_w[l, :].partition_broadcast(128))
                    kb.DMA("sp", sinkB[:], attn_sink[l, :].partition_broadcast(128))
                    kb.ACT(sinkB[:], sinkB[:], AF.Exp)
                    dB = SB(ess, "dB", [128, 16], F32)
                    kb.DMA("sp", dB[:], ssd_d[l, :].partition_broadcast(128))
                    for h in range(16):
                        kb.TS("dve", dI[:, h, :], identf, dB[:, h:h + 1], None, ALU.mult)
                    rows = SB(ess, "rows", [41, 1536], F32)
                    kb.MSET("dve", rows[:], 0.0)
                    kb.DMA("sp", rows[0:5, :], ssd_conv_w[l, :, :])
                    kb.DMA("sp", rows[5:6, :], ssd_conv_b[l:l + 1, :])
                    kb.DMA("sp", rows[6:37, 0:512], conv_dw_w[l, :, :])
                    kb.DMA("sp", rows[37:38, 0:512], conv_dw_b[l:l + 1, :])
                    kb.DMA("sp", rows[38:39, 0:512], conv_ln_w[l:l + 1, :])
                    kb.DMA("sp", rows[39:40, 0:512], conv_ln_b[l:l + 1, :])
                    kb.DMA("sp", rows[40:41, 0:512], conv_pw_b[l:l + 1, :])
                    for b in range(12):
                        kb.TR(pss[:, b * 41:(b + 1) * 41], rows[:, b * 128:(b + 1) * 128], identf[0:41, 0:41], signal=(b == 11))
                    kb.CP("dve", colsT[:].rearrange("p b r -> p (b r)"), pss[:, 0:492])

                kb.mark("A%d" % l)
                sts = []
                for s in seqs:
                    TS_ = 512 if s["name"] == "l" else 256
                    for t0 in range(0, s["n"], TS_):
                        sts.append((s, t0, TS_))
                with scope(kb) as esa:
                    wfm = SB(esa, "wfm", [128, 8, 3584], BF16)
                    wtm = SB(esa, "wtm", [128, 8, 1824], BF16)
                    wv = w_in[l].rearrange("(k p) n -> p k n", p=128)
                    with scope(kb) as esw:
                        wst = [SB(esw, "wst%d" % i, [128, 5408], F32) for i in range(2)]
                        ce = ["dve", "pool", "act"]
                        nce = 0
                        for kt in range(8):
                            w_ = wst[kt % 2]
                            kb.DMA("sp", w_[:, 0:2704], wv[:, kt, 0:2704])
                            kb.DMA("sp", w_[:, 2704:5408], wv[:, kt, 2704:5408])
                            segs = [(wfm, 0, 0, 768), (wfm, 768, 768, 768), (wfm, 2560, 4896, 512),
                                    (wtm, 0, 1536, 512), (wtm, 512, 2048, 512), (wtm, 1024, 2592, 512), (wtm, 1536, 3104, 256),
                                    (wtm, 1792, 2560, 32), (wfm, 3072, 3360, 512)]
                            for i in range(4):
                                segs.append((wfm, 1536 + 256 * i, 3872 + 128 * i, 128))
                                segs.append((wfm, 1536 + 256 * i + 128, 4384 + 128 * i, 128))
                            for (dstt, d0, s0, n_) in segs:
                                kb.CP(ce[nce % 3], dstt[:, kt, d0:d0 + n_], w_[:, s0:s0 + n_])
                                nce += 1
                    xin = [SB(esa, "xin%d" % i, [128, D], F32) for i in range(3)]
                    hb = [SB(esa, "hb%d" % i, [128, D], BF16) for i in range(2)]
                    junk = SB(esa, "junk", [128, D], BF16)
                    tmpf = SB(esa, "tmpf", [128, D], F32)
                    hT = [SB(esa, "hT%d" % i, [128, 8, 512], BF16) for i in range(2)]
                    st4 = SB(esa, "st4", [128, 8], F32)
                    fst = [SB(esa, "fst%d" % i, [128, 512], BF16) for i in range(3)]
                    sig = SB(esa, "sig", [128, 512], F32)
                    zs = [SB(esa, "zs%d" % i, [128, 1024], BF16) for i in range(2)]
                    qk = [SB(esa, "qk%d" % i, [128, 640], BF16) for i in range(2)]
                    vst = [SB(esa, "vst%d" % i, [128, 2, 65], BF16) for i in range(2)]
                    for i in range(2):
                        kb.MSET("pool", vst[i][:], 1.0)
                    qkT = [SB(esa, "qkT%d" % i, [64, 10, 128], BF16) for i in range(2)]
                    dtw = [SB(esa, "dtw%d" % i, [128, 6, 32], F32) for i in range(2)]
                    dtT = [SB(esa, "dtT%d" % i, [32, 128], F32) for i in range(2)]
                    rp = [SB(esa, "rp%d" % i, [128, 256], F32) for i in range(2)]
                    rt = SB(esa, "rt", [128, 2, 640], F32)
                    pT = [PS(esa, "pT%d" % i, [128, 1024], BF16) for i in range(2)]
                    ppool = [PS(esa, "pp%d" % i, [128, 512], F32) for i in range(6)]
                    subs = [(si, j) for si, (s, t0, TS_) in enumerate(sts) for j in range(TS_ // 128)]
                    cntA = dict(fm=0, tm=0, sc=0)

                    def a_load(k):
                        si, j = subs[k]
                        s, t0, TS_ = sts[si]
                        if l == 0:
                            src = (ctx_in if s["name"] == "c" else x_in)[t0 + j * 128:t0 + (j + 1) * 128, :]
                        else:
                            src = XR[s["off"] + t0 + j * 128:s["off"] + t0 + (j + 1) * 128, :]
                        kb.DMA("sp", xin[k % 3][:], src, key=(s["name"], t0 + j * 128))

                    def a_norm(k):
                        si, j = subs[k]
                        s, t0, TS_ = sts[si]
                        m = 0 if s["name"] == "c" else 1
                        xi, hbj = xin[k % 3], hb[k % 2]
                        col = st4[:, k % 8:k % 8 + 1]
                        kb.ACT(junk[:], xi[:], AF.Square, accum=col)
                        kb.TS("dve", col, col, 1.0 / D, EPS, ALU.mult, ALU.add)
                        kb.ACT(col, col, AF.Sqrt)
                        kb.RECIP(col, col)
                        kb.STT(tmpf[:], xi[:], col, modB[:, m, 0, :], ALU.mult, ALU.mult)
                        kb.TT("pool", hbj[:], tmpf[:], modB[:, m, 1, :], ALU.add)

                    smallq = []

                    def drain(n):
                        while n > 0 and smallq:
                            smallq.pop(0)()
                            n -= 1

                    def a_transp(k, defer=True):
                        si, j = subs[k]
                        hTt, p_, hbj = hT[si % 2], pT[k % 2], hb[k % 2]

                        def piece(q4):
                            for kt in range(2 * q4, 2 * q4 + 2):
                                kb.TR(p_[:, kt * 128:(kt + 1) * 128], hbj[:, kt * 128:(kt + 1) * 128], identb, signal=(kt == 7))
                            if q4 == 3:
                                kb.CP("act" if k % 2 else "dve", hTt[:, :, j * 128:(j + 1) * 128], p_[:].rearrange("p (k t) -> p k t", k=8))
                        for q4 in range(4):
                            if defer:
                                smallq.append(lambda q4=q4: piece(q4))
                            else:
                                piece(q4)

                    def a_fmblock(si, cbk):
                        s, t0, TS_ = sts[si]
                        hTt = hT[si % 2]
                        p_ = ppool[cntA["tm"] % 6]
                        cntA["tm"] += 1
                        for kt in range(8):
                            kb.MM(p_[:, 0:TS_], wfm[:, kt, cbk * 128:(cbk + 1) * 128], hTt[:, kt, 0:TS_], start=(kt == 0), stop=(kt == 7))
                        drain(1)
                        return p_

                    def a_fmunit(si, u):
                        s, t0, TS_ = sts[si]
                        g0 = s["off"] + t0
                        if u < 12:
                            p_ = a_fmblock(si, u)
                            f_ = fst[u % 3]
                            kb.CP("dve", f_[:, 0:TS_], p_[:, 0:TS_])
                            kb.DMA("sp", XBCT[u * 128:(u + 1) * 128, s["xoff"] + t0:s["xoff"] + t0 + TS_], f_[:, 0:TS_], key=s["name"])
                        elif u < 16:
                            i = u - 12
                            pv = a_fmblock(si, 12 + 2 * i)
                            pg = a_fmblock(si, 13 + 2 * i)
                            kb.ACT(sig[:, 0:TS_], pg[:, 0:TS_], AF.Sigmoid)
                            f_ = fst[i % 3]
                            kb.TT("dve", f_[:, 0:TS_], pv[:, 0:TS_], sig[:, 0:TS_], ALU.mult)
                            kb.DMA("sp", GLUT[i * 128:(i + 1) * 128, s["goff"] + t0:s["goff"] + t0 + TS_], f_[:, 0:TS_], key=s["name"])
                        else:
                            i = u - 16
                            p_ = a_fmblock(si, 20 + i)
                            f_ = fst[i % 3]
                            kb.ACT(f_[:, 0:TS_], p_[:, 0:TS_], AF.Silu)
                            dstT = GCT if i < 4 else GAT
                            kb.DMA("sp", dstT[(i % 4) * 128:(i % 4 + 1) * 128, g0:g0 + TS_], f_[:, 0:TS_], key=s["name"])

                    def a_tmsub(si, j):
                        s, t0, TS_ = sts[si]
                        hTt = hT[si % 2]
                        gt = s["off"] + t0 + j * 128
                        sl = cntA["sc"] % 2
                        cntA["sc"] += 1

                        def tm_block(c0, c1):
                            p_ = ppool[cntA["tm"] % 6]
                            cntA["tm"] += 1
                            for kt in range(8):
                                kb.MM(p_[:, 0:c1 - c0], hTt[:, kt, j * 128:(j + 1) * 128], wtm[:, kt, c0:c1], start=(kt == 0), stop=(kt == 7))
                            drain(1)
                            return p_
                        kb.DMA("sp", rp[sl][:], rope[gt:gt + 128, :])
                        for i in range(2):
                            p_ = tm_block(i * 512, (i + 1) * 512)
                            kb.ACT(zs[sl][:, i * 512:(i + 1) * 512], p_[:], AF.Silu)
                        kb.DMA("sp", ZS[gt:gt + 128, :], zs[sl][:], key=s["name"])
                        pq = tm_block(1024, 1536)
                        pk = tm_block(1536, 1824)
                        for (p_, nh, o0, tb) in ((pq, 8, 0, 128), (pk, 2, 512, 0)):
                            srcv = p_[:, 0:nh * 64]
                            a_ = rt[:, 0, 0:nh * 64]
                            b_ = rt[:, 1, 0:nh * 64]
                            cos2 = rp[sl][:, tb:tb + 64].unsqueeze(1).to_broadcast([128, nh, 64])
                            kb.TT("dve", a_.rearrange("p (h d) -> p h d", h=nh), srcv.rearrange("p (h d) -> p h d", h=nh), cos2, ALU.mult)
                            s4 = srcv.rearrange("p (h a x f) -> p h a x f", h=nh, a=2, x=2)
                            b4 = b_.rearrange("p (h a x f) -> p h a x f", h=nh, a=2, x=2)
                            sm = rp[sl][:, tb + 64:tb + 96].rearrange("p (a f) -> p a f", a=2).unsqueeze(1).to_broadcast([128, nh, 2, 16])
                            spl = rp[sl][:, tb + 96:tb + 128].rearrange("p (a f) -> p a f", a=2).unsqueeze(1).to_broadcast([128, nh, 2, 16])
                            kb.TT("dve", b4[:, :, :, 0, :], s4[:, :, :, 1, :], sm, ALU.mult)
                            kb.TT("dve", b4[:, :, :, 1, :], s4[:, :, :, 0, :], spl, ALU.mult)
                            kb.TT("pool", qk[sl][:, o0:o0 + nh * 64], a_, b_, ALU.add)
                        kb.CP("act", vst[sl][:, :, 0:64], pk[:, 128:256].rearrange("p (g d) -> p g d", g=2))
                        kb.DMA("sp", VV[gt:gt + 128, :], vst[sl][:].rearrange("p g d -> p (g d)"), key=s["name"])
                        dw = dtw[sl]
                        kb.TT("dve", dw[:, 0, :], pk[:, 256:288], dtbB[:], ALU.add)
                        kb.ACT(dw[:, 1, :], dw[:, 0, :], AF.Abs)
                        kb.ACT(dw[:, 2, :], dw[:, 1, :], AF.Exp, scale=-1.0)
                        kb.ACT(dw[:, 3, :], dw[:, 2, :], AF.Ln, bias=1.0)
                        kb.STT(dw[:, 5, :], dw[:, 0, :], 0.0, dw[:, 3, :], ALU.max, ALU.add)
                        kb.TS("dve", dw[:, 4, :], dw[:, 5, :], 1e-30, None, ALU.max)
                        kb.DMA("sp", DTS[gt:gt + 128, :], dw[:, 4, :], key=s["name"])
                        return (s, gt, sl, j)

                    def a_tmsub2(arg):
                        s, gt, sl, j = arg
                        dw = dtw[sl]
                        p_ = pT[j % 2]
                        p2 = pT[(j + 1) % 2]

                        def pq(q4):
                            for h in range(2 * q4, 2 * q4 + 2):
                                kb.TR(p_[0:64, h * 128:(h + 1) * 128], qk[sl][:, h * 64:(h + 1) * 64], identb, signal=(h == 7))
                            if q4 == 3:
                                kb.CP("act", qkT[sl][:, 0:8, :], p_[0:64, :].rearrange("p (h t) -> p h t", h=8))
                                kb.DMA("sp", QT[:, :, gt:gt + 128], qkT[sl][:, 0:8, :], key=s["name"])

                        def pk2():
                            for h in range(2):
                                kb.TR(p2[0:64, h * 128:(h + 1) * 128], qk[sl][:, 512 + h * 64:512 + (h + 1) * 64], identb, signal=(h == 1))
                            kb.CP("dve", qkT[sl][:, 8:10, :], p2[0:64, 0:256].rearrange("p (h t) -> p h t", h=2))
                            kb.DMA("sp", KT[:, :, gt:gt + 128], qkT[sl][:, 8:10, :], key=s["name"])
                            p3 = ppool[cntA["tm"] % 6]
                            cntA["tm"] += 1
                            kb.TR(p3[0:32, 0:128], dw[:, 4, :], identf)
                            kb.CP("dve", dtT[sl][:], p3[0:32, 0:128])
                            kb.DMA("sp", DTT[:, gt:gt + 128], dtT[sl][:], key=s["name"])
                        for q4 in range(4):
                            smallq.append(lambda q4=q4: pq(q4))
                        smallq.append(pk2)

                    nsubs = len(subs)
                    ks_of = [[k for k, (si2, j) in enumerate(subs) if si2 == si] for si in range(len(sts))]
                    pend2 = [None]
                    a_load(0)
                    for k in ks_of[0]:
                        if k + 1 < nsubs:
                            a_load(k + 1)
                        a_norm(k)
                        a_transp(k, defer=False)
                    for si in range(len(sts)):
                        nsub = len(ks_of[si])
                        nxt = ks_of[si + 1] if si + 1 < len(sts) else []
                        per = 24 // nsub
                        for q in range(nsub):
                            kn = nxt[q] if q < len(nxt) else None
                            if kn is not None:
                                if kn + 1 < nsubs:
                                    a_load(kn + 1)
                                a_norm(kn)
                            if pend2[0] is not None:
                                a_tmsub2(pend2[0])
                                pend2[0] = None
                            if kn is not None:
                                a_transp(kn)
                            for u in range(q * per, (q + 1) * per):
                                a_fmunit(si, u)
                            pend2[0] = a_tmsub(si, q)
                            drain(100)
                        for q in range(nsub, len(nxt)):
                            kn = nxt[q]
                            if kn + 1 < nsubs:
                                a_load(kn + 1)
                            a_norm(kn)
                            a_transp(kn, defer=False)
                    a_tmsub2(pend2[0])
                    drain(100)

                with scope(kb) as ex:
                    sL = seqs[1]
                    end_x, end_g, end_t = sL["xoff"] + sL["n"], sL["goff"] + sL["n"], sL["off"] + sL["n"]
                    snd = SB(ex, "snd", [128, 470], BF16)
                    kb.MSET("dve", snd[:], 0.0)
                    kb.DMA("sp", snd[:, 0:24].rearrange("p (b w) -> p b w", w=2), XBCT[:, end_x - 2:end_x].rearrange("(b p) w -> p b w", p=128), key="l")
                    kb.DMA("sp", snd[:, 24:84].rearrange("p (b w) -> p b w", w=15), GLUT[:, end_g - 15:end_g].rearrange("(b p) w -> p b w", p=128), key="l")
                    kb.DMA("sp", snd[0:64, 84:340].rearrange("p (g t) -> p g t", g=2), KT[:, :, end_t - 128:end_t], key="l")
                    kb.DMA("sp", snd[:, 340:470], VV[end_t - 128:end_t, :], key="l")
                    kb.DMA("sp", SND1[:, :], snd[:])
                    kb.CC(SND1[:, :], RCV1[:, :], groups)
                    rcv = SB(ex, "rcv", [128, 2, 470], BF16)
                    kb.DMA("sp", rcv[:], RCV1.rearrange("(r p) c -> p r c", p=128))
                    tmpx = SB(ex, "tmpx", [128, 470], F32)
                    kb.TS("dve", tmpx[:], rcv[:, 0, :], selB[:, 0:1], None, ALU.mult)
                    kb.STT(nbr1[:], rcv[:, 1, :], selB[:, 1:2], tmpx[:], ALU.mult, ALU.add)

                with scope(kb) as esb:
                    kb.mark("B0%d" % l)
                    kb.mark("B1%d" % l)
                    with scope(kb) as e1:
                        dg = SB(e1, "dg", [128, 60, 128], BF16)
                        for b in range(12):
                            for j in range(5):
                                if (b + j) % 2:
                                    kb.TS("dve", dg[:, b * 5 + j, :], identf, colsT[:, b, j:j + 1], None, ALU.mult)
                                else:
                                    kb.ACT(dg[:, b * 5 + j, :], identf, AF.Copy, scale=colsT[:, b, j:j + 1])
                        win = [SB(e1, "win%d" % i, [128, 12, 516], BF16) for i in range(2)]
                        xc = [SB(e1, "xc%d" % i, [128, 12, 512], BF16) for i in range(2)]
                        xst = [SB(e1, "xst%d" % i, [128, 1280], BF16) for i in range(2)]
                        pcv = [PS(e1, "pcv%d" % i, [128, 512], F32) for i in range(2)]
                        ptr = [PS(e1, "ptr%d" % i, [128, 1024], BF16) for i in range(2)]
                        ptb = [PS(e1, "ptb%d" % i, [128, 256], BF16) for i in range(2)]
                        cntB = dict(j=0)

                        def b1_s0(i):
                            s, t0, TS_ = sts[i]
                            if i == len(sts) - 1:
                                kb.DMA("sp", win[i % 2][:, :, 0:TS_ + 2], XBCT[:, s["xoff"] + t0 - 2:s["xoff"] + t0 + TS_].rearrange("(b p) w -> p b w", p=128), key=s["name"])
                                for r in range(2):
                                    kb.CP("pool", win[i % 2][:, :, TS_ + 2 + r:TS_ + 3 + r], nx[:, :, 1 - r:2 - r])
                            else:
                                kb.DMA("sp", win[i % 2][:, :, 0:TS_ + 4], XBCT[:, s["xoff"] + t0 - 2:s["xoff"] + t0 + TS_ + 2].rearrange("(b p) w -> p b w", p=128), key=s["name"])

                        def b1_s1(i):
                            s, t0, TS_ = sts[i]
                            w_, x_ = win[i % 2], xc[i % 2]
                            g0 = s["off"] + t0
                            for b in range(12):
                                p_ = pcv[b % 2]
                                for j in range(5):
                                    kb.MM(p_[:, 0:TS_], dg[:, b * 5 + j, :], w_[:, b, j:j + TS_], start=(j == 0), stop=(j == 4))
                                kb.ACT(x_[:, b, 0:TS_], p_[:, 0:TS_], AF.Silu, bias=colsT[:, b, 5:6])
                            kb.DMA("sp", BCT[:, :, g0:g0 + TS_], x_[:, 8:12, 0:TS_], key=s["name"])

                        def b1_s2(i):
                            s, t0, TS_ = sts[i]
                            x_ = xc[i % 2]
                            g0 = s["off"] + t0
                            for j in range(TS_ // 128):
                                gt = g0 + j * 128
                                nj = cntB["j"]
                                cntB["j"] += 1
                                p_, pb_, xs_ = ptr[nj % 2], ptb[nj % 2], xst[nj % 2]
                                for b in range(8):
                                    kb.TR(p_[:, b * 128:(b + 1) * 128], x_[:, b, j * 128:(j + 1) * 128], identb, signal=(b == 7))
                                for b in range(2):
                                    kb.TR(pb_[:, b * 128:(b + 1) * 128], x_[:, 8 + b, j * 128:(j + 1) * 128], identb, signal=(b == 1))
                                kb.CP("dve", xs_[:, 0:1024], p_[:])
                                kb.CP("pool" if False else "dve", xs_[:, 1024:1280], pb_[:])
                                kb.DMA("sp", XS[gt:gt + 128, :], xs_[:, 0:1024], key=s["name"])
                                kb.DMA("sp", BTM[gt:gt + 128, :], xs_[:, 1024:1280], key=s["name"])
                        rmask = SB(e1, "rmask", [32, 16, 128], F32)
                        kb.MSET("dve", rmask[:], 1.0)
                        kb.MSET("dve", rmask[:, :, 0:1], 0.0)
                        dr = SB(e1, "dr", [32, 16, 128], F32)
                        da = SB(e1, "da", [32, 16, 128], F32)
                        cs = SB(e1, "cs", [32, 16, 128], F32)
                        ld = SB(e1, "ld", [32, 16, 128], F32)
                        al = SB(e1, "al", [32, 16, 128], F32)
                        r1 = SB(e1, "r1", [32, 16, 128], F32)
                        sp3 = SB(e1, "sp3", [32, 3, 2048], BF16)
                        assert NCH <= 32
                        b0_items = [0, 1]

                        def b0_piece(d_):
                            n_ = NCH
                            for s in seqs:
                                c0, nch = s["off"] // 128, s["n"] // 128
                                kb.DMA("sp", dr[c0:c0 + nch], DTT[d_ * 16:(d_ + 1) * 16, s["off"]:s["off"] + s["n"]].rearrange("h (c t) -> c h t", t=128), key=s["name"])
                            kb.TT("dve", da[0:n_], dr[0:n_], bc(aB[0:n_, d_ * 16:(d_ + 1) * 16], [n_, 16, 128]), ALU.mult)
                            kb.op("dve", lambda g: g.tensor_tensor_scan(out=cs[0:n_].rearrange("c h t -> c (h t)"), data0=rmask[0:n_].rearrange("c h t -> c (h t)"),
                                                                          data1=da[0:n_].rearrange("c h t -> c (h t)"), initial=0.0, op0=ALU.mult, op1=ALU.add),
                                  [rmask[0:n_], da[0:n_]], [cs[0:n_]])
                            if d_ == 1:
                                kb.TT("dve", r1[0:n_], da[0:n_], cs[0:n_], ALU.subtract)
                                kb.CP("dve", al[0:n_, :, 0:1], cs[0:n_, :, 127:128])
                                kb.TT("dve", cs[0:n_], r1[0:n_], al[0:n_, :, 0:1].to_broadcast([n_, 16, 128]), ALU.add)
                            kb.ACT(ld[0:n_], dr[0:n_], AF.Ln)
                            kb.TT("dve", al[0:n_], ld[0:n_], cs[0:n_], ALU.subtract)
                            for side, srcx in ((0, al), (1, cs)):
                                sx = srcx[0:n_].rearrange("c h t -> c (h t)")
                                r1f = r1[0:n_].rearrange("c h t -> c (h t)")
                                kb.CP("act", sp3[0:n_, 0, :], sx)
                                kb.TT("dve", r1f, sx, sp3[0:n_, 0, :], ALU.subtract)
                                kb.CP("act", sp3[0:n_, 1, :], r1f)
                                kb.TT("dve", r1f, r1f, sp3[0:n_, 1, :], ALU.subtract)
                                kb.CP("act", sp3[0:n_, 2, :], r1f)
                                for r_ in range(3):
                                    for s in seqs:
                                        c0, nch = s["off"] // 128, s["n"] // 128
                                        kb.DMA("sp", ABR[d_, side, 3 * side + r_, c0:c0 + nch, :], sp3[c0:c0 + nch, r_, :], key=s["name"])

                        nb0 = 0
                        for t_ in range(len(sts) + 2):
                            for k_, st_ in enumerate([b1_s0, b1_s1, b1_s2]):
                                if 0 <= t_ - k_ < len(sts):
                                    st_(t_ - k_)
                            if t_ in (1, 3):
                                b0_piece(nb0)
                                nb0 += 1
                        assert nb0 == 2

                    with scope(kb) as e2:
                        Sf = SB(e2, "Sf", [128, 1024], F32)
                        Sb_ = SB(e2, "Sb", [128, 1024], F32)
                        Sfb = SB(e2, "Sfb", [128, 1024], BF16)
                        Sbb = [SB(e2, "Sbb%d" % i, [128, 1024], BF16) for i in range(2)]
                        kb.MSET("dve", Sf[:], 0.0)
                        kb.MSET("dve", Sb_[:], 0.0)
                        kb.MSET("pool", Sfb[:], 0.0)
                        xsb = [SB(e2, "xsb%d" % i, [128, 1024], BF16) for i in range(3)]
                        btm = [SB(e2, "btm%d" % i, [128, 256], BF16) for i in range(3)]
                        dts = [SB(e2, "dts%d" % i, [128, 32], F32) for i in range(3)]
                        xdw = [SB(e2, "xdw%d" % i, [128, 1024], BF16) for i in range(2)]
                        sm_ = [SB(e2, "sm%d" % i, [128, 8, 32], F32) for i in range(2)]
                        psm = PS(e2, "psm", [128, 512], F32)
                        pO = PS(e2, "pO", [128, 1024], F32)

                        def small_terms(sl, d_, dt_ap):
                            S_ = sm_[sl]
                            o = d_ * 16
                            kb.TT("dve", S_[:, 0, o:o + 16], dt_ap[:, o:o + 16], aB[:, o:o + 16], ALU.mult)
                            kb.MM(psm[:, o:o + 16], triU if d_ == 0 else triL, S_[:, 0, o:o + 16], start=True, stop=True)
                            kb.MM(psm[:, 64 + o:64 + o + 16], onesf, S_[:, 0, o:o + 16], start=True, stop=True)
                            kb.CP("dve", S_[:, 1, o:o + 16], psm[:, o:o + 16])
                            kb.TT("dve", S_[:, 2, o:o + 16], psm[:, 64 + o:64 + o + 16], S_[:, 1, o:o + 16], ALU.subtract)
                            kb.ACT(S_[:, 3, o:o + 16], S_[:, 2, o:o + 16], AF.Exp)
                            kb.TT("dve", S_[:, 3, o:o + 16], S_[:, 3, o:o + 16], dt_ap[:, o:o + 16], ALU.mult)
                            kb.ACT(S_[:, 4, o:o + 16], S_[:, 1, o:o + 16], AF.Exp)
                            kb.ACT(S_[:, 5, o:o + 16], psm[:, 64 + o:64 + o + 16], AF.Exp)
                            return S_[:, 4, o:o + 16], S_[:, 3, o:o + 16], S_[:, 5, o:o + 16]

                        def mk_xdw(xd_t, xs_ap, w_ap):
                            kb.TT("pool", xd_t[:].rearrange("p (h d) -> p h d", h=16), xs_ap.rearrange("p (h d) -> p h d", h=16), bc(w_ap, [128, 16, 64]), ALU.mult)

                        def state_update(Smaster, bt_ap, etot_ap, xd_t):
                            for g in range(2):
                                kb.MM(pO[:, g * 512:(g + 1) * 512], bt_ap[:, g * 128:(g + 1) * 128], xd_t[:, g * 512:(g + 1) * 512], start=True, stop=True)
                            kb.TT("dve", Smaster[:].rearrange("p (h d) -> p h d", h=16), Smaster[:].rearrange("p (h d) -> p h d", h=16), bc(etot_ap, [128, 16, 64]), ALU.mult)
                            kb.TT("dve", Smaster[:], Smaster[:], pO[:], ALU.add)

                        kb.mark("B2%d" % l)
                        chf = []
                        for s in seqs:
                            for c in range(s["n"] // 128):
                                chf.append((s, c))

                        def bf_s0(i):
                            s, c = chf[i]
                            gt = s["off"] + c * 128
                            kb.DMA("sp", xsb[i % 3][:], XS[gt:gt + 128, :], key=s["name"])
                            kb.DMA("sp", btm[i % 3][:], BTM[gt:gt + 128, :], key=s["name"])
                            kb.DMA("sp", dts[i % 3][:], DTS[gt:gt + 128, :], key=s["name"])

                        def bf_s1(i):
                            s, c = chf[i]
                            gc = (s["off"] + c * 128) // 128
                            _, w_ap, et_ap = small_terms(i % 2, 0, dts[i % 3])
                            mk_xdw(xdw[i % 2], xsb[i % 3][:], w_ap)
                            kb.CP("act", Sbb[i % 2][:], Sf[:])
                            kb.DMA("sp", SFD[gc, :, :], Sbb[i % 2][:], key=gc)
                            state_update(Sf, btm[i % 3], et_ap, xdw[i % 2])
                        pipe(len(chf), [bf_s0, bf_s1])
                        nbS = SB(e2, "nbS", [128, 1024], F32)
                        with scope(kb) as ex2:
                            rcs2 = SB(ex2, "rcs2", [128, 2, 1024], F32)
                            kb.DMA("sp", SND2[:, :], Sf[:])
                            kb.CC(SND2[:, :], RCV2[:, :], groups)
                            kb.DMA("sp", rcs2[:], RCV2.rearrange("(r p) c -> p r c", p=128))
                            kb.TS("dve", nbS[:], rcs2[:, 0, :], selB[:, 0:1], None, ALU.mult)
                            kb.STT(nbS[:], rcs2[:, 1, :], selB[:, 1:2], nbS[:], ALU.mult, ALU.add)
                        chd = []
                        for s in seqs:
                            for c in reversed(range(s["n"] // 128)):
                                chd.append((s, c))
                        first_lat = seqs[0]["n"] // 128

                        def b2_s0(i):
                            s, c = chd[i]
                            gt = s["off"] + c * 128
                            kb.DMA("sp", xsb[i % 3][:], XS[gt:gt + 128, :], key=s["name"])
                            kb.DMA("sp", btm[i % 3][:], BTM[gt:gt + 128, :], key=s["name"])
                            kb.DMA("sp", dts[i % 3][:], DTS[gt:gt + 128, :], key=s["name"])

                        def b2_s1(i):
                            s, c = chd[i]
                            gc = (s["off"] + c * 128) // 128
                            if i == first_lat:
                                kb.CP("dve", Sb_[:], nbS[:])
                            _, w_ap, et_ap = small_terms(i % 2, 1, dts[i % 3])
                            mk_xdw(xdw[i % 2], xsb[i % 3][:], w_ap)
                            kb.CP("act", Sbb[i % 2][:], Sb_[:])
                            kb.DMA("sp", SBD[gc, :, :], Sbb[i % 2][:], key=gc)
                            state_update(Sb_, btm[i % 3], et_ap, xdw[i % 2])
                        pipe(len(chd), [b2_s0, b2_s1])

                        kb.mark("B3%d" % l)
                        with scope(kb) as e3:
                            bct = [SB(e3, "bct%d" % i, [128, 4, 128], BF16) for i in range(3)]
                            zsb = [SB(e3, "zsb%d" % i, [128, 1024], BF16) for i in range(2)]
                            sbe = [SB(e3, "sbe%d" % i, [128, 1024], BF16) for i in range(2)]
                            sfe = [SB(e3, "sfe%d" % i, [128, 1024], BF16) for i in range(2)]
                            ab6 = [SB(e3, "ab6%d" % i, [6, 2, 2, 2048], BF16) for i in range(2)]
                            cbT = SB(e3, "cbT", [128, 2, 128], BF16)
                            E_ = [SB(e3, "E%d" % i, [128, 4, 128], BF16) for i in range(2)]
                            MT = SB(e3, "MT", [128, 2, 16, 128], BF16)
                            ysb = [SB(e3, "ysb%d" % i, [128, 1024], F32) for i in range(2)]
                            t1 = SB(e3, "t1", [128, 1024], F32)
                            t2 = SB(e3, "t2", [128, 1024], F32)
                            us = [SB(e3, "u%d" % i, [128, 1024], F32) for i in range(2)]
                            obs = [SB(e3, "ob%d" % i, [128, 1024], BF16) for i in range(2)]
                            sq = SB(e3, "sq", [128, 512], F32)
                            oT = [SB(e3, "oT%d" % i, [128, 8, 128], BF16) for i in range(2)]
                            st2 = SB(e3, "st2", [128, 2, 2], F32)
                            pDs = [PS(e3, "pD%d" % i, [128, 512], F32) for i in range(2)]
                            pYs = [PS(e3, "pY%d" % i, [128, 512], F32) for i in range(2)]
                            pX = PS(e3, "pX", [128, 1024], BF16)
                            cntD = dict(d=0)
                            cha = []
                            for s in seqs:
                                for c in range(s["n"] // 128):
                                    cha.append((s, c))

                            def ny(s):
                                return not (last and s["name"] == "c")

                            def b3_s0(i):
                                s, c = cha[i]
                                gt = s["off"] + c * 128
                                gc = gt // 128
                                kb.DMA("sp", xsb[i % 3][:], XS[gt:gt + 128, :], key=s["name"])
                                kb.DMA("sp", dts[i % 3][:], DTS[gt:gt + 128, :], key=s["name"])
                                if ny(s):
                                    kb.DMA("sp", bct[i % 3][:], BCT[:, :, gt:gt + 128], key=s["name"])
                                    kb.DMA("sp", ab6[i % 2][:], ABR[:, :, :, gc, :].rearrange("d s r x -> r d s x"), key=s["name"])

                            def b3_s1(i):
                                s, c = cha[i]
                                gt = s["off"] + c * 128
                                gc = gt // 128
                                sl3, sl = i % 3, i % 2
                                if not ny(s):
                                    return
                                ef, wf, etf = small_terms(sl, 0, dts[sl3])
                                kb.DMA("sp", zsb[sl][:], ZS[gt:gt + 128, :], key=s["name"])
                                kb.DMA("sp", sbe[sl][:], SBD[gc, :, :], key=gc)
                                kb.DMA("sp", sfe[sl][:], SFD[gc, :, :], key=gc)
                                small_terms(sl, 1, dts[sl3])
                                for g in range(2):
                                    kb.MM(psm[:, 256 + g * 128:256 + (g + 1) * 128], bct[sl3][:, g, :], bct[sl3][:, 2 + g, :], start=True, stop=True)
                                kb.CP("dve", cbT[:].rearrange("p g t -> p (g t)"), psm[:, 256:512])
                                for g in range(2):
                                    for hb_ in range(2):
                                        es_ = []
                                        for d_ in range(2):
                                            pD = pDs[cntD["d"] % 2]
                                            e_ = E_[cntD["d"] % 2]
                                            cntD["d"] += 1
                                            kb.MM(pD[:], identb, mneg[:, d_, :, :].rearrange("p r t -> p (r t)"), start=True, stop=False, signal=False)
                                            for hh in range(4):
                                                h = g * 8 + hb_ * 4 + hh
                                                kb.MM(pD[:, hh * 128:(hh + 1) * 128], ab6[sl][:, d_, 0, h * 128:(h + 1) * 128],
                                                      ab6[sl][:, d_, 1, h * 128:(h + 1) * 128], start=False, stop=(hh == 3), signal=(hh == 3))
                                            kb.ACT(e_[:].rearrange("p h t -> p (h t)"), pD[:], AF.Exp)
                                            es_.append(e_)
                                        h0 = g * 8 + hb_ * 4
                                        kb.TT("dve", es_[0][:], es_[0][:], es_[1][:], ALU.add)
                                        kb.TT("dve", MT[:, 0, h0:h0 + 4, :], es_[0][:], cbT[:, g, :].unsqueeze(1).to_broadcast([128, 4, 128]), ALU.mult)
                                    pY = pYs[g]
                                    for hh in range(8):
                                        h = g * 8 + hh
                                        rhs = xsb[sl3][:, h * 64:(h + 1) * 64]
                                        kb.MM(pY[:, hh * 64:(hh + 1) * 64], MT[:, 0, h, :], rhs, start=True, stop=False, signal=False)
                                        kb.MM(pY[:, hh * 64:(hh + 1) * 64], dI[:, h, :], rhs, start=False, stop=True, signal=(hh == 7))
                                    kb.CP("act", ysb[sl][:, g * 512:(g + 1) * 512], pY[:])

                            def b3_s2(i):
                                s, c = cha[i]
                                if not ny(s):
                                    return
                                sl3, sl = i % 3, i % 2
                                S_ = sm_[sl]
                                ef, eb = S_[:, 4, 0:16], S_[:, 4, 16:32]
                                u = us[sl]
                                for g in range(2):
                                    kb.MM(pO[:, g * 512:(g + 1) * 512], bct[sl3][:, 2 + g, :], sfe[sl][:, g * 512:(g + 1) * 512], start=True, stop=True)
                                kb.TT("dve", t1[:].rearrange("p (h d) -> p h d", h=16), pO[:].rearrange("p (h d) -> p h d", h=16), bc(ef, [128, 16, 64]), ALU.mult)
                                for g in range(2):
                                    kb.MM(pO[:, g * 512:(g + 1) * 512], bct[sl3][:, 2 + g, :], sbe[sl][:, g * 512:(g + 1) * 512], start=True, stop=True)
                                kb.TT("dve", t2[:].rearrange("p (h d) -> p h d", h=16), pO[:].rearrange("p (h d) -> p h d", h=16), bc(eb, [128, 16, 64]), ALU.mult)
                                kb.TT("dve", u[:], t1[:], t2[:], ALU.add)
                                kb.TT("dve", u[:], u[:], ysb[sl][:], ALU.add)
                                kb.TT("dve", u[:], u[:], zsb[sl][:], ALU.mult)

                            def b3_s3(i):
                                s, c = cha[i]
                                if not ny(s):
                                    return
                                sl = i % 2
                                u = us[sl]
                                ss_ = st2[:, sl, :]
                                for g in range(2):
                                    kb.STT(sq[:], u[:, g * 512:(g + 1) * 512], 1.0, u[:, g * 512:(g + 1) * 512], ALU.mult, ALU.mult, accum=st2[:, sl, g:g + 1])
                                kb.TS("dve", ss_, ss_, 1.0 / 512.0, EPS, ALU.mult, ALU.add)
                                kb.ACT(ss_, ss_, AF.Ln)
                                kb.ACT(ss_, ss_, AF.Exp, scale=-0.5)
                                for g in range(2):
                                    kb.STT(obs[sl][:, g * 512:(g + 1) * 512], u[:, g * 512:(g + 1) * 512], st2[:, sl, g:g + 1], snwB[:, g * 512:(g + 1) * 512], ALU.mult, ALU.mult)

                            def b3_s4(i):
                                s, c = cha[i]
                                if not ny(s):
                                    return
                                gt = s["off"] + c * 128
                                sl = i % 2
                                for b in range(8):
                                    kb.TR(pX[:, b * 128:(b + 1) * 128], obs[sl][:, b * 128:(b + 1) * 128], identb, signal=(b == 7))
                                kb.CP("act", oT[sl][:].rearrange("p b t -> p (b t)"), pX[:])
                                kb.DMA("sp", OT[0:1024, gt:gt + 128].rearrange("(b p) t -> p b t", p=128), oT[sl][:], key=s["name"])
                            pipe(len(cha), [b3_s0, b3_s1, b3_s2, b3_s3, b3_s4])

                kb.mark("C%d" % l)
                with scope(kb) as ec:
                    ktb = SB(ec, "ktb", [64, 2, Ltot + 128], BF16)
                    vvb = SB(ec, "vvb", [128, NCH + 1, 130], BF16)
                    kb.CP("dve", ktb[:, :, Ltot:Ltot + 128], nbr1[0:64, 84:340].rearrange("p (g t) -> p g t", g=2))
                    kb.CP("dve", vvb[:, NCH, :], nbr1[:, 340:470])
                    for s in seqs:
                        o_, n_ = s["off"], s["n"]
                        kb.DMA("sp", ktb[:, :, o_:o_ + n_], KT[:, :, o_:o_ + n_], key=s["name"])
                        kb.DMA("sp", vvb[:, o_ // 128:(o_ + n_) // 128, :], VV[o_:o_ + n_, :].rearrange("(c p) x -> p c x", p=128), key=s["name"])
                    qtb = [SB(ec, "qtb%d" % i, [64, 8, 128], BF16) for i in range(3)]
                    gab = [SB(ec, "gab%d" % i, [64, 8, 128], BF16) for i in range(3)]
                    Ee = [SB(ec, "Ee%d" % i, [128, 512], BF16) for i in range(20)]
                    den = [SB(ec, "den%d" % i, [64, 512], F32) for i in range(2)]
                    oa = [SB(ec, "oa%d" % i, [64, 512], F32) for i in range(2)]
                    oaT = [SB(ec, "oaT%d" % i, [64, 512], BF16) for i in range(2)]
                    pS = [PS(ec, "pS%d" % i, [128, 512], F32) for i in range(3)]
                    pV = [PS(ec, "pV%d" % i, [64, 512], F32) for i in range(2)]
                    pU = [PS(ec, "pU%d" % i, [64, 512], F32) for i in range(2)]
                    qbs = []
                    for s in seqs:
                        if last and s["name"] == "c":
                            continue
                        for qb in range(s["n"] // 128):
                            qbs.append((s, qb))
                    cntC = dict(s=0, o=0)

                    def keyblocks(s, qb):
                        kbl = [(0, None), (1, None)]
                        if s["name"] == "l":
                            nb = s["n"] // 128
                            c0 = s["off"] // 128
                            if qb > 0:
                                kbl.append((c0 + qb - 1, 1))
                            kbl.append((c0 + qb, None))
                            if qb < nb - 1:
                                kbl.append((c0 + qb + 1, 0))
                            else:
                                kbl.append((NCH, 2))
                        return kbl

                    def c_s0(i):
                        s, qb = qbs[i]
                        gt = s["off"] + qb * 128
                        kb.DMA("sp", qtb[i % 3][:], QT[:, :, gt:gt + 128], key=s["name"])
                        kb.DMA("sp", gab[i % 3][:], GAT[:, gt:gt + 128].rearrange("(h d) t -> d h t", d=64), key=s["name"])

                    def c_s1(i):
                        s, qb = qbs[i]
                        q_ = qtb[i % 3]
                        for g in range(2):
                            for ki, (kc, md) in enumerate(keyblocks(s, qb)):
                                p_ = pS[cntC["s"] % 3]
                                cntC["s"] += 1
                                if md is not None:
                                    kb.MM(p_[:], identb, mneg[:, md, :, :].rearrange("p r t -> p (r t)"), start=True, stop=False, signal=False)
                                kb.MM(p_[:], ktb[:, g, kc * 128:(kc + 1) * 128], q_[:, g * 4:(g + 1) * 4, :].rearrange("p h t -> p (h t)"),
                                      start=(md is None), stop=True)
                                kb.ACT(Ee[(i % 2) * 10 + g * 5 + ki][:], p_[:], AF.Exp)

                    def c_s2(i):
                        s, qb = qbs[i]
                        gt = s["off"] + qb * 128
                        g_ = gab[i % 3]
                        kbl = keyblocks(s, qb)
                        for g in range(2):
                            no = cntC["o"]
                            cntC["o"] += 1
                            pv, pu = pV[no % 2], pU[no % 2]
                            for ki, (kc, md) in enumerate(kbl):
                                kb.MM(pv[:], vvb[:, kc, g * 65:g * 65 + 64], Ee[(i % 2) * 10 + g * 5 + ki][:], start=(ki == 0), stop=(ki == len(kbl) - 1))
                            for ki, (kc, md) in enumerate(kbl):
                                kb.MM(pu[:], onesb[:, 0:64], Ee[(i % 2) * 10 + g * 5 + ki][:], start=(ki == 0), stop=(ki == len(kbl) - 1))
                            dn = den[no % 2]
                            kb.TT("dve", dn[:].rearrange("p (r t) -> p r t", r=4), pu[:].rearrange("p (r t) -> p r t", r=4), bc(sinkB[0:64, g * 4:(g + 1) * 4], [64, 4, 128]), ALU.add)
                            kb.ACT(dn[:], dn[:], AF.Ln)
                            kb.ACT(dn[:], dn[:], AF.Exp, scale=-1.0)
                            o_ = oa[no % 2]
                            kb.TT("dve", o_[:], pv[:], dn[:], ALU.mult)
                            ot_ = oaT[no % 2]
                            kb.TT("dve", ot_[:], o_[:], g_[:, g * 4:(g + 1) * 4, :].rearrange("p h t -> p (h t)"), ALU.mult)
                            kb.DMA("sp", OT[1024 + g * 256:1024 + (g + 1) * 256, gt:gt + 128].rearrange("(r d) t -> d r t", d=64), ot_[:].rearrange("p (r t) -> p r t", r=4), key=s["name"])
                    pipe(len(qbs), [c_s0, c_s1, c_s2])

                kb.mark("D%d" % l)
                with scope(kb) as ed:
                    dgc = SB(ed, "dgc", [128, 124, 128], BF16)
                    for b in range(4):
                        for j in range(31):
                            if (b + j) % 2:
                                kb.TS("dve", dgc[:, b * 31 + j, :], identf, colsT[:, b, 6 + j:7 + j], None, ALU.mult)
                            else:
                                kb.ACT(dgc[:, b * 31 + j, :], identf, AF.Copy, scale=colsT[:, b, 6 + j:7 + j])
                    pww = SB(ed, "pww", [128, 4, 512], BF16)
                    pwst = SB(ed, "pwst", [128, 4, 512], F32)
                    kb.DMA("sp", pwst[:], conv_pw_w[l].rearrange("(k p) n -> p k n", p=128))
                    kb.CP("act", pww[:], pwst[:])
                    gw = [SB(ed, "gw%d" % i, [128, 4, 542], BF16) for i in range(2)]
                    gcb = [SB(ed, "gcb%d" % i, [128, 4, 512], BF16) for i in range(2)]
                    hbb = [SB(ed, "hbb%d" % i, [128, 4, 512], BF16) for i in range(2)]
                    hsq = [SB(ed, "hsq%d" % i, [128, 4, 512], BF16) for i in range(2)]
                    mean = SB(ed, "mean", [128, 512], F32)
                    rstd = SB(ed, "rstd", [128, 512], F32)
                    xcn = [SB(ed, "xcn%d" % i, [128, 512], F32) for i in range(2)]
                    h2 = [SB(ed, "h2%d" % i, [128, 4, 512], BF16) for i in range(2)]
                    oc = [SB(ed, "oc%d" % i, [128, 512], BF16) for i in range(2)]
                    pc = [PS(ed, "pc%d" % i, [128, 512], F32) for i in range(2)]
                    pm = PS(ed, "pm", [128, 512], F32)
                    pe2 = PS(ed, "pe2", [128, 512], F32)
                    pw_ = [PS(ed, "pw%d" % i, [128, 512], F32) for i in range(2)]
                    dst = [x for x in sts if not (last and x[0]["name"] == "c")]

                    def d_s0(i):
                        s, t0, TS_ = dst[i]
                        if i == len(dst) - 1:
                            kb.DMA("sp", gw[i % 2][:, :, 0:TS_ + 15], GLUT[:, s["goff"] + t0 - 15:s["goff"] + t0 + TS_].rearrange("(b p) w -> p b w", p=128), key=s["name"])
                            for r in range(15):
                                kb.CP("pool", gw[i % 2][:, :, TS_ + 15 + r:TS_ + 16 + r], ng[:, :, 14 - r:15 - r])
                        else:
                            kb.DMA("sp", gw[i % 2][:, :, 0:TS_ + 30], GLUT[:, s["goff"] + t0 - 15:s["goff"] + t0 + TS_ + 15].rearrange("(b p) w -> p b w", p=128), key=s["name"])

                    def d_s1(i):
                        s, t0, TS_ = dst[i]
                        w_ = gw[i % 2]
                        for b in range(4):
                            p_ = pc[b % 2]
                            for j in range(31):
                                kb.MM(p_[:, 0:TS_], dgc[:, b * 31 + j, :], w_[:, b, j:j + TS_], start=(j == 0), stop=(j == 30))
                            kb.ACT(hbb[i % 2][:, b, 0:TS_], p_[:, 0:TS_], AF.Identity, bias=colsT[:, b, 37:38])
                            kb.TT("dve", hsq[i % 2][:, b, 0:TS_], hbb[i % 2][:, b, 0:TS_], hbb[i % 2][:, b, 0:TS_], ALU.mult)

                    def d_s2(i):
                        s, t0, TS_ = dst[i]
                        g0 = s["off"] + t0
                        kb.DMA("sp", gcb[i % 2][:, :, 0:TS_], GCT[:, g0:g0 + TS_].rearrange("(b p) w -> p b w", p=128), key=s["name"])
                        hb_, hs_ = hbb[i % 2], hsq[i % 2]
                        for b in range(4):
                            kb.MM(pm[:, 0:TS_], ones512[:], hb_[:, b, 0:TS_], start=(b == 0), stop=(b == 3))
                        for b in range(4):
                            kb.MM(pe2[:, 0:TS_], ones512[:], hs_[:, b, 0:TS_], start=(b == 0), stop=(b == 3))
                        kb.CP("act", mean[:, 0:TS_], pm[:, 0:TS_])
                        kb.TT("dve", rstd[:, 0:TS_], mean[:, 0:TS_], mean[:, 0:TS_], ALU.mult)
                        kb.TT("dve", rstd[:, 0:TS_], pe2[:, 0:TS_], rstd[:, 0:TS_], ALU.subtract)
                        kb.ACT(rstd[:, 0:TS_], rstd[:, 0:TS_], AF.Ln, bias=epsb[:])
                        kb.ACT(rstd[:, 0:TS_], rstd[:, 0:TS_], AF.Exp, scale=-0.5)
                        for b in range(4):
                            x_ = xcn[b % 2]
                            kb.TT("dve", x_[:, 0:TS_], hb_[:, b, 0:TS_], mean[:, 0:TS_], ALU.subtract)
                            kb.TT("dve", x_[:, 0:TS_], x_[:, 0:TS_], rstd[:, 0:TS_], ALU.mult)
                            kb.ACT(h2[i % 2][:, b, 0:TS_], x_[:, 0:TS_], AF.Silu, bias=colsT[:, b, 39:40], scale=colsT[:, b, 38:39])

                    def d_s3(i):
                        s, t0, TS_ = dst[i]
                        g0 = s["off"] + t0
                        for co in range(4):
                            p_ = pw_[co % 2]
                            for ci in range(4):
                                kb.MM(p_[:, 0:TS_], pww[:, ci, co * 128:(co + 1) * 128], h2[i % 2][:, ci, 0:TS_], start=(ci == 0), stop=(ci == 3))
                            o_ = oc[co % 2]
                            kb.STT(o_[:, 0:TS_], p_[:, 0:TS_], colsT[:, co, 40:41], gcb[i % 2][:, co, 0:TS_], ALU.add, ALU.mult)
                            kb.DMA("sp", OT[1536 + co * 128:1536 + (co + 1) * 128, g0:g0 + TS_], o_[:, 0:TS_], key=s["name"])
                    pipe(len(dst), [d_s0, d_s1, d_s2, d_s3])

                kb.mark("E%d" % l)
                with scope(kb) as ee:
                    wo = SB(ee, "wo", [128, 16, D], BF16)
                    wov = w_out[l].rearrange("(k p) n -> p k n", p=128)
                    with scope(kb) as eow:
                        wos = [SB(eow, "wos%d" % i, [128, 4, D], F32) for i in range(2)]
                        for k4 in range(4):
                            kb.DMA("sp", wos[k4 % 2][:], wov[:, k4 * 4:(k4 + 1) * 4, :])
                            for j4 in range(4):
                                kb.CP(["dve", "pool", "act", "pool"][j4], wo[:, k4 * 4 + j4, :], wos[k4 % 2][:, j4, :])
                    fnw = SB(ee, "fnw", [128, D], F32)
                    kb.DMA("sp", fnw[:], final_norm_w[0, :].partition_broadcast(128))
                    otb = [SB(ee, "otb%d" % i, [128, 16, 128], BF16) for i in range(3)]
                    xo = [SB(ee, "xo%d" % i, [128, D], F32) for i in range(3)]
                    xn = [SB(ee, "xn%d" % i, [128, D], F32) for i in range(2)]
                    yo = [SB(ee, "yo%d" % i, [128, D], F32) for i in range(2)]
                    jk = SB(ee, "jk", [128, D], BF16)
                    s1 = SB(ee, "s1", [128, 2], F32)
                    po = [PS(ee, "po%d" % i, [128, 1024], F32) for i in range(2)]
                    che = []
                    for s in seqs:
                        if last and s["name"] == "c":
                            continue
                        for c in range(s["n"] // 128):
                            che.append((s, c))

                    def e_s0(i):
                        s, c = che[i]
                        gt = s["off"] + c * 128
                        kb.DMA("sp", otb[i % 3][:], OT[:, gt:gt + 128].rearrange("(k p) t -> p k t", p=128), key=s["name"])
                        if l == 0:
                            src = (ctx_in if s["name"] == "c" else x_in)[c * 128:(c + 1) * 128, :]
                        else:
                            src = XR[gt:gt + 128, :]
                        kb.DMA("sp", xo[i % 3][:], src, key=(s["name"], c * 128))

                    def e_nop(i):
                        pass

                    def e_s1(i):
                        s, c = che[i]
                        m = 0 if s["name"] == "c" else 1
                        sl = i % 2
                        p_ = po[sl]
                        for nb_ in range(2):
                            for kt in range(16):
                                kb.MM(p_[:, nb_ * 512:(nb_ + 1) * 512], otb[i % 3][:, kt, :], wo[:, kt, nb_ * 512:(nb_ + 1) * 512], start=(kt == 0), stop=(kt == 15))
                        kb.TT("dve", xn[sl][:], p_[:], modB[:, m, 2, :], ALU.mult)
                        kb.TT("dve", xn[sl][:], xn[sl][:], xo[i % 3][:], ALU.add)

                    def e_s2(i):
                        s, c = che[i]
                        gt = s["off"] + c * 128
                        sl = i % 2
                        if not last:
                            kb.DMA("sp", XR[gt:gt + 128, :], xn[sl][:], key=(s["name"], c * 128))
                        else:
                            sc_ = s1[:, sl:sl + 1]
                            kb.STT(yo[sl][:], xn[sl][:], 1.0, xn[sl][:], ALU.mult, ALU.mult, accum=sc_)
                            kb.TS("dve", sc_, sc_, 1.0 / D, EPS, ALU.mult, ALU.add)
                            kb.ACT(sc_, sc_, AF.Ln)
                            kb.ACT(sc_, sc_, AF.Exp, scale=-0.5)
                            kb.STT(yo[sl][:], xn[sl][:], sc_, fnw[:], ALU.mult, ALU.mult)
                            kb.DMA("sp", out[c * 128:(c + 1) * 128, :], yo[sl][:], key="out")
                    pipe(len(che), [e_s0, e_nop, e_s1, e_s2])
        kb.mark("end")
        kb.finish()
    return nc, kb


def host_consts(L, T, pos, grid_w=64):
    k = np.arange(128)[:, None]
    t = np.arange(128)[None, :]
    cst = np.zeros((128, 7, 128), np.float32)
    cst[:, 0] = np.eye(128)
    cst[:, 1] = (k <= t)
    cst[:, 2] = (k >= t)
    cst[:, 3] = 1.0
    cst[:, 4] = np.where(k <= t, 0.0, NEG)
    cst[:, 5] = np.where(k >= t, 0.0, NEG)
    cst[:, 6] = np.where(k + t >= 127, 0.0, NEG)
    rope = np.zeros((T + L, 256), np.float32)
    rope[:, 0:64] = 1.0
    rope[:, 128:192] = 0.125
    row = (pos // grid_w).astype(np.float32)
    col = (pos % grid_w).astype(np.float32)
    inv = (10000.0 ** (-np.arange(0, 32, 2, dtype=np.float32) / 32.0)).astype(np.float32)
    ang = np.stack([row[:, None] * inv, col[:, None] * inv], axis=1).astype(np.float32)
    cs_, sn_ = np.cos(ang), np.sin(ang)
    cos2 = np.repeat(cs_[:, :, None, :], 2, axis=2).reshape(L, 64)
    rope[T:, 0:64] = cos2
    rope[T:, 64:96] = (-sn_).reshape(L, 32)
    rope[T:, 96:128] = sn_.reshape(L, 32)
    rope[T:, 128:192] = cos2 * 0.125
    rope[T:, 192:224] = (-sn_).reshape(L, 32) * 0.125
    rope[T:, 224:256] = sn_.reshape(L, 32) * 0.125
    return cst, rope


_CACHE = {}


def run(inputs, L, T, NL, batches):
    npairs = len(batches)
    key = (L, T, NL, npairs)
    if key not in _CACHE:
        _CACHE[key] = build(L, T, NL, npairs)[0]
    nc = _CACHE[key]
    f = lambda a: np.ascontiguousarray(np.asarray(a, dtype=np.float32))
    base = {
        "c_ctx": f(inputs["c_ctx"]).reshape(128, 8),
        "w_mod": f(inputs["w_mod"]), "b_mod": f(inputs["b_mod"]), "norm_w": f(inputs["norm_w"]),
        "ssd_conv_b": f(inputs["ssd_conv_b"]),
        "ssd_d": f(inputs["ssd_d"]), "ssd_norm_w": f(inputs["ssd_norm_w"]), "attn_sink": f(inputs["attn_sink"]),
        "conv_dw_b": f(inputs["conv_dw_b"]), "conv_ln_w": f(inputs["conv_ln_w"]),
        "conv_ln_b": f(inputs["conv_ln_b"]), "conv_pw_w": f(inputs["conv_pw_w"]), "conv_pw_b": f(inputs["conv_pw_b"]),
        "w_out": f(inputs["w_out"]), "final_norm_w": f(inputs["final_norm_w"]).reshape(1, D),
    }
    w_in = f(inputs["w_in"])
    w_in_m = w_in.copy()
    w_in_m[:, :, 2560:2576] = w_in[:, :, 2576:2592]
    w_in_m[:, :, 2576:2592] = w_in[:, :, 2560:2576]
    dtb = f(inputs["ssd_dt_bias"])
    alog = f(inputs["ssd_a_log"])
    scw = f(inputs["ssd_conv_w"])
    dww = f(inputs["conv_dw_w"])
    half = []
    for h in range(2):
        pos = np.arange(L) if h == 0 else (2 * L - 1 - np.arange(L))
        cst, rope = host_consts(L, T, pos)
        sel = np.zeros((128, 2), np.float32)
        sel[:, 1 - h] = 1.0
        d = dict(base)
        d.update({
            "cst": cst, "rope": rope, "sel": sel,
            "w_in": w_in if h == 0 else w_in_m,
            "ssd_dt_bias": f(dtb if h == 0 else dtb[:, ::-1]).reshape(NL, 32),
            "ssd_a_log": f(alog if h == 0 else alog[:, ::-1]).reshape(NL, 32),
            "ssd_conv_w": f(scw if h == 0 else scw[:, ::-1]),
            "conv_dw_w": f(dww if h == 0 else dww[:, ::-1]),
        })
        half.append(d)
    x = f(inputs["x"])
    c = f(inputs["c"])
    ctx = f(inputs["ctx"])
    in_maps = []
    for b in batches:
        for h in range(2):
            m = dict(half[h])
            if h == 0:
                m["x"] = f(x[b, 0:L])
                m["ctx"] = f(ctx[b])
            else:
                m["x"] = f(x[b, L:2 * L][::-1])
                m["ctx"] = f(ctx[b][::-1])
            m["c"] = c[b].reshape(128, 8)
            in_maps.append(m)
    res = run_bass_kernel_spmd(nc, in_maps, core_ids=list(range(2 * npairs)))
    outs = []
    for i in range(npairs):
        o0 = res.results[2 * i]["out"]
        o1 = res.results[2 * i + 1]["out"][::-1]
        outs.append(np.concatenate([o0, o1], axis=0))
    return outs


def kernel(**inputs):
    B = np.asarray(inputs["x"]).shape[0]
    outs = run(inputs, 2048, 256, 2, list(range(B)))
    return np.stack(outs, axis=0).astype(np.float32)
```

```python
import numpy as np
from contextlib import ExitStack, contextmanager
import concourse.bass as bass
import concourse.mybir as mybir
from concourse.bass_utils import run_bass_kernel_spmd

F32 = mybir.dt.float32
BF16 = mybir.dt.bfloat16
AF = mybir.ActivationFunctionType
ALU = mybir.AluOpType
NDMA = 48
EPS = 1e-6
D = 1024
NEG = -30000.0


class Tok:
    __slots__ = ("sem", "val", "key")

    def __init__(self, sem, val, key):
        self.sem, self.val, self.key = sem, val, key


class Buf:
    __slots__ = ("w", "r", "multi")

    def __init__(self, multi=False):
        self.w = {}
        self.r = {}
        self.multi = multi


class KB:
    def __init__(self, nc, es):
        self.nc = nc
        self.eng = {"pe": nc.tensor, "act": nc.scalar, "dve": nc.vector, "pool": nc.gpsimd, "sp": nc.sync}
        self.sem = {e: es.enter_context(nc.semaphore("s_" + e)) for e in self.eng}
        self.cnt = {e: 0 for e in self.eng}
        self.known = {e: {} for e in self.eng}
        self.dsem = [es.enter_context(nc.semaphore("d%d" % i)) for i in range(NDMA)]
        self.dval = [0] * NDMA
        self.dnext = 0
        self.pend = {e: [] for e in self.eng}
        self.bufs = {}
        self.ccsem = es.enter_context(nc.semaphore("ccsem"))
        self.ccval = 0
        self.ninst = 0
        self.nmm = 0
        self.marks = []

    def buf(self, ap, key=None):
        t = ap.tensor
        name = t.name
        dram = str(ap.space).lower().find("dram") >= 0 or str(ap.space).lower().find("hbm") >= 0
        k = (name, key if dram else None)
        b = self.bufs.get(k)
        if b is None:
            b = Buf(multi=dram)
            self.bufs[k] = b
        return b

    def _wait(self, e, tok):
        if e == "pe" and tok.key == "pe":
            return
        k = self.known[e]
        if k.get(tok.key, 0) >= tok.val:
            return
        self.eng[e].wait_ge(tok.sem, tok.val)
        k[tok.key] = tok.val

    def _deps(self, e, reads, writes):
        for b in reads:
            for t in b.w.values():
                self._wait(e, t)
        for b in writes:
            if not b.multi:
                for t in b.w.values():
                    self._wait(e, t)
            for t in b.r.values():
                self._wait(e, t)

    def _record(self, tok, reads, writes):
        for b in reads:
            b.r[tok.key] = tok
        for b in writes:
            if b.multi:
                b.w[tok.key] = tok
            else:
                b.w = {tok.key: tok}
                b.r = {}

    def op(self, e, fn, rd, wr, signal=True, key=None):
        reads = [self.buf(a, key) for a in rd if a is not None and not isinstance(a, (int, float))]
        writes = [self.buf(a, key) for a in wr if a is not None]
        self._deps(e, reads, writes)
        ins = fn(self.eng[e])
        self.ninst += 1
        if not signal:
            self.pend[e].append((reads, writes))
            return
        self.cnt[e] += 1
        ins.then_inc(self.sem[e], 1)
        tok = Tok(self.sem[e], self.cnt[e], e)
        for (r, w) in self.pend[e]:
            self._record(tok, r, w)
        self.pend[e] = []
        self._record(tok, reads, writes)

    def DMA(self, q, out, in_, key=None, **kw):
        reads = [self.buf(in_, key)]
        writes = [self.buf(out, key)]
        self._deps(q, reads, writes)
        i = self.dnext
        self.dnext = (i + 1) % NDMA
        k = "d%d" % i
        if self.dval[i] > 0:
            self._wait(q, Tok(self.dsem[i], self.dval[i], k))
        ins = self.eng[q].dma_start(out=out, in_=in_, **kw)
        self.ninst += 1
        self.dval[i] += 16
        ins.then_inc(self.dsem[i], 16)
        tok = Tok(self.dsem[i], self.dval[i], k)
        self._record(tok, reads, writes)

    def CC(self, in_ap, out_ap, groups):
        reads = [self.buf(in_ap)]
        writes = [self.buf(out_ap)]
        self._deps("pool", reads, writes)
        ins = self.eng["pool"].collective_compute("AllGather", ALU.bypass, replica_groups=groups, ins=[in_ap], outs=[out_ap])
        self.ninst += 1
        self.ccval += 1
        ins.then_inc(self.ccsem)
        tok = Tok(self.ccsem, self.ccval, "cc")
        self._record(tok, reads, writes)

    def barrier(self):
        assert all(len(p) == 0 for p in self.pend.values())
        toks = [Tok(self.sem[e], self.cnt[e], e) for e in self.eng if self.cnt[e] > 0]
        if self.ccval:
            toks.append(Tok(self.ccsem, self.ccval, "cc"))
        toks += [Tok(self.dsem[i], self.dval[i], "d%d" % i) for i in range(NDMA) if self.dval[i]]
        for e in self.eng:
            for t in toks:
                if t.key != e:
                    self._wait(e, t)

    def finish(self):
        for i in range(NDMA):
            if self.dval[i]:
                self._wait("sp", Tok(self.dsem[i], self.dval[i], "d%d" % i))

    def ACT(self, out, in_, func, bias=None, scale=None, accum=None):
        kw = {}
        if bias is not None:
            kw["bias"] = bias
        if scale is not None:
            kw["scale"] = scale
        if accum is not None:
            kw["accum_out"] = accum
        self.op("act", lambda e: e.activation(out=out, in_=in_, func=func, **kw), [in_, bias, scale], [out, accum])

    def TT(self, e, out, in0, in1, op):
        self.op(e, lambda g: g.tensor_tensor(out=out, in0=in0, in1=in1, op=op), [in0, in1], [out])

    def TS(self, e, out, in0, s1, s2, op0, op1=None):
        if op1 is None:
            self.op(e, lambda g: g.tensor_scalar(out=out, in0=in0, scalar1=s1, scalar2=None, op0=op0), [in0, s1], [out])
        else:
            self.op(e, lambda g: g.tensor_scalar(out=out, in0=in0, scalar1=s1, scalar2=s2, op0=op0, op1=op1), [in0, s1, s2], [out])

    def STT(self, out, in0, scalar, in1, op0, op1, accum=None):
        if accum is None:
            self.op("dve", lambda g: g.scalar_tensor_tensor(out=out, in0=in0, scalar=scalar, in1=in1, op0=op0, op1=op1), [in0, scalar, in1], [out])
        else:
            self.op("dve", lambda g: g.scalar_tensor_tensor(out=out, in0=in0, scalar=scalar, in1=in1, op0=op0, op1=op1, accum_out=accum),
                    [in0, scalar, in1], [out, accum])

    def CP(self, e, out, in_):
        if e == "act":
            self.ACT(out, in_, AF.Copy)
        else:
            self.op(e, lambda g: g.tensor_copy(out=out, in_=in_), [in_], [out])

    def RECIP(self, out, in_):
        self.op("dve", lambda g: g.reciprocal(out=out, in_=in_), [in_], [out])

    def MSET(self, e, out, val):
        self.op(e, lambda g: g.memset(out, val), [], [out])

    def mark(self, label):
        self.marks.append((label, self.nmm))

    def MM(self, out, lhsT, rhs, start, stop, signal=None):
        self.nmm += 1
        self.op("pe", lambda g: g.matmul(out, lhsT, rhs, start=start, stop=stop), [lhsT, rhs], [out],
                signal=(stop if signal is None else signal))

    def TR(self, out, in_, ident, signal=True):
        self.nmm += 1
        self.op("pe", lambda g: g.transpose(out=out, in_=in_, identity=ident), [in_, ident], [out], signal=signal)


@contextmanager
def scope(kb):
    with ExitStack() as es:
        yield es
        kb.barrier()


def pipe(n, stages):
    K = len(stages)
    for t in range(n + K - 1):
        for k in range(K):
            i = t - k
            if 0 <= i < n:
                stages[k](i)


def bc(ap, shape):
    return ap.unsqueeze(2).to_broadcast(shape)


def build(L, T, NL=2, npairs=4):
    nc = bass.Bass("TRN2", target_bir_lowering=False)
    Ltot = T + L
    NCH = Ltot // 128
    seqs = [dict(name="c", off=0, n=T, xoff=2, goff=15), dict(name="l", off=T, n=L, xoff=T + 6, goff=T + 45)]
    Wx = Ltot + 8
    Wg = Ltot + 60

    def din(name, shape, dt=F32):
        return nc.dram_tensor(name, list(shape), dt, kind="ExternalInput").ap()

    def dsc(name, shape, dt):
        return nc.dram_tensor(name, list(shape), dt, kind="Internal").ap()

    x_in = din("x", [L, D])
    ctx_in = din("ctx", [T, D])
    c_in = din("c", [128, 8])
    cc_in = din("c_ctx", [128, 8])
    w_mod = din("w_mod", [NL, D, 3 * D])
    b_mod = din("b_mod", [NL, 3 * D])
    norm_w = din("norm_w", [NL, D])
    w_in = din("w_in", [NL, D, 5408])
    ssd_conv_w = din("ssd_conv_w", [NL, 5, 1536])
    ssd_conv_b = din("ssd_conv_b", [NL, 1536])
    ssd_dt_bias = din("ssd_dt_bias", [NL, 32])
    ssd_a_log = din("ssd_a_log", [NL, 32])
    ssd_d = din("ssd_d", [NL, 16])
    ssd_norm_w = din("ssd_norm_w", [NL, 1024])
    attn_sink = din("attn_sink", [NL, 8])
    conv_dw_w = din("conv_dw_w", [NL, 31, 512])
    conv_dw_b = din("conv_dw_b", [NL, 512])
    conv_ln_w = din("conv_ln_w", [NL, 512])
    conv_ln_b = din("conv_ln_b", [NL, 512])
    conv_pw_w = din("conv_pw_w", [NL, 512, 512])
    conv_pw_b = din("conv_pw_b", [NL, 512])
    w_out = din("w_out", [NL, 2048, D])
    final_norm_w = din("final_norm_w", [1, D])
    cst = din("cst", [128, 7, 128])
    sel_in = din("sel", [128, 2])
    groups = [[2 * i, 2 * i + 1] for i in range(npairs)]
    rope = din("rope", [Ltot, 256])
    out = nc.dram_tensor("out", [L, D], F32, kind="ExternalOutput").ap()

    XR = dsc("XR", [Ltot, D], F32)
    XBCT = dsc("XBCT", [1536, Wx], BF16)
    GLUT = dsc("GLUT", [512, Wg], BF16)
    GCT = dsc("GCT", [512, Ltot], BF16)
    ZS = dsc("ZS", [Ltot, 1024], BF16)
    GAT = dsc("GAT", [512, Ltot], BF16)
    DTS = dsc("DTS", [Ltot, 32], F32)
    DTT = dsc("DTT", [32, Ltot], F32)
    QT = dsc("QT", [64, 8, Ltot], BF16)
    KT = dsc("KT", [64, 2, Ltot], BF16)
    VV = dsc("VV", [Ltot, 130], BF16)
    BCT = dsc("BCT", [128, 4, Ltot], BF16)
    XS = dsc("XS", [Ltot, 1024], BF16)
    BTM = dsc("BTM", [Ltot, 256], BF16)
    ABR = dsc("ABR", [2, 2, 6, NCH, 2048], BF16)
    SBD = dsc("SBD", [NCH, 128, 1024], BF16)
    OT = dsc("OT", [2048, Ltot], BF16)
    MOD = dsc("MOD", [NL, 2, 3 * D], F32)
    SFD = dsc("SFD", [NCH, 128, 1024], BF16)
    SND1 = dsc("SND1", [128, 470], BF16)
    RCV1 = nc.dram_tensor("RCV1", [256, 470], BF16, kind="Internal", addr_space="Local").ap()
    SND2 = dsc("SND2", [128, 1024], F32)
    RCV2 = nc.dram_tensor("RCV2", [256, 1024], F32, kind="Internal", addr_space="Local").ap()

    with ExitStack() as es0:
        kb = KB(nc, es0)

        uid = [0]

        def SB(es, name, shape, dt):
            uid[0] += 1
            return es.enter_context(nc.sbuf_tensor("%s_%d" % (name, uid[0]), list(shape), dt))

        def PS(es, name, shape, dt):
            uid[0] += 1
            return es.enter_context(nc.psum_tensor("%s_%d" % (name, uid[0]), list(shape), dt))

        cf = SB(es0, "cf", [128, 7, 128], F32)
        selB = SB(es0, "selB", [128, 2], F32)
        kb.DMA("sp", selB[:], sel_in[:, :])
        kb.DMA("sp", cf[:], cst[:, :, :])
        identf = cf[:, 0, :]
        triU = cf[:, 1, :]
        triL = cf[:, 2, :]
        onesf = cf[:, 3, :]
        cb_ = SB(es0, "cb", [128, 7, 128], BF16)
        kb.CP("dve", cb_[:], cf[:])
        identb = cb_[:, 0, :]
        onesb = cb_[:, 3, :]
        mneg = SB(es0, "mneg", [128, 3, 4, 128], BF16)
        for d_ in range(3):
            for r_ in range(4):
                kb.CP("dve", mneg[:, d_, r_, :], cf[:, 4 + d_, :])
        ones512 = SB(es0, "ones512", [128, 128], BF16)
        kb.MSET("dve", ones512[:], 1.0 / 512.0)
        zpad = SB(es0, "zpad", [128, 64], BF16)
        kb.MSET("dve", zpad[:], 0.0)
        for s in seqs:
            for side in range(2):
                cx = s["xoff"] - 2 if side == 0 else s["xoff"] + s["n"]
                kb.DMA("sp", XBCT[:, cx:cx + 2].rearrange("(b p) w -> p b w", p=128), zpad[:, 0:24].rearrange("p (b w) -> p b w", w=2), key=s["name"])
                cg = s["goff"] - 15 if side == 0 else s["goff"] + s["n"]
                kb.DMA("sp", GLUT[:, cg:cg + 15].rearrange("(b p) w -> p b w", p=128), zpad[:, 0:60].rearrange("p (b w) -> p b w", w=15), key=s["name"])
        onesrow = SB(es0, "onesrow", [128, 2048], BF16)
        kb.MSET("pool", onesrow[:], 1.0)
        for s in seqs:
            c0_, n0_ = s["off"] // 128, s["n"] // 128
            for d_ in range(2):
                for r_ in range(3):
                    kb.DMA("sp", ABR[d_, 0, 3 + r_, c0_:c0_ + n0_, :], onesrow[0:n0_, :], key=s["name"])
                    kb.DMA("sp", ABR[d_, 1, r_, c0_:c0_ + n0_, :], onesrow[0:n0_, :], key=s["name"])
        epsb = SB(es0, "epsb", [128, 1], F32)
        kb.MSET("pool", epsb[:], EPS)

        for l in range(NL):
            last = (l == NL - 1)
            with scope(kb) as esl:
                kb.mark("setup%d" % l)
                modB = SB(esl, "modB", [128, 2, 3, D], F32)
                aB = SB(esl, "aB", [128, 32], F32)
                dtbB = SB(esl, "dtbB", [128, 32], F32)
                snwB = SB(esl, "snwB", [128, 1024], F32)
                sinkB = SB(esl, "sinkB", [128, 8], F32)
                colsT = SB(esl, "colsT", [128, 12, 41], F32)
                dI = SB(esl, "dI", [128, 16, 128], BF16)
                nbr1 = SB(esl, "nbr1", [128, 470], BF16)
                nx = nbr1[:, 0:24].rearrange("p (b w) -> p b w", w=2)
                ng = nbr1[:, 24:84].rearrange("p (b w) -> p b w", w=15)
                with scope(kb) as ess:
                    pss = PS(ess, "pss", [128, 512], F32)
                    csil = SB(ess, "csil", [128, 2, 8], F32)
                    kb.DMA("sp", csil[:, 0, :], cc_in[:, :])
                    kb.DMA("sp", csil[:, 1, :], c_in[:, :])
                    kb.ACT(csil[:], csil[:], AF.Silu)
                    bm = SB(ess, "bm", [1, 3 * D], F32)
                    kb.DMA("sp", bm[:], b_mod[l:l + 1, :])
                    modrow = SB(ess, "modrow", [1, 2, 3 * D], F32)
                    wmA = [SB(ess, "wmA%d" % i, [128, 4, 512], F32) for i in range(2)]
                    wmB = [SB(ess, "wmB%d" % i, [128, 4, 512], F32) for i in range(2)]
                    for cbk in range(6):
                        wa_, wb_ = wmA[cbk % 2], wmB[cbk % 2]
                        wsrc_ = w_mod[l, :, cbk * 512:(cbk + 1) * 512].rearrange("(p k) n -> p k n", k=8)
                        kb.DMA("sp", wa_[:], wsrc_[:, 0:4, :])
                        kb.DMA("act", wb_[:], wsrc_[:, 4:8, :])
                        for m in range(2):
                            for kt in range(8):
                                kb.MM(pss[0:1, :], csil[:, m, kt:kt + 1], (wa_ if kt < 4 else wb_)[:, kt % 4, :], start=(kt == 0), stop=(kt == 7))
                            kb.TT("dve", modrow[:, m, cbk * 512:(cbk + 1) * 512], pss[0:1, :], bm[:, cbk * 512:(cbk + 1) * 512], ALU.add)
                    kb.DMA("sp", MOD[l:l + 1, :, :], modrow[:])
                    nwB = SB(ess, "nwB", [128, D], F32)
                    kb.DMA("sp", nwB[:], norm_w[l, :].partition_broadcast(128))
                    for m in range(2):
                        kb.DMA("sp", modB[:, m, :, :], MOD[l, m, :].rearrange("(t d) -> t d", t=3).partition_broadcast(128))
                        tmpw = SB(ess, "tmpw%d" % m, [128, D], F32)
                        kb.STT(tmpw[:], modB[:, m, 1, :], 1.0, nwB[:], ALU.add, ALU.mult)
                        kb.CP("dve", modB[:, m, 1, :], modB[:, m, 0, :])
                        kb.CP("dve", modB[:, m, 0, :], tmpw[:])
                    kb.DMA("sp", aB[:], ssd_a_log[l, :].partition_broadcast(128))
                    kb.ACT(aB[:], aB[:], AF.Exp)
                    kb.TS("dve", aB[:], aB[:], -1.0, None, ALU.mult)
                    kb.DMA("sp", dtbB[:], ssd_dt_bias[l, :].partition_broadcast(128))
                    kb.DMA("sp", snwB[:], ssd_norm_w[l, :].partition_broadcast(128))
                    kb.DMA("sp", sinkB[:], attn_sink[l, :].partition_broadcast(128))
                    kb.ACT(sinkB[:], sinkB[:], AF.Exp)
                    dB = SB(ess, "dB", [128, 16], F32)
                    kb.DMA("sp", dB[:], ssd_d[l, :].partition_broadcast(128))
                    for h in range(16):
                        kb.TS("dve", dI[:, h, :], identf, dB[:, h:h + 1], None, ALU.mult)
                    rows = SB(ess, "rows", [41, 1536], F32)
                    kb.MSET("dve", rows[:], 0.0)
                    kb.DMA("sp", rows[0:5, :], ssd_conv_w[l, :, :])
                    kb.DMA("sp", rows[5:6, :], ssd_conv_b[l:l + 1, :])
                    kb.DMA("sp", rows[6:37, 0:512], conv_dw_w[l, :, :])
                    kb.DMA("sp", rows[37:38, 0:512], conv_dw_b[l:l + 1, :])
                    kb.DMA("sp", rows[38:39, 0:512], conv_ln_w[l:l + 1, :])
                    kb.DMA("sp", rows[39:40, 0:512], conv_ln_b[l:l + 1, :])
                    kb.DMA("sp", rows[40:41, 0:512], conv_pw_b[l:l + 1, :])
                    for b in range(12):
                        kb.TR(pss[:, b * 41:(b + 1) * 41], rows[:, b * 128:(b + 1) * 128], identf[0:41, 0:41], signal=(b == 11))
                    kb.CP("dve", colsT[:].rearrange("p b r -> p (b r)"), pss[:, 0:492])

                kb.mark("A%d" % l)
                sts = []
                for s in seqs:
                    TS_ = 512 if s["name"] == "l" else 256
                    for t0 in range(0, s["n"], TS_):
                        sts.append((s, t0, TS_))
                with scope(kb) as esa:
                    wfm = SB(esa, "wfm", [128, 8, 3584], BF16)
                    wtm = SB(esa, "wtm", [128, 8, 1824], BF16)
                    wv = w_in[l].rearrange("(k p) n -> p k n", p=128)
                    with scope(kb) as esw:
                        wstA = [SB(esw, "wstA%d" % i, [128, 2592], F32) for i in range(2)]
                        wstB = [SB(esw, "wstB%d" % i, [128, 2816], F32) for i in range(2)]
                        ce = ["dve", "pool", "dve"]
                        nce = 0
                        for kt in range(8):
                            wa_, wb_ = wstA[kt % 2], wstB[kt % 2]
                            kb.DMA("sp", wa_[:], wv[:, kt, 0:2592])
                            kb.DMA("act", wb_[:], wv[:, kt, 2592:5408])
                            segs = [(wfm, 0, 0, 768), (wfm, 768, 768, 768), (wfm, 2560, 4896, 512),
                                    (wtm, 0, 1536, 512), (wtm, 512, 2048, 512), (wtm, 1024, 2592, 512), (wtm, 1536, 3104, 256),
                                    (wtm, 1792, 2560, 32), (wfm, 3072, 3360, 512)]
                            for i in range(4):
                                segs.append((wfm, 1536 + 256 * i, 3872 + 128 * i, 128))
                                segs.append((wfm, 1536 + 256 * i + 128, 4384 + 128 * i, 128))
                            for (dstt, d0, s0, n_) in segs:
                                assert s0 + n_ <= 2592 or s0 >= 2592
                                src_ = wa_[:, s0:s0 + n_] if s0 < 2592 else wb_[:, s0 - 2592:s0 - 2592 + n_]
                                kb.CP(ce[nce % 3], dstt[:, kt, d0:d0 + n_], src_)
                                nce += 1
                    xin = [SB(esa, "xin%d" % i, [128, D], F32) for i in range(3)]
                    hb = [SB(esa, "hb%d" % i, [128, D], BF16) for i in range(2)]
                    junk = SB(esa, "junk", [128, D], BF16)
                    tmpf = SB(esa, "tmpf", [128, D], F32)
                    hT = [SB(esa, "hT%d" % i, [128, 8, 512], BF16) for i in range(2)]
                    st4 = SB(esa, "st4", [128, 8], F32)
                    fst = [SB(esa, "fst%d" % i, [128, 512], BF16) for i in range(3)]
                    sig = SB(esa, "sig", [128, 512], F32)
                    zs = [SB(esa, "zs%d" % i, [128, 1024], BF16) for i in range(2)]
                    qk = [SB(esa, "qk%d" % i, [128, 640], BF16) for i in range(2)]
                    vst = [SB(esa, "vst%d" % i, [128, 2, 65], BF16) for i in range(2)]
                    for i in range(2):
                        kb.MSET("pool", vst[i][:], 1.0)
                    qkT = [SB(esa, "qkT%d" % i, [64, 10, 128], BF16) for i in range(2)]
                    dtw = [SB(esa, "dtw%d" % i, [128, 6, 32], F32) for i in range(2)]
                    dtT = [SB(esa, "dtT%d" % i, [32, 128], F32) for i in range(2)]
                    rp = [SB(esa, "rp%d" % i, [128, 256], F32) for i in range(2)]
                    rt = SB(esa, "rt", [128, 2, 640], F32)
                    pT = [PS(esa, "pT%d" % i, [128, 1024], BF16) for i in range(2)]
                    ppool = [PS(esa, "pp%d" % i, [128, 512], F32) for i in range(6)]
                    subs = [(si, j) for si, (s, t0, TS_) in enumerate(sts) for j in range(TS_ // 128)]
                    cntA = dict(fm=0, tm=0, sc=0)

                    def a_load(k):
                        si, j = subs[k]
                        s, t0, TS_ = sts[si]
                        if l == 0:
                            src = (ctx_in if s["name"] == "c" else x_in)[t0 + j * 128:t0 + (j + 1) * 128, :]
                        else:
                            src = XR[s["off"] + t0 + j * 128:s["off"] + t0 + (j + 1) * 128, :]
                        kb.DMA("sp", xin[k % 3][:], src, key=(s["name"], t0 + j * 128))

                    def a_norm(k):
                        si, j = subs[k]
                        s, t0, TS_ = sts[si]
                        m = 0 if s["name"] == "c" else 1
                        xi, hbj = xin[k % 3], hb[k % 2]
                        col = st4[:, k % 8:k % 8 + 1]
                        kb.ACT(junk[:], xi[:], AF.Square, accum=col)
                        kb.TS("dve", col, col, 1.0 / D, EPS, ALU.mult, ALU.add)
                        kb.ACT(col, col, AF.Sqrt)
                        kb.RECIP(col, col)
                        kb.STT(tmpf[:], xi[:], col, modB[:, m, 0, :], ALU.mult, ALU.mult)
                        kb.TT("pool", hbj[:], tmpf[:], modB[:, m, 1, :], ALU.add)

                    smallq = []

                    def drain(n):
                        while n > 0 and smallq:
                            smallq.pop(0)()
                            n -= 1

                    def a_transp(k, defer=True):
                        si, j = subs[k]
                        hTt, p_, hbj = hT[si % 2], pT[k % 2], hb[k % 2]

                        def piece(q4):
                            for kt in range(2 * q4, 2 * q4 + 2):
                                kb.TR(p_[:, kt * 128:(kt + 1) * 128], hbj[:, kt * 128:(kt + 1) * 128], identb, signal=(kt == 7))
                            if q4 == 3:
                                kb.CP("act" if k % 2 else "dve", hTt[:, :, j * 128:(j + 1) * 128], p_[:].rearrange("p (k t) -> p k t", k=8))
                        for q4 in range(4):
                            if defer:
                                smallq.append(lambda q4=q4: piece(q4))
                            else:
                                piece(q4)

                    def a_fmblock(si, cbk):
                        s, t0, TS_ = sts[si]
                        hTt = hT[si % 2]
                        p_ = ppool[cntA["tm"] % 6]
                        cntA["tm"] += 1
                        for kt in range(8):
                            kb.MM(p_[:, 0:TS_], wfm[:, kt, cbk * 128:(cbk + 1) * 128], hTt[:, kt, 0:TS_], start=(kt == 0), stop=(kt == 7))
                        drain(1)
                        return p_

                    def a_fmunit(si, u):
                        s, t0, TS_ = sts[si]
                        g0 = s["off"] + t0
                        if u < 12:
                            p_ = a_fmblock(si, u)
                            f_ = fst[u % 3]
                            kb.CP("dve", f_[:, 0:TS_], p_[:, 0:TS_])
                            kb.DMA("sp", XBCT[u * 128:(u + 1) * 128, s["xoff"] + t0:s["xoff"] + t0 + TS_], f_[:, 0:TS_], key=s["name"])
                        elif u < 16:
                            i = u - 12
                            pv = a_fmblock(si, 12 + 2 * i)
                            pg = a_fmblock(si, 13 + 2 * i)
                            kb.ACT(sig[:, 0:TS_], pg[:, 0:TS_], AF.Sigmoid)
                            f_ = fst[i % 3]
                            kb.TT("dve", f_[:, 0:TS_], pv[:, 0:TS_], sig[:, 0:TS_], ALU.mult)
                            kb.DMA("sp", GLUT[i * 128:(i + 1) * 128, s["goff"] + t0:s["goff"] + t0 + TS_], f_[:, 0:TS_], key=s["name"])
                        else:
                            i = u - 16
                            p_ = a_fmblock(si, 20 + i)
                            f_ = fst[i % 3]
                            kb.ACT(f_[:, 0:TS_], p_[:, 0:TS_], AF.Silu)
                            dstT = GCT if i < 4 else GAT
                            kb.DMA("sp", dstT[(i % 4) * 128:(i % 4 + 1) * 128, g0:g0 + TS_], f_[:, 0:TS_], key=s["name"])

                    def a_tmsub(si, j):
                        s, t0, TS_ = sts[si]
                        hTt = hT[si % 2]
                        gt = s["off"] + t0 + j * 128
                        sl = cntA["sc"] % 2
                        cntA["sc"] += 1

                        def tm_block(c0, c1):
                            p_ = ppool[cntA["tm"] % 6]
                            cntA["tm"] += 1
                            for kt in range(8):
                                kb.MM(p_[:, 0:c1 - c0], hTt[:, kt, j * 128:(j + 1) * 128], wtm[:, kt, c0:c1], start=(kt == 0), stop=(kt == 7))
                            drain(1)
                            return p_
                        kb.DMA("sp", rp[sl][:], rope[gt:gt + 128, :])
                        for i in range(2):
                            p_ = tm_block(i * 512, (i + 1) * 512)
                            kb.ACT(zs[sl][:, i * 512:(i + 1) * 512], p_[:], AF.Silu)
                        kb.DMA("sp", ZS[gt:gt + 128, :], zs[sl][:], key=s["name"])
                        pq = tm_block(1024, 1536)
                        pk = tm_block(1536, 1824)
                        for (p_, nh, o0, tb) in ((pq, 8, 0, 128), (pk, 2, 512, 0)):
                            srcv = p_[:, 0:nh * 64]
                            a_ = rt[:, 0, 0:nh * 64]
                            b_ = rt[:, 1, 0:nh * 64]
                            cos2 = rp[sl][:, tb:tb + 64].unsqueeze(1).to_broadcast([128, nh, 64])
                            kb.TT("dve", a_.rearrange("p (h d) -> p h d", h=nh), srcv.rearrange("p (h d) -> p h d", h=nh), cos2, ALU.mult)
                            s4 = srcv.rearrange("p (h a x f) -> p h a x f", h=nh, a=2, x=2)
                            b4 = b_.rearrange("p (h a x f) -> p h a x f", h=nh, a=2, x=2)
                            sm = rp[sl][:, tb + 64:tb + 96].rearrange("p (a f) -> p a f", a=2).unsqueeze(1).to_broadcast([128, nh, 2, 16])
                            spl = rp[sl][:, tb + 96:tb + 128].rearrange("p (a f) -> p a f", a=2).unsqueeze(1).to_broadcast([128, nh, 2, 16])
                            kb.TT("dve", b4[:, :, :, 0, :], s4[:, :, :, 1, :], sm, ALU.mult)
                            kb.TT("dve", b4[:, :, :, 1, :], s4[:, :, :, 0, :], spl, ALU.mult)
                            kb.TT("pool", qk[sl][:, o0:o0 + nh * 64], a_, b_, ALU.add)
                        kb.CP("act", vst[sl][:, :, 0:64], pk[:, 128:256].rearrange("p (g d) -> p g d", g=2))
                        kb.DMA("sp", VV[gt:gt + 128, :], vst[sl][:].rearrange("p g d -> p (g d)"), key=s["name"])
                        dw = dtw[sl]
                        kb.TT("dve", dw[:, 0, :], pk[:, 256:288], dtbB[:], ALU.add)
                        kb.ACT(dw[:, 1, :], dw[:, 0, :], AF.Abs)
                        kb.ACT(dw[:, 2, :], dw[:, 1, :], AF.Exp, scale=-1.0)
                        kb.ACT(dw[:, 3, :], dw[:, 2, :], AF.Ln, bias=1.0)
                        kb.STT(dw[:, 5, :], dw[:, 0, :], 0.0, dw[:, 3, :], ALU.max, ALU.add)
                        kb.TS("dve", dw[:, 4, :], dw[:, 5, :], 1e-30, None, ALU.max)
                        kb.DMA("sp", DTS[gt:gt + 128, :], dw[:, 4, :], key=s["name"])
                        return (s, gt, sl, j)

                    def a_tmsub2(arg):
                        s, gt, sl, j = arg
                        dw = dtw[sl]
                        p_ = pT[j % 2]
                        p2 = pT[(j + 1) % 2]

                        def pq(q4):
                            for h in range(2 * q4, 2 * q4 + 2):
                                kb.TR(p_[0:64, h * 128:(h + 1) * 128], qk[sl][:, h * 64:(h + 1) * 64], identb, signal=(h == 7))
                            if q4 == 3:
                                kb.CP("act", qkT[sl][:, 0:8, :], p_[0:64, :].rearrange("p (h t) -> p h t", h=8))
                                kb.DMA("sp", QT[:, :, gt:gt + 128], qkT[sl][:, 0:8, :], key=s["name"])

                        def pk2():
                            for h in range(2):
                                kb.TR(p2[0:64, h * 128:(h + 1) * 128], qk[sl][:, 512 + h * 64:512 + (h + 1) * 64], identb, signal=(h == 1))
                            kb.CP("dve", qkT[sl][:, 8:10, :], p2[0:64, 0:256].rearrange("p (h t) -> p h t", h=2))
                            kb.DMA("sp", KT[:, :, gt:gt + 128], qkT[sl][:, 8:10, :], key=s["name"])
                            p3 = ppool[cntA["tm"] % 6]
                            cntA["tm"] += 1
                            kb.TR(p3[0:32, 0:128], dw[:, 4, :], identf)
                            kb.CP("dve", dtT[sl][:], p3[0:32, 0:128])
                            kb.DMA("sp", DTT[:, gt:gt + 128], dtT[sl][:], key=s["name"])
                        for q4 in range(4):
                            smallq.append(lambda q4=q4: pq(q4))
                        smallq.append(pk2)

                    nsubs = len(subs)
                    ks_of = [[k for k, (si2, j) in enumerate(subs) if si2 == si] for si in range(len(sts))]
                    pend2 = [None]
                    a_load(0)
                    for k in ks_of[0]:
                        if k + 1 < nsubs:
                            a_load(k + 1)
                        a_norm(k)
                        a_transp(k, defer=False)
                    for si in range(len(sts)):
                        nsub = len(ks_of[si])
                        nxt = ks_of[si + 1] if si + 1 < len(sts) else []
                        per = 24 // nsub
                        for q in range(nsub):
                            kn = nxt[q] if q < len(nxt) else None
                            if kn is not None:
                                if kn + 1 < nsubs:
                                    a_load(kn + 1)
                                a_norm(kn)
                            if pend2[0] is not None:
                                a_tmsub2(pend2[0])
                                pend2[0] = None
                            if kn is not None:
                                a_transp(kn)
                            for u in range(q * per, (q + 1) * per):
                                a_fmunit(si, u)
                            pend2[0] = a_tmsub(si, q)
                            drain(100)
                        for q in range(nsub, len(nxt)):
                            kn = nxt[q]
                            if kn + 1 < nsubs:
                                a_load(kn + 1)
                            a_norm(kn)
                            a_transp(kn, defer=False)
                    a_tmsub2(pend2[0])
                    drain(100)

                with scope(kb) as ex:
                    sL = seqs[1]
                    end_x, end_g, end_t = sL["xoff"] + sL["n"], sL["goff"] + sL["n"], sL["off"] + sL["n"]
                    snd = SB(ex, "snd", [128, 470], BF16)
                    kb.MSET("dve", snd[:], 0.0)
                    kb.DMA("sp", snd[:, 0:24].rearrange("p (b w) -> p b w", w=2), XBCT[:, end_x - 2:end_x].rearrange("(b p) w -> p b w", p=128), key="l")
                    kb.DMA("sp", snd[:, 24:84].rearrange("p (b w) -> p b w", w=15), GLUT[:, end_g - 15:end_g].rearrange("(b p) w -> p b w", p=128), key="l")
                    kb.DMA("sp", snd[0:64, 84:340].rearrange("p (g t) -> p g t", g=2), KT[:, :, end_t - 128:end_t], key="l")
                    kb.DMA("sp", snd[:, 340:470], VV[end_t - 128:end_t, :], key="l")
                    kb.DMA("sp", SND1[:, :], snd[:])
                    kb.CC(SND1[:, :], RCV1[:, :], groups)
                    rcv = SB(ex, "rcv", [128, 2, 470], BF16)
                    kb.DMA("sp", rcv[:], RCV1.rearrange("(r p) c -> p r c", p=128))
                    tmpx = SB(ex, "tmpx", [128, 470], F32)
                    kb.TS("dve", tmpx[:], rcv[:, 0, :], selB[:, 0:1], None, ALU.mult)
                    kb.STT(nbr1[:], rcv[:, 1, :], selB[:, 1:2], tmpx[:], ALU.mult, ALU.add)

                with scope(kb) as esb:
                    kb.mark("B0%d" % l)
                    kb.mark("B1%d" % l)
                    with scope(kb) as e1:
                        dg = SB(e1, "dg", [128, 60, 128], BF16)
                        for b in range(12):
                            for j in range(5):
                                if (b + j) % 2:
                                    kb.TS("dve", dg[:, b * 5 + j, :], identf, colsT[:, b, j:j + 1], None, ALU.mult)
                                else:
                                    kb.ACT(dg[:, b * 5 + j, :], identf, AF.Copy, scale=colsT[:, b, j:j + 1])
                        win = [SB(e1, "win%d" % i, [128, 12, 516], BF16) for i in range(2)]
                        xc = [SB(e1, "xc%d" % i, [128, 12, 512], BF16) for i in range(2)]
                        xst = [SB(e1, "xst%d" % i, [128, 1280], BF16) for i in range(2)]
                        pcv = [PS(e1, "pcv%d" % i, [128, 512], F32) for i in range(2)]
                        ptr = [PS(e1, "ptr%d" % i, [128, 1024], BF16) for i in range(2)]
                        ptb = [PS(e1, "ptb%d" % i, [128, 256], BF16) for i in range(2)]
                        cntB = dict(j=0)

                        def b1_s0(i):
                            s, t0, TS_ = sts[i]
                            if i == len(sts) - 1:
                                kb.DMA("sp", win[i % 2][:, :, 0:TS_ + 2], XBCT[:, s["xoff"] + t0 - 2:s["xoff"] + t0 + TS_].rearrange("(b p) w -> p b w", p=128), key=s["name"])
                                for r in range(2):
                                    kb.CP("pool", win[i % 2][:, :, TS_ + 2 + r:TS_ + 3 + r], nx[:, :, 1 - r:2 - r])
                            else:
                                kb.DMA("sp", win[i % 2][:, :, 0:TS_ + 4], XBCT[:, s["xoff"] + t0 - 2:s["xoff"] + t0 + TS_ + 2].rearrange("(b p) w -> p b w", p=128), key=s["name"])

                        def b1_s1(i):
                            s, t0, TS_ = sts[i]
                            w_, x_ = win[i % 2], xc[i % 2]
                            g0 = s["off"] + t0
                            for b in range(12):
                                p_ = pcv[b % 2]
                                for j in range(5):
                                    kb.MM(p_[:, 0:TS_], dg[:, b * 5 + j, :], w_[:, b, j:j + TS_], start=(j == 0), stop=(j == 4))
                                kb.ACT(x_[:, b, 0:TS_], p_[:, 0:TS_], AF.Silu, bias=colsT[:, b, 5:6])
                            kb.DMA("sp", BCT[:, :, g0:g0 + TS_], x_[:, 8:12, 0:TS_], key=s["name"])

                        def b1_s2(i):
                            s, t0, TS_ = sts[i]
                            x_ = xc[i % 2]
                            g0 = s["off"] + t0
                            for j in range(TS_ // 128):
                                gt = g0 + j * 128
                                nj = cntB["j"]
                                cntB["j"] += 1
                                p_, pb_, xs_ = ptr[nj % 2], ptb[nj % 2], xst[nj % 2]
                                for b in range(8):
                                    kb.TR(p_[:, b * 128:(b + 1) * 128], x_[:, b, j * 128:(j + 1) * 128], identb, signal=(b == 7))
                                for b in range(2):
                                    kb.TR(pb_[:, b * 128:(b + 1) * 128], x_[:, 8 + b, j * 128:(j + 1) * 128], identb, signal=(b == 1))
                                kb.CP("dve", xs_[:, 0:1024], p_[:])
                                kb.CP("pool" if False else "dve", xs_[:, 1024:1280], pb_[:])
                                kb.DMA("sp", XS[gt:gt + 128, :], xs_[:, 0:1024], key=s["name"])
                                kb.DMA("sp", BTM[gt:gt + 128, :], xs_[:, 1024:1280], key=s["name"])
                        rmask = SB(e1, "rmask", [32, 16, 128], F32)
                        kb.MSET("dve", rmask[:], 1.0)
                        kb.MSET("dve", rmask[:, :, 0:1], 0.0)
                        dr = SB(e1, "dr", [32, 16, 128], F32)
                        da = SB(e1, "da", [32, 16, 128], F32)
                        cs = SB(e1, "cs", [32, 16, 128], F32)
                        ld = SB(e1, "ld", [32, 16, 128], F32)
                        al = SB(e1, "al", [32, 16, 128], F32)
                        r1 = SB(e1, "r1", [32, 16, 128], F32)
                        sp3 = SB(e1, "sp3", [32, 3, 2048], BF16)
                        assert NCH <= 32
                        b0_items = [0, 1]

                        def b0_piece(d_):
                            n_ = NCH
                            for s in seqs:
                                c0, nch = s["off"] // 128, s["n"] // 128
                                kb.DMA("sp", dr[c0:c0 + nch], DTT[d_ * 16:(d_ + 1) * 16, s["off"]:s["off"] + s["n"]].rearrange("h (c t) -> c h t", t=128), key=s["name"])
                            kb.TT("dve", da[0:n_], dr[0:n_], bc(aB[0:n_, d_ * 16:(d_ + 1) * 16], [n_, 16, 128]), ALU.mult)
                            kb.op("dve", lambda g: g.tensor_tensor_scan(out=cs[0:n_].rearrange("c h t -> c (h t)"), data0=rmask[0:n_].rearrange("c h t -> c (h t)"),
                                                                          data1=da[0:n_].rearrange("c h t -> c (h t)"), initial=0.0, op0=ALU.mult, op1=ALU.add),
                                  [rmask[0:n_], da[0:n_]], [cs[0:n_]])
                            if d_ == 1:
                                kb.TT("dve", r1[0:n_], da[0:n_], cs[0:n_], ALU.subtract)
                                kb.CP("dve", al[0:n_, :, 0:1], cs[0:n_, :, 127:128])
                                kb.TT("dve", cs[0:n_], r1[0:n_], al[0:n_, :, 0:1].to_broadcast([n_, 16, 128]), ALU.add)
                            kb.ACT(ld[0:n_], dr[0:n_], AF.Ln)
                            kb.TT("dve", al[0:n_], ld[0:n_], cs[0:n_], ALU.subtract)
                            for side, srcx in ((0, al), (1, cs)):
                                sx = srcx[0:n_].rearrange("c h t -> c (h t)")
                                r1f = r1[0:n_].rearrange("c h t -> c (h t)")
                                kb.CP("act", sp3[0:n_, 0, :], sx)
                                kb.TT("dve", r1f, sx, sp3[0:n_, 0, :], ALU.subtract)
                                kb.CP("act", sp3[0:n_, 1, :], r1f)
                                kb.TT("dve", r1f, r1f, sp3[0:n_, 1, :], ALU.subtract)
                                kb.CP("act", sp3[0:n_, 2, :], r1f)
                                for r_ in range(3):
                                    for s in seqs:
                                        c0, nch = s["off"] // 128, s["n"] // 128
                                        kb.DMA("sp", ABR[d_, side, 3 * side + r_, c0:c0 + nch, :], sp3[c0:c0 + nch, r_, :], key=s["name"])

                        nb0 = 0
                        for t_ in range(len(sts) + 2):
                            for k_, st_ in enumerate([b1_s0, b1_s1, b1_s2]):
                                if 0 <= t_ - k_ < len(sts):
                                    st_(t_ - k_)
                            if t_ in (1, 3):
                                b0_piece(nb0)
                                nb0 += 1
                        assert nb0 == 2

                    with scope(kb) as e2:
                        Sf = SB(e2, "Sf", [128, 1024], F32)
                        Sb_ = SB(e2, "Sb", [128, 1024], F32)
                        Sfb = SB(e2, "Sfb", [128, 1024], BF16)
                        Sbb = [SB(e2, "Sbb%d" % i, [128, 1024], BF16) for i in range(2)]
                        kb.MSET("dve", Sf[:], 0.0)
                        kb.MSET("dve", Sb_[:], 0.0)
                        kb.MSET("pool", Sfb[:], 0.0)
                        xsb = [SB(e2, "xsb%d" % i, [128, 1024], BF16) for i in range(3)]
                        btm = [SB(e2, "btm%d" % i, [128, 256], BF16) for i in range(3)]
                        dts = [SB(e2, "dts%d" % i, [128, 32], F32) for i in range(3)]
                        xdw = [SB(e2, "xdw%d" % i, [128, 1024], BF16) for i in range(2)]
                        sm_ = [SB(e2, "sm%d" % i, [128, 8, 32], F32) for i in range(2)]
                        psm = PS(e2, "psm", [128, 512], F32)
                        pO = PS(e2, "pO", [128, 1024], F32)

                        def small_terms(sl, d_, dt_ap):
                            S_ = sm_[sl]
                            o = d_ * 16
                            kb.TT("dve", S_[:, 0, o:o + 16], dt_ap[:, o:o + 16], aB[:, o:o + 16], ALU.mult)
                            kb.MM(psm[:, o:o + 16], triU if d_ == 0 else triL, S_[:, 0, o:o + 16], start=True, stop=True)
                            kb.MM(psm[:, 64 + o:64 + o + 16], onesf, S_[:, 0, o:o + 16], start=True, stop=True)
                            kb.CP("dve", S_[:, 1, o:o + 16], psm[:, o:o + 16])
                            kb.TT("dve", S_[:, 2, o:o + 16], psm[:, 64 + o:64 + o + 16], S_[:, 1, o:o + 16], ALU.subtract)
                            kb.ACT(S_[:, 3, o:o + 16], S_[:, 2, o:o + 16], AF.Exp)
                            kb.TT("dve", S_[:, 3, o:o + 16], S_[:, 3, o:o + 16], dt_ap[:, o:o + 16], ALU.mult)
                            kb.ACT(S_[:, 4, o:o + 16], S_[:, 1, o:o + 16], AF.Exp)
                            kb.ACT(S_[:, 5, o:o + 16], psm[:, 64 + o:64 + o + 16], AF.Exp)
                            return S_[:, 4, o:o + 16], S_[:, 3, o:o + 16], S_[:, 5, o:o + 16]

                        def mk_xdw(xd_t, xs_ap, w_ap):
                            kb.TT("pool", xd_t[:].rearrange("p (h d) -> p h d", h=16), xs_ap.rearrange("p (h d) -> p h d", h=16), bc(w_ap, [128, 16, 64]), ALU.mult)

                        def state_update(Smaster, bt_ap, etot_ap, xd_t):
                            for g in range(2):
                                kb.MM(pO[:, g * 512:(g + 1) * 512], bt_ap[:, g * 128:(g + 1) * 128], xd_t[:, g * 512:(g + 1) * 512], start=True, stop=True)
                            kb.TT("dve", Smaster[:].rearrange("p (h d) -> p h d", h=16), Smaster[:].rearrange("p (h d) -> p h d", h=16), bc(etot_ap, [128, 16, 64]), ALU.mult)
                            kb.TT("dve", Smaster[:], Smaster[:], pO[:], ALU.add)

                        kb.mark("B2%d" % l)
                        chf = []
                        for s in seqs:
                            for c in range(s["n"] // 128):
                                chf.append((s, c))

                        def bf_s0(i):
                            s, c = chf[i]
                            gt = s["off"] + c * 128
                            kb.DMA("sp", xsb[i % 3][:], XS[gt:gt + 128, :], key=s["name"])
                            kb.DMA("sp", btm[i % 3][:], BTM[gt:gt + 128, :], key=s["name"])
                            kb.DMA("sp", dts[i % 3][:], DTS[gt:gt + 128, :], key=s["name"])

                        def bf_s1(i):
                            s, c = chf[i]
                            gc = (s["off"] + c * 128) // 128
                            _, w_ap, et_ap = small_terms(i % 2, 0, dts[i % 3])
                            mk_xdw(xdw[i % 2], xsb[i % 3][:], w_ap)
                            kb.CP("act", Sbb[i % 2][:], Sf[:])
                            kb.DMA("sp", SFD[gc, :, :], Sbb[i % 2][:], key=gc)
                            state_update(Sf, btm[i % 3], et_ap, xdw[i % 2])
                        pipe(len(chf), [bf_s0, bf_s1])
                        nbS = SB(e2, "nbS", [128, 1024], F32)
                        with scope(kb) as ex2:
                            rcs2 = SB(ex2, "rcs2", [128, 2, 1024], F32)
                            kb.DMA("sp", SND2[:, :], Sf[:])
                            kb.CC(SND2[:, :], RCV2[:, :], groups)
                            kb.DMA("sp", rcs2[:], RCV2.rearrange("(r p) c -> p r c", p=128))
                            kb.TS("dve", nbS[:], rcs2[:, 0, :], selB[:, 0:1], None, ALU.mult)
                            kb.STT(nbS[:], rcs2[:, 1, :], selB[:, 1:2], nbS[:], ALU.mult, ALU.add)
                        chd = []
                        for s in seqs:
                            for c in reversed(range(s["n"] // 128)):
                                chd.append((s, c))
                        first_lat = seqs[0]["n"] // 128

                        def b2_s0(i):
                            s, c = chd[i]
                            gt = s["off"] + c * 128
                            kb.DMA("sp", xsb[i % 3][:], XS[gt:gt + 128, :], key=s["name"])
                            kb.DMA("sp", btm[i % 3][:], BTM[gt:gt + 128, :], key=s["name"])
                            kb.DMA("sp", dts[i % 3][:], DTS[gt:gt + 128, :], key=s["name"])

                        def b2_s1(i):
                            s, c = chd[i]
                            gc = (s["off"] + c * 128) // 128
                            if i == first_lat:
                                kb.CP("dve", Sb_[:], nbS[:])
                            _, w_ap, et_ap = small_terms(i % 2, 1, dts[i % 3])
                            mk_xdw(xdw[i % 2], xsb[i % 3][:], w_ap)
                            kb.CP("act", Sbb[i % 2][:], Sb_[:])
                            kb.DMA("sp", SBD[gc, :, :], Sbb[i % 2][:], key=gc)
                            state_update(Sb_, btm[i % 3], et_ap, xdw[i % 2])
                        pipe(len(chd), [b2_s0, b2_s1])

                        kb.mark("B3%d" % l)
                        with scope(kb) as e3:
                            bct = [SB(e3, "bct%d" % i, [128, 4, 128], BF16) for i in range(3)]
                            zsb = [SB(e3, "zsb%d" % i, [128, 1024], BF16) for i in range(2)]
                            sbe = [SB(e3, "sbe%d" % i, [128, 1024], BF16) for i in range(2)]
                            sfe = [SB(e3, "sfe%d" % i, [128, 1024], BF16) for i in range(2)]
                            ab6 = [SB(e3, "ab6%d" % i, [6, 2, 2, 2048], BF16) for i in range(2)]
                            cbT = SB(e3, "cbT", [128, 2, 128], BF16)
                            E_ = [SB(e3, "E%d" % i, [128, 4, 128], BF16) for i in range(2)]
                            MT = SB(e3, "MT", [128, 2, 16, 128], BF16)
                            ysb = [SB(e3, "ysb%d" % i, [128, 1024], F32) for i in range(2)]
                            t1 = SB(e3, "t1", [128, 1024], F32)
                            t2 = SB(e3, "t2", [128, 1024], F32)
                            us = [SB(e3, "u%d" % i, [128, 1024], F32) for i in range(2)]
                            obs = [SB(e3, "ob%d" % i, [128, 1024], BF16) for i in range(2)]
                            sq = SB(e3, "sq", [128, 512], F32)
                            oT = [SB(e3, "oT%d" % i, [128, 8, 128], BF16) for i in range(2)]
                            st2 = SB(e3, "st2", [128, 2, 2], F32)
                            pDs = [PS(e3, "pD%d" % i, [128, 512], F32) for i in range(2)]
                            pYs = [PS(e3, "pY%d" % i, [128, 512], F32) for i in range(2)]
                            pX = PS(e3, "pX", [128, 1024], BF16)
                            cntD = dict(d=0)
                            cha = []
                            for s in seqs:
                                for c in range(s["n"] // 128):
                                    cha.append((s, c))

                            def ny(s):
                                return not (last and s["name"] == "c")

                            def b3_s0(i):
                                s, c = cha[i]
                                gt = s["off"] + c * 128
                                gc = gt // 128
                                kb.DMA("sp", xsb[i % 3][:], XS[gt:gt + 128, :], key=s["name"])
                                kb.DMA("sp", dts[i % 3][:], DTS[gt:gt + 128, :], key=s["name"])
                                if ny(s):
                                    kb.DMA("sp", bct[i % 3][:], BCT[:, :, gt:gt + 128], key=s["name"])
                                    kb.DMA("sp", ab6[i % 2][:], ABR[:, :, :, gc, :].rearrange("d s r x -> r d s x"), key=s["name"])

                            def b3_s1(i):
                                s, c = cha[i]
                                gt = s["off"] + c * 128
                                gc = gt // 128
                                sl3, sl = i % 3, i % 2
                                if not ny(s):
                                    return
                                ef, wf, etf = small_terms(sl, 0, dts[sl3])
                                kb.DMA("sp", zsb[sl][:], ZS[gt:gt + 128, :], key=s["name"])
                                kb.DMA("sp", sbe[sl][:], SBD[gc, :, :], key=gc)
                                kb.DMA("sp", sfe[sl][:], SFD[gc, :, :], key=gc)
                                small_terms(sl, 1, dts[sl3])
                                for g in range(2):
                                    kb.MM(psm[:, 256 + g * 128:256 + (g + 1) * 128], bct[sl3][:, g, :], bct[sl3][:, 2 + g, :], start=True, stop=True)
                                kb.CP("dve", cbT[:].rearrange("p g t -> p (g t)"), psm[:, 256:512])
                                for g in range(2):
                                    for hb_ in range(2):
                                        es_ = []
                                        for d_ in range(2):
                                            pD = pDs[cntD["d"] % 2]
                                            e_ = E_[cntD["d"] % 2]
                                            cntD["d"] += 1
                                            kb.MM(pD[:], identb, mneg[:, d_, :, :].rearrange("p r t -> p (r t)"), start=True, stop=False, signal=False)
                                            for hh in range(4):
                                                h = g * 8 + hb_ * 4 + hh
                                                kb.MM(pD[:, hh * 128:(hh + 1) * 128], ab6[sl][:, d_, 0, h * 128:(h + 1) * 128],
                                                      ab6[sl][:, d_, 1, h * 128:(h + 1) * 128], start=False, stop=(hh == 3), signal=(hh == 3))
                                            kb.ACT(e_[:].rearrange("p h t -> p (h t)"), pD[:], AF.Exp)
                                            es_.append(e_)
                                        h0 = g * 8 + hb_ * 4
                                        kb.TT("dve", es_[0][:], es_[0][:], es_[1][:], ALU.add)
                                        kb.TT("dve", MT[:, 0, h0:h0 + 4, :], es_[0][:], cbT[:, g, :].unsqueeze(1).to_broadcast([128, 4, 128]), ALU.mult)
                                    pY = pYs[g]
                                    for hh in range(8):
                                        h = g * 8 + hh
                                        rhs = xsb[sl3][:, h * 64:(h + 1) * 64]
                                        kb.MM(pY[:, hh * 64:(hh + 1) * 64], MT[:, 0, h, :], rhs, start=True, stop=False, signal=False)
                                        kb.MM(pY[:, hh * 64:(hh + 1) * 64], dI[:, h, :], rhs, start=False, stop=True, signal=(hh == 7))
                                    kb.CP("act", ysb[sl][:, g * 512:(g + 1) * 512], pY[:])

                            def b3_s2(i):
                                s, c = cha[i]
                                if not ny(s):
                                    return
                                sl3, sl = i % 3, i % 2
                                S_ = sm_[sl]
                                ef, eb = S_[:, 4, 0:16], S_[:, 4, 16:32]
                                u = us[sl]
                                for g in range(2):
                                    kb.MM(pO[:, g * 512:(g + 1) * 512], bct[sl3][:, 2 + g, :], sfe[sl][:, g * 512:(g + 1) * 512], start=True, stop=True)
                                kb.TT("dve", t1[:].rearrange("p (h d) -> p h d", h=16), pO[:].rearrange("p (h d) -> p h d", h=16), bc(ef, [128, 16, 64]), ALU.mult)
                                for g in range(2):
                                    kb.MM(pO[:, g * 512:(g + 1) * 512], bct[sl3][:, 2 + g, :], sbe[sl][:, g * 512:(g + 1) * 512], start=True, stop=True)
                                kb.TT("dve", t2[:].rearrange("p (h d) -> p h d", h=16), pO[:].rearrange("p (h d) -> p h d", h=16), bc(eb, [128, 16, 64]), ALU.mult)
                                kb.TT("pool", u[:], t1[:], t2[:], ALU.add)
                                kb.TT("pool", u[:], u[:], ysb[sl][:], ALU.add)
                                kb.TT("pool", u[:], u[:], zsb[sl][:], ALU.mult)

                            def b3_s3(i):
                                s, c = cha[i]
                                if not ny(s):
                                    return
                                sl = i % 2
                                u = us[sl]
                                ss_ = st2[:, sl, :]
                                for g in range(2):
                                    kb.STT(sq[:], u[:, g * 512:(g + 1) * 512], 1.0, u[:, g * 512:(g + 1) * 512], ALU.mult, ALU.mult, accum=st2[:, sl, g:g + 1])
                                kb.TS("dve", ss_, ss_, 1.0 / 512.0, EPS, ALU.mult, ALU.add)
                                kb.ACT(ss_, ss_, AF.Ln)
                                kb.ACT(ss_, ss_, AF.Exp, scale=-0.5)
                                for g in range(2):
                                    kb.STT(obs[sl][:, g * 512:(g + 1) * 512], u[:, g * 512:(g + 1) * 512], st2[:, sl, g:g + 1], snwB[:, g * 512:(g + 1) * 512], ALU.mult, ALU.mult)

                            def b3_s4(i):
                                s, c = cha[i]
                                if not ny(s):
                                    return
                                gt = s["off"] + c * 128
                                sl = i % 2
                                for b in range(8):
                                    kb.TR(pX[:, b * 128:(b + 1) * 128], obs[sl][:, b * 128:(b + 1) * 128], identb, signal=(b == 7))
                                kb.CP("act", oT[sl][:].rearrange("p b t -> p (b t)"), pX[:])
                                kb.DMA("sp", OT[0:1024, gt:gt + 128].rearrange("(b p) t -> p b t", p=128), oT[sl][:], key=s["name"])
                            pipe(len(cha), [b3_s0, b3_s1, b3_s2, b3_s3, b3_s4])

                kb.mark("C%d" % l)
                with scope(kb) as ec:
                    ktb = SB(ec, "ktb", [64, 2, Ltot + 128], BF16)
                    vvb = SB(ec, "vvb", [128, NCH + 1, 130], BF16)
                    kb.CP("dve", ktb[:, :, Ltot:Ltot + 128], nbr1[0:64, 84:340].rearrange("p (g t) -> p g t", g=2))
                    kb.CP("dve", vvb[:, NCH, :], nbr1[:, 340:470])
                    for s in seqs:
                        o_, n_ = s["off"], s["n"]
                        kb.DMA("sp", ktb[:, :, o_:o_ + n_], KT[:, :, o_:o_ + n_], key=s["name"])
                        kb.DMA("sp", vvb[:, o_ // 128:(o_ + n_) // 128, :], VV[o_:o_ + n_, :].rearrange("(c p) x -> p c x", p=128), key=s["name"])
                    qtb = [SB(ec, "qtb%d" % i, [64, 8, 128], BF16) for i in range(3)]
                    gab = [SB(ec, "gab%d" % i, [64, 8, 128], BF16) for i in range(3)]
                    Ee = [SB(ec, "Ee%d" % i, [128, 512], BF16) for i in range(20)]
                    den = [SB(ec, "den%d" % i, [64, 512], F32) for i in range(2)]
                    oa = [SB(ec, "oa%d" % i, [64, 512], F32) for i in range(2)]
                    oaT = [SB(ec, "oaT%d" % i, [64, 512], BF16) for i in range(2)]
                    pS = [PS(ec, "pS%d" % i, [128, 512], F32) for i in range(3)]
                    pV = [PS(ec, "pV%d" % i, [64, 512], F32) for i in range(2)]
                    pU = [PS(ec, "pU%d" % i, [64, 512], F32) for i in range(2)]
                    qbs = []
                    for s in seqs:
                        if last and s["name"] == "c":
                            continue
                        for qb in range(s["n"] // 128):
                            qbs.append((s, qb))
                    cntC = dict(s=0, o=0)

                    def keyblocks(s, qb):
                        kbl = [(0, None), (1, None)]
                        if s["name"] == "l":
                            nb = s["n"] // 128
                            c0 = s["off"] // 128
                            if qb > 0:
                                kbl.append((c0 + qb - 1, 1))
                            kbl.append((c0 + qb, None))
                            if qb < nb - 1:
                                kbl.append((c0 + qb + 1, 0))
                            else:
                                kbl.append((NCH, 2))
                        return kbl

                    def c_s0(i):
                        s, qb = qbs[i]
                        gt = s["off"] + qb * 128
                        kb.DMA("sp", qtb[i % 3][:], QT[:, :, gt:gt + 128], key=s["name"])
                        kb.DMA("sp", gab[i % 3][:], GAT[:, gt:gt + 128].rearrange("(h d) t -> d h t", d=64), key=s["name"])

                    def c_s1(i):
                        s, qb = qbs[i]
                        q_ = qtb[i % 3]
                        for g in range(2):
                            for ki, (kc, md) in enumerate(keyblocks(s, qb)):
                                p_ = pS[cntC["s"] % 3]
                                cntC["s"] += 1
                                if md is not None:
                                    kb.MM(p_[:], identb, mneg[:, md, :, :].rearrange("p r t -> p (r t)"), start=True, stop=False, signal=False)
                                kb.MM(p_[:], ktb[:, g, kc * 128:(kc + 1) * 128], q_[:, g * 4:(g + 1) * 4, :].rearrange("p h t -> p (h t)"),
                                      start=(md is None), stop=True)
                                kb.ACT(Ee[(i % 2) * 10 + g * 5 + ki][:], p_[:], AF.Exp)

                    def c_s2(i):
                        s, qb = qbs[i]
                        gt = s["off"] + qb * 128
                        g_ = gab[i % 3]
                        kbl = keyblocks(s, qb)
                        for g in range(2):
                            no = cntC["o"]
                            cntC["o"] += 1
                            pv, pu = pV[no % 2], pU[no % 2]
                            for ki, (kc, md) in enumerate(kbl):
                                kb.MM(pv[:], vvb[:, kc, g * 65:g * 65 + 64], Ee[(i % 2) * 10 + g * 5 + ki][:], start=(ki == 0), stop=(ki == len(kbl) - 1))
                            for ki, (kc, md) in enumerate(kbl):
                                kb.MM(pu[:], onesb[:, 0:64], Ee[(i % 2) * 10 + g * 5 + ki][:], start=(ki == 0), stop=(ki == len(kbl) - 1))
                            dn = den[no % 2]
                            kb.TT("dve", dn[:].rearrange("p (r t) -> p r t", r=4), pu[:].rearrange("p (r t) -> p r t", r=4), bc(sinkB[0:64, g * 4:(g + 1) * 4], [64, 4, 128]), ALU.add)
                            kb.ACT(dn[:], dn[:], AF.Ln)
                            kb.ACT(dn[:], dn[:], AF.Exp, scale=-1.0)
                            o_ = oa[no % 2]
                            kb.TT("dve", o_[:], pv[:], dn[:], ALU.mult)
                            ot_ = oaT[no % 2]
                            kb.TT("pool", ot_[:], o_[:], g_[:, g * 4:(g + 1) * 4, :].rearrange("p h t -> p (h t)"), ALU.mult)
                            kb.DMA("sp", OT[1024 + g * 256:1024 + (g + 1) * 256, gt:gt + 128].rearrange("(r d) t -> d r t", d=64), ot_[:].rearrange("p (r t) -> p r t", r=4), key=s["name"])
                    pipe(len(qbs), [c_s0, c_s1, c_s2])

                kb.mark("D%d" % l)
                with scope(kb) as ed:
                    dgc = SB(ed, "dgc", [128, 124, 128], BF16)
                    for b in range(4):
                        for j in range(31):
                            if (b + j) % 2:
                                kb.TS("dve", dgc[:, b * 31 + j, :], identf, colsT[:, b, 6 + j:7 + j], None, ALU.mult)
                            else:
                                kb.ACT(dgc[:, b * 31 + j, :], identf, AF.Copy, scale=colsT[:, b, 6 + j:7 + j])
                    pww = SB(ed, "pww", [128, 4, 512], BF16)
                    pwst = SB(ed, "pwst", [128, 4, 512], F32)
                    kb.DMA("sp", pwst[:], conv_pw_w[l].rearrange("(k p) n -> p k n", p=128))
                    kb.CP("act", pww[:], pwst[:])
                    gw = [SB(ed, "gw%d" % i, [128, 4, 542], BF16) for i in range(2)]
                    gcb = [SB(ed, "gcb%d" % i, [128, 4, 512], BF16) for i in range(2)]
                    hbb = [SB(ed, "hbb%d" % i, [128, 4, 512], BF16) for i in range(2)]
                    hsq = [SB(ed, "hsq%d" % i, [128, 4, 512], BF16) for i in range(2)]
                    mean = SB(ed, "mean", [128, 512], F32)
                    rstd = SB(ed, "rstd", [128, 512], F32)
                    xcn = [SB(ed, "xcn%d" % i, [128, 512], F32) for i in range(2)]
                    h2 = [SB(ed, "h2%d" % i, [128, 4, 512], BF16) for i in range(2)]
                    oc = [SB(ed, "oc%d" % i, [128, 512], BF16) for i in range(2)]
                    pc = [PS(ed, "pc%d" % i, [128, 512], F32) for i in range(2)]
                    pm = PS(ed, "pm", [128, 512], F32)
                    pe2 = PS(ed, "pe2", [128, 512], F32)
                    pw_ = [PS(ed, "pw%d" % i, [128, 512], F32) for i in range(2)]
                    dst = [x for x in sts if not (last and x[0]["name"] == "c")]

                    def d_s0(i):
                        s, t0, TS_ = dst[i]
                        if i == len(dst) - 1:
                            kb.DMA("sp", gw[i % 2][:, :, 0:TS_ + 15], GLUT[:, s["goff"] + t0 - 15:s["goff"] + t0 + TS_].rearrange("(b p) w -> p b w", p=128), key=s["name"])
                            for r in range(15):
                                kb.CP("pool", gw[i % 2][:, :, TS_ + 15 + r:TS_ + 16 + r], ng[:, :, 14 - r:15 - r])
                        else:
                            kb.DMA("sp", gw[i % 2][:, :, 0:TS_ + 30], GLUT[:, s["goff"] + t0 - 15:s["goff"] + t0 + TS_ + 15].rearrange("(b p) w -> p b w", p=128), key=s["name"])

                    def d_s1(i):
                        s, t0, TS_ = dst[i]
                        w_ = gw[i % 2]
                        for b in range(4):
                            p_ = pc[b % 2]
                            for j in range(31):
                                kb.MM(p_[:, 0:TS_], dgc[:, b * 31 + j, :], w_[:, b, j:j + TS_], start=(j == 0), stop=(j == 30))
                            kb.ACT(hbb[i % 2][:, b, 0:TS_], p_[:, 0:TS_], AF.Identity, bias=colsT[:, b, 37:38])
                            kb.TT("dve", hsq[i % 2][:, b, 0:TS_], hbb[i % 2][:, b, 0:TS_], hbb[i % 2][:, b, 0:TS_], ALU.mult)

                    def d_s2(i):
                        s, t0, TS_ = dst[i]
                        g0 = s["off"] + t0
                        kb.DMA("sp", gcb[i % 2][:, :, 0:TS_], GCT[:, g0:g0 + TS_].rearrange("(b p) w -> p b w", p=128), key=s["name"])
                        hb_, hs_ = hbb[i % 2], hsq[i % 2]
                        for b in range(4):
                            kb.MM(pm[:, 0:TS_], ones512[:], hb_[:, b, 0:TS_], start=(b == 0), stop=(b == 3))
                        for b in range(4):
                            kb.MM(pe2[:, 0:TS_], ones512[:], hs_[:, b, 0:TS_], start=(b == 0), stop=(b == 3))
                        kb.CP("act", mean[:, 0:TS_], pm[:, 0:TS_])
                        kb.TT("dve", rstd[:, 0:TS_], mean[:, 0:TS_], mean[:, 0:TS_], ALU.mult)
                        kb.TT("dve", rstd[:, 0:TS_], pe2[:, 0:TS_], rstd[:, 0:TS_], ALU.subtract)
                        kb.ACT(rstd[:, 0:TS_], rstd[:, 0:TS_], AF.Ln, bias=epsb[:])
                        kb.ACT(rstd[:, 0:TS_], rstd[:, 0:TS_], AF.Exp, scale=-0.5)
                        for b in range(4):
                            x_ = xcn[b % 2]
                            kb.TT("dve", x_[:, 0:TS_], hb_[:, b, 0:TS_], mean[:, 0:TS_], ALU.subtract)
                            kb.TT("dve", x_[:, 0:TS_], x_[:, 0:TS_], rstd[:, 0:TS_], ALU.mult)
                            kb.ACT(h2[i % 2][:, b, 0:TS_], x_[:, 0:TS_], AF.Silu, bias=colsT[:, b, 39:40], scale=colsT[:, b, 38:39])

                    def d_s3(i):
                        s, t0, TS_ = dst[i]
                        g0 = s["off"] + t0
                        for co in range(4):
                            p_ = pw_[co % 2]
                            for ci in range(4):
                                kb.MM(p_[:, 0:TS_], pww[:, ci, co * 128:(co + 1) * 128], h2[i % 2][:, ci, 0:TS_], start=(ci == 0), stop=(ci == 3))
                            o_ = oc[co % 2]
                            kb.STT(o_[:, 0:TS_], p_[:, 0:TS_], colsT[:, co, 40:41], gcb[i % 2][:, co, 0:TS_], ALU.add, ALU.mult)
                            kb.DMA("sp", OT[1536 + co * 128:1536 + (co + 1) * 128, g0:g0 + TS_], o_[:, 0:TS_], key=s["name"])
                    pipe(len(dst), [d_s0, d_s1, d_s2, d_s3])

                kb.mark("E%d" % l)
                with scope(kb) as ee:
                    wo = SB(ee, "wo", [128, 16, D], BF16)
                    wov = w_out[l].rearrange("(k p) n -> p k n", p=128)
                    with scope(kb) as eow:
                        wos = [SB(eow, "wos%d" % i, [128, 4, D], F32) for i in range(2)]
                        for k4 in range(4):
                            kb.DMA("sp", wos[k4 % 2][:], wov[:, k4 * 4:(k4 + 1) * 4, :])
                            for j4 in range(4):
                                kb.CP(["dve", "pool", "act", "pool"][j4], wo[:, k4 * 4 + j4, :], wos[k4 % 2][:, j4, :])
                    fnw = SB(ee, "fnw", [128, D], F32)
                    kb.DMA("sp", fnw[:], final_norm_w[0, :].partition_broadcast(128))
                    otb = [SB(ee, "otb%d" % i, [128, 16, 128], BF16) for i in range(3)]
                    xo = [SB(ee, "xo%d" % i, [128, D], F32) for i in range(3)]
                    xn = [SB(ee, "xn%d" % i, [128, D], F32) for i in range(2)]
                    yo = [SB(ee, "yo%d" % i, [128, D], F32) for i in range(2)]
                    jk = SB(ee, "jk", [128, D], BF16)
                    s1 = SB(ee, "s1", [128, 2], F32)
                    po = [PS(ee, "po%d" % i, [128, 1024], F32) for i in range(2)]
                    che = []
                    for s in seqs:
                        if last and s["name"] == "c":
                            continue
                        for c in range(s["n"] // 128):
                            che.append((s, c))

                    def e_s0(i):
                        s, c = che[i]
                        gt = s["off"] + c * 128
                        kb.DMA("sp", otb[i % 3][:], OT[:, gt:gt + 128].rearrange("(k p) t -> p k t", p=128), key=s["name"])
                        if l == 0:
                            src = (ctx_in if s["name"] == "c" else x_in)[c * 128:(c + 1) * 128, :]
                        else:
                            src = XR[gt:gt + 128, :]
                        kb.DMA("sp", xo[i % 3][:], src, key=(s["name"], c * 128))

                    def e_nop(i):
                        pass

                    def e_s1(i):
                        s, c = che[i]
                        m = 0 if s["name"] == "c" else 1
                        sl = i % 2
                        p_ = po[sl]
                        for nb_ in range(2):
                            for kt in range(16):
                                kb.MM(p_[:, nb_ * 512:(nb_ + 1) * 512], otb[i % 3][:, kt, :], wo[:, kt, nb_ * 512:(nb_ + 1) * 512], start=(kt == 0), stop=(kt == 15))
                        kb.TT("dve", xn[sl][:], p_[:], modB[:, m, 2, :], ALU.mult)
                        kb.TT("pool", xn[sl][:], xn[sl][:], xo[i % 3][:], ALU.add)

                    def e_s2(i):
                        s, c = che[i]
                        gt = s["off"] + c * 128
                        sl = i % 2
                        if not last:
                            kb.DMA("sp", XR[gt:gt + 128, :], xn[sl][:], key=(s["name"], c * 128))
                        else:
                            sc_ = s1[:, sl:sl + 1]
                            kb.STT(yo[sl][:], xn[sl][:], 1.0, xn[sl][:], ALU.mult, ALU.mult, accum=sc_)
                            kb.TS("dve", sc_, sc_, 1.0 / D, EPS, ALU.mult, ALU.add)
                            kb.ACT(sc_, sc_, AF.Ln)
                            kb.ACT(sc_, sc_, AF.Exp, scale=-0.5)
                            kb.STT(yo[sl][:], xn[sl][:], sc_, fnw[:], ALU.mult, ALU.mult)
                            kb.DMA("sp", out[c * 128:(c + 1) * 128, :], yo[sl][:], key="out")
                    pipe(len(che), [e_s0, e_nop, e_s1, e_s2])
        kb.mark("end")
        kb.finish()
    return nc, kb


def host_consts(L, T, pos, grid_w=64):
    k = np.arange(128)[:, None]
    t = np.arange(128)[None, :]
    cst = np.zeros((128, 7, 128), np.float32)
    cst[:, 0] = np.eye(128)
    cst[:, 1] = (k <= t)
    cst[:, 2] = (k >= t)
    cst[:, 3] = 1.0
    cst[:, 4] = np.where(k <= t, 0.0, NEG)
    cst[:, 5] = np.where(k >= t, 0.0, NEG)
    cst[:, 6] = np.where(k + t >= 127, 0.0, NEG)
    rope = np.zeros((T + L, 256), np.float32)
    rope[:, 0:64] = 1.0
    rope[:, 128:192] = 0.125
    row = (pos // grid_w).astype(np.float32)
    col = (pos % grid_w).astype(np.float32)
    inv = (10000.0 ** (-np.arange(0, 32, 2, dtype=np.float32) / 32.0)).astype(np.float32)
    ang = np.stack([row[:, None] * inv, col[:, None] * inv], axis=1).astype(np.float32)
    cs_, sn_ = np.cos(ang), np.sin(ang)
    cos2 = np.repeat(cs_[:, :, None, :], 2, axis=2).reshape(L, 64)
    rope[T:, 0:64] = cos2
    rope[T:, 64:96] = (-sn_).reshape(L, 32)
    rope[T:, 96:128] = sn_.reshape(L, 32)
    rope[T:, 128:192] = cos2 * 0.125
    rope[T:, 192:224] = (-sn_).reshape(L, 32) * 0.125
    rope[T:, 224:256] = sn_.reshape(L, 32) * 0.125
    return cst, rope


_CACHE = {}


def run(inputs, L, T, NL, batches):
    npairs = len(batches)
    key = (L, T, NL, npairs)
    if key not in _CACHE:
        _CACHE[key] = build(L, T, NL, npairs)[0]
    nc = _CACHE[key]
    f = lambda a: np.ascontiguousarray(np.asarray(a, dtype=np.float32))
    base = {
        "c_ctx": f(inputs["c_ctx"]).reshape(128, 8),
        "w_mod": f(inputs["w_mod"]), "b_mod": f(inputs["b_mod"]), "norm_w": f(inputs["norm_w"]),
        "ssd_conv_b": f(inputs["ssd_conv_b"]),
        "ssd_d": f(inputs["ssd_d"]), "ssd_norm_w": f(inputs["ssd_norm_w"]), "attn_sink": f(inputs["attn_sink"]),
        "conv_dw_b": f(inputs["conv_dw_b"]), "conv_ln_w": f(inputs["conv_ln_w"]),
        "conv_ln_b": f(inputs["conv_ln_b"]), "conv_pw_w": f(inputs["conv_pw_w"]), "conv_pw_b": f(inputs["conv_pw_b"]),
        "w_out": f(inputs["w_out"]), "final_norm_w": f(inputs["final_norm_w"]).reshape(1, D),
    }
    w_in = f(inputs["w_in"])
    w_in_m = w_in.copy()
    w_in_m[:, :, 2560:2576] = w_in[:, :, 2576:2592]
    w_in_m[:, :, 2576:2592] = w_in[:, :, 2560:2576]
    dtb = f(inputs["ssd_dt_bias"])
    alog = f(inputs["ssd_a_log"])
    scw = f(inputs["ssd_conv_w"])
    dww = f(inputs["conv_dw_w"])
    half = []
    for h in range(2):
        pos = np.arange(L) if h == 0 else (2 * L - 1 - np.arange(L))
        cst, rope = host_consts(L, T, pos)
        sel = np.zeros((128, 2), np.float32)
        sel[:, 1 - h] = 1.0
        d = dict(base)
        d.update({
            "cst": cst, "rope": rope, "sel": sel,
            "w_in": w_in if h == 0 else w_in_m,
            "ssd_dt_bias": f(dtb if h == 0 else dtb[:, ::-1]).reshape(NL, 32),
            "ssd_a_log": f(alog if h == 0 else alog[:, ::-1]).reshape(NL, 32),
            "ssd_conv_w": f(scw if h == 0 else scw[:, ::-1]),
            "conv_dw_w": f(dww if h == 0 else dww[:, ::-1]),
        })
        half.append(d)
    x = f(inputs["x"])
    c = f(inputs["c"])
    ctx = f(inputs["ctx"])
    in_maps = []
    for b in batches:
        for h in range(2):
            m = dict(half[h])
            if h == 0:
                m["x"] = f(x[b, 0:L])
                m["ctx"] = f(ctx[b])
            else:
                m["x"] = f(x[b, L:2 * L][::-1])
                m["ctx"] = f(ctx[b][::-1])
            m["c"] = c[b].reshape(128, 8)
            in_maps.append(m)
    res = run_bass_kernel_spmd(nc, in_maps, core_ids=list(range(2 * npairs)))
    outs = []
    for i in range(npairs):
        o0 = res.results[2 * i]["out"]
        o1 = res.results[2 * i + 1]["out"][::-1]
        outs.append(np.concatenate([o0, o1], axis=0))
    return outs


def kernel(**inputs):
    B = np.asarray(inputs["x"]).shape[0]
    outs = run(inputs, 2048, 256, 2, list(range(B)))
    return np.stack(outs, axis=0).astype(np.float32)
```

```python
import numpy as np
from contextlib import ExitStack, contextmanager
import concourse.bass as bass
import concourse.mybir as mybir
from concourse.bass_utils import run_bass_kernel_spmd

F32 = mybir.dt.float32
BF16 = mybir.dt.bfloat16
AF = mybir.ActivationFunctionType
ALU = mybir.AluOpType
NDMA = 48
EPS = 1e-6
D = 1024
NEG = -30000.0


class Tok:
    __slots__ = ("sem", "val", "key")

    def __init__(self, sem, val, key):
        self.sem, self.val, self.key = sem, val, key


class Buf:
    __slots__ = ("w", "r", "multi")

    def __init__(self, multi=False):
        self.w = {}
        self.r = {}
        self.multi = multi


class KB:
    def __init__(self, nc, es):
        self.nc = nc
        self.eng = {"pe": nc.tensor, "act": nc.scalar, "dve": nc.vector, "pool": nc.gpsimd, "sp": nc.sync}
        self.sem = {e: es.enter_context(nc.semaphore("s_" + e)) for e in self.eng}
        self.cnt = {e: 0 for e in self.eng}
        self.known = {e: {} for e in self.eng}
        self.dsem = [es.enter_context(nc.semaphore("d%d" % i)) for i in range(NDMA)]
        self.dval = [0] * NDMA
        self.dnext = 0
        self.pend = {e: [] for e in self.eng}
        self.bufs = {}
        self.ccsem = es.enter_context(nc.semaphore("ccsem"))
        self.ccval = 0
        self.ninst = 0
        self.nmm = 0
        self.marks = []

    def buf(self, ap, key=None):
        t = ap.tensor
        name = t.name
        dram = str(ap.space).lower().find("dram") >= 0 or str(ap.space).lower().find("hbm") >= 0
        k = (name, key if dram else None)
        b = self.bufs.get(k)
        if b is None:
            b = Buf(multi=dram)
            self.bufs[k] = b
        return b

    def _wait(self, e, tok):
        if e == "pe" and tok.key == "pe":
            return
        k = self.known[e]
        if k.get(tok.key, 0) >= tok.val:
            return
        self.eng[e].wait_ge(tok.sem, tok.val)
        k[tok.key] = tok.val

    def _deps(self, e, reads, writes):
        for b in reads:
            for t in b.w.values():
                self._wait(e, t)
        for b in writes:
            if not b.multi:
                for t in b.w.values():
                    self._wait(e, t)
            for t in b.r.values():
                self._wait(e, t)

    def _record(self, tok, reads, writes):
        for b in reads:
            b.r[tok.key] = tok
        for b in writes:
            if b.multi:
                b.w[tok.key] = tok
            else:
                b.w = {tok.key: tok}
                b.r = {}

    def op(self, e, fn, rd, wr, signal=True, key=None):
        reads = [self.buf(a, key) for a in rd if a is not None and not isinstance(a, (int, float))]
        writes = [self.buf(a, key) for a in wr if a is not None]
        self._deps(e, reads, writes)
        ins = fn(self.eng[e])
        self.ninst += 1
        if not signal:
            self.pend[e].append((reads, writes))
            return
        self.cnt[e] += 1
        ins.then_inc(self.sem[e], 1)
        tok = Tok(self.sem[e], self.cnt[e], e)
        for (r, w) in self.pend[e]:
            self._record(tok, r, w)
        self.pend[e] = []
        self._record(tok, reads, writes)

    def DMA(self, q, out, in_, key=None, **kw):
        reads = [self.buf(in_, key)]
        writes = [self.buf(out, key)]
        self._deps(q, reads, writes)
        i = self.dnext
        self.dnext = (i + 1) % NDMA
        k = "d%d" % i
        if self.dval[i] > 0:
            self._wait(q, Tok(self.dsem[i], self.dval[i], k))
        ins = self.eng[q].dma_start(out=out, in_=in_, **kw)
        self.ninst += 1
        self.dval[i] += 16
        ins.then_inc(self.dsem[i], 16)
        tok = Tok(self.dsem[i], self.dval[i], k)
        self._record(tok, reads, writes)

    def CC(self, in_ap, out_ap, groups):
        reads = [self.buf(in_ap)]
        writes = [self.buf(out_ap)]
        self._deps("pool", reads, writes)
        ins = self.eng["pool"].collective_compute("AllGather", ALU.bypass, replica_groups=groups, ins=[in_ap], outs=[out_ap])
        self.ninst += 1
        self.ccval += 1
        ins.then_inc(self.ccsem)
        tok = Tok(self.ccsem, self.ccval, "cc")
        self._record(tok, reads, writes)

    def barrier(self):
        assert all(len(p) == 0 for p in self.pend.values())
        toks = [Tok(self.sem[e], self.cnt[e], e) for e in self.eng if self.cnt[e] > 0]
        if self.ccval:
            toks.append(Tok(self.ccsem, self.ccval, "cc"))
        toks += [Tok(self.dsem[i], self.dval[i], "d%d" % i) for i in range(NDMA) if self.dval[i]]
        for e in self.eng:
            for t in toks:
                if t.key != e:
                    self._wait(e, t)

    def finish(self):
        for i in range(NDMA):
            if self.dval[i]:
                self._wait("sp", Tok(self.dsem[i], self.dval[i], "d%d" % i))

    def ACT(self, out, in_, func, bias=None, scale=None, accum=None):
        kw = {}
        if bias is not None:
            kw["bias"] = bias
        if scale is not None:
            kw["scale"] = scale
        if accum is not None:
            kw["accum_out"] = accum
        self.op("act", lambda e: e.activation(out=out, in_=in_, func=func, **kw), [in_, bias, scale], [out, accum])

    def TT(self, e, out, in0, in1, op):
        self.op(e, lambda g: g.tensor_tensor(out=out, in0=in0, in1=in1, op=op), [in0, in1], [out])

    def TS(self, e, out, in0, s1, s2, op0, op1=None):
        if op1 is None:
            self.op(e, lambda g: g.tensor_scalar(out=out, in0=in0, scalar1=s1, scalar2=None, op0=op0), [in0, s1], [out])
        else:
            self.op(e, lambda g: g.tensor_scalar(out=out, in0=in0, scalar1=s1, scalar2=s2, op0=op0, op1=op1), [in0, s1, s2], [out])

    def STT(self, out, in0, scalar, in1, op0, op1, accum=None):
        if accum is None:
            self.op("dve", lambda g: g.scalar_tensor_tensor(out=out, in0=in0, scalar=scalar, in1=in1, op0=op0, op1=op1), [in0, scalar, in1], [out])
        else:
            self.op("dve", lambda g: g.scalar_tensor_tensor(out=out, in0=in0, scalar=scalar, in1=in1, op0=op0, op1=op1, accum_out=accum),
                    [in0, scalar, in1], [out, accum])

    def CP(self, e, out, in_):
        if e == "act":
            self.ACT(out, in_, AF.Copy)
        else:
            self.op(e, lambda g: g.tensor_copy(out=out, in_=in_), [in_], [out])

    def RECIP(self, out, in_):
        self.op("dve", lambda g: g.reciprocal(out=out, in_=in_), [in_], [out])

    def MSET(self, e, out, val):
        self.op(e, lambda g: g.memset(out, val), [], [out])

    def mark(self, label):
        self.marks.append((label, self.nmm))

    def MM(self, out, lhsT, rhs, start, stop, signal=None):
        self.nmm += 1
        self.op("pe", lambda g: g.matmul(out, lhsT, rhs, start=start, stop=stop), [lhsT, rhs], [out],
                signal=(stop if signal is None else signal))

    def TR(self, out, in_, ident, signal=True):
        self.nmm += 1
        self.op("pe", lambda g: g.transpose(out=out, in_=in_, identity=ident), [in_, ident], [out], signal=signal)


@contextmanager
def scope(kb):
    with ExitStack() as es:
        yield es
        kb.barrier()


def pipe(n, stages):
    K = len(stages)
    for t in range(n + K - 1):
        for k in range(K):
            i = t - k
            if 0 <= i < n:
                stages[k](i)


def bc(ap, shape):
    return ap.unsqueeze(2).to_broadcast(shape)


def build(L, T, NL=2, npairs=4):
    nc = bass.Bass("TRN2", target_bir_lowering=False)
    Ltot = T + L
    NCH = Ltot // 128
    seqs = [dict(name="c", off=0, n=T, xoff=2, goff=15), dict(name="l", off=T, n=L, xoff=T + 6, goff=T + 45)]
    Wx = Ltot + 8
    Wg = Ltot + 60

    def din(name, shape, dt=F32):
        return nc.dram_tensor(name, list(shape), dt, kind="ExternalInput").ap()

    def dsc(name, shape, dt):
        return nc.dram_tensor(name, list(shape), dt, kind="Internal").ap()

    x_in = din("x", [L, D])
    ctx_in = din("ctx", [T, D])
    c_in = din("c", [128, 8])
    cc_in = din("c_ctx", [128, 8])
    w_mod = din("w_mod", [NL, D, 3 * D])
    b_mod = din("b_mod", [NL, 3 * D])
    norm_w = din("norm_w", [NL, D])
    w_in = din("w_in", [NL, D, 5408])
    ssd_conv_w = din("ssd_conv_w", [NL, 5, 1536])
    ssd_conv_b = din("ssd_conv_b", [NL, 1536])
    ssd_dt_bias = din("ssd_dt_bias", [NL, 32])
    ssd_a_log = din("ssd_a_log", [NL, 32])
    ssd_d = din("ssd_d", [NL, 16])
    ssd_norm_w = din("ssd_norm_w", [NL, 1024])
    attn_sink = din("attn_sink", [NL, 8])
    conv_dw_w = din("conv_dw_w", [NL, 31, 512])
    conv_dw_b = din("conv_dw_b", [NL, 512])
    conv_ln_w = din("conv_ln_w", [NL, 512])
    conv_ln_b = din("conv_ln_b", [NL, 512])
    conv_pw_w = din("conv_pw_w", [NL, 512, 512])
    conv_pw_b = din("conv_pw_b", [NL, 512])
    w_out = din("w_out", [NL, 2048, D])
    final_norm_w = din("final_norm_w", [1, D])
    cst = din("cst", [128, 7, 128])
    sel_in = din("sel", [128, 2])
    groups = [[2 * i, 2 * i + 1] for i in range(npairs)]
    rope = din("rope", [Ltot, 256])
    out = nc.dram_tensor("out", [L, D], F32, kind="ExternalOutput").ap()

    XR = dsc("XR", [Ltot, D], F32)
    XBCT = dsc("XBCT", [1536, Wx], BF16)
    GLUT = dsc("GLUT", [512, Wg], BF16)
    GCT = dsc("GCT", [512, Ltot], BF16)
    ZS = dsc("ZS", [Ltot, 1024], BF16)
    GAT = dsc("GAT", [512, Ltot], BF16)
    DTS = dsc("DTS", [Ltot, 32], F32)
    DTT = dsc("DTT", [32, Ltot], F32)
    QT = dsc("QT", [64, 8, Ltot], BF16)
    KT = dsc("KT", [64, 2, Ltot], BF16)
    VV = dsc("VV", [Ltot, 130], BF16)
    BCT = dsc("BCT", [128, 4, Ltot], BF16)
    XS = dsc("XS", [Ltot, 1024], BF16)
    BTM = dsc("BTM", [Ltot, 256], BF16)
    ABR = dsc("ABR", [2, 2, 6, NCH, 2048], BF16)
    SBD = dsc("SBD", [NCH, 128, 1024], BF16)
    OT = dsc("OT", [2048, Ltot], BF16)
    MOD = dsc("MOD", [NL, 2, 3 * D], F32)
    SFD = dsc("SFD", [NCH, 128, 1024], BF16)
    SND1 = dsc("SND1", [128, 470], BF16)
    RCV1 = nc.dram_tensor("RCV1", [256, 470], BF16, kind="Internal", addr_space="Local").ap()
    SND2 = dsc("SND2", [128, 1024], F32)
    RCV2 = nc.dram_tensor("RCV2", [256, 1024], F32, kind="Internal", addr_space="Local").ap()

    with ExitStack() as es0:
        kb = KB(nc, es0)

        uid = [0]

        def SB(es, name, shape, dt):
            uid[0] += 1
            return es.enter_context(nc.sbuf_tensor("%s_%d" % (name, uid[0]), list(shape), dt))

        def PS(es, name, shape, dt):
            uid[0] += 1
            return es.enter_context(nc.psum_tensor("%s_%d" % (name, uid[0]), list(shape), dt))

        cf = SB(es0, "cf", [128, 7, 128], F32)
        selB = SB(es0, "selB", [128, 2], F32)
        kb.DMA("sp", selB[:], sel_in[:, :])
        kb.DMA("sp", cf[:], cst[:, :, :])
        identf = cf[:, 0, :]
        triU = cf[:, 1, :]
        triL = cf[:, 2, :]
        onesf = cf[:, 3, :]
        cb_ = SB(es0, "cb", [128, 7, 128], BF16)
        kb.CP("dve", cb_[:], cf[:])
        identb = cb_[:, 0, :]
        onesb = cb_[:, 3, :]
        mneg = SB(es0, "mneg", [128, 3, 4, 128], BF16)
        for d_ in range(3):
            for r_ in range(4):
                kb.CP("dve", mneg[:, d_, r_, :], cf[:, 4 + d_, :])
        ones512 = SB(es0, "ones512", [128, 128], BF16)
        kb.MSET("dve", ones512[:], 1.0 / 512.0)
        zpad = SB(es0, "zpad", [128, 64], BF16)
        kb.MSET("dve", zpad[:], 0.0)
        for s in seqs:
            for side in range(2):
                cx = s["xoff"] - 2 if side == 0 else s["xoff"] + s["n"]
                kb.DMA("sp", XBCT[:, cx:cx + 2].rearrange("(b p) w -> p b w", p=128), zpad[:, 0:24].rearrange("p (b w) -> p b w", w=2), key=s["name"])
                cg = s["goff"] - 15 if side == 0 else s["goff"] + s["n"]
                kb.DMA("sp", GLUT[:, cg:cg + 15].rearrange("(b p) w -> p b w", p=128), zpad[:, 0:60].rearrange("p (b w) -> p b w", w=15), key=s["name"])
        onesrow = SB(es0, "onesrow", [128, 2048], BF16)
        kb.MSET("pool", onesrow[:], 1.0)
        for s in seqs:
            c0_, n0_ = s["off"] // 128, s["n"] // 128
            for d_ in range(2):
                for r_ in range(3):
                    kb.DMA("sp", ABR[d_, 0, 3 + r_, c0_:c0_ + n0_, :], onesrow[0:n0_, :], key=s["name"])
                    kb.DMA("sp", ABR[d_, 1, r_, c0_:c0_ + n0_, :], onesrow[0:n0_, :], key=s["name"])
        epsb = SB(es0, "epsb", [128, 1], F32)
        kb.MSET("pool", epsb[:], EPS)

        for l in range(NL):
            last = (l == NL - 1)
            with scope(kb) as esl:
                kb.mark("setup%d" % l)
                modB = SB(esl, "modB", [128, 2, 3, D], F32)
                aB = SB(esl, "aB", [128, 32], F32)
                dtbB = SB(esl, "dtbB", [128, 32], F32)
                snwB = SB(esl, "snwB", [128, 1024], F32)
                sinkB = SB(esl, "sinkB", [128, 8], F32)
                colsT = SB(esl, "colsT", [128, 12, 41], F32)
                dI = SB(esl, "dI", [128, 16, 128], BF16)
                nbr1 = SB(esl, "nbr1", [128, 470], BF16)
                nx = nbr1[:, 0:24].rearrange("p (b w) -> p b w", w=2)
                ng = nbr1[:, 24:84].rearrange("p (b w) -> p b w", w=15)
                with scope(kb) as ess:
                    pss = PS(ess, "pss", [128, 512], F32)
                    csil = SB(ess, "csil", [128, 2, 8], F32)
                    kb.DMA("sp", csil[:, 0, :], cc_in[:, :])
                    kb.DMA("sp", csil[:, 1, :], c_in[:, :])
                    kb.ACT(csil[:], csil[:], AF.Silu)
                    bm = SB(ess, "bm", [2, 3 * D], F32)
                    kb.DMA("sp", bm[:], b_mod[l, :].partition_broadcast(2))
                    modrow = SB(ess, "modrow", [2, 3 * D], F32)
                    wmA = [SB(ess, "wmA%d" % i, [128, 4, 512], F32) for i in range(2)]
                    wmB = [SB(ess, "wmB%d" % i, [128, 4, 512], F32) for i in range(2)]
                    nwB = SB(ess, "nwB", [128, D], F32)
                    kb.DMA("sp", nwB[:], norm_w[l, :].partition_broadcast(128))
                    kb.DMA("sp", aB[:], ssd_a_log[l, :].partition_broadcast(128))
                    kb.ACT(aB[:], aB[:], AF.Exp)
                    kb.TS("dve", aB[:], aB[:], -1.0, None, ALU.mult)
                    kb.DMA("sp", dtbB[:], ssd_dt_bias[l, :].partition_broadcast(128))
                    kb.DMA("sp", snwB[:], ssd_norm_w[l, :].partition_broadcast(128))
                    kb.DMA("sp", sinkB[:], attn_sink[l, :].partition_broadcast(128))
                    kb.ACT(sinkB[:], sinkB[:], AF.Exp)
                    dB = SB(ess, "dB", [128, 16], F32)
                    kb.DMA("sp", dB[:], ssd_d[l, :].partition_broadcast(128))
                    for h in range(16):
                        kb.TS("dve", dI[:, h, :], identf, dB[:, h:h + 1], None, ALU.mult)
                    rows = SB(ess, "rows", [41, 1536], F32)
                    kb.MSET("dve", rows[:], 0.0)
                    kb.DMA("sp", rows[0:5, :], ssd_conv_w[l, :, :])
                    kb.DMA("sp", rows[5:6, :], ssd_conv_b[l:l + 1, :])
                    kb.DMA("sp", rows[6:37, 0:512], conv_dw_w[l, :, :])
                    kb.DMA("sp", rows[37:38, 0:512], conv_dw_b[l:l + 1, :])
                    kb.DMA("sp", rows[38:39, 0:512], conv_ln_w[l:l + 1, :])
                    kb.DMA("sp", rows[39:40, 0:512], conv_ln_b[l:l + 1, :])
                    kb.DMA("sp", rows[40:41, 0:512], conv_pw_b[l:l + 1, :])
                    for cbk in range(6):
                        wa_, wb_ = wmA[cbk % 2], wmB[cbk % 2]
                        wsrc_ = w_mod[l, :, cbk * 512:(cbk + 1) * 512].rearrange("(p k) n -> p k n", k=8)
                        kb.DMA("act", wa_[:], wsrc_[:, 0:4, :])
                        kb.DMA("act", wb_[:], wsrc_[:, 4:8, :])
                        for kt in range(8):
                            kb.MM(pss[0:2, :], csil[:, :, kt], (wa_ if kt < 4 else wb_)[:, kt % 4, :], start=(kt == 0), stop=(kt == 7))
                        kb.TT("dve", modrow[:, cbk * 512:(cbk + 1) * 512], pss[0:2, :], bm[:, cbk * 512:(cbk + 1) * 512], ALU.add)
                    kb.DMA("sp", MOD[l, :, :], modrow[:])
                    kb.DMA("sp", modB[:], MOD[l, :, :].rearrange("m (t d) -> m t d", t=3).partition_broadcast(128))
                    for m in range(2):
                        tmpw = SB(ess, "tmpw%d" % m, [128, D], F32)
                        kb.STT(tmpw[:], modB[:, m, 1, :], 1.0, nwB[:], ALU.add, ALU.mult)
                        kb.CP("dve", modB[:, m, 1, :], modB[:, m, 0, :])
                        kb.CP("dve", modB[:, m, 0, :], tmpw[:])
                    for b in range(12):
                        kb.TR(pss[:, b * 41:(b + 1) * 41], rows[:, b * 128:(b + 1) * 128], identf[0:41, 0:41], signal=(b == 11))
                    kb.CP("dve", colsT[:].rearrange("p b r -> p (b r)"), pss[:, 0:492])

                kb.mark("A%d" % l)
                sts = []
                for s in seqs:
                    TS_ = 512 if s["name"] == "l" else 256
                    for t0 in range(0, s["n"], TS_):
                        sts.append((s, t0, TS_))
                with scope(kb) as esa:
                    wfm = SB(esa, "wfm", [128, 8, 3584], BF16)
                    wtm = SB(esa, "wtm", [128, 8, 1824], BF16)
                    wv = w_in[l].rearrange("(k p) n -> p k n", p=128)
                    with scope(kb) as esw:
                        wstA = [SB(esw, "wstA%d" % i, [128, 2592], F32) for i in range(2)]
                        wstB = [SB(esw, "wstB%d" % i, [128, 2816], F32) for i in range(2)]
                        ce = ["dve", "pool", "dve"]
                        nce = 0
                        for kt in range(8):
                            wa_, wb_ = wstA[kt % 2], wstB[kt % 2]
                            kb.DMA("sp", wa_[:], wv[:, kt, 0:2592])
                            kb.DMA("act", wb_[:], wv[:, kt, 2592:5408])
                            segs = [(wfm, 0, 0, 768), (wfm, 768, 768, 768), (wfm, 2560, 4896, 512),
                                    (wtm, 0, 1536, 512), (wtm, 512, 2048, 512), (wtm, 1024, 2592, 512), (wtm, 1536, 3104, 256),
                                    (wtm, 1792, 2560, 32), (wfm, 3072, 3360, 512)]
                            for i in range(4):
                                segs.append((wfm, 1536 + 256 * i, 3872 + 128 * i, 128))
                                segs.append((wfm, 1536 + 256 * i + 128, 4384 + 128 * i, 128))
                            for (dstt, d0, s0, n_) in segs:
                                assert s0 + n_ <= 2592 or s0 >= 2592
                                src_ = wa_[:, s0:s0 + n_] if s0 < 2592 else wb_[:, s0 - 2592:s0 - 2592 + n_]
                                kb.CP(ce[nce % 3], dstt[:, kt, d0:d0 + n_], src_)
                                nce += 1
                    xin = [SB(esa, "xin%d" % i, [128, D], F32) for i in range(3)]
                    hb = [SB(esa, "hb%d" % i, [128, D], BF16) for i in range(2)]
                    junk = SB(esa, "junk", [128, D], BF16)
                    tmpf = SB(esa, "tmpf", [128, D], F32)
                    hT = [SB(esa, "hT%d" % i, [128, 8, 512], BF16) for i in range(2)]
                    st4 = SB(esa, "st4", [128, 8], F32)
                    fst = [SB(esa, "fst%d" % i, [128, 512], BF16) for i in range(3)]
                    sig = SB(esa, "sig", [128, 512], F32)
                    zs = [SB(esa, "zs%d" % i, [128, 1024], BF16) for i in range(2)]
                    qk = [SB(esa, "qk%d" % i, [128, 640], BF16) for i in range(2)]
                    vst = [SB(esa, "vst%d" % i, [128, 2, 65], BF16) for i in range(2)]
                    for i in range(2):
                        kb.MSET("pool", vst[i][:], 1.0)
                    qkT = [SB(esa, "qkT%d" % i, [64, 10, 128], BF16) for i in range(2)]
                    dtw = [SB(esa, "dtw%d" % i, [128, 6, 32], F32) for i in range(2)]
                    dtT = [SB(esa, "dtT%d" % i, [32, 128], F32) for i in range(2)]
                    rp = [SB(esa, "rp%d" % i, [128, 256], F32) for i in range(2)]
                    rt = SB(esa, "rt", [128, 2, 640], F32)
                    pT = [PS(esa, "pT%d" % i, [128, 1024], BF16) for i in range(2)]
                    ppool = [PS(esa, "pp%d" % i, [128, 512], F32) for i in range(6)]
                    subs = [(si, j) for si, (s, t0, TS_) in enumerate(sts) for j in range(TS_ // 128)]
                    cntA = dict(fm=0, tm=0, sc=0)

                    def a_load(k):
                        si, j = subs[k]
                        s, t0, TS_ = sts[si]
                        if l == 0:
                            src = (ctx_in if s["name"] == "c" else x_in)[t0 + j * 128:t0 + (j + 1) * 128, :]
                        else:
                            src = XR[s["off"] + t0 + j * 128:s["off"] + t0 + (j + 1) * 128, :]
                        kb.DMA("sp", xin[k % 3][:], src, key=(s["name"], t0 + j * 128))

                    def a_norm(k):
                        si, j = subs[k]
                        s, t0, TS_ = sts[si]
                        m = 0 if s["name"] == "c" else 1
                        xi, hbj = xin[k % 3], hb[k % 2]
                        col = st4[:, k % 8:k % 8 + 1]
                        kb.ACT(junk[:], xi[:], AF.Square, accum=col)
                        kb.TS("dve", col, col, 1.0 / D, EPS, ALU.mult, ALU.add)
                        kb.ACT(col, col, AF.Sqrt)
                        kb.RECIP(col, col)
                        kb.STT(tmpf[:], xi[:], col, modB[:, m, 0, :], ALU.mult, ALU.mult)
                        kb.TT("pool", hbj[:], tmpf[:], modB[:, m, 1, :], ALU.add)

                    smallq = []

                    def drain(n):
                        while n > 0 and smallq:
                            smallq.pop(0)()
                            n -= 1

                    def a_transp(k, defer=True):
                        si, j = subs[k]
                        hTt, p_, hbj = hT[si % 2], pT[k % 2], hb[k % 2]

                        def piece(q4):
                            for kt in range(2 * q4, 2 * q4 + 2):
                                kb.TR(p_[:, kt * 128:(kt + 1) * 128], hbj[:, kt * 128:(kt + 1) * 128], identb, signal=(kt == 7))
                            if q4 == 3:
                                kb.CP("act" if k % 2 else "dve", hTt[:, :, j * 128:(j + 1) * 128], p_[:].rearrange("p (k t) -> p k t", k=8))
                        for q4 in range(4):
                            if defer:
                                smallq.append(lambda q4=q4: piece(q4))
                            else:
                                piece(q4)

                    def a_fmblock(si, cbk):
                        s, t0, TS_ = sts[si]
                        hTt = hT[si % 2]
                        p_ = ppool[cntA["tm"] % 6]
                        cntA["tm"] += 1
                        for kt in range(8):
                            kb.MM(p_[:, 0:TS_], wfm[:, kt, cbk * 128:(cbk + 1) * 128], hTt[:, kt, 0:TS_], start=(kt == 0), stop=(kt == 7))
                        drain(1)
                        return p_

                    def a_fmunit(si, u):
                        s, t0, TS_ = sts[si]
                        g0 = s["off"] + t0
                        if u < 12:
                            p_ = a_fmblock(si, u)
                            f_ = fst[u % 3]
                            kb.CP("dve", f_[:, 0:TS_], p_[:, 0:TS_])
                            kb.DMA("sp", XBCT[u * 128:(u + 1) * 128, s["xoff"] + t0:s["xoff"] + t0 + TS_], f_[:, 0:TS_], key=s["name"])
                        elif u < 16:
                            i = u - 12
                            pv = a_fmblock(si, 12 + 2 * i)
                            pg = a_fmblock(si, 13 + 2 * i)
                            kb.ACT(sig[:, 0:TS_], pg[:, 0:TS_], AF.Sigmoid)
                            f_ = fst[i % 3]
                            kb.TT("dve", f_[:, 0:TS_], pv[:, 0:TS_], sig[:, 0:TS_], ALU.mult)
                            kb.DMA("sp", GLUT[i * 128:(i + 1) * 128, s["goff"] + t0:s["goff"] + t0 + TS_], f_[:, 0:TS_], key=s["name"])
                        else:
                            i = u - 16
                            p_ = a_fmblock(si, 20 + i)
                            f_ = fst[i % 3]
                            kb.ACT(f_[:, 0:TS_], p_[:, 0:TS_], AF.Silu)
                            dstT = GCT if i < 4 else GAT
                            kb.DMA("sp", dstT[(i % 4) * 128:(i % 4 + 1) * 128, g0:g0 + TS_], f_[:, 0:TS_], key=s["name"])

                    def a_tmsub(si, j):
                        s, t0, TS_ = sts[si]
                        hTt = hT[si % 2]
                        gt = s["off"] + t0 + j * 128
                        sl = cntA["sc"] % 2
                        cntA["sc"] += 1

                        def tm_block(c0, c1):
                            p_ = ppool[cntA["tm"] % 6]
                            cntA["tm"] += 1
                            for kt in range(8):
                                kb.MM(p_[:, 0:c1 - c0], hTt[:, kt, j * 128:(j + 1) * 128], wtm[:, kt, c0:c1], start=(kt == 0), stop=(kt == 7))
                            drain(1)
                            return p_
                        kb.DMA("sp", rp[sl][:], rope[gt:gt + 128, :])
                        for i in range(2):
                            p_ = tm_block(i * 512, (i + 1) * 512)
                            kb.ACT(zs[sl][:, i * 512:(i + 1) * 512], p_[:], AF.Silu)
                        kb.DMA("sp", ZS[gt:gt + 128, :], zs[sl][:], key=s["name"])
                        pq = tm_block(1024, 1536)
                        pk = tm_block(1536, 1824)
                        for (p_, nh, o0, tb) in ((pq, 8, 0, 128), (pk, 2, 512, 0)):
                            srcv = p_[:, 0:nh * 64]
                            a_ = rt[:, 0, 0:nh * 64]
                            b_ = rt[:, 1, 0:nh * 64]
                            cos2 = rp[sl][:, tb:tb + 64].unsqueeze(1).to_broadcast([128, nh, 64])
                            kb.TT("dve", a_.rearrange("p (h d) -> p h d", h=nh), srcv.rearrange("p (h d) -> p h d", h=nh), cos2, ALU.mult)
                            s4 = srcv.rearrange("p (h a x f) -> p h a x f", h=nh, a=2, x=2)
                            b4 = b_.rearrange("p (h a x f) -> p h a x f", h=nh, a=2, x=2)
                            sm = rp[sl][:, tb + 64:tb + 96].rearrange("p (a f) -> p a f", a=2).unsqueeze(1).to_broadcast([128, nh, 2, 16])
                            spl = rp[sl][:, tb + 96:tb + 128].rearrange("p (a f) -> p a f", a=2).unsqueeze(1).to_broadcast([128, nh, 2, 16])
                            kb.TT("dve", b4[:, :, :, 0, :], s4[:, :, :, 1, :], sm, ALU.mult)
                            kb.TT("dve", b4[:, :, :, 1, :], s4[:, :, :, 0, :], spl, ALU.mult)
                            kb.TT("pool", qk[sl][:, o0:o0 + nh * 64], a_, b_, ALU.add)
                        kb.CP("act", vst[sl][:, :, 0:64], pk[:, 128:256].rearrange("p (g d) -> p g d", g=2))
                        kb.DMA("sp", VV[gt:gt + 128, :], vst[sl][:].rearrange("p g d -> p (g d)"), key=s["name"])
                        dw = dtw[sl]
                        kb.TT("dve", dw[:, 0, :], pk[:, 256:288], dtbB[:], ALU.add)
                        kb.ACT(dw[:, 1, :], dw[:, 0, :], AF.Abs)
                        kb.ACT(dw[:, 2, :], dw[:, 1, :], AF.Exp, scale=-1.0)
                        kb.ACT(dw[:, 3, :], dw[:, 2, :], AF.Ln, bias=1.0)
                        kb.STT(dw[:, 5, :], dw[:, 0, :], 0.0, dw[:, 3, :], ALU.max, ALU.add)
                        kb.TS("dve", dw[:, 4, :], dw[:, 5, :], 1e-30, None, ALU.max)
                        kb.DMA("sp", DTS[gt:gt + 128, :], dw[:, 4, :], key=s["name"])
                        return (s, gt, sl, j)

                    def a_tmsub2(arg):
                        s, gt, sl, j = arg
                        dw = dtw[sl]
                        p_ = pT[j % 2]
                        p2 = pT[(j + 1) % 2]

                        def pq(q4):
                            for h in range(2 * q4, 2 * q4 + 2):
                                kb.TR(p_[0:64, h * 128:(h + 1) * 128], qk[sl][:, h * 64:(h + 1) * 64], identb, signal=(h == 7))
                            if q4 == 3:
                                kb.CP("act", qkT[sl][:, 0:8, :], p_[0:64, :].rearrange("p (h t) -> p h t", h=8))
                                kb.DMA("sp", QT[:, :, gt:gt + 128], qkT[sl][:, 0:8, :], key=s["name"])

                        def pk2():
                            for h in range(2):
                                kb.TR(p2[0:64, h * 128:(h + 1) * 128], qk[sl][:, 512 + h * 64:512 + (h + 1) * 64], identb, signal=(h == 1))
                            kb.CP("dve", qkT[sl][:, 8:10, :], p2[0:64, 0:256].rearrange("p (h t) -> p h t", h=2))
                            kb.DMA("sp", KT[:, :, gt:gt + 128], qkT[sl][:, 8:10, :], key=s["name"])
                            p3 = ppool[cntA["tm"] % 6]
                            cntA["tm"] += 1
                            kb.TR(p3[0:32, 0:128], dw[:, 4, :], identf)
                            kb.CP("dve", dtT[sl][:], p3[0:32, 0:128])
                            kb.DMA("sp", DTT[:, gt:gt + 128], dtT[sl][:], key=s["name"])
                        for q4 in range(4):
                            smallq.append(lambda q4=q4: pq(q4))
                        smallq.append(pk2)

                    nsubs = len(subs)
                    ks_of = [[k for k, (si2, j) in enumerate(subs) if si2 == si] for si in range(len(sts))]
                    pend2 = [None]
                    a_load(0)
                    for k in ks_of[0]:
                        if k + 1 < nsubs:
                            a_load(k + 1)
                        a_norm(k)
                        a_transp(k, defer=False)
                    for si in range(len(sts)):
                        nsub = len(ks_of[si])
                        nxt = ks_of[si + 1] if si + 1 < len(sts) else []
                        per = 24 // nsub
                        for q in range(nsub):
                            kn = nxt[q] if q < len(nxt) else None
                            if kn is not None:
                                if kn + 1 < nsubs:
                                    a_load(kn + 1)
                                a_norm(kn)
                            if pend2[0] is not None:
                                a_tmsub2(pend2[0])
                                pend2[0] = None
                            if kn is not None:
                                a_transp(kn)
                            for u in range(q * per, (q + 1) * per):
                                a_fmunit(si, u)
                            pend2[0] = a_tmsub(si, q)
                            drain(100)
                        for q in range(nsub, len(nxt)):
                            kn = nxt[q]
                            if kn + 1 < nsubs:
                                a_load(kn + 1)
                            a_norm(kn)
                            a_transp(kn, defer=False)
                    a_tmsub2(pend2[0])
                    drain(100)

                with scope(kb) as ex:
                    sL = seqs[1]
                    end_x, end_g, end_t = sL["xoff"] + sL["n"], sL["goff"] + sL["n"], sL["off"] + sL["n"]
                    snd = SB(ex, "snd", [128, 470], BF16)
                    kb.MSET("dve", snd[:], 0.0)
                    kb.DMA("sp", snd[:, 0:24].rearrange("p (b w) -> p b w", w=2), XBCT[:, end_x - 2:end_x].rearrange("(b p) w -> p b w", p=128), key="l")
                    kb.DMA("sp", snd[:, 24:84].rearrange("p (b w) -> p b w", w=15), GLUT[:, end_g - 15:end_g].rearrange("(b p) w -> p b w", p=128), key="l")
                    kb.DMA("sp", snd[0:64, 84:340].rearrange("p (g t) -> p g t", g=2), KT[:, :, end_t - 128:end_t], key="l")
                    kb.DMA("sp", snd[:, 340:470], VV[end_t - 128:end_t, :], key="l")
                    kb.DMA("sp", SND1[:, :], snd[:])
                    kb.CC(SND1[:, :], RCV1[:, :], groups)
                    rcv = SB(ex, "rcv", [128, 2, 470], BF16)
                    kb.DMA("sp", rcv[:], RCV1.rearrange("(r p) c -> p r c", p=128))
                    tmpx = SB(ex, "tmpx", [128, 470], F32)
                    kb.TS("dve", tmpx[:], rcv[:, 0, :], selB[:, 0:1], None, ALU.mult)
                    kb.STT(nbr1[:], rcv[:, 1, :], selB[:, 1:2], tmpx[:], ALU.mult, ALU.add)

                with scope(kb) as esb:
                    kb.mark("B0%d" % l)
                    kb.mark("B1%d" % l)
                    with scope(kb) as e1:
                        dg = SB(e1, "dg", [128, 60, 128], BF16)
                        for b in range(12):
                            for j in range(5):
                                if (b + j) % 2:
                                    kb.TS("dve", dg[:, b * 5 + j, :], identf, colsT[:, b, j:j + 1], None, ALU.mult)
                                else:
                                    kb.ACT(dg[:, b * 5 + j, :], identf, AF.Copy, scale=colsT[:, b, j:j + 1])
                        win = [SB(e1, "win%d" % i, [128, 12, 516], BF16) for i in range(2)]
                        xc = [SB(e1, "xc%d" % i, [128, 12, 512], BF16) for i in range(2)]
                        xst = [SB(e1, "xst%d" % i, [128, 1280], BF16) for i in range(2)]
                        pcv = [PS(e1, "pcv%d" % i, [128, 512], F32) for i in range(2)]
                        ptr = [PS(e1, "ptr%d" % i, [128, 1024], BF16) for i in range(2)]
                        ptb = [PS(e1, "ptb%d" % i, [128, 256], BF16) for i in range(2)]
                        cntB = dict(j=0)

                        def b1_s0(i):
                            s, t0, TS_ = sts[i]
                            if i == len(sts) - 1:
                                kb.DMA("sp", win[i % 2][:, :, 0:TS_ + 2], XBCT[:, s["xoff"] + t0 - 2:s["xoff"] + t0 + TS_].rearrange("(b p) w -> p b w", p=128), key=s["name"])
                                for r in range(2):
                                    kb.CP("pool", win[i % 2][:, :, TS_ + 2 + r:TS_ + 3 + r], nx[:, :, 1 - r:2 - r])
                            else:
                                kb.DMA("sp", win[i % 2][:, :, 0:TS_ + 4], XBCT[:, s["xoff"] + t0 - 2:s["xoff"] + t0 + TS_ + 2].rearrange("(b p) w -> p b w", p=128), key=s["name"])

                        def b1_s1(i):
                            s, t0, TS_ = sts[i]
                            w_, x_ = win[i % 2], xc[i % 2]
                            g0 = s["off"] + t0
                            for b in range(12):
                                p_ = pcv[b % 2]
                                for j in range(5):
                                    kb.MM(p_[:, 0:TS_], dg[:, b * 5 + j, :], w_[:, b, j:j + TS_], start=(j == 0), stop=(j == 4))
                                kb.ACT(x_[:, b, 0:TS_], p_[:, 0:TS_], AF.Silu, bias=colsT[:, b, 5:6])
                            kb.DMA("sp", BCT[:, :, g0:g0 + TS_], x_[:, 8:12, 0:TS_], key=s["name"])

                        def b1_s2(i):
                            s, t0, TS_ = sts[i]
                            x_ = xc[i % 2]
                            g0 = s["off"] + t0
                            for j in range(TS_ // 128):
                                gt = g0 + j * 128
                                nj = cntB["j"]
                                cntB["j"] += 1
                                p_, pb_, xs_ = ptr[nj % 2], ptb[nj % 2], xst[nj % 2]
                                for b in range(8):
                                    kb.TR(p_[:, b * 128:(b + 1) * 128], x_[:, b, j * 128:(j + 1) * 128], identb, signal=(b == 7))
                                for b in range(2):
                                    kb.TR(pb_[:, b * 128:(b + 1) * 128], x_[:, 8 + b, j * 128:(j + 1) * 128], identb, signal=(b == 1))
                                kb.CP("dve", xs_[:, 0:1024], p_[:])
                                kb.CP("pool" if False else "dve", xs_[:, 1024:1280], pb_[:])
                                kb.DMA("sp", XS[gt:gt + 128, :], xs_[:, 0:1024], key=s["name"])
                                kb.DMA("sp", BTM[gt:gt + 128, :], xs_[:, 1024:1280], key=s["name"])
                        rmask = SB(e1, "rmask", [32, 16, 128], F32)
                        kb.MSET("dve", rmask[:], 1.0)
                        kb.MSET("dve", rmask[:, :, 0:1], 0.0)
                        dr = SB(e1, "dr", [32, 16, 128], F32)
                        da = SB(e1, "da", [32, 16, 128], F32)
                        cs = SB(e1, "cs", [32, 16, 128], F32)
                        ld = SB(e1, "ld", [32, 16, 128], F32)
                        al = SB(e1, "al", [32, 16, 128], F32)
                        r1 = SB(e1, "r1", [32, 16, 128], F32)
                        sp3 = SB(e1, "sp3", [32, 3, 2048], BF16)
                        assert NCH <= 32
                        b0_items = [0, 1]

                        def b0_piece(d_):
                            n_ = NCH
                            for s in seqs:
                                c0, nch = s["off"] // 128, s["n"] // 128
                                kb.DMA("sp", dr[c0:c0 + nch], DTT[d_ * 16:(d_ + 1) * 16, s["off"]:s["off"] + s["n"]].rearrange("h (c t) -> c h t", t=128), key=s["name"])
                            kb.TT("dve", da[0:n_], dr[0:n_], bc(aB[0:n_, d_ * 16:(d_ + 1) * 16], [n_, 16, 128]), ALU.mult)
                            kb.op("dve", lambda g: g.tensor_tensor_scan(out=cs[0:n_].rearrange("c h t -> c (h t)"), data0=rmask[0:n_].rearrange("c h t -> c (h t)"),
                                                                          data1=da[0:n_].rearrange("c h t -> c (h t)"), initial=0.0, op0=ALU.mult, op1=ALU.add),
                                  [rmask[0:n_], da[0:n_]], [cs[0:n_]])
                            if d_ == 1:
                                kb.TT("dve", r1[0:n_], da[0:n_], cs[0:n_], ALU.subtract)
                                kb.CP("dve", al[0:n_, :, 0:1], cs[0:n_, :, 127:128])
                                kb.TT("dve", cs[0:n_], r1[0:n_], al[0:n_, :, 0:1].to_broadcast([n_, 16, 128]), ALU.add)
                            kb.ACT(ld[0:n_], dr[0:n_], AF.Ln)
                            kb.TT("dve", al[0:n_], ld[0:n_], cs[0:n_], ALU.subtract)
                            for side, srcx in ((0, al), (1, cs)):
                                sx = srcx[0:n_].rearrange("c h t -> c (h t)")
                                r1f = r1[0:n_].rearrange("c h t -> c (h t)")
                                kb.CP("act", sp3[0:n_, 0, :], sx)
                                kb.TT("dve", r1f, sx, sp3[0:n_, 0, :], ALU.subtract)
                                kb.CP("act", sp3[0:n_, 1, :], r1f)
                                kb.TT("dve", r1f, r1f, sp3[0:n_, 1, :], ALU.subtract)
                                kb.CP("act", sp3[0:n_, 2, :], r1f)
                                for r_ in range(3):
                                    for s in seqs:
                                        c0, nch = s["off"] // 128, s["n"] // 128
                                        kb.DMA("sp", ABR[d_, side, 3 * side + r_, c0:c0 + nch, :], sp3[c0:c0 + nch, r_, :], key=s["name"])

                        nb0 = 0
                        for t_ in range(len(sts) + 2):
                            for k_, st_ in enumerate([b1_s0, b1_s1, b1_s2]):
                                if 0 <= t_ - k_ < len(sts):
                                    st_(t_ - k_)
                            if t_ in (1, 3):
                                b0_piece(nb0)
                                nb0 += 1
                        assert nb0 == 2

                    with scope(kb) as e2:
                        Sf = SB(e2, "Sf", [128, 1024], F32)
                        Sb_ = SB(e2, "Sb", [128, 1024], F32)
                        Sfb = SB(e2, "Sfb", [128, 1024], BF16)
                        Sbb = [SB(e2, "Sbb%d" % i, [128, 1024], BF16) for i in range(2)]
                        kb.MSET("dve", Sf[:], 0.0)
                        kb.MSET("dve", Sb_[:], 0.0)
                        kb.MSET("pool", Sfb[:], 0.0)
                        xsb = [SB(e2, "xsb%d" % i, [128, 1024], BF16) for i in range(3)]
                        btm = [SB(e2, "btm%d" % i, [128, 256], BF16) for i in range(3)]
                        dts = [SB(e2, "dts%d" % i, [128, 32], F32) for i in range(3)]
                        xdw = [SB(e2, "xdw%d" % i, [128, 1024], BF16) for i in range(2)]
                        sm_ = [SB(e2, "sm%d" % i, [128, 8, 32], F32) for i in range(2)]
                        psm = PS(e2, "psm", [128, 512], F32)
                        pO = PS(e2, "pO", [128, 1024], F32)

                        def small_terms(sl, d_, dt_ap):
                            S_ = sm_[sl]
                            o = d_ * 16
                            kb.TT("dve", S_[:, 0, o:o + 16], dt_ap[:, o:o + 16], aB[:, o:o + 16], ALU.mult)
                            kb.MM(psm[:, o:o + 16], triU if d_ == 0 else triL, S_[:, 0, o:o + 16], start=True, stop=True)
                            kb.MM(psm[:, 64 + o:64 + o + 16], onesf, S_[:, 0, o:o + 16], start=True, stop=True)
                            kb.CP("dve", S_[:, 1, o:o + 16], psm[:, o:o + 16])
                            kb.TT("dve", S_[:, 2, o:o + 16], psm[:, 64 + o:64 + o + 16], S_[:, 1, o:o + 16], ALU.subtract)
                            kb.ACT(S_[:, 3, o:o + 16], S_[:, 2, o:o + 16], AF.Exp)
                            kb.TT("dve", S_[:, 3, o:o + 16], S_[:, 3, o:o + 16], dt_ap[:, o:o + 16], ALU.mult)
                            kb.ACT(S_[:, 4, o:o + 16], S_[:, 1, o:o + 16], AF.Exp)
                            kb.ACT(S_[:, 5, o:o + 16], psm[:, 64 + o:64 + o + 16], AF.Exp)
                            return S_[:, 4, o:o + 16], S_[:, 3, o:o + 16], S_[:, 5, o:o + 16]

                        def mk_xdw(xd_t, xs_ap, w_ap):
                            kb.TT("pool", xd_t[:].rearrange("p (h d) -> p h d", h=16), xs_ap.rearrange("p (h d) -> p h d", h=16), bc(w_ap, [128, 16, 64]), ALU.mult)

                        def state_update(Smaster, bt_ap, etot_ap, xd_t):
                            for g in range(2):
                                kb.MM(pO[:, g * 512:(g + 1) * 512], bt_ap[:, g * 128:(g + 1) * 128], xd_t[:, g * 512:(g + 1) * 512], start=True, stop=True)
                            kb.TT("dve", Smaster[:].rearrange("p (h d) -> p h d", h=16), Smaster[:].rearrange("p (h d) -> p h d", h=16), bc(etot_ap, [128, 16, 64]), ALU.mult)
                            kb.TT("dve", Smaster[:], Smaster[:], pO[:], ALU.add)

                        kb.mark("B2%d" % l)
                        chf = []
                        for s in seqs:
                            for c in range(s["n"] // 128):
                                chf.append((s, c))

                        def bf_s0(i):
                            s, c = chf[i]
                            gt = s["off"] + c * 128
                            kb.DMA("sp", xsb[i % 3][:], XS[gt:gt + 128, :], key=s["name"])
                            kb.DMA("sp", btm[i % 3][:], BTM[gt:gt + 128, :], key=s["name"])
                            kb.DMA("sp", dts[i % 3][:], DTS[gt:gt + 128, :], key=s["name"])

                        def bf_s1(i):
                            s, c = chf[i]
                            gc = (s["off"] + c * 128) // 128
                            _, w_ap, et_ap = small_terms(i % 2, 0, dts[i % 3])
                            mk_xdw(xdw[i % 2], xsb[i % 3][:], w_ap)
                            kb.CP("act", Sbb[i % 2][:], Sf[:])
                            kb.DMA("sp", SFD[gc, :, :], Sbb[i % 2][:], key=gc)
                            state_update(Sf, btm[i % 3], et_ap, xdw[i % 2])
                        pipe(len(chf), [bf_s0, bf_s1])
                        nbS = SB(e2, "nbS", [128, 1024], F32)
                        with scope(kb) as ex2:
                            rcs2 = SB(ex2, "rcs2", [128, 2, 1024], F32)
                            kb.DMA("sp", SND2[:, :], Sf[:])
                            kb.CC(SND2[:, :], RCV2[:, :], groups)
                            kb.DMA("sp", rcs2[:], RCV2.rearrange("(r p) c -> p r c", p=128))
                            kb.TS("dve", nbS[:], rcs2[:, 0, :], selB[:, 0:1], None, ALU.mult)
                            kb.STT(nbS[:], rcs2[:, 1, :], selB[:, 1:2], nbS[:], ALU.mult, ALU.add)
                        chd = []
                        for s in seqs:
                            for c in reversed(range(s["n"] // 128)):
                                chd.append((s, c))
                        first_lat = seqs[0]["n"] // 128

                        def b2_s0(i):
                            s, c = chd[i]
                            gt = s["off"] + c * 128
                            kb.DMA("sp", xsb[i % 3][:], XS[gt:gt + 128, :], key=s["name"])
                            kb.DMA("sp", btm[i % 3][:], BTM[gt:gt + 128, :], key=s["name"])
                            kb.DMA("sp", dts[i % 3][:], DTS[gt:gt + 128, :], key=s["name"])

                        def b2_s1(i):
                            s, c = chd[i]
                            gc = (s["off"] + c * 128) // 128
                            if i == first_lat:
                                kb.CP("dve", Sb_[:], nbS[:])
                            _, w_ap, et_ap = small_terms(i % 2, 1, dts[i % 3])
                            mk_xdw(xdw[i % 2], xsb[i % 3][:], w_ap)
                            kb.CP("act", Sbb[i % 2][:], Sb_[:])
                            kb.DMA("sp", SBD[gc, :, :], Sbb[i % 2][:], key=gc)
                            state_update(Sb_, btm[i % 3], et_ap, xdw[i % 2])
                        pipe(len(chd), [b2_s0, b2_s1])

                        kb.mark("B3%d" % l)
                        with scope(kb) as e3:
                            bct = [SB(e3, "bct%d" % i, [128, 4, 128], BF16) for i in range(3)]
                            zsb = [SB(e3, "zsb%d" % i, [128, 1024], BF16) for i in range(2)]
                            sbe = [SB(e3, "sbe%d" % i, [128, 1024], BF16) for i in range(2)]
                            sfe = [SB(e3, "sfe%d" % i, [128, 1024], BF16) for i in range(2)]
                            ab6 = [SB(e3, "ab6%d" % i, [6, 2, 2, 2048], BF16) for i in range(2)]
                            cbT = SB(e3, "cbT", [128, 2, 128], BF16)
                            E_ = [SB(e3, "E%d" % i, [128, 4, 128], BF16) for i in range(2)]
                            MT = SB(e3, "MT", [128, 2, 16, 128], BF16)
                            ysb = [SB(e3, "ysb%d" % i, [128, 1024], F32) for i in range(2)]
                            t1 = SB(e3, "t1", [128, 1024], F32)
                            t2 = SB(e3, "t2", [128, 1024], F32)
                            us = [SB(e3, "u%d" % i, [128, 1024], F32) for i in range(2)]
                            obs = [SB(e3, "ob%d" % i, [128, 1024], BF16) for i in range(2)]
                            sq = SB(e3, "sq", [128, 512], F32)
                            oT = [SB(e3, "oT%d" % i, [128, 8, 128], BF16) for i in range(2)]
                            st2 = SB(e3, "st2", [128, 2, 2], F32)
                            pDs = [PS(e3, "pD%d" % i, [128, 512], F32) for i in range(2)]
                            pYs = [PS(e3, "pY%d" % i, [128, 512], F32) for i in range(2)]
                            pX = PS(e3, "pX", [128, 1024], BF16)
                            cntD = dict(d=0)
                            cha = []
                            for s in seqs:
                                for c in range(s["n"] // 128):
                                    cha.append((s, c))

                            def ny(s):
                                return not (last and s["name"] == "c")

                            def b3_s0(i):
                                s, c = cha[i]
                                gt = s["off"] + c * 128
                                gc = gt // 128
                                kb.DMA("sp", xsb[i % 3][:], XS[gt:gt + 128, :], key=s["name"])
                                kb.DMA("sp", dts[i % 3][:], DTS[gt:gt + 128, :], key=s["name"])
                                if ny(s):
                                    kb.DMA("sp", bct[i % 3][:], BCT[:, :, gt:gt + 128], key=s["name"])
                                    kb.DMA("sp", ab6[i % 2][:], ABR[:, :, :, gc, :].rearrange("d s r x -> r d s x"), key=s["name"])

                            def b3_s1(i):
                                s, c = cha[i]
                                gt = s["off"] + c * 128
                                gc = gt // 128
                                sl3, sl = i % 3, i % 2
                                if not ny(s):
                                    return
                                ef, wf, etf = small_terms(sl, 0, dts[sl3])
                                kb.DMA("sp", zsb[sl][:], ZS[gt:gt + 128, :], key=s["name"])
                                kb.DMA("sp", sbe[sl][:], SBD[gc, :, :], key=gc)
                                kb.DMA("sp", sfe[sl][:], SFD[gc, :, :], key=gc)
                                small_terms(sl, 1, dts[sl3])
                                for g in range(2):
                                    kb.MM(psm[:, 256 + g * 128:256 + (g + 1) * 128], bct[sl3][:, g, :], bct[sl3][:, 2 + g, :], start=True, stop=True)
                                kb.CP("dve", cbT[:].rearrange("p g t -> p (g t)"), psm[:, 256:512])
                                for g in range(2):
                                    for hb_ in range(2):
                                        es_ = []
                                        for d_ in range(2):
                                            pD = pDs[cntD["d"] % 2]
                                            e_ = E_[cntD["d"] % 2]
                                            cntD["d"] += 1
                                            kb.MM(pD[:], identb, mneg[:, d_, :, :].rearrange("p r t -> p (r t)"), start=True, stop=False, signal=False)
                                            for hh in range(4):
                                                h = g * 8 + hb_ * 4 + hh
                                                kb.MM(pD[:, hh * 128:(hh + 1) * 128], ab6[sl][:, d_, 0, h * 128:(h + 1) * 128],
                                                      ab6[sl][:, d_, 1, h * 128:(h + 1) * 128], start=False, stop=(hh == 3), signal=(hh == 3))
                                            kb.ACT(e_[:].rearrange("p h t -> p (h t)"), pD[:], AF.Exp)
                                            es_.append(e_)
                                        h0 = g * 8 + hb_ * 4
                                        kb.TT("dve", es_[0][:], es_[0][:], es_[1][:], ALU.add)
                                        kb.TT("dve", MT[:, 0, h0:h0 + 4, :], es_[0][:], cbT[:, g, :].unsqueeze(1).to_broadcast([128, 4, 128]), ALU.mult)
                                    pY = pYs[g]
                                    for hh in range(8):
                                        h = g * 8 + hh
                                        rhs = xsb[sl3][:, h * 64:(h + 1) * 64]
                                        kb.MM(pY[:, hh * 64:(hh + 1) * 64], MT[:, 0, h, :], rhs, start=True, stop=False, signal=False)
                                        kb.MM(pY[:, hh * 64:(hh + 1) * 64], dI[:, h, :], rhs, start=False, stop=True, signal=(hh == 7))
                                    kb.CP("act", ysb[sl][:, g * 512:(g + 1) * 512], pY[:])

                            def b3_s2(i):
                                s, c = cha[i]
                                if not ny(s):
                                    return
                                sl3, sl = i % 3, i % 2
                                S_ = sm_[sl]
                                ef, eb = S_[:, 4, 0:16], S_[:, 4, 16:32]
                                u = us[sl]
                                for g in range(2):
                                    kb.MM(pO[:, g * 512:(g + 1) * 512], bct[sl3][:, 2 + g, :], sfe[sl][:, g * 512:(g + 1) * 512], start=True, stop=True)
                                kb.TT("dve", t1[:].rearrange("p (h d) -> p h d", h=16), pO[:].rearrange("p (h d) -> p h d", h=16), bc(ef, [128, 16, 64]), ALU.mult)
                                for g in range(2):
                                    kb.MM(pO[:, g * 512:(g + 1) * 512], bct[sl3][:, 2 + g, :], sbe[sl][:, g * 512:(g + 1) * 512], start=True, stop=True)
                                kb.TT("dve", t2[:].rearrange("p (h d) -> p h d", h=16), pO[:].rearrange("p (h d) -> p h d", h=16), bc(eb, [128, 16, 64]), ALU.mult)
                                kb.TT("pool", u[:], t1[:], t2[:], ALU.add)
                                kb.TT("pool", u[:], u[:], ysb[sl][:], ALU.add)
                                kb.TT("pool", u[:], u[:], zsb[sl][:], ALU.mult)

                            def b3_s3(i):
                                s, c = cha[i]
                                if not ny(s):
                                    return
                                sl = i % 2
                                u = us[sl]
                                ss_ = st2[:, sl, :]
                                for g in range(2):
                                    kb.STT(sq[:], u[:, g * 512:(g + 1) * 512], 1.0, u[:, g * 512:(g + 1) * 512], ALU.mult, ALU.mult, accum=st2[:, sl, g:g + 1])
                                kb.TS("dve", ss_, ss_, 1.0 / 512.0, EPS, ALU.mult, ALU.add)
                                kb.ACT(ss_, ss_, AF.Ln)
                                kb.ACT(ss_, ss_, AF.Exp, scale=-0.5)
                                for g in range(2):
                                    kb.STT(obs[sl][:, g * 512:(g + 1) * 512], u[:, g * 512:(g + 1) * 512], st2[:, sl, g:g + 1], snwB[:, g * 512:(g + 1) * 512], ALU.mult, ALU.mult)

                            def b3_s4(i):
                                s, c = cha[i]
                                if not ny(s):
                                    return
                                gt = s["off"] + c * 128
                                sl = i % 2
                                for b in range(8):
                                    kb.TR(pX[:, b * 128:(b + 1) * 128], obs[sl][:, b * 128:(b + 1) * 128], identb, signal=(b == 7))
                                kb.CP("act", oT[sl][:].rearrange("p b t -> p (b t)"), pX[:])
                                kb.DMA("sp", OT[0:1024, gt:gt + 128].rearrange("(b p) t -> p b t", p=128), oT[sl][:], key=s["name"])
                            pipe(len(cha), [b3_s0, b3_s1, b3_s2, b3_s3, b3_s4])

                kb.mark("C%d" % l)
                with scope(kb) as ec:
                    ktb = SB(ec, "ktb", [64, 2, Ltot + 128], BF16)
                    vvb = SB(ec, "vvb", [128, NCH + 1, 130], BF16)
                    kb.CP("dve", ktb[:, :, Ltot:Ltot + 128], nbr1[0:64, 84:340].rearrange("p (g t) -> p g t", g=2))
                    kb.CP("dve", vvb[:, NCH, :], nbr1[:, 340:470])
                    for s in seqs:
                        o_, n_ = s["off"], s["n"]
                        kb.DMA("sp", ktb[:, :, o_:o_ + n_], KT[:, :, o_:o_ + n_], key=s["name"])
                        kb.DMA("sp", vvb[:, o_ // 128:(o_ + n_) // 128, :], VV[o_:o_ + n_, :].rearrange("(c p) x -> p c x", p=128), key=s["name"])
                    qtb = [SB(ec, "qtb%d" % i, [64, 8, 128], BF16) for i in range(3)]
                    gab = [SB(ec, "gab%d" % i, [64, 8, 128], BF16) for i in range(3)]
                    Ee = [SB(ec, "Ee%d" % i, [128, 512], BF16) for i in range(20)]
                    den = [SB(ec, "den%d" % i, [64, 512], F32) for i in range(2)]
                    oa = [SB(ec, "oa%d" % i, [64, 512], F32) for i in range(2)]
                    oaT = [SB(ec, "oaT%d" % i, [64, 512], BF16) for i in range(2)]
                    pS = [PS(ec, "pS%d" % i, [128, 512], F32) for i in range(3)]
                    pV = [PS(ec, "pV%d" % i, [64, 512], F32) for i in range(2)]
                    pU = [PS(ec, "pU%d" % i, [64, 512], F32) for i in range(2)]
                    qbs = []
                    for s in seqs:
                        if last and s["name"] == "c":
                            continue
                        for qb in range(s["n"] // 128):
                            qbs.append((s, qb))
                    cntC = dict(s=0, o=0)

                    def keyblocks(s, qb):
                        kbl = [(0, None), (1, None)]
                        if s["name"] == "l":
                            nb = s["n"] // 128
                            c0 = s["off"] // 128
                            if qb > 0:
                                kbl.append((c0 + qb - 1, 1))
                            kbl.append((c0 + qb, None))
                            if qb < nb - 1:
                                kbl.append((c0 + qb + 1, 0))
                            else:
                                kbl.append((NCH, 2))
                        return kbl

                    def c_s0(i):
                        s, qb = qbs[i]
                        gt = s["off"] + qb * 128
                        kb.DMA("sp", qtb[i % 3][:], QT[:, :, gt:gt + 128], key=s["name"])
                        kb.DMA("sp", gab[i % 3][:], GAT[:, gt:gt + 128].rearrange("(h d) t -> d h t", d=64), key=s["name"])

                    def c_s1(i):
                        s, qb = qbs[i]
                        q_ = qtb[i % 3]
                        for g in range(2):
                            for ki, (kc, md) in enumerate(keyblocks(s, qb)):
                                p_ = pS[cntC["s"] % 3]
                                cntC["s"] += 1
                                if md is not None:
                                    kb.MM(p_[:], identb, mneg[:, md, :, :].rearrange("p r t -> p (r t)"), start=True, stop=False, signal=False)
                                kb.MM(p_[:], ktb[:, g, kc * 128:(kc + 1) * 128], q_[:, g * 4:(g + 1) * 4, :].rearrange("p h t -> p (h t)"),
                                      start=(md is None), stop=True)
                                kb.ACT(Ee[(i % 2) * 10 + g * 5 + ki][:], p_[:], AF.Exp)

                    def c_s2(i):
                        s, qb = qbs[i]
                        gt = s["off"] + qb * 128
                        g_ = gab[i % 3]
                        kbl = keyblocks(s, qb)
                        for g in range(2):
                            no = cntC["o"]
                            cntC["o"] += 1
                            pv, pu = pV[no % 2], pU[no % 2]
                            for ki, (kc, md) in enumerate(kbl):
                                kb.MM(pv[:], vvb[:, kc, g * 65:g * 65 + 64], Ee[(i % 2) * 10 + g * 5 + ki][:], start=(ki == 0), stop=(ki == len(kbl) - 1))
                            for ki, (kc, md) in enumerate(kbl):
                                kb.MM(pu[:], onesb[:, 0:64], Ee[(i % 2) * 10 + g * 5 + ki][:], start=(ki == 0), stop=(ki == len(kbl) - 1))
                            dn = den[no % 2]
                            kb.TT("dve", dn[:].rearrange("p (r t) -> p r t", r=4), pu[:].rearrange("p (r t) -> p r t", r=4), bc(sinkB[0:64, g * 4:(g + 1) * 4], [64, 4, 128]), ALU.add)
                            kb.ACT(dn[:], dn[:], AF.Ln)
                            kb.ACT(dn[:], dn[:], AF.Exp, scale=-1.0)
                            o_ = oa[no % 2]
                            kb.TT("dve", o_[:], pv[:], dn[:], ALU.mult)
                            ot_ = oaT[no % 2]
                            kb.TT("pool", ot_[:], o_[:], g_[:, g * 4:(g + 1) * 4, :].rearrange("p h t -> p (h t)"), ALU.mult)
                            kb.DMA("sp", OT[1024 + g * 256:1024 + (g + 1) * 256, gt:gt + 128].rearrange("(r d) t -> d r t", d=64), ot_[:].rearrange("p (r t) -> p r t", r=4), key=s["name"])
                    pipe(len(qbs), [c_s0, c_s1, c_s2])

                kb.mark("D%d" % l)
                with scope(kb) as ed:
                    dgc = SB(ed, "dgc", [128, 124, 128], BF16)
                    for b in range(4):
                        for j in range(31):
                            if (b + j) % 2:
                                kb.TS("dve", dgc[:, b * 31 + j, :], identf, colsT[:, b, 6 + j:7 + j], None, ALU.mult)
                            else:
                                kb.ACT(dgc[:, b * 31 + j, :], identf, AF.Copy, scale=colsT[:, b, 6 + j:7 + j])
                    pww = SB(ed, "pww", [128, 4, 512], BF16)
                    pwst = SB(ed, "pwst", [128, 4, 512], F32)
                    kb.DMA("sp", pwst[:], conv_pw_w[l].rearrange("(k p) n -> p k n", p=128))
                    kb.CP("act", pww[:], pwst[:])
                    gw = [SB(ed, "gw%d" % i, [128, 4, 542], BF16) for i in range(2)]
                    gcb = [SB(ed, "gcb%d" % i, [128, 4, 512], BF16) for i in range(2)]
                    hbb = [SB(ed, "hbb%d" % i, [128, 4, 512], BF16) for i in range(2)]
                    hsq = [SB(ed, "hsq%d" % i, [128, 4, 512], BF16) for i in range(2)]
                    mean = SB(ed, "mean", [128, 512], F32)
                    rstd = SB(ed, "rstd", [128, 512], F32)
                    xcn = [SB(ed, "xcn%d" % i, [128, 512], F32) for i in range(2)]
                    h2 = [SB(ed, "h2%d" % i, [128, 4, 512], BF16) for i in range(2)]
                    oc = [SB(ed, "oc%d" % i, [128, 512], BF16) for i in range(2)]
                    pc = [PS(ed, "pc%d" % i, [128, 512], F32) for i in range(2)]
                    pm = PS(ed, "pm", [128, 512], F32)
                    pe2 = PS(ed, "pe2", [128, 512], F32)
                    pw_ = [PS(ed, "pw%d" % i, [128, 512], F32) for i in range(2)]
                    dst = [x for x in sts if not (last and x[0]["name"] == "c")]

                    def d_s0(i):
                        s, t0, TS_ = dst[i]
                        if i == len(dst) - 1:
                            kb.DMA("sp", gw[i % 2][:, :, 0:TS_ + 15], GLUT[:, s["goff"] + t0 - 15:s["goff"] + t0 + TS_].rearrange("(b p) w -> p b w", p=128), key=s["name"])
                            for r in range(15):
                                kb.CP("pool", gw[i % 2][:, :, TS_ + 15 + r:TS_ + 16 + r], ng[:, :, 14 - r:15 - r])
                        else:
                            kb.DMA("sp", gw[i % 2][:, :, 0:TS_ + 30], GLUT[:, s["goff"] + t0 - 15:s["goff"] + t0 + TS_ + 15].rearrange("(b p) w -> p b w", p=128), key=s["name"])

                    def d_s1(i):
                        s, t0, TS_ = dst[i]
                        w_ = gw[i % 2]
                        for b in range(4):
                            p_ = pc[b % 2]
                            for j in range(31):
                                kb.MM(p_[:, 0:TS_], dgc[:, b * 31 + j, :], w_[:, b, j:j + TS_], start=(j == 0), stop=(j == 30))
                            kb.ACT(hbb[i % 2][:, b, 0:TS_], p_[:, 0:TS_], AF.Identity, bias=colsT[:, b, 37:38])
                            kb.TT("dve", hsq[i % 2][:, b, 0:TS_], hbb[i % 2][:, b, 0:TS_], hbb[i % 2][:, b, 0:TS_], ALU.mult)

                    def d_s2(i):
                        s, t0, TS_ = dst[i]
                        g0 = s["off"] + t0
                        kb.DMA("sp", gcb[i % 2][:, :, 0:TS_], GCT[:, g0:g0 + TS_].rearrange("(b p) w -> p b w", p=128), key=s["name"])
                        hb_, hs_ = hbb[i % 2], hsq[i % 2]
                        for b in range(4):
                            kb.MM(pm[:, 0:TS_], ones512[:], hb_[:, b, 0:TS_], start=(b == 0), stop=(b == 3))
                        for b in range(4):
                            kb.MM(pe2[:, 0:TS_], ones512[:], hs_[:, b, 0:TS_], start=(b == 0), stop=(b == 3))
                        kb.CP("act", mean[:, 0:TS_], pm[:, 0:TS_])
                        kb.TT("dve", rstd[:, 0:TS_], mean[:, 0:TS_], mean[:, 0:TS_], ALU.mult)
                        kb.TT("dve", rstd[:, 0:TS_], pe2[:, 0:TS_], rstd[:, 0:TS_], ALU.subtract)
                        kb.ACT(rstd[:, 0:TS_], rstd[:, 0:TS_], AF.Ln, bias=epsb[:])
                        kb.ACT(rstd[:, 0:TS_], rstd[:, 0:TS_], AF.Exp, scale=-0.5)
                        for b in range(4):
                            x_ = xcn[b % 2]
                            kb.TT("dve", x_[:, 0:TS_], hb_[:, b, 0:TS_], mean[:, 0:TS_], ALU.subtract)
                            kb.TT("dve", x_[:, 0:TS_], x_[:, 0:TS_], rstd[:, 0:TS_], ALU.mult)
                            kb.ACT(h2[i % 2][:, b, 0:TS_], x_[:, 0:TS_], AF.Silu, bias=colsT[:, b, 39:40], scale=colsT[:, b, 38:39])

                    def d_s3(i):
                        s, t0, TS_ = dst[i]
                        g0 = s["off"] + t0
                        for co in range(4):
                            p_ = pw_[co % 2]
                            for ci in range(4):
                                kb.MM(p_[:, 0:TS_], pww[:, ci, co * 128:(co + 1) * 128], h2[i % 2][:, ci, 0:TS_], start=(ci == 0), stop=(ci == 3))
                            o_ = oc[co % 2]
                            kb.STT(o_[:, 0:TS_], p_[:, 0:TS_], colsT[:, co, 40:41], gcb[i % 2][:, co, 0:TS_], ALU.add, ALU.mult)
                            kb.DMA("sp", OT[1536 + co * 128:1536 + (co + 1) * 128, g0:g0 + TS_], o_[:, 0:TS_], key=s["name"])
                    pipe(len(dst), [d_s0, d_s1, d_s2, d_s3])

                kb.mark("E%d" % l)
                with scope(kb) as ee:
                    wo = SB(ee, "wo", [128, 16, D], BF16)
                    wov = w_out[l].rearrange("(k p) n -> p k n", p=128)
                    with scope(kb) as eow:
                        wos = [SB(eow, "wos%d" % i, [128, 4, D], F32) for i in range(2)]
                        for k4 in range(4):
                            kb.DMA("sp", wos[k4 % 2][:], wov[:, k4 * 4:(k4 + 1) * 4, :])
                            for j4 in range(4):
                                kb.CP(["dve", "pool", "act", "pool"][j4], wo[:, k4 * 4 + j4, :], wos[k4 % 2][:, j4, :])
                    fnw = SB(ee, "fnw", [128, D], F32)
                    kb.DMA("sp", fnw[:], final_norm_w[0, :].partition_broadcast(128))
                    otb = [SB(ee, "otb%d" % i, [128, 16, 128], BF16) for i in range(3)]
                    xo = [SB(ee, "xo%d" % i, [128, D], F32) for i in range(3)]
                    xn = [SB(ee, "xn%d" % i, [128, D], F32) for i in range(2)]
                    yo = [SB(ee, "yo%d" % i, [128, D], F32) for i in range(2)]
                    jk = SB(ee, "jk", [128, D], BF16)
                    s1 = SB(ee, "s1", [128, 2], F32)
                    po = [PS(ee, "po%d" % i, [128, 1024], F32) for i in range(2)]
                    che = []
                    for s in seqs:
                        if last and s["name"] == "c":
                            continue
                        for c in range(s["n"] // 128):
                            che.append((s, c))

                    def e_s0(i):
                        s, c = che[i]
                        gt = s["off"] + c * 128
                        kb.DMA("sp", otb[i % 3][:], OT[:, gt:gt + 128].rearrange("(k p) t -> p k t", p=128), key=s["name"])
                        if l == 0:
                            src = (ctx_in if s["name"] == "c" else x_in)[c * 128:(c + 1) * 128, :]
                        else:
                            src = XR[gt:gt + 128, :]
                        kb.DMA("sp", xo[i % 3][:], src, key=(s["name"], c * 128))

                    def e_nop(i):
                        pass

                    def e_s1(i):
                        s, c = che[i]
                        m = 0 if s["name"] == "c" else 1
                        sl = i % 2
                        p_ = po[sl]
                        for nb_ in range(2):
                            for kt in range(16):
                                kb.MM(p_[:, nb_ * 512:(nb_ + 1) * 512], otb[i % 3][:, kt, :], wo[:, kt, nb_ * 512:(nb_ + 1) * 512], start=(kt == 0), stop=(kt == 15))
                        kb.TT("dve", xn[sl][:], p_[:], modB[:, m, 2, :], ALU.mult)
                        kb.TT("pool", xn[sl][:], xn[sl][:], xo[i % 3][:], ALU.add)

                    def e_s2(i):
                        s, c = che[i]
                        gt = s["off"] + c * 128
                        sl = i % 2
                        if not last:
                            kb.DMA("sp", XR[gt:gt + 128, :], xn[sl][:], key=(s["name"], c * 128))
                        else:
                            sc_ = s1[:, sl:sl + 1]
                            kb.STT(yo[sl][:], xn[sl][:], 1.0, xn[sl][:], ALU.mult, ALU.mult, accum=sc_)
                            kb.TS("dve", sc_, sc_, 1.0 / D, EPS, ALU.mult, ALU.add)
                            kb.ACT(sc_, sc_, AF.Ln)
                            kb.ACT(sc_, sc_, AF.Exp, scale=-0.5)
                            kb.STT(yo[sl][:], xn[sl][:], sc_, fnw[:], ALU.mult, ALU.mult)
                            kb.DMA("sp", out[c * 128:(c + 1) * 128, :], yo[sl][:], key="out")
                    pipe(len(che), [e_s0, e_nop, e_s1, e_s2])
        kb.mark("end")
        kb.finish()
    return nc, kb


def host_consts(L, T, pos, grid_w=64):
    k = np.arange(128)[:, None]
    t = np.arange(128)[None, :]
    cst = np.zeros((128, 7, 128), np.float32)
    cst[:, 0] = np.eye(128)
    cst[:, 1] = (k <= t)
    cst[:, 2] = (k >= t)
    cst[:, 3] = 1.0
    cst[:, 4] = np.where(k <= t, 0.0, NEG)
    cst[:, 5] = np.where(k >= t, 0.0, NEG)
    cst[:, 6] = np.where(k + t >= 127, 0.0, NEG)
    rope = np.zeros((T + L, 256), np.float32)
    rope[:, 0:64] = 1.0
    rope[:, 128:192] = 0.125
    row = (pos // grid_w).astype(np.float32)
    col = (pos % grid_w).astype(np.float32)
    inv = (10000.0 ** (-np.arange(0, 32, 2, dtype=np.float32) / 32.0)).astype(np.float32)
    ang = np.stack([row[:, None] * inv, col[:, None] * inv], axis=1).astype(np.float32)
    cs_, sn_ = np.cos(ang), np.sin(ang)
    cos2 = np.repeat(cs_[:, :, None, :], 2, axis=2).reshape(L, 64)
    rope[T:, 0:64] = cos2
    rope[T:, 64:96] = (-sn_).reshape(L, 32)
    rope[T:, 96:128] = sn_.reshape(L, 32)
    rope[T:, 128:192] = cos2 * 0.125
    rope[T:, 192:224] = (-sn_).reshape(L, 32) * 0.125
    rope[T:, 224:256] = sn_.reshape(L, 32) * 0.125
    return cst, rope


_CACHE = {}


def run(inputs, L, T, NL, batches):
    npairs = len(batches)
    key = (L, T, NL, npairs)
    if key not in _CACHE:
        _CACHE[key] = build(L, T, NL, npairs)[0]
    nc = _CACHE[key]
    f = lambda a: np.ascontiguousarray(np.asarray(a, dtype=np.float32))
    base = {
        "c_ctx": f(inputs["c_ctx"]).reshape(128, 8),
        "w_mod": f(inputs["w_mod"]), "b_mod": f(inputs["b_mod"]), "norm_w": f(inputs["norm_w"]),
        "ssd_conv_b": f(inputs["ssd_conv_b"]),
        "ssd_d": f(inputs["ssd_d"]), "ssd_norm_w": f(inputs["ssd_norm_w"]), "attn_sink": f(inputs["attn_sink"]),
        "conv_dw_b": f(inputs["conv_dw_b"]), "conv_ln_w": f(inputs["conv_ln_w"]),
        "conv_ln_b": f(inputs["conv_ln_b"]), "conv_pw_w": f(inputs["conv_pw_w"]), "conv_pw_b": f(inputs["conv_pw_b"]),
        "w_out": f(inputs["w_out"]), "final_norm_w": f(inputs["final_norm_w"]).reshape(1, D),
    }
    w_in = f(inputs["w_in"])
    w_in_m = w_in.copy()
    w_in_m[:, :, 2560:2576] = w_in[:, :, 2576:2592]
    w_in_m[:, :, 2576:2592] = w_in[:, :, 2560:2576]
    dtb = f(inputs["ssd_dt_bias"])
    alog = f(inputs["ssd_a_log"])
    scw = f(inputs["ssd_conv_w"])
    dww = f(inputs["conv_dw_w"])
    half = []
    for h in range(2):
        pos = np.arange(L) if h == 0 else (2 * L - 1 - np.arange(L))
        cst, rope = host_consts(L, T, pos)
        sel = np.zeros((128, 2), np.float32)
        sel[:, 1 - h] = 1.0
        d = dict(base)
        d.update({
            "cst": cst, "rope": rope, "sel": sel,
            "w_in": w_in if h == 0 else w_in_m,
            "ssd_dt_bias": f(dtb if h == 0 else dtb[:, ::-1]).reshape(NL, 32),
            "ssd_a_log": f(alog if h == 0 else alog[:, ::-1]).reshape(NL, 32),
            "ssd_conv_w": f(scw if h == 0 else scw[:, ::-1]),
            "conv_dw_w": f(dww if h == 0 else dww[:, ::-1]),
        })
        half.append(d)
    x = f(inputs["x"])
    c = f(inputs["c"])
    ctx = f(inputs["ctx"])
    in_maps = []
    for b in batches:
        for h in range(2):
            m = dict(half[h])
            if h == 0:
                m["x"] = f(x[b, 0:L])
                m["ctx"] = f(ctx[b])
            else:
                m["x"] = f(x[b, L:2 * L][::-1])
                m["ctx"] = f(ctx[b][::-1])
            m["c"] = c[b].reshape(128, 8)
            in_maps.append(m)
    res = run_bass_kernel_spmd(nc, in_maps, core_ids=list(range(2 * npairs)))
    outs = []
    for i in range(npairs):
        o0 = res.results[2 * i]["out"]
        o1 = res.results[2 * i + 1]["out"][::-1]
        outs.append(np.concatenate([o0, o1], axis=0))
    return outs


def kernel(**inputs):
    B = np.asarray(inputs["x"]).shape[0]
    outs = run(inputs, 2048, 256, 2, list(range(B)))
    return np.stack(outs, axis=0).astype(np.float32)
```
